# Optimizing a Trainium2 kernel written in Bass

```python
import math
import jax, jax.numpy as jnp
from jax import lax
import numpy as np

D_MODEL = 1024
BATCH = 8
SEQ = 4096
DEPTH = 4

N_MIXERS = 2
N_SUBLAYERS = 3
FFN_RES = 0.5
D_FF = 2816
R_HEADS = 4
R_DK = 256
R_DV = 512
R_QK = R_HEADS * R_DK
R_VTOT = R_HEADS * R_DV
R_IN = 2 * R_QK + 2 * R_VTOT
CHUNK = 128
M_HEADS = 8
M_NOPE = 128
M_ROPE = 64
M_V = 128
Q_LORA = 384
KV_LORA = 256
M_DOWN = Q_LORA + KV_LORA + M_ROPE
M_SCALE = (M_NOPE + M_ROPE) ** -0.5
Q_BLOCK = 128
ROPE_THETA = 10000.0
RMS_EPS = 1e-6
GN_EPS = 1e-5

kernel_name = 'hybrid_retention_mla_macaron_adaln_encoder'


def rms_norm(x, g):
    xf = x.astype(jnp.float32)
    y = xf * lax.rsqrt(jnp.mean(xf * xf, axis=-1, keepdims=True) + RMS_EPS)
    return (y * g.astype(jnp.float32)).astype(x.dtype)


def adaln(x, g, shift, scale):
    return rms_norm(x, g) * (1 + scale[:, None, :]) + shift[:, None, :]


def rope(x, pos):
    d = x.shape[-1]
    inv = jnp.power(jnp.float32(ROPE_THETA), -jnp.arange(0, d, 2, dtype=jnp.float32) / d)
    ang = pos.astype(jnp.float32)[:, :, None, None] * inv
    cos, sin = jnp.cos(ang), jnp.sin(ang)
    x1, x2 = jnp.split(x.astype(jnp.float32), 2, axis=-1)
    out = jnp.concatenate([x1 * cos - x2 * sin, x1 * sin + x2 * cos], axis=-1)
    return out.astype(x.dtype)


def swiglu(h, w_in, w_out):
    g, u = jnp.split(h @ w_in, 2, axis=-1)
    return (jax.nn.silu(g) * u) @ w_out


def retention_scan(q, k, v, log_gamma, include_diag):
    B, H, S, dk = q.shape
    dv = v.shape[-1]
    nc = S // CHUNK
    t = jnp.arange(CHUNK, dtype=jnp.float32)
    diff = t[:, None] - t[None, :]
    mask = (diff >= 0) if include_diag else (diff > 0)
    lg = log_gamma[:, None, None]
    dmat = jnp.where(mask, jnp.exp(lg * jnp.maximum(diff, 0.0)), 0.0)
    q_dec = jnp.exp(log_gamma[:, None] * (t + 1))[:, :, None]
    k_dec = jnp.exp(log_gamma[:, None] * (CHUNK - 1 - t))[:, :, None]
    c_dec = jnp.exp(log_gamma * CHUNK)[:, None, None]

    def to_chunks(a):
        return jnp.moveaxis(a.reshape(B, H, nc, CHUNK, a.shape[-1]), 2, 0)

    def step(state, inp):
        qc, kc, vc = inp
        s = jnp.einsum('bhtd,bhsd->bhts', qc, kc) * dmat
        inner = jnp.einsum('bhts,bhse->bhte', s, vc)
        cross = jnp.einsum('bhtd,bhde->bhte', qc * q_dec, state)
        state = c_dec * state + jnp.einsum('bhsd,bhse->bhde', kc * k_dec, vc)
        return state, inner + cross

    state0 = jnp.zeros((B, H, dk, dv), jnp.float32)
    _, out = lax.scan(step, state0, (to_chunks(q), to_chunks(k), to_chunks(v)))
    return jnp.moveaxis(out, 0, 2).reshape(B, H, S, dv)


def retention_mixer(h, pos, w_in, w_out, gn_g, gn_b, decay_fwd, decay_bwd):
    B, S, _ = h.shape
    q, k, v, g = jnp.split(h @ w_in, [R_QK, 2 * R_QK, 2 * R_QK + R_VTOT], axis=-1)
    q = rope(q.reshape(B, S, R_HEADS, R_DK), pos)
    k = rope(k.reshape(B, S, R_HEADS, R_DK), pos) * (R_DK ** -0.5)
    v = v.reshape(B, S, R_HEADS, R_DV)
    q, k, v = (jnp.swapaxes(a, 1, 2).astype(jnp.float32) for a in (q, k, v))
    lg_f = jax.nn.log_sigmoid(decay_fwd.astype(jnp.float32))
    lg_b = jax.nn.log_sigmoid(decay_bwd.astype(jnp.float32))
    flip = lambda a: jnp.flip(a, axis=2)
    y_f = retention_scan(q, k, v, lg_f, True)
    y_b = flip(retention_scan(flip(q), flip(k), flip(v), lg_b, False))
    y = y_f + y_b
    mu = jnp.mean(y, axis=-1, keepdims=True)
    var = jnp.mean(jnp.square(y - mu), axis=-1, keepdims=True)
    y = (y - mu) * lax.rsqrt(var + GN_EPS)
    y = jnp.swapaxes(y, 1, 2).reshape(B, S, R_VTOT)
    y = (y * gn_g.astype(jnp.float32) + gn_b.astype(jnp.float32)).astype(h.dtype)
    return (jax.nn.silu(g) * y) @ w_out


def mla_attention(q_nope, q_rope, k_nope, k_rope, v):
    B, S, H, _ = q_nope.shape
    nb = S // Q_BLOCK

    def blocks(a):
        return jnp.moveaxis(a.reshape(B, nb, Q_BLOCK, *a.shape[2:]), 1, 0)

    def one_block(qb):
        qn, qr = qb
        s = (jnp.einsum('bqhd,bkhd->bhqk', qn, k_nope)
             + jnp.einsum('bqhd,bkd->bhqk', qr, k_rope))
        p = jax.nn.softmax(s.astype(jnp.float32) * M_SCALE, axis=-1).astype(v.dtype)
        return jnp.einsum('bhqk,bkhd->bqhd', p, v)

    o = lax.map(one_block, (blocks(q_nope), blocks(q_rope)))
    return jnp.moveaxis(o, 0, 1).reshape(B, S, H, v.shape[-1])


def mla_mixer(h, pos, w_down, q_norm_g, kv_norm_g, w_uq, w_ukv, w_o):
    B, S, _ = h.shape
    c_q, c_kv, k_rope = jnp.split(h @ w_down, [Q_LORA, Q_LORA + KV_LORA], axis=-1)
    q = (rms_norm(c_q, q_norm_g) @ w_uq).reshape(B, S, M_HEADS, M_NOPE + M_ROPE)
    q_nope, q_rope = jnp.split(q, [M_NOPE], axis=-1)
    q_rope = rope(q_rope, pos)
    kv = (rms_norm(c_kv, kv_norm_g) @ w_ukv).reshape(B, S, M_HEADS, M_NOPE + M_V)
    k_nope, v = jnp.split(kv, [M_NOPE], axis=-1)
    k_rope = rope(k_rope[:, :, None, :], pos)[:, :, 0, :]
    o = mla_attention(q_nope, q_rope, k_nope, k_rope, v)
    return o.reshape(B, S, M_HEADS * M_V) @ w_o


def setup_inputs(seed: int = 0) -> dict:
    key = jax.random.key(seed)
    ks = jax.random.split(key, 24)
    n_a = (DEPTH + N_MIXERS - 1) // N_MIXERS
    n_b = DEPTH // N_MIXERS
    f32 = jnp.float32
    nrm = lambda k, shape, s: jax.random.normal(k, shape, f32) * s
    strides = jax.random.randint(ks[2], (BATCH, SEQ), 1, 3, dtype=jnp.int32)
    positions = (jnp.cumsum(strides, axis=1) - 1).astype(jnp.int32)
    base = np.log(2.0 ** (5 + np.arange(R_HEADS)) - 1.0).astype(np.float32)
    base = jnp.asarray(base)
    return {
        'x': nrm(ks[0], (BATCH, SEQ, D_MODEL), 1.0),
        'c': nrm(ks[1], (BATCH, D_MODEL), 1.0),
        'positions': positions,
        'norm_g': 1.0 + nrm(ks[3], (DEPTH, N_SUBLAYERS, D_MODEL), 0.02),
        'final_norm_g': 1.0 + nrm(ks[4], (D_MODEL,), 0.02),
        'mod_w': nrm(ks[5], (DEPTH, D_MODEL, 3 * N_SUBLAYERS * D_MODEL), 0.5 * D_MODEL ** -0.5),
        'mod_b': nrm(ks[6], (DEPTH, 3 * N_SUBLAYERS * D_MODEL), 0.02),
        'ffn_w_in': nrm(ks[7], (DEPTH, 2, D_MODEL, 2 * D_FF), D_MODEL ** -0.5),
        'ffn_w_out': nrm(ks[8], (DEPTH, 2, D_FF, D_MODEL), D_FF ** -0.5),
        'ret_w_in': nrm(ks[9], (n_a, D_MODEL, R_IN), D_MODEL ** -0.5),
        'ret_w_out': nrm(ks[10], (n_a, R_VTOT, D_MODEL), R_VTOT ** -0.5),
        'ret_gn_g': 1.0 + nrm(ks[11], (n_a, R_VTOT), 0.02),
        'ret_gn_b': nrm(ks[12], (n_a, R_VTOT), 0.02),
        'ret_decay_fwd': base + nrm(ks[13], (n_a, R_HEADS), 0.1),
        'ret_decay_bwd': base + nrm(ks[14], (n_a, R_HEADS), 0.1),
        'mla_w_down': nrm(ks[15], (n_b, D_MODEL, M_DOWN), D_MODEL ** -0.5),
        'mla_q_norm_g': 1.0 + nrm(ks[16], (n_b, Q_LORA), 0.02),
        'mla_kv_norm_g': 1.0 + nrm(ks[17], (n_b, KV_LORA), 0.02),
        'mla_w_uq': nrm(ks[18], (n_b, Q_LORA, M_HEADS * (M_NOPE + M_ROPE)), Q_LORA ** -0.5),
        'mla_w_ukv': nrm(ks[19], (n_b, KV_LORA, M_HEADS * (M_NOPE + M_V)), KV_LORA ** -0.5),
        'mla_w_o': nrm(ks[20], (n_b, M_HEADS * M_V, D_MODEL), (M_HEADS * M_V) ** -0.5),
    }


def reference(x, c, positions, norm_g, final_norm_g, mod_w, mod_b, ffn_w_in, ffn_w_out,
              ret_w_in, ret_w_out, ret_gn_g, ret_gn_b, ret_decay_fwd, ret_decay_bwd,
              mla_w_down, mla_q_norm_g, mla_kv_norm_g, mla_w_uq, mla_w_ukv, mla_w_o):
    c_act = jax.nn.silu(c)
    for i in range(DEPTH):
        mod = c_act @ mod_w[i] + mod_b[i]
        sh1, sc1, g1, sh2, sc2, g2, sh3, sc3, g3 = jnp.split(mod, 3 * N_SUBLAYERS, axis=-1)
        h = adaln(x, norm_g[i, 0], sh1, sc1)
        x = x + FFN_RES * g1[:, None, :] * swiglu(h, ffn_w_in[i, 0], ffn_w_out[i, 0])
        h = adaln(x, norm_g[i, 1], sh2, sc2)
        j = i // N_MIXERS
        if i % N_MIXERS == 0:
            y = retention_mixer(h, positions, ret_w_in[j], ret_w_out[j], ret_gn_g[j], ret_gn_b[j],
                                ret_decay_fwd[j], ret_decay_bwd[j])
        else:
            y = mla_mixer(h, positions, mla_w_down[j], mla_q_norm_g[j], mla_kv_norm_g[j],
                          mla_w_uq[j], mla_w_ukv[j], mla_w_o[j])
        x = x + g2[:, None, :] * y
        h = adaln(x, norm_g[i, 2], sh3, sc3)
        x = x + FFN_RES * g3[:, None, :] * swiglu(h, ffn_w_in[i, 1], ffn_w_out[i, 1])
    return rms_norm(x, final_norm_g)
```

```python
import contextlib
import numpy as np
import concourse.bass as bass
import concourse.mybir as mybir
from concourse.bass_utils import run_bass_kernel_spmd

F32 = mybir.dt.float32
BF16 = mybir.dt.bfloat16
I32 = mybir.dt.int32
AF = mybir.ActivationFunctionType
ALU = mybir.AluOpType
AX = mybir.AxisListType

D = 1024
DFF = 2816
KC = D // 128
FC = DFF // 128
RMS_EPS = 1e-6

ANNOTATE = False
ENGS = ["pe", "act", "dve", "pool", "sp"]
NDSEM = {"sp": 12, "pool": 8, "act": 4, "pe": 0, "dve": 0}


class Buf:
    __slots__ = ("name", "w", "rc", "rd")

    def __init__(self, name=""):
        self.name = name
        self.w = None
        self.rc = {}
        self.rd = set()
        reg = Arena.cur
        if reg is not None:
            for r in Arena.regions:
                if r is not reg and r[0] < reg[1] and reg[0] < r[1]:
                    for ob in r[2]:
                        self._inherit(ob)
            reg[2].append(self)

    def _inherit(self, ob):
        if ob.w is not None:
            if ob.w[0] == "d":
                self.rd.add(ob.w[1])
            elif self.rc.get(ob.w[1], -1) < ob.w[2]:
                self.rc[ob.w[1]] = ob.w[2]
        for e, i in ob.rc.items():
            if self.rc.get(e, -1) < i:
                self.rc[e] = i
        self.rd |= ob.rd


class Tracker:
    phase = ""

    def __init__(self):
        self.ops = {e: [] for e in ENGS}
        self.dmas = []
        self.ndma = {e: 0 for e in ENGS}
        self.dma_by_k = {e: [] for e in ENGS}

    def add(self, eng, method, reads, writes, *args, **kw):
        dma = method == "dma_start"
        fn = (method, args, kw)
        dc = {}
        dd = set()

        def dep(d):
            if d is None:
                return
            if d[0] == "d":
                dd.add(d[1])
            elif dc.get(d[1], -1) < d[2]:
                dc[d[1]] = d[2]

        for b in reads:
            dep(b.w)
        for b in writes:
            dep(b.w)
            for e, i in b.rc.items():
                dep(("c", e, i))
            for did in b.rd:
                dep(("d", did))
        idx = len(self.ops[eng])
        did = None
        if dma:
            did = len(self.dmas)
            k = self.ndma[eng]
            self.ndma[eng] += 1
            self.dmas.append((eng, k))
            n = NDSEM[eng]
            if k >= n:
                dd.add(self.dma_by_k[eng][k - n])
            self.dma_by_k[eng].append(did)
            me = ("d", did)
        else:
            me = ("c", eng, idx)
        self.ops[eng].append(dict(fn=fn, dc=dc, dd=dd, did=did, ph=Tracker.phase))
        for b in reads:
            if dma:
                b.rd.add(did)
            else:
                b.rc[eng] = idx
        for b in writes:
            b.w = me
            b.rc = {}
            b.rd = set()
        return me

    def emit(self, nc):
        ops = self.ops
        signal = {e: [False] * len(ops[e]) for e in ENGS}
        waits = {e: [None] * len(ops[e]) for e in ENGS}
        for e in ENGS:
            seen_c = {p: -1 for p in ENGS}
            seen_d = set()
            for i, op in enumerate(ops[e]):
                wl = []
                for p, j in op["dc"].items():
                    if p == "pe" and e == "pe" and op["did"] is None:
                        continue
                    if seen_c[p] >= j:
                        continue
                    seen_c[p] = j
                    signal[p][j] = True
                    wl.append(("c", p, j))
                for did in sorted(op["dd"]):
                    if did in seen_d:
                        continue
                    seen_d.add(did)
                    wl.append(("d", did))
                waits[e][i] = wl
        sigval = {}
        for e in ENGS:
            c = 0
            for i in range(len(ops[e])):
                if signal[e][i]:
                    c += 1
                    sigval[(e, i)] = c
        with contextlib.ExitStack() as st:
            csem = {e: st.enter_context(nc.semaphore(f"c_{e}")) for e in ENGS if e != "sp"}
            dsem = {e: [st.enter_context(nc.semaphore(f"d_{e}{k}")) for k in range(NDSEM[e])]
                    for e in ENGS if self.ndma[e] > 0}

            def dma_semval(did):
                q, k = self.dmas[did]
                n = NDSEM[q]
                return dsem[q][k % n], 16 * (k // n + 1)

            final = []
            for q in ENGS:
                nd = self.ndma[q]
                n = NDSEM[q]
                for s in range(min(n, nd)):
                    cnt = (nd - 1 - s) // n + 1
                    final.append((dsem[q][s], 16 * cnt))

            with nc.Block() as block:
                regs = {"pe": block.tensor, "act": block.scalar, "dve": block.vector,
                        "pool": block.gpsimd, "sp": block.sync}
                for e in ENGS:
                    def body(eng, e=e):
                        for i, op in enumerate(ops[e]):
                            for w in waits[e][i]:
                                if w[0] == "c":
                                    eng.wait_ge(csem[w[1]], sigval[(w[1], w[2])])
                                else:
                                    s, v = dma_semval(w[1])
                                    eng.wait_ge(s, v)
                            m, a, k = op["fn"]
                            ins = getattr(eng, m)(*a, **k)
                            if ANNOTATE and op["ph"]:
                                ins.annotate(op["ph"])
                            if op["did"] is not None:
                                s, v = dma_semval(op["did"])
                                ins.then_inc(s, 16)
                            elif signal[e][i]:
                                ins.then_inc(csem[e], 1)
                        if e == "sp":
                            for s, v in final:
                                eng.wait_ge(s, v)
                    regs[e](body)
        return {e: len(v) for e, v in ops.items()}


class Arena:
    cur = None
    regions = []

    def __init__(self, handle, nbytes):
        self.h = handle
        self.n = nbytes
        self.off = 0
        self.marks = []
        Arena.cur = None
        Arena.regions = []

    def alloc(self, nelem, dtype, shape=None):
        sz = 2 if dtype == BF16 else 4
        nb = (nelem * sz + 31) // 32 * 32
        assert self.off + nb <= self.n, f"arena overflow {self.off}+{nb}>{self.n}"
        a = self.h[:, self.off // 4:(self.off + nb) // 4]
        Arena.cur = [self.off, self.off + nb, []]
        Arena.regions.append(Arena.cur)
        self.off += nb
        if dtype != F32:
            a = a.bitcast(dtype)
        a = a[:, 0:nelem]
        return a

    def mark(self):
        self.marks.append(self.off)

    def release(self):
        self.off = self.marks.pop()


class Ctx:
    pass


def _mk(eng):
    def f(self, method, reads, writes, *a, **k):
        return self.tr.add(eng, method, reads, writes, *a, **k)
    return f


for _e in ENGS:
    setattr(Ctx, _e, _mk(_e))


def emit_front(c, x_src, x_bufs, hT, hT_buf, col0):
    slot = c.xslot
    c.xslot = (c.xslot + 1) % len(c.xt)
    xt, xb = c.xt[slot]
    hb_ap, hb_buf = c.hb[slot % len(c.hb)]
    ss_ap, ss_buf = c.ss[slot % len(c.ss)]
    c.sp("dma_start", list(x_bufs), [xb], out=xt, in_=x_src)
    c.dve("scalar_tensor_tensor", [xb], [hb_buf, ss_buf], out=hb_ap, in0=xt, scalar=1.0, in1=xt,
          op0=ALU.mult, op1=ALU.mult, accum_out=ss_ap[:, 0:1])
    c.act("activation", [ss_buf, c.eps_rms[1]], [ss_buf], out=ss_ap[:, 1:2], in_=ss_ap[:, 0:1], func=AF.Sqrt,
          scale=1.0 / D, bias=c.eps_rms[0])
    c.dve("reciprocal", [ss_buf], [ss_buf], out=ss_ap[:, 2:3], in_=ss_ap[:, 1:2])
    c.dve("scalar_tensor_tensor", [xb, ss_buf, c.A[1]], [xb], out=xt, in0=xt, scalar=ss_ap[:, 2:3],
          in1=c.A[0], op0=ALU.mult, op1=ALU.mult)
    c.dve("tensor_tensor", [xb, c.B[1]], [hb_buf], out=hb_ap, in0=xt, in1=c.B[0], op=ALU.add)
    pt_ap, pt_buf = c.ptr[c.ptr_i % len(c.ptr)]
    c.ptr_i += 1
    for kc in range(KC):
        c.pe("transpose", [hb_buf, c.ident[1]], [pt_buf], out=pt_ap[:, kc * 128:(kc + 1) * 128],
             in_=hb_ap[:, kc * 128:(kc + 1) * 128], identity=c.ident[0])
    c.act("activation", [pt_buf], [hT_buf], out=hT[:, :, col0:col0 + 128],
          in_=pt_ap.rearrange("p (k n) -> p k n", k=KC), func=AF.Copy)


def emit_resid(c, o_ap, o_buf, x_src, x_src_b, x_dst, x_dst_b, g_ap):
    r_ap, r_buf = c.xr[c.xr_i % len(c.xr)]
    t_ap, t_buf = c.ot[c.xr_i % len(c.ot)]
    c.xr_i += 1
    c.sp("dma_start", [x_src_b], [r_buf], out=r_ap, in_=x_src)
    c.dve("tensor_tensor", [o_buf, c.G[1]], [t_buf], out=t_ap, in0=o_ap, in1=g_ap, op=ALU.mult)
    c.dve("tensor_tensor", [t_buf, r_buf], [r_buf], out=r_ap, in0=r_ap, in1=t_ap, op=ALU.add)
    c.sp("dma_start", [r_buf], [x_dst_b], out=x_dst, in_=r_ap)


def load_bcast_rows(c, rows3):
    rb = []
    if isinstance(rows3, tuple):
        rows3, rb = rows3
    for i, (ap, buf) in enumerate((c.A, c.B, c.G)):
        c.sp("dma_start", list(rb), [buf], out=ap, in_=rows3[i, :].partition_broadcast(128))


def phase_ffn(c, xin, xout, w_in, w_out, rows3, S):
    Tracker.phase = "ffn"
    x_in, xin_b = xin
    x_out, xout_b = xout
    ar = c.arena
    ar.mark()
    win = ar.alloc(KC * 2 * DFF, BF16).rearrange("p (k n) -> p k n", k=KC)
    NJB = FC // 2
    wg_b = [Buf(f"wing{j}") for j in range(NJB)]
    wu_b = [Buf(f"winu{j}") for j in range(NJB)]
    wout = ar.alloc(FC * D, BF16).rearrange("p (k n) -> p k n", k=FC)
    wout_b = [Buf("wout0"), Buf("wout1")]
    NT = 512
    nsup = S // NT
    hT = [ar.alloc(KC * NT, BF16).rearrange("p (k n) -> p k n", k=KC) for _ in range(2)]
    hT_b = [Buf("hT0"), Buf("hT1")]
    aT = ar.alloc(FC * NT, BF16).rearrange("p (k n) -> p k n", k=FC)
    aT_b = [Buf(f"aT{j}") for j in range(FC)]
    sg = [(ar.alloc(NT, F32), Buf(f"sg{i}")) for i in range(2)]

    w_in_v = w_in.rearrange("(k p) n -> p k n", p=128)
    for jb in range(NJB):
        for (bb, c0) in ((wg_b, 0), (wu_b, DFF)):
            cs_ = slice(c0 + jb * 256, c0 + (jb + 1) * 256)
            c.pool("dma_start", [], [bb[jb]], out=win[:, :, cs_], in_=w_in_v[:, :, cs_])
    w_out_v = w_out.rearrange("(k p) n -> p k n", p=128)
    for hh in range(2):
        c.pool("dma_start", [], [wout_b[hh]], out=wout[:, hh * 11:(hh + 1) * 11, :],
               in_=w_out_v[:, hh * 11:(hh + 1) * 11, :])
    load_bcast_rows(c, rows3)

    xin_v = x_in.rearrange("(n p) d -> n p d", p=128)
    xout_v = x_out.rearrange("(n p) d -> n p d", p=128)
    gu = c.psum_gu
    gu_i = 0
    for su in range(nsup):
        hTs, hTb = hT[su % 2], hT_b[su % 2]
        for ts in range(4):
            emit_front(c, xin_v[su * 4 + ts], xin_b[su * 4 + ts], hTs, hTb, ts * 128)
        for j in range(FC):
            g_ap, g_buf = gu[gu_i % 4]
            u_ap, u_buf = gu[(gu_i + 1) % 4]
            gu_i += 2
            for k in range(KC):
                c.pe("matmul", [wg_b[j // 2], hTb], [g_buf], g_ap, lhsT=win[:, k, j * 128:(j + 1) * 128],
                     rhs=hTs[:, k, :], start=(k == 0), stop=(k == KC - 1))
            for k in range(KC):
                c.pe("matmul", [wu_b[j // 2], hTb], [u_buf], u_ap, lhsT=win[:, k, DFF + j * 128:DFF + (j + 1) * 128],
                     rhs=hTs[:, k, :], start=(k == 0), stop=(k == KC - 1))
            s_ap, s_buf = sg[j % 2]
            c.act("activation", [g_buf], [s_buf], out=s_ap, in_=g_ap, func=AF.Silu)
            c.dve("tensor_tensor", [s_buf, u_buf], [aT_b[j]], out=aT[:, j, :], in0=s_ap, in1=u_ap, op=ALU.mult)
        for ts in range(4):
            for nb in range(2):
                o_ap, o_buf = c.psum_o[c.po_i % len(c.psum_o)]
                c.po_i += 1
                for j in range(FC):
                    c.pe("matmul", [aT_b[j], wout_b[j // 11]], [o_buf], o_ap,
                         lhsT=aT[:, j, ts * 128:(ts + 1) * 128], rhs=wout[:, j, nb * 512:(nb + 1) * 512],
                         start=(j == 0), stop=(j == FC - 1))
                cs = slice(nb * 512, (nb + 1) * 512)
                emit_resid(c, o_ap, o_buf, xin_v[su * 4 + ts][:, cs], xin_b[su * 4 + ts][nb],
                           xout_v[su * 4 + ts][:, cs], xout_b[su * 4 + ts][nb], c.G[0][:, cs])
    ar.release()


DEBUG = False


def dbg(c, name, ap, buf):
    if not DEBUG:
        return
    shp = list(ap.shape)
    d = c.nc.dram_tensor("dbg_" + name, shp, ap.dtype, kind="ExternalOutput").ap()
    c.sp("dma_start", [buf], [], out=d, in_=ap)


def setup_common(c, nc, arena_handle, arena_bytes, psum):
    c.nc = nc
    c.arena = Arena(arena_handle, arena_bytes)
    ar = c.arena
    c.psum = psum
    c.ident = (ar.alloc(128, BF16), Buf("ident"))
    c.A = (ar.alloc(D, F32), Buf("A"))
    c.B = (ar.alloc(D, F32), Buf("B"))
    c.G = (ar.alloc(D, F32), Buf("G"))
    c.xt = [(ar.alloc(D, F32), Buf(f"xt{i}")) for i in range(2)]
    c.xslot = 0
    c.xr = [(ar.alloc(512, F32), Buf(f"xr{i}")) for i in range(2)]
    c.ot = [(ar.alloc(512, F32), Buf(f"ot{i}")) for i in range(2)]
    c.xr_i = 0
    c.hb = [(ar.alloc(D, BF16), Buf(f"hb{i}")) for i in range(2)]
    c.ss = [(ar.alloc(8, F32), Buf(f"ss{i}")) for i in range(6)]
    c.pb = [(psum[:, b * 512:(b + 1) * 512], Buf(f"pb{b}")) for b in range(8)]
    c.psum_gu = c.pb[0:4]
    c.psum_o = c.pb[4:7]
    c.po_i = 0
    c.ptr = [(c.pb[7][0].bitcast(BF16), c.pb[7][1])]
    c.ptr_i = 0
    c.sp("dma_start", [], [c.ident[1]], out=c.ident[0], in_=c.ident_dram)
    c.eps_rms = (ar.alloc(8, F32)[:, 0:1], Buf("eps"))
    c.dve("memset", [], [c.eps_rms[1]], c.eps_rms[0], RMS_EPS)


ARENA_BYTES = 212800

R_HEADS, R_DK, R_DV = 4, 256, 512
R_QK, R_VTOT = 1024, 2048
R_IN = 6144
M_HEADS, M_NOPE, M_ROPE, M_V = 8, 128, 64, 128
Q_LORA, KV_LORA = 384, 256
M_DOWN = 704
M_SCALE = (M_NOPE + M_ROPE) ** -0.5
GN_EPS = 1e-5
TWO_PI = 2.0 * np.pi
CW1 = 6.28125
CW2 = float(TWO_PI - 6.28125)
MAGIC = 12582912.0


CTW = 680


def host_consts():
    import ml_dtypes
    cst = {}
    cst["ident"] = np.eye(128, dtype=ml_dtypes.bfloat16)
    inv_r = np.power(np.float32(10000.0), -np.arange(0, R_DK, 2, dtype=np.float32) / np.float32(R_DK)).astype(np.float32)
    inv_m = np.power(np.float32(10000.0), -np.arange(0, M_ROPE, 2, dtype=np.float32) / np.float32(M_ROPE)).astype(np.float32)
    t = np.arange(128, dtype=np.float32)
    sI, tI = np.meshgrid(t, t, indexing="ij")
    tab = np.zeros((128, CTW), np.float32)
    tab[:, 552:680] = np.eye(128, dtype=np.float32)
    tab[:, 0:128] = np.maximum(tI - sI, 0)
    tab[:, 128:256] = (tI >= sI).astype(np.float32) / 16.0
    tab[:, 256:384] = np.maximum(sI - tI, 0)
    tab[:, 384:512] = (sI > tI).astype(np.float32) / 16.0
    tab[:, 512] = t + 1.0
    tab[:, 513] = 128.0 - t
    tab[:, 514] = 127.0 - t
    tab[:, 515] = t
    tab[:, 516] = 128.0
    tab[:, 517] = inv_r
    tab[:, 520:552] = inv_m[None, :]
    cst["ctab"] = tab
    return cst


def emit_sincos(c, ang, n, cos_out, sin_out, bufs):
    ang_b, cos_b, sin_b = bufs
    ar = c.arena
    ar.mark()
    k_ap, k_b = ar.alloc(n, F32), Buf("k")
    c.dve("tensor_scalar", [ang_b], [k_b], out=k_ap, in0=ang, scalar1=float(1.0 / TWO_PI), scalar2=MAGIC,
          op0=ALU.mult, op1=ALU.add)
    c.dve("tensor_scalar", [k_b], [k_b], out=k_ap, in0=k_ap, scalar1=MAGIC, scalar2=None, op0=ALU.subtract)
    c.dve("scalar_tensor_tensor", [k_b, ang_b], [ang_b], out=ang, in0=k_ap, scalar=-CW1, in1=ang,
          op0=ALU.mult, op1=ALU.add)
    c.dve("scalar_tensor_tensor", [k_b, ang_b], [ang_b], out=ang, in0=k_ap, scalar=-CW2, in1=ang,
          op0=ALU.mult, op1=ALU.add)
    c.dve("tensor_scalar", [ang_b], [ang_b], out=ang, in0=ang, scalar1=float(-np.pi), scalar2=float(np.pi),
          op0=ALU.max, op1=ALU.min)
    c.act("activation", [ang_b], [sin_b], out=sin_out, in_=ang, func=AF.Sin)
    c.act("activation", [ang_b], [k_b], out=k_ap, in_=ang, func=AF.Sin, scale=0.5)
    c.dve("tensor_tensor", [k_b], [k_b], out=k_ap, in0=k_ap, in1=k_ap, op=ALU.mult)
    c.dve("tensor_scalar", [k_b], [cos_b], out=cos_out, in0=k_ap, scalar1=-2.0, scalar2=1.0,
          op0=ALU.mult, op1=ALU.add)
    ar.release()


def phase_setup_tables(c, pos, S):
    Tracker.phase = "setup_tables"
    ar = c.arena
    ar.mark()
    ct = fetch_ctab(c)
    nt = S // 128
    pi_ap, pi_b = ar.alloc(S, I32), Buf("posi")
    c.sp("dma_start", [], [pi_b], out=pi_ap, in_=pos.partition_broadcast(128))
    ang, ang_b = ar.alloc(S, F32), Buf("ang")
    c.dve("tensor_copy", [pi_b], [ang_b], out=ang, in_=pi_ap)
    c.dve("tensor_scalar", [ang_b, ct[1]], [ang_b], out=ang, in0=ang, scalar1=ct[0][:, 517:518], scalar2=None,
          op0=ALU.mult)
    cs, cs_b = ar.alloc(S, F32), Buf("cos")
    sn, sn_b = ar.alloc(S, F32), Buf("sin")
    emit_sincos(c, ang, S, cs, sn, (ang_b, cs_b, sn_b))
    c.sp("dma_start", [cs_b], [c.CR_b], out=c.CR, in_=cs)
    c.sp("dma_start", [sn_b], [c.SR_b], out=c.SR, in_=sn)
    pf_ap, pf_b = ar.alloc(nt, F32), Buf("posf")
    pb2, pb2_b = ar.alloc(S, F32), Buf("posf_row")
    c.dve("tensor_copy", [pi_b], [pb2_b], out=pb2, in_=pi_ap)
    junk, junk_b = ar.alloc(128, F32), Buf("junk")
    for t in range(nt):
        c.dve("scalar_tensor_tensor", [pb2_b, ct[1]], [junk_b, pf_b], out=junk, in0=pb2[:, t * 128:(t + 1) * 128],
              scalar=1.0, in1=ct[0][:, 552:680], op0=ALU.mult, op1=ALU.mult, accum_out=pf_ap[:, t:t + 1])
    am, am_b = ar.alloc(nt * 32, F32), Buf("am")
    for t in range(nt):
        c.dve("tensor_scalar", [pf_b, ct[1]], [am_b], out=am[:, t * 32:(t + 1) * 32], in0=ct[0][:, 520:552],
              scalar1=pf_ap[:, t:t + 1], scalar2=None, op0=ALU.mult)
    cm, cm_b = ar.alloc(nt * 32, F32), Buf("cm")
    sm, sm_b = ar.alloc(nt * 32, F32), Buf("sm")
    emit_sincos(c, am, nt * 32, cm, sm, (am_b, cm_b, sm_b))
    c.sp("dma_start", [cm_b], [c.CM_b], out=c.CM.rearrange("(t p) j -> p t j", p=128),
         in_=cm.rearrange("p (t j) -> p t j", j=32))
    c.sp("dma_start", [sm_b], [c.SM_b], out=c.SM.rearrange("(t p) j -> p t j", p=128),
         in_=sm.rearrange("p (t j) -> p t j", j=32))
    ar.release()


def phase_mod(c, cvec, mod_w, mod_b, norm_g, depth):
    Tracker.phase = "mod"
    ar = c.arena
    ar.mark()
    cf, cf_b = ar.alloc(128, F32), Buf("cf")
    c.sp("dma_start", [], [cf_b], out=cf[0:KC, :], in_=cvec.rearrange("(k p) -> k p", p=128))
    cab, cab_b = ar.alloc(128, BF16), Buf("cab")
    c.act("activation", [cf_b], [cab_b], out=cab[0:KC, :], in_=cf[0:KC, :], func=AF.Silu)
    pt_ap, pt_buf = c.ptr[0]
    c.pe("transpose", [cab_b, c.ident[1]], [pt_buf], out=pt_ap[:, 0:KC], in_=cab[0:KC, :],
         identity=c.ident[0][0:KC, 0:KC])
    ca, ca_b = ar.alloc(KC, BF16), Buf("ca")
    c.act("activation", [pt_buf], [ca_b], out=ca, in_=pt_ap[:, 0:KC], func=AF.Copy)
    NB = 512
    nblk = 9 * D // NB
    wb = [(ar.alloc(KC * NB, BF16).rearrange("p (k n) -> p k n", k=KC), Buf(f"mw{i}")) for i in range(3)]
    row, row_b = ar.alloc(9 * D, F32), Buf("modrow")
    mb_ap, mb_b = ar.alloc(9 * D, F32), Buf("modb")
    ng_ap, ng_b = ar.alloc(3 * D, F32), Buf("normg")
    orow, orow_b = ar.alloc(9 * D, F32), Buf("orow")
    bi = 0
    for i in range(depth):
        c.sp("dma_start", [], [mb_b], out=mb_ap[0:1, :], in_=mod_b[i:i + 1, :])
        c.sp("dma_start", [], [ng_b], out=ng_ap[0:1, :], in_=norm_g[i:i + 1].rearrange("o s d -> o (s d)"))
        mw_v = mod_w[i].rearrange("(k p) n -> p k n", p=128)
        for b in range(nblk):
            w_ap, w_b = wb[bi % 3]
            p_ap, p_b = c.pb[bi % 2]
            bi += 1
            c.pool("dma_start", [], [w_b], out=w_ap, in_=mw_v[:, :, b * NB:(b + 1) * NB])
            for k in range(KC):
                c.pe("matmul", [ca_b, w_b], [p_b], p_ap[0:1, :], lhsT=ca[:, k:k + 1], rhs=w_ap[:, k, :],
                     start=(k == 0), stop=(k == KC - 1))
            c.dve("tensor_tensor", [p_b, mb_b], [row_b], out=row[0:1, b * NB:(b + 1) * NB], in0=p_ap[0:1, :],
                  in1=mb_ap[0:1, b * NB:(b + 1) * NB], op=ALU.add)
        for sl in range(3):
            sh = row[0:1, (3 * sl) * D:(3 * sl + 1) * D]
            sc = row[0:1, (3 * sl + 1) * D:(3 * sl + 2) * D]
            gt = row[0:1, (3 * sl + 2) * D:(3 * sl + 3) * D]
            oa = orow[0:1, (3 * sl) * D:(3 * sl + 1) * D]
            ob = orow[0:1, (3 * sl + 1) * D:(3 * sl + 2) * D]
            og = orow[0:1, (3 * sl + 2) * D:(3 * sl + 3) * D]
            c.dve("scalar_tensor_tensor", [row_b, ng_b], [orow_b], out=oa, in0=sc, scalar=1.0,
                  in1=ng_ap[0:1, sl * D:(sl + 1) * D], op0=ALU.add, op1=ALU.mult)
            c.dve("tensor_copy", [row_b], [orow_b], out=ob, in_=sh)
            c.dve("tensor_scalar", [row_b], [orow_b], out=og, in0=gt, scalar1=(1.0 if sl == 1 else 0.5),
                  scalar2=None, op0=ALU.mult)
        c.sp("dma_start", [orow_b], [c.modrows_b[i]], out=c.modrows[i:i + 1].rearrange("o s r d -> o (s r d)"),
             in_=orow[0:1, :])
    ar.release()


def ret_tables(c, dec_f, dec_b):
    ar = c.arena
    ct, ct_b = fetch_ctab(c)
    t = Ctx()
    raw, raw_b = ar.alloc(8, F32), Buf("decraw")
    c.sp("dma_start", [], [raw_b], out=raw[:, 0:4], in_=dec_f.partition_broadcast(128))
    c.sp("dma_start", [], [raw_b], out=raw[:, 4:8], in_=dec_b.partition_broadcast(128))
    lg, lg_b = ar.alloc(8, F32), Buf("lg")
    c.act("activation", [raw_b], [lg_b], out=lg, in_=raw, func=AF.Exp, scale=-1.0)
    c.act("activation", [lg_b], [lg_b], out=lg, in_=lg, func=AF.Ln, bias=1.0)
    c.dve("tensor_scalar", [lg_b], [lg_b], out=lg, in0=lg, scalar1=-1.0, scalar2=None, op0=ALU.mult)
    cols, cols_b = ar.alloc(24, F32), Buf("deccols")
    src = [(512, 0), (513, 4), (514, 0), (515, 4), (516, 0), (516, 4)]
    for kind, (ccol, lgo) in enumerate(src):
        for h in range(R_HEADS):
            c.act("activation", [lg_b, ct_b], [cols_b], out=cols[:, kind * 4 + h:kind * 4 + h + 1],
                  in_=ct[:, ccol:ccol + 1], func=AF.Exp, scale=lg[:, lgo + h:lgo + h + 1])
    t.cols, t.cols_b = cols, cols_b
    DT, DT_b = ar.alloc(4 * 128, F32), Buf("DT")
    e1, e1_b = ar.alloc(128, F32), Buf("e1")
    e2, e2_b = ar.alloc(128, F32), Buf("e2")
    for h in range(R_HEADS):
        c.act("activation", [lg_b, ct_b], [e1_b], out=e1, in_=ct[:, 0:128], func=AF.Exp, scale=lg[:, h:h + 1])
        c.dve("tensor_tensor", [e1_b, ct_b], [e1_b], out=e1, in0=e1, in1=ct[:, 128:256], op=ALU.mult)
        c.act("activation", [lg_b, ct_b], [e2_b], out=e2, in_=ct[:, 256:384], func=AF.Exp, scale=lg[:, 4 + h:5 + h])
        c.dve("tensor_tensor", [e2_b, ct_b], [e2_b], out=e2, in0=e2, in1=ct[:, 384:512], op=ALU.mult)
        c.dve("tensor_tensor", [e1_b, e2_b], [DT_b], out=DT[:, h * 128:(h + 1) * 128], in0=e1, in1=e2, op=ALU.add)
    t.DT, t.DT_b = DT, DT_b
    return t


def phase_ret_in(c, xin, w_in, rows3, S, dt):
    Tracker.phase = "ret_in"
    x_in, xin_b = xin
    ar = c.arena
    ar.mark()
    NKC = R_IN
    win = ar.alloc(KC * NKC, BF16).rearrange("p (k n) -> p k n", k=KC)
    win_b = [Buf(f"rwin{k}") for k in range(KC)]
    w_in_v = w_in.rearrange("(k p) n -> p k n", p=128)
    for k in range(KC):
        c.pool("dma_start", [], [win_b[k]], out=win[:, k, :], in_=w_in_v[:, k, :])
    load_bcast_rows(c, rows3)
    NT = 512
    nsup = S // NT
    hT = [ar.alloc(KC * NT, BF16).rearrange("p (k n) -> p k n", k=KC) for _ in range(2)]
    hT_b = [Buf("hT0"), Buf("hT1")]
    cs = [(ar.alloc(NT, F32), Buf(f"cs{i}")) for i in range(2)]
    sn = [(ar.alloc(NT, F32), Buf(f"sn{i}")) for i in range(2)]
    tmp = [(ar.alloc(NT, F32), Buf(f"rt{i}")) for i in range(4)]
    qst = [(ar.alloc(2 * NT, BF16).rearrange("p (a n) -> p a n", a=2), Buf(f"qst{i}")) for i in range(2)]
    KDf, KDf_b = ar.alloc(1024, F32), Buf("KDf")
    KDb, KDb_b = ar.alloc(1024, F32), Buf("KDb")
    for h in range(R_HEADS):
        for (KD, KD_b, kind) in ((KDf, KDf_b, 2), (KDb, KDb_b, 3)):
            c.dve("memset", [], [KD_b], KD[:, h * 256:(h + 1) * 256], 1.0 / 16.0)
            c.dve("tensor_scalar", [KD_b, dt.cols_b], [KD_b], out=KD[:, h * 256:(h + 1) * 256],
                  in0=KD[:, h * 256:(h + 1) * 256], scalar1=dt.cols[:, kind * 4 + h:kind * 4 + h + 1],
                  scalar2=None, op0=ALU.mult)
    kst = [(ar.alloc(1024, BF16), Buf(f"kst{i}")) for i in range(2)]
    vst = [(ar.alloc(512, BF16), Buf(f"vst{i}")) for i in range(2)]
    gst = [(ar.alloc(512, F32), Buf(f"gst{i}")) for i in range(2)]
    ktb = [(ar.alloc(8 * NT, BF16).rearrange("p (a n) -> p a n", a=8), Buf("ktb"))]
    xin_v = x_in.rearrange("(n p) d -> n p d", p=128)
    pbi = 0
    qi = 0
    vi = 0
    for su in range(nsup):
        hTs, hTb = hT[su % 2], hT_b[su % 2]
        tok = slice(su * NT, (su + 1) * NT)
        c_ap, c_b = cs[su % 2]
        s_ap, s_b = sn[su % 2]
        c.sp("dma_start", [c.CR_b], [c_b], out=c_ap, in_=c.CR[:, tok])
        c.sp("dma_start", [c.SR_b], [s_b], out=s_ap, in_=c.SR[:, tok])
        for ts in range(4):
            emit_front(c, xin_v[su * 4 + ts], xin_b[su * 4 + ts], hTs, hTb, ts * 128)
        kt_ap, kt_b = ktb[0]
        if su == 0:
            dbg(c, "hT", hTs, hTb)
            dbg(c, "win0", win[:, 0, :], win_b[0])
            dbg(c, "win7", win[:, 7, :], win_b[7])
        for which in range(2):
            for h in range(R_HEADS):
                base = which * R_QK + h * R_DK
                p1, p1_b = c.pb[pbi % 6]
                p2, p2_b = c.pb[(pbi + 1) % 6]
                pbi += 2
                for half, (pp, pp_b) in enumerate(((p1, p1_b), (p2, p2_b))):
                    for k in range(KC):
                        c.pe("matmul", [win_b[k], hTb], [pp_b], pp,
                             lhsT=win[:, k, base + half * 128:base + (half + 1) * 128], rhs=hTs[:, k, :],
                             start=(k == 0), stop=(k == KC - 1))
                t1, t1b = tmp[0]
                t2, t2b = tmp[1]
                t3, t3b = tmp[2]
                t4, t4b = tmp[3]
                c.dve("tensor_tensor", [p1_b, c_b], [t1b], out=t1, in0=p1, in1=c_ap, op=ALU.mult)
                c.dve("tensor_tensor", [p2_b, s_b], [t2b], out=t2, in0=p2, in1=s_ap, op=ALU.mult)
                c.dve("tensor_tensor", [p1_b, s_b], [t3b], out=t3, in0=p1, in1=s_ap, op=ALU.mult)
                c.dve("tensor_tensor", [p2_b, c_b], [t4b], out=t4, in0=p2, in1=c_ap, op=ALU.mult)
                if which == 0:
                    o_ap, o_b = qst[qi % 2]
                    qi += 1
                    o1, o2 = o_ap[:, 0, :], o_ap[:, 1, :]
                else:
                    o_ap, o_b = kt_ap, kt_b
                    o1, o2 = kt_ap[:, 2 * h, :], kt_ap[:, 2 * h + 1, :]
                c.dve("tensor_tensor", [t1b, t2b], [o_b], out=o1, in0=t1, in1=t2, op=ALU.subtract)
                c.dve("tensor_tensor", [t3b, t4b], [o_b], out=o2, in0=t3, in1=t4, op=ALU.add)
                if which == 0:
                    dst = c.QT[h * 256:(h + 1) * 256, tok].rearrange("(a p) n -> p a n", p=128)
                    c.sp("dma_start", [o_b], [c.QT_b[su][h]], out=dst, in_=o_ap)
            if which == 1:
                dst = c.KT[:, tok].rearrange("(a p) n -> p a n", p=128)
                c.sp("dma_start", [kt_b], [c.KT_b[su]], out=dst, in_=kt_ap)
        for ts in range(4):
            pt_ap, pt_buf = c.ptr[0]
            for a in range(8):
                c.pe("transpose", [kt_b, c.ident[1]], [pt_buf], out=pt_ap[:, a * 128:(a + 1) * 128],
                     in_=kt_ap[:, a, ts * 128:(ts + 1) * 128], identity=c.ident[0])
            for di, (KD, KD_b, dst, dst_b) in enumerate(((KDf, KDf_b, c.KF, c.KF_b), (KDb, KDb_b, c.KB, c.KB_b))):
                k_ap, k_b = kst[di]
                c.dve("tensor_tensor", [pt_buf, KD_b], [k_b], out=k_ap, in0=pt_ap, in1=KD, op=ALU.mult)
                c.sp("dma_start", [k_b], [dst_b[su * 4 + ts]], out=dst[(su * 4 + ts) * 128:(su * 4 + ts + 1) * 128, :],
                     in_=k_ap)
        for ts in range(4):
            rows = slice((su * 4 + ts) * 128, (su * 4 + ts + 1) * 128)
            for nb in range(8):
                pp, pp_b = c.pb[pbi % 6]
                pbi += 1
                col0 = 2 * R_QK + nb * 512
                for k in range(KC):
                    c.pe("matmul", [win_b[k], hTb], [pp_b], pp, lhsT=hTs[:, k, ts * 128:(ts + 1) * 128],
                         rhs=win[:, k, col0:col0 + 512], start=(k == 0), stop=(k == KC - 1))
                if nb < 4:
                    v_ap, v_b = vst[vi % 2]
                    vi += 1
                    c.act("activation", [pp_b], [v_b], out=v_ap, in_=pp, func=AF.Copy)
                    c.sp("dma_start", [v_b], [c.V_b[su * 4 + ts][nb]], out=c.V[rows, nb * 512:(nb + 1) * 512], in_=v_ap)
                else:
                    g_ap, g_b = gst[vi % 2]
                    vi += 1
                    c.act("activation", [pp_b], [g_b], out=g_ap, in_=pp, func=AF.Silu)
                    c.sp("dma_start", [g_b], [c.Gs_b[su * 4 + ts][nb - 4]], out=c.Gs[rows, (nb - 4) * 512:(nb - 3) * 512],
                         in_=g_ap)
    ar.release()


def phase_ret_bwd(c, S, dt):
    Tracker.phase = "ret_bwd"
    ar = c.arena
    ar.mark()
    nch = S // 128
    St = ar.alloc(R_HEADS * 1024, F32).rearrange("p (h n) -> p h n", h=R_HEADS)
    St_b = [Buf(f"St{h}") for h in range(R_HEADS)]
    Sb = ar.alloc(R_HEADS * 1024, BF16).rearrange("p (h n) -> p h n", h=R_HEADS)
    Sb_b = [Buf(f"Sb{h}") for h in range(R_HEADS)]
    for h in range(R_HEADS):
        c.dve("memset", [], [St_b[h]], St[:, h], 0.0)
        c.dve("memset", [], [Sb_b[h]], Sb[:, h], 0.0)
    kb = [(ar.alloc(1024, BF16), Buf(f"kbt{i}")) for i in range(2)]
    vt = [(ar.alloc(2048, BF16), Buf(f"vt{i}")) for i in range(2)]
    pbi = 0
    for i, ch in enumerate(range(nch - 1, -1, -1)):
        k_ap, k_b = kb[i % 2]
        v_ap, v_b = vt[i % 2]
        rows = slice(ch * 128, (ch + 1) * 128)
        c.sp("dma_start", [c.KB_b[ch]], [k_b], out=k_ap, in_=c.KB[rows, :])
        c.sp("dma_start", c.V_b[ch], [v_b], out=v_ap, in_=c.V[rows, :])
        for h in range(R_HEADS):
            c.sp("dma_start", [Sb_b[h]], [c.SB_b[ch][h]], out=c.SB[ch, h], in_=Sb[:, h])
            for a in range(2):
                pp, pp_b = c.pb[pbi % 8]
                pbi += 1
                c.pe("matmul", [k_b, v_b], [pp_b], pp, lhsT=k_ap[:, h * 256 + a * 128:h * 256 + (a + 1) * 128],
                     rhs=v_ap[:, h * 512:(h + 1) * 512], start=True, stop=True)
                c.dve("scalar_tensor_tensor", [St_b[h], dt.cols_b, pp_b], [St_b[h]], out=St[:, h, a * 512:(a + 1) * 512],
                      in0=St[:, h, a * 512:(a + 1) * 512], scalar=dt.cols[:, 5 * 4 + h:5 * 4 + h + 1], in1=pp,
                      op0=ALU.mult, op1=ALU.add)
            c.act("activation", [St_b[h]], [Sb_b[h]], out=Sb[:, h], in_=St[:, h], func=AF.Copy)
    ar.release()


def phase_ret_fwd(c, xio, w_out, gn_g, gn_b, S, dt):
    Tracker.phase = "ret_fwd"
    x_io, xio_b = xio
    ar = c.arena
    ar.mark()
    nch = S // 128
    wo = ar.alloc(16 * D, BF16).rearrange("p (k n) -> p k n", k=16)
    wo_b = Buf("rwo")
    c.pool("dma_start", [], [wo_b], out=wo, in_=w_out.rearrange("(k p) n -> p k n", p=128))
    gg, gg_b = ar.alloc(R_VTOT, F32), Buf("gng")
    gb, gb_b = ar.alloc(R_VTOT, F32), Buf("gnb")
    c.sp("dma_start", [], [gg_b], out=gg, in_=gn_g.partition_broadcast(128))
    c.sp("dma_start", [], [gb_b], out=gb, in_=gn_b.partition_broadcast(128))
    eps, eps_b = ar.alloc(8, F32)[:, 0:1], Buf("gneps")
    c.dve("memset", [], [eps_b], eps, GN_EPS)
    St = ar.alloc(R_HEADS * 1024, F32).rearrange("p (h n) -> p h n", h=R_HEADS)
    St_b = [Buf(f"Sf{h}") for h in range(R_HEADS)]
    Sb = ar.alloc(R_HEADS * 1024, BF16).rearrange("p (h n) -> p h n", h=R_HEADS)
    Sb_b = [Buf(f"Sfb{h}") for h in range(R_HEADS)]
    for h in range(R_HEADS):
        c.dve("memset", [], [St_b[h]], St[:, h], 0.0)
        c.dve("memset", [], [Sb_b[h]], Sb[:, h], 0.0)
    qt = [(ar.alloc(1024, BF16).rearrange("p (a n) -> p a n", a=8), Buf(f"qt{i}")) for i in range(2)]
    kt = [(ar.alloc(1024, BF16).rearrange("p (a n) -> p a n", a=8), Buf(f"kt{i}")) for i in range(2)]
    kf = [(ar.alloc(1024, BF16), Buf(f"kf{i}")) for i in range(2)]
    vt = [(ar.alloc(2048, BF16), Buf(f"vt{i}")) for i in range(2)]
    gt = [(ar.alloc(2048, F32), Buf(f"gt{i}")) for i in range(2)]
    sbt = [(ar.alloc(R_HEADS * 1024, BF16).rearrange("p (h n) -> p h n", h=R_HEADS), Buf(f"sbt{i}"))
           for i in range(2)]
    y, y_b = ar.alloc(R_VTOT, F32), [Buf(f"y{h}") for h in range(R_HEADS)]
    z, z_b = ar.alloc(R_VTOT, BF16), Buf("z")
    zT = ar.alloc(16 * 128, BF16).rearrange("p (k n) -> p k n", k=16)
    zT_b = Buf("zT")
    pT = [(ar.alloc(128, BF16), Buf(f"pT{i}")) for i in range(2)]
    st6, st6_b = ar.alloc(4 * 8, F32), [Buf(f"st6{h}") for h in range(R_HEADS)]
    xv = x_io.rearrange("(n p) d -> n p d", p=128)
    for ch in range(nch):
        i = ch
        q_ap, q_b = qt[i % 2]
        k_ap, k_b = kt[i % 2]
        f_ap, f_b = kf[i % 2]
        v_ap, v_b = vt[i % 2]
        g_ap, g_b = gt[i % 2]
        s_ap, s_b = sbt[i % 2]
        rows = slice(ch * 128, (ch + 1) * 128)
        su = ch // 4
        c.sp("dma_start", c.QT_b[su], [q_b], out=q_ap, in_=c.QT[:, rows].rearrange("(a p) n -> p a n", p=128))
        c.sp("dma_start", [c.KT_b[su]], [k_b], out=k_ap, in_=c.KT[:, rows].rearrange("(a p) n -> p a n", p=128))
        c.sp("dma_start", [c.KF_b[ch]], [f_b], out=f_ap, in_=c.KF[rows, :])
        c.sp("dma_start", c.V_b[ch], [v_b], out=v_ap, in_=c.V[rows, :])
        c.sp("dma_start", c.Gs_b[ch], [g_b], out=g_ap, in_=c.Gs[rows, :])
        c.sp("dma_start", c.SB_b[ch], [s_b], out=s_ap, in_=c.SB[ch].rearrange("h p n -> p h n"))
        for h in range(R_HEADS):
            ps_ap, ps_b = c.pb[0]
            st_ap = ps_ap[:, (h % 4) * 128:(h % 4 + 1) * 128]
            for a in range(2):
                c.pe("matmul", [k_b, q_b], [ps_b], st_ap, lhsT=k_ap[:, 2 * h + a, :], rhs=q_ap[:, 2 * h + a, :],
                     start=(a == 0), stop=(a == 1))
            p_ap, p_b = pT[h % 2]
            c.dve("tensor_tensor", [ps_b, dt.DT_b], [p_b], out=p_ap, in0=st_ap, in1=dt.DT[:, h * 128:(h + 1) * 128],
                  op=ALU.mult)
            y0, y0_b = c.pb[1]
            y1, y1_b = c.pb[2]
            y2, y2_b = c.pb[3]
            vh = v_ap[:, h * 512:(h + 1) * 512]
            c.pe("matmul", [p_b, v_b], [y0_b], y0, lhsT=p_ap, rhs=vh, start=True, stop=True)
            for a in range(2):
                c.pe("matmul", [q_b, Sb_b[h]], [y1_b], y1, lhsT=q_ap[:, 2 * h + a, :], rhs=Sb[:, h, a * 512:(a + 1) * 512],
                     start=(a == 0), stop=(a == 1))
            for a in range(2):
                c.pe("matmul", [q_b, s_b], [y2_b], y2, lhsT=q_ap[:, 2 * h + a, :], rhs=s_ap[:, h, a * 512:(a + 1) * 512],
                     start=(a == 0), stop=(a == 1))
            yh = y[:, h * 512:(h + 1) * 512]
            c.act("activation", [y0_b], [y_b[h]], out=yh, in_=y0, func=AF.Copy)
            c.dve("scalar_tensor_tensor", [y1_b, dt.cols_b, y_b[h]], [y_b[h]], out=yh, in0=y1,
                  scalar=dt.cols[:, 0 * 4 + h:0 * 4 + h + 1], in1=yh, op0=ALU.mult, op1=ALU.add)
            c.dve("scalar_tensor_tensor", [y2_b, dt.cols_b, y_b[h]], [y_b[h]], out=yh, in0=y2,
                  scalar=dt.cols[:, 1 * 4 + h:1 * 4 + h + 1], in1=yh, op0=ALU.mult, op1=ALU.add)
            for a in range(2):
                u, u_b = c.pb[4 + a]
                c.pe("matmul", [f_b, v_b], [u_b], u, lhsT=f_ap[:, h * 256 + a * 128:h * 256 + (a + 1) * 128], rhs=vh,
                     start=True, stop=True)
                c.dve("scalar_tensor_tensor", [St_b[h], dt.cols_b, u_b], [St_b[h]], out=St[:, h, a * 512:(a + 1) * 512],
                      in0=St[:, h, a * 512:(a + 1) * 512], scalar=dt.cols[:, 4 * 4 + h:4 * 4 + h + 1], in1=u,
                      op0=ALU.mult, op1=ALU.add)
            c.act("activation", [St_b[h]], [Sb_b[h]], out=Sb[:, h], in_=St[:, h], func=AF.Copy)
            s6 = st6[:, h * 8:h * 8 + 8]
            c.dve("bn_stats", [y_b[h]], [st6_b[h]], out=s6[:, 0:6], in_=yh)
            c.dve("bn_aggr", [st6_b[h]], [st6_b[h]], out=s6[:, 6:8], in_=s6[:, 0:6])
            c.act("activation", [st6_b[h], eps_b], [st6_b[h]], out=s6[:, 0:1], in_=s6[:, 7:8], func=AF.Sqrt,
                  bias=eps, scale=1.0)
            c.dve("reciprocal", [st6_b[h]], [st6_b[h]], out=s6[:, 1:2], in_=s6[:, 0:1])
            c.dve("tensor_scalar", [y_b[h], st6_b[h]], [y_b[h]], out=yh, in0=yh, scalar1=s6[:, 6:7],
                  scalar2=s6[:, 1:2], op0=ALU.subtract, op1=ALU.mult)
            hs = slice(h * 512, (h + 1) * 512)
            c.dve("tensor_tensor", [y_b[h], gg_b], [y_b[h]], out=yh, in0=yh, in1=gg[:, hs], op=ALU.mult)
            c.dve("tensor_tensor", [y_b[h], gb_b], [y_b[h]], out=yh, in0=yh, in1=gb[:, hs], op=ALU.add)
            c.dve("tensor_tensor", [y_b[h], g_b], [z_b], out=z[:, hs], in0=yh, in1=g_ap[:, hs], op=ALU.mult)
        pt_ap, pt_buf = c.ptr[0]
        for half in range(2):
            for a in range(8):
                kc = half * 8 + a
                c.pe("transpose", [z_b, c.ident[1]], [pt_buf], out=pt_ap[:, a * 128:(a + 1) * 128],
                     in_=z[:, kc * 128:(kc + 1) * 128], identity=c.ident[0])
            c.act("activation", [pt_buf], [zT_b], out=zT[:, half * 8:(half + 1) * 8, :],
                  in_=pt_ap.rearrange("p (k n) -> p k n", k=8), func=AF.Copy)
        for nb in range(2):
            o_ap, o_buf = c.pb[6]
            for kc in range(16):
                c.pe("matmul", [zT_b, wo_b], [o_buf], o_ap, lhsT=zT[:, kc, :], rhs=wo[:, kc, nb * 512:(nb + 1) * 512],
                     start=(kc == 0), stop=(kc == 15))
            csl = slice(nb * 512, (nb + 1) * 512)
            emit_resid(c, o_ap, o_buf, xv[ch][:, csl], xio_b[ch][nb], xv[ch][:, csl], xio_b[ch][nb], c.G[0][:, csl])
    ar.release()


def alloc_mla_scratch(c, nc, S):
    nt = S // 128
    c.QTN = nc.dram_tensor("QTN", [M_HEADS, 128, S], BF16, kind=SCRATCH_KIND).ap()
    c.QTR = nc.dram_tensor("QTR", [M_HEADS, 64, S], BF16, kind=SCRATCH_KIND).ap()
    c.KTN = nc.dram_tensor("KTN", [M_HEADS, 128, S], BF16, kind=SCRATCH_KIND).ap()
    c.KTR = nc.dram_tensor("KTR", [64, S], BF16, kind=SCRATCH_KIND).ap()
    c.VM = nc.dram_tensor("VM", [S, M_HEADS * M_V], BF16, kind=SCRATCH_KIND).ap()
    c.QTN_b = [Buf("QTN") for _ in range(nt)]
    c.QTR_b = [Buf("QTR") for _ in range(nt)]
    c.KTN_b = [[Buf("KTN") for _ in range(M_HEADS)] for _ in range(S // 512)]
    c.KTR_b = [Buf("KTR") for _ in range(S // 512)]
    c.VM_b = [Buf("VM") for _ in range(nt)]


def phase_mla_in(c, xin, w_down, qng, kvng, w_uq, w_ukv, rows3, S):
    Tracker.phase = "mla_in"
    x_in, xin_b = xin
    ar = c.arena
    ar.mark()
    nt = S // 128
    NT = 512
    nsup = S // NT
    wdn = ar.alloc(KC * M_DOWN, BF16).rearrange("p (k n) -> p k n", k=KC)
    wdn_b = Buf("wdn")
    c.pool("dma_start", [], [wdn_b], out=wdn, in_=w_down.rearrange("(k p) n -> p k n", p=128))
    wuq = ar.alloc(3 * 1536, BF16).rearrange("p (k n) -> p k n", k=3)
    wuq_b = Buf("wuq")
    c.pool("dma_start", [], [wuq_b], out=wuq, in_=w_uq.rearrange("(k p) n -> p k n", p=128))
    wuk = ar.alloc(2 * 1024, BF16).rearrange("p (k n) -> p k n", k=2)
    wuk_b = Buf("wuk")
    wuv = ar.alloc(2 * 1024, BF16).rearrange("p (k n) -> p k n", k=2)
    wuv_b = Buf("wuv")
    for kc in range(2):
        src = w_ukv[kc * 128:(kc + 1) * 128, :].rearrange("p (h two d) -> p h two d", two=2, d=128)
        c.pool("dma_start", [], [wuk_b], out=wuk[:, kc, :].rearrange("p (h d) -> p h d", d=128), in_=src[:, :, 0, :])
        c.pool("dma_start", [], [wuv_b], out=wuv[:, kc, :].rearrange("p (h d) -> p h d", d=128), in_=src[:, :, 1, :])
    qg, qg_b = ar.alloc(Q_LORA, F32), Buf("qg")
    kg, kg_b = ar.alloc(KV_LORA, F32), Buf("kg")
    c.sp("dma_start", [], [qg_b], out=qg, in_=qng.partition_broadcast(128))
    c.sp("dma_start", [], [kg_b], out=kg, in_=kvng.partition_broadcast(128))
    cm, cm_b = ar.alloc(nt * 32, F32), Buf("cmr")
    sm, sm_b = ar.alloc(nt * 32, F32), Buf("smr")
    c.sp("dma_start", [c.CM_b], [cm_b], out=cm.rearrange("p (t j) -> p t j", j=32),
         in_=c.CM.rearrange("(t p) j -> p t j", p=128))
    c.sp("dma_start", [c.SM_b], [sm_b], out=sm.rearrange("p (t j) -> p t j", j=32),
         in_=c.SM.rearrange("(t p) j -> p t j", p=128))
    load_bcast_rows(c, rows3)
    hT = [ar.alloc(KC * NT, BF16).rearrange("p (k n) -> p k n", k=KC) for _ in range(2)]
    hT_b = [Buf("hT0"), Buf("hT1")]
    cd = [(ar.alloc(M_DOWN, F32), Buf(f"cd{i}")) for i in range(2)]
    cqn, cqn_b = ar.alloc(Q_LORA, BF16), Buf("cqn")
    ckn, ckn_b = ar.alloc(KV_LORA, BF16), Buf("ckn")
    krr, krr_b = ar.alloc(64, BF16), Buf("krr")
    st, st_b = ar.alloc(8, F32), Buf("mst")
    cqT = ar.alloc(3 * NT, BF16).rearrange("p (k n) -> p k n", k=3)
    cqT_b = Buf("cqT")
    ckT = ar.alloc(2 * NT, BF16).rearrange("p (k n) -> p k n", k=2)
    ckT_b = Buf("ckT")
    krT, krT_b = ar.alloc(NT, BF16), Buf("krT")
    kts = [(ar.alloc(NT, BF16), Buf(f"kts{i}")) for i in range(2)]
    vst = [(ar.alloc(1024, BF16), Buf(f"mvst{i}")) for i in range(2)]
    qf, qf_b = ar.alloc(1536, F32), Buf("qf")
    qb, qb_b = ar.alloc(1536, BF16), Buf("qb")
    rt = [(ar.alloc(256, F32), Buf(f"mrt{i}")) for i in range(4)]
    kt4 = [(ar.alloc(32, F32), Buf(f"kt4{i}")) for i in range(4)]
    qtn = [(ar.alloc(1024, BF16).rearrange("p (h n) -> p h n", h=8), Buf(f"qtn{i}")) for i in range(2)]
    qtr = [(ar.alloc(1024, BF16).rearrange("p (h n) -> p h n", h=8), Buf(f"qtr{i}")) for i in range(2)]
    eq, eq_b = ar.alloc(8, F32)[:, 0:1], Buf("meps")
    c.dve("memset", [], [eq_b], eq, RMS_EPS)
    xin_v = x_in.rearrange("(n p) d -> n p d", p=128)
    ki = 0
    for su in range(nsup):
        hTs, hTb = hT[su % 2], hT_b[su % 2]
        tok = slice(su * NT, (su + 1) * NT)
        for ts in range(4):
            emit_front(c, xin_v[su * 4 + ts], xin_b[su * 4 + ts], hTs, hTb, ts * 128)
        for ts in range(4):
            tile_i = su * 4 + ts
            tcs = slice(ts * 128, (ts + 1) * 128)
            d0, d0_b = c.pb[0]
            d1, d1_b = c.pb[1]
            for k in range(KC):
                c.pe("matmul", [hTb, wdn_b], [d0_b], d0, lhsT=hTs[:, k, tcs], rhs=wdn[:, k, 0:512],
                     start=(k == 0), stop=(k == KC - 1))
            for k in range(KC):
                c.pe("matmul", [hTb, wdn_b], [d1_b], d1[:, 0:192], lhsT=hTs[:, k, tcs], rhs=wdn[:, k, 512:704],
                     start=(k == 0), stop=(k == KC - 1))
            cd_ap, cd_b = cd[tile_i % 2]
            c.act("activation", [d0_b], [cd_b], out=cd_ap[:, 0:512], in_=d0, func=AF.Copy)
            c.act("activation", [d1_b], [cd_b], out=cd_ap[:, 512:704], in_=d1[:, 0:192], func=AF.Copy)
            c.dve("scalar_tensor_tensor", [cd_b], [cqn_b, st_b], out=cqn, in0=cd_ap[:, 0:384], scalar=1.0,
                  in1=cd_ap[:, 0:384], op0=ALU.mult, op1=ALU.mult, accum_out=st[:, 0:1])
            c.dve("scalar_tensor_tensor", [cd_b], [ckn_b, st_b], out=ckn, in0=cd_ap[:, 384:640], scalar=1.0,
                  in1=cd_ap[:, 384:640], op0=ALU.mult, op1=ALU.mult, accum_out=st[:, 1:2])
            c.act("activation", [st_b, eq_b], [st_b], out=st[:, 2:3], in_=st[:, 0:1], func=AF.Sqrt,
                  scale=1.0 / Q_LORA, bias=eq)
            c.act("activation", [st_b, eq_b], [st_b], out=st[:, 3:4], in_=st[:, 1:2], func=AF.Sqrt,
                  scale=1.0 / KV_LORA, bias=eq)
            c.dve("reciprocal", [st_b], [st_b], out=st[:, 4:6], in_=st[:, 2:4])
            c.dve("scalar_tensor_tensor", [cd_b, st_b, qg_b], [cqn_b], out=cqn, in0=cd_ap[:, 0:384],
                  scalar=st[:, 4:5], in1=qg, op0=ALU.mult, op1=ALU.mult)
            c.dve("scalar_tensor_tensor", [cd_b, st_b, kg_b], [ckn_b], out=ckn, in0=cd_ap[:, 384:640],
                  scalar=st[:, 5:6], in1=kg, op0=ALU.mult, op1=ALU.mult)
            cs_t = cm[:, tile_i * 32:(tile_i + 1) * 32]
            sn_t = sm[:, tile_i * 32:(tile_i + 1) * 32]
            x1, x2 = cd_ap[:, 640:672], cd_ap[:, 672:704]
            (a1, a1b), (a2, a2b), (a3, a3b), (a4, a4b) = kt4
            c.dve("tensor_tensor", [cd_b, cm_b], [a1b], out=a1, in0=x1, in1=cs_t, op=ALU.mult)
            c.dve("tensor_tensor", [cd_b, sm_b], [a2b], out=a2, in0=x2, in1=sn_t, op=ALU.mult)
            c.dve("tensor_tensor", [cd_b, sm_b], [a3b], out=a3, in0=x1, in1=sn_t, op=ALU.mult)
            c.dve("tensor_tensor", [cd_b, cm_b], [a4b], out=a4, in0=x2, in1=cs_t, op=ALU.mult)
            c.dve("tensor_tensor", [a1b, a2b], [krr_b], out=krr[:, 0:32], in0=a1, in1=a2, op=ALU.subtract)
            c.dve("tensor_tensor", [a3b, a4b], [krr_b], out=krr[:, 32:64], in0=a3, in1=a4, op=ALU.add)
            pt_ap, pt_buf = c.ptr[0]
            for a in range(3):
                c.pe("transpose", [cqn_b, c.ident[1]], [pt_buf], out=pt_ap[:, a * 128:(a + 1) * 128],
                     in_=cqn[:, a * 128:(a + 1) * 128], identity=c.ident[0])
            for a in range(2):
                c.pe("transpose", [ckn_b, c.ident[1]], [pt_buf], out=pt_ap[:, 384 + a * 128:384 + (a + 1) * 128],
                     in_=ckn[:, a * 128:(a + 1) * 128], identity=c.ident[0])
            c.pe("transpose", [krr_b, c.ident[1]], [pt_buf], out=pt_ap[0:64, 640:768], in_=krr,
                 identity=c.ident[0])
            c.act("activation", [pt_buf], [cqT_b], out=cqT[:, :, tcs],
                  in_=pt_ap[:, 0:384].rearrange("p (k n) -> p k n", k=3), func=AF.Copy)
            c.act("activation", [pt_buf], [ckT_b], out=ckT[:, :, tcs],
                  in_=pt_ap[:, 384:640].rearrange("p (k n) -> p k n", k=2), func=AF.Copy)
            c.act("activation", [pt_buf], [krT_b], out=krT[0:64, tcs], in_=pt_ap[0:64, 640:768], func=AF.Copy)
        c.sp("dma_start", [krT_b], [c.KTR_b[su]], out=c.KTR[:, tok], in_=krT[0:64, :])
        for h in range(M_HEADS):
            pp, pp_b = c.pb[2 + h % 2]
            for kc in range(2):
                c.pe("matmul", [wuk_b, ckT_b], [pp_b], pp, lhsT=wuk[:, kc, h * 128:(h + 1) * 128], rhs=ckT[:, kc, :],
                     start=(kc == 0), stop=(kc == 1))
            k_ap, k_b = kts[ki % 2]
            ki += 1
            c.act("activation", [pp_b], [k_b], out=k_ap, in_=pp, func=AF.Copy)
            c.sp("dma_start", [k_b], [c.KTN_b[su][h]], out=c.KTN[h][:, tok], in_=k_ap)
        for ts in range(4):
            tile_i = su * 4 + ts
            tcs = slice(ts * 128, (ts + 1) * 128)
            rows = slice(tile_i * 128, (tile_i + 1) * 128)
            v_ap, v_b = vst[tile_i % 2]
            for nb in range(2):
                pp, pp_b = c.pb[2 + nb]
                for kc in range(2):
                    c.pe("matmul", [wuv_b, ckT_b], [pp_b], pp, lhsT=ckT[:, kc, tcs], rhs=wuv[:, kc, nb * 512:(nb + 1) * 512],
                         start=(kc == 0), stop=(kc == 1))
                c.act("activation", [pp_b], [v_b], out=v_ap[:, nb * 512:(nb + 1) * 512], in_=pp, func=AF.Copy)
            c.sp("dma_start", [v_b], [c.VM_b[tile_i]], out=c.VM[rows, :], in_=v_ap)
            for nb in range(3):
                pp, pp_b = c.pb[(4, 5, 1)[nb]]
                for kc in range(3):
                    c.pe("matmul", [wuq_b, cqT_b], [pp_b], pp, lhsT=cqT[:, kc, tcs], rhs=wuq[:, kc, nb * 512:(nb + 1) * 512],
                         start=(kc == 0), stop=(kc == 2))
                c.act("activation", [pp_b], [qf_b], out=qf[:, nb * 512:(nb + 1) * 512], in_=pp, func=AF.Copy)
            qf3 = qf.rearrange("p (h d) -> p h d", h=8)
            qb3 = qb.rearrange("p (h d) -> p h d", h=8)
            cs_t = cm[:, tile_i * 32:(tile_i + 1) * 32].unsqueeze(1).broadcast_to([128, 8, 32])
            sn_t = sm[:, tile_i * 32:(tile_i + 1) * 32].unsqueeze(1).broadcast_to([128, 8, 32])
            x1, x2 = qf3[:, :, 128:160], qf3[:, :, 160:192]
            rr = [(r[0].rearrange("p (h j) -> p h j", h=8), r[1]) for r in rt]
            (a1, a1b), (a2, a2b), (a3, a3b), (a4, a4b) = rr
            c.dve("tensor_tensor", [qf_b, cm_b], [a1b], out=a1, in0=x1, in1=cs_t, op=ALU.mult)
            c.dve("tensor_tensor", [qf_b, sm_b], [a2b], out=a2, in0=x2, in1=sn_t, op=ALU.mult)
            c.dve("tensor_tensor", [qf_b, sm_b], [a3b], out=a3, in0=x1, in1=sn_t, op=ALU.mult)
            c.dve("tensor_tensor", [qf_b, cm_b], [a4b], out=a4, in0=x2, in1=cs_t, op=ALU.mult)
            c.dve("tensor_tensor", [a1b, a2b], [qb_b], out=qb3[:, :, 128:160], in0=a1, in1=a2, op=ALU.subtract)
            c.dve("tensor_tensor", [a3b, a4b], [qb_b], out=qb3[:, :, 160:192], in0=a3, in1=a4, op=ALU.add)
            c.dve("tensor_copy", [qf_b], [qb_b], out=qb3[:, :, 0:128], in_=qf3[:, :, 0:128])
            p6, p6_b = c.pb[6][0].bitcast(BF16), c.pb[6][1]
            p7, p7_b = c.ptr[0]
            for h in range(M_HEADS):
                c.pe("transpose", [qb_b, c.ident[1]], [p6_b], out=p6[:, h * 128:(h + 1) * 128], in_=qb3[:, h, 0:128],
                     identity=c.ident[0])
            for h in range(M_HEADS):
                c.pe("transpose", [qb_b, c.ident[1]], [p7_b], out=p7[0:64, h * 128:(h + 1) * 128],
                     in_=qb3[:, h, 128:192], identity=c.ident[0])
            n_ap, n_b = qtn[tile_i % 2]
            r_ap, r_b = qtr[tile_i % 2]
            c.act("activation", [p6_b], [n_b], out=n_ap, in_=p6.rearrange("p (h n) -> p h n", h=8), func=AF.Copy)
            c.act("activation", [p7_b], [r_b], out=r_ap[0:64], in_=p7[0:64, :].rearrange("p (h n) -> p h n", h=8),
                  func=AF.Copy)
            c.sp("dma_start", [n_b], [c.QTN_b[tile_i]], out=c.QTN[:, :, rows].rearrange("h d s -> d h s"), in_=n_ap)
            c.sp("dma_start", [r_b], [c.QTR_b[tile_i]], out=c.QTR[:, :, rows].rearrange("h d s -> d h s"),
                 in_=r_ap[0:64])
    ar.release()


def phase_mla_attn(c, xio, w_o, S):
    Tracker.phase = "mla_attn"
    x_io, xio_b = xio
    ar = c.arena
    ar.mark()
    nt = S // 128
    nq = S // 512
    OT = ar.alloc(M_HEADS * S, BF16).rearrange("p (h s) -> p h s", h=M_HEADS)
    OT_b = [Buf(f"OT{h}") for h in range(M_HEADS)]
    ar.mark()
    ktr, ktr_b = ar.alloc(S, BF16), Buf("ktr")
    c.sp("dma_start", c.KTR_b, [ktr_b], out=ktr[0:64, :], in_=c.KTR)
    hd = []
    for i in range(2):
        hd.append(dict(qn=(ar.alloc(S, BF16), Buf(f"qn{i}")), qr=(ar.alloc(S, BF16), Buf(f"qr{i}")),
                       kn=(ar.alloc(S, BF16), Buf(f"kn{i}")),
                       vh=(ar.alloc(S, BF16).rearrange("p (t e) -> p t e", e=128), Buf(f"vh{i}"))))
    NST = 4
    st_banks = [c.pb[0], c.pb[1], c.pb[2], c.pb[6]]
    pTl = [(ar.alloc(512, BF16), Buf(f"pT{i}")) for i in range(NST)]
    Lacc = [(ar.alloc(512, F32), Buf(f"Lacc{i}")) for i in range(2)]
    RLs = [(ar.alloc(512, F32), Buf(f"RLs{i}")) for i in range(2)]
    ones_f, ones_fb = ar.alloc(128, F32), Buf("ones_f")
    c.dve("memset", [], [ones_fb], ones_f, 1.0)
    cnt = 0
    for h in range(M_HEADS):
        H = hd[h % 2]
        qn, qn_b = H["qn"]
        qr, qr_b = H["qr"]
        kn, kn_b = H["kn"]
        vh, vh_b = H["vh"]
        c.sp("dma_start", c.QTN_b, [qn_b], out=qn, in_=c.QTN[h])
        c.sp("dma_start", c.QTR_b, [qr_b], out=qr[0:64, :], in_=c.QTR[h])
        c.sp("dma_start", [b[h] for b in c.KTN_b], [kn_b], out=kn, in_=c.KTN[h])
        c.sp("dma_start", c.VM_b, [vh_b], out=vh,
             in_=c.VM[:, h * 128:(h + 1) * 128].rearrange("(t p) e -> p t e", p=128))
        iters = [(qt, kt) for qt in range(nq) for kt in range(nt)]
        slots = {}

        def emit_st(i):
            nonlocal cnt
            qt, kt = iters[i]
            qs = slice(qt * 512, (qt + 1) * 512)
            ks = slice(kt * 128, (kt + 1) * 128)
            sT, sT_b = st_banks[cnt % NST]
            p_ap, p_b = pTl[cnt % NST]
            cnt += 1
            slots[i] = (sT, sT_b, p_ap, p_b)
            c.pe("matmul", [kn_b, qn_b], [sT_b], sT, lhsT=kn[:, ks], rhs=qn[:, qs], start=True, stop=False)
            c.pe("matmul", [ktr_b, qr_b], [sT_b], sT, lhsT=ktr[0:64, ks], rhs=qr[0:64, qs], start=False, stop=True)

        AHEAD = 3
        for i in range(min(AHEAD, len(iters))):
            emit_st(i)
        for i, (qt, kt) in enumerate(iters):
            if i + AHEAD < len(iters):
                emit_st(i + AHEAD)
            qs = slice(qt * 512, (qt + 1) * 512)
            oT, oT_b = c.pb[3 + qt % 2]
            la, la_b = Lacc[qt % 2]
            sT, sT_b, p_ap, p_b = slots.pop(i)
            c.act("activation", [sT_b], [p_b], out=p_ap, in_=sT, func=AF.Exp, scale=float(M_SCALE))
            c.pe("matmul", [vh_b, p_b], [oT_b], oT, lhsT=vh[:, kt, :], rhs=p_ap, start=(kt == 0), stop=(kt == nt - 1))
            if kt == 0:
                c.dve("tensor_copy", [p_b], [la_b], out=la, in_=p_ap)
            else:
                c.dve("tensor_tensor", [p_b, la_b], [la_b], out=la, in0=la, in1=p_ap, op=ALU.add)
            if kt == nt - 1:
                RB, RB_b = c.pb[5]
                c.pe("matmul", [ones_fb, la_b], [RB_b], RB, lhsT=ones_f, rhs=la, start=True, stop=True)
                R_ap, R_b = RLs[qt % 2]
                c.dve("reciprocal", [RB_b], [R_b], out=R_ap, in_=RB)
                c.dve("tensor_tensor", [oT_b, R_b], [OT_b[h]], out=OT[:, h, qs], in0=oT, in1=R_ap, op=ALU.mult)
    ar.release()
    Tracker.phase = "mla_out"
    wo = ar.alloc(M_HEADS * D, BF16).rearrange("p (k n) -> p k n", k=M_HEADS)
    wo_b = Buf("mwo")
    c.pool("dma_start", [], [wo_b], out=wo, in_=w_o.rearrange("(k p) n -> p k n", p=128))
    xv = x_io.rearrange("(n p) d -> n p d", p=128)
    for t in range(nt):
        for nb in range(2):
            o_ap, o_buf = c.psum_o[c.po_i % len(c.psum_o)]
            c.po_i += 1
            for h in range(M_HEADS):
                c.pe("matmul", [OT_b[h], wo_b], [o_buf], o_ap, lhsT=OT[:, h, t * 128:(t + 1) * 128],
                     rhs=wo[:, h, nb * 512:(nb + 1) * 512], start=(h == 0), stop=(h == M_HEADS - 1))
            csl = slice(nb * 512, (nb + 1) * 512)
            emit_resid(c, o_ap, o_buf, xv[t][:, csl], xio_b[t][nb], xv[t][:, csl], xio_b[t][nb], c.G[0][:, csl])
    ar.release()


def phase_final(c, xin, out, out_b, fg, S):
    Tracker.phase = "final"
    x_in, xin_b = xin
    c.sp("dma_start", [], [c.A[1]], out=c.A[0], in_=fg.partition_broadcast(128))
    xv = x_in.rearrange("(n p) d -> n p d", p=128)
    ov = out.rearrange("(n p) d -> n p d", p=128)
    for t in range(S // 128):
        slot = c.xslot
        c.xslot = (c.xslot + 1) % len(c.xt)
        xt, xb = c.xt[slot]
        hb_ap, hb_buf = c.hb[slot % len(c.hb)]
        ss_ap, ss_buf = c.ss[slot % len(c.ss)]
        c.sp("dma_start", list(xin_b[t]), [xb], out=xt, in_=xv[t])
        c.dve("scalar_tensor_tensor", [xb], [hb_buf, ss_buf], out=hb_ap, in0=xt, scalar=1.0, in1=xt,
              op0=ALU.mult, op1=ALU.mult, accum_out=ss_ap[:, 0:1])
        c.act("activation", [ss_buf, c.eps_rms[1]], [ss_buf], out=ss_ap[:, 1:2], in_=ss_ap[:, 0:1], func=AF.Sqrt,
              scale=1.0 / D, bias=c.eps_rms[0])
        c.dve("reciprocal", [ss_buf], [ss_buf], out=ss_ap[:, 2:3], in_=ss_ap[:, 1:2])
        c.dve("scalar_tensor_tensor", [xb, ss_buf, c.A[1]], [xb], out=xt, in0=xt, scalar=ss_ap[:, 2:3],
              in1=c.A[0], op0=ALU.mult, op1=ALU.mult)
        c.sp("dma_start", [xb], list(out_b[t]), out=ov[t], in_=xt)


def xbufs(S, name):
    return [[Buf(f"{name}{t}_{h}") for h in range(2)] for t in range(S // 128)]


SCRATCH_KIND = "Internal"


def alloc_ret_scratch(c, nc, S):
    nt = S // 128
    c.QT = nc.dram_tensor("QT", [R_QK, S], BF16, kind=SCRATCH_KIND).ap()
    c.KT = nc.dram_tensor("KT", [R_QK, S], BF16, kind=SCRATCH_KIND).ap()
    c.KF = nc.dram_tensor("KF", [S, R_QK], BF16, kind=SCRATCH_KIND).ap()
    c.KB = nc.dram_tensor("KB", [S, R_QK], BF16, kind=SCRATCH_KIND).ap()
    c.V = nc.dram_tensor("Vr", [S, R_VTOT], BF16, kind=SCRATCH_KIND).ap()
    c.Gs = nc.dram_tensor("Gs", [S, R_VTOT], F32, kind=SCRATCH_KIND).ap()
    c.SB = nc.dram_tensor("SBs", [nt, R_HEADS, 128, 1024], BF16, kind=SCRATCH_KIND).ap()
    c.QT_b = [[Buf("QT") for h in range(R_HEADS)] for _ in range(S // 512)]
    c.KT_b = [Buf("KT") for _ in range(S // 512)]
    c.KF_b = [Buf("KF") for _ in range(nt)]
    c.KB_b = [Buf("KB") for _ in range(nt)]
    c.V_b = [[Buf("V") for _ in range(4)] for _ in range(nt)]
    c.Gs_b = [[Buf("Gs") for _ in range(4)] for _ in range(nt)]
    c.SB_b = [[Buf("SB") for _ in range(R_HEADS)] for _ in range(nt)]


def alloc_tables(c, nc, S):
    c.CR = nc.dram_tensor("CR", [128, S], F32, kind=SCRATCH_KIND).ap()
    c.SR = nc.dram_tensor("SR", [128, S], F32, kind=SCRATCH_KIND).ap()
    c.CM = nc.dram_tensor("CM", [S, 32], F32, kind=SCRATCH_KIND).ap()
    c.SM = nc.dram_tensor("SM", [S, 32], F32, kind=SCRATCH_KIND).ap()
    c.CR_b, c.SR_b, c.CM_b, c.SM_b = Buf("CR"), Buf("SR"), Buf("CM"), Buf("SM")


def load_ctab(c, ctab_dram):
    c.ctab_dram = ctab_dram


def fetch_ctab(c):
    ar = c.arena
    ap, b = ar.alloc(CTW, F32), Buf("ctab")
    c.sp("dma_start", [], [b], out=ap, in_=c.ctab_dram)
    c.ctab = (ap, b)
    return c.ctab


def emit_copy_x(c, src, dst, dst_b, S):
    for t in range(S // 128):
        for hf in range(2):
            r_ap, r_buf = c.xr[c.xr_i % 2]
            c.xr_i += 1
            sl = (slice(t * 128, (t + 1) * 128), slice(hf * 512, (hf + 1) * 512))
            c.sp("dma_start", [], [r_buf], out=r_ap, in_=src[sl])
            c.sp("dma_start", [r_buf], [dst_b[t][hf]], out=dst[sl], in_=r_ap)


def build_ret_test(S):
    nc = bass.Bass("TRN2", target_bir_lowering=False)
    x = nc.dram_tensor("x", [S, D], F32, kind="ExternalInput").ap()
    pos = nc.dram_tensor("pos", [S], I32, kind="ExternalInput").ap()
    w_in = nc.dram_tensor("w_in", [D, R_IN], F32, kind="ExternalInput").ap()
    w_out = nc.dram_tensor("w_out", [R_VTOT, D], F32, kind="ExternalInput").ap()
    gn_g = nc.dram_tensor("gn_g", [R_VTOT], F32, kind="ExternalInput").ap()
    gn_b = nc.dram_tensor("gn_b", [R_VTOT], F32, kind="ExternalInput").ap()
    dec_f = nc.dram_tensor("dec_f", [4], F32, kind="ExternalInput").ap()
    dec_b = nc.dram_tensor("dec_b", [4], F32, kind="ExternalInput").ap()
    rows3 = nc.dram_tensor("rows3", [3, D], F32, kind="ExternalInput").ap()
    ident = nc.dram_tensor("ident", [128, 128], BF16, kind="ExternalInput").ap()
    ctab = nc.dram_tensor("ctab", [128, CTW], F32, kind="ExternalInput").ap()
    out = nc.dram_tensor("out", [S, D], F32, kind="ExternalOutput").ap()
    c = Ctx()
    c.tr = Tracker()
    c.ident_dram = ident
    alloc_tables(c, nc, S)
    alloc_ret_scratch(c, nc, S)
    with nc.sbuf_tensor("arena", [128, ARENA_BYTES // 4], F32) as ah, \
            nc.psum_tensor("psum", [128, 4096], F32) as ps:
        setup_common(c, nc, ah, ARENA_BYTES, ps)
        load_ctab(c, ctab)
        phase_setup_tables(c, pos, S)
        xb = xbufs(S, "x")
        ob = xbufs(S, "o")
        emit_copy_x(c, x, out, ob, S)
        c.arena.mark()
        dt = ret_tables(c, dec_f, dec_b)
        phase_ret_in(c, (x, xb), w_in, rows3, S, dt)
        phase_ret_bwd(c, S, dt)
        phase_ret_fwd(c, (out, ob), w_out, gn_g, gn_b, S, dt)
        c.arena.release()
        n = c.tr.emit(nc)
    print("ops", n)
    return nc


def build_ffn_test(S):
    nc = bass.Bass("TRN2", target_bir_lowering=False)
    x = nc.dram_tensor("x", [S, D], F32, kind="ExternalInput").ap()
    w_in = nc.dram_tensor("w_in", [D, 2 * DFF], F32, kind="ExternalInput").ap()
    w_out = nc.dram_tensor("w_out", [DFF, D], F32, kind="ExternalInput").ap()
    rows3 = nc.dram_tensor("rows3", [3, D], F32, kind="ExternalInput").ap()
    ident = nc.dram_tensor("ident", [128, 128], BF16, kind="ExternalInput").ap()
    out = nc.dram_tensor("out", [S, D], F32, kind="ExternalOutput").ap()
    c = Ctx()
    c.tr = Tracker()
    c.ident_dram = ident
    with nc.sbuf_tensor("arena", [128, ARENA_BYTES // 4], F32) as ah, \
            nc.psum_tensor("psum", [128, 4096], F32) as ps:
        setup_common(c, nc, ah, ARENA_BYTES, ps)
        phase_ffn(c, (x, xbufs(S, "x")), (out, xbufs(S, "o")), w_in, w_out, rows3, S)
        n = c.tr.emit(nc)
    print("ops", n)
    return nc


def build_mla_test(S):
    nc = bass.Bass("TRN2", target_bir_lowering=False)
    x = nc.dram_tensor("x", [S, D], F32, kind="ExternalInput").ap()
    pos = nc.dram_tensor("pos", [S], I32, kind="ExternalInput").ap()
    w_down = nc.dram_tensor("w_down", [D, M_DOWN], F32, kind="ExternalInput").ap()
    qng = nc.dram_tensor("qng", [Q_LORA], F32, kind="ExternalInput").ap()
    kvng = nc.dram_tensor("kvng", [KV_LORA], F32, kind="ExternalInput").ap()
    w_uq = nc.dram_tensor("w_uq", [Q_LORA, 1536], F32, kind="ExternalInput").ap()
    w_ukv = nc.dram_tensor("w_ukv", [KV_LORA, 2048], F32, kind="ExternalInput").ap()
    w_o = nc.dram_tensor("w_o", [1024, D], F32, kind="ExternalInput").ap()
    rows3 = nc.dram_tensor("rows3", [3, D], F32, kind="ExternalInput").ap()
    ident = nc.dram_tensor("ident", [128, 128], BF16, kind="ExternalInput").ap()
    ctab = nc.dram_tensor("ctab", [128, CTW], F32, kind="ExternalInput").ap()
    out = nc.dram_tensor("out", [S, D], F32, kind="ExternalOutput").ap()
    c = Ctx()
    c.tr = Tracker()
    c.ident_dram = ident
    alloc_tables(c, nc, S)
    alloc_mla_scratch(c, nc, S)
    with nc.sbuf_tensor("arena", [128, ARENA_BYTES // 4], F32) as ah, \
            nc.psum_tensor("psum", [128, 4096], F32) as ps:
        setup_common(c, nc, ah, ARENA_BYTES, ps)
        load_ctab(c, ctab)
        phase_setup_tables(c, pos, S)
        xb = xbufs(S, "x")
        ob = xbufs(S, "o")
        emit_copy_x(c, x, out, ob, S)
        phase_mla_in(c, (x, xb), w_down, qng, kvng, w_uq, w_ukv, rows3, S)
        phase_mla_attn(c, (out, ob), w_o, S)
        n = c.tr.emit(nc)
    print("ops", n)
    return nc


DEPTH = 4
_NC_CACHE = {}
W_SPECS = [
    ("norm_g", [DEPTH, 3, D]), ("final_norm_g", [D]), ("mod_w", [DEPTH, D, 9 * D]), ("mod_b", [DEPTH, 9 * D]),
    ("ffn_w_in", [DEPTH, 2, D, 2 * DFF]), ("ffn_w_out", [DEPTH, 2, DFF, D]),
    ("ret_w_in", [2, D, R_IN]), ("ret_w_out", [2, R_VTOT, D]), ("ret_gn_g", [2, R_VTOT]), ("ret_gn_b", [2, R_VTOT]),
    ("ret_decay_fwd", [2, 4]), ("ret_decay_bwd", [2, 4]),
    ("mla_w_down", [2, D, M_DOWN]), ("mla_q_norm_g", [2, Q_LORA]), ("mla_kv_norm_g", [2, KV_LORA]),
    ("mla_w_uq", [2, Q_LORA, 1536]), ("mla_w_ukv", [2, KV_LORA, 2048]), ("mla_w_o", [2, 1024, D]),
]


def build_full(S, depth=DEPTH, layers=None):
    nc = bass.Bass("TRN2", target_bir_lowering=False)
    x = nc.dram_tensor("x", [S, D], F32, kind="ExternalInput").ap()
    cvec = nc.dram_tensor("c", [D], F32, kind="ExternalInput").ap()
    pos = nc.dram_tensor("positions", [S], I32, kind="ExternalInput").ap()
    W = {n: nc.dram_tensor(n, shp, F32, kind="ExternalInput").ap() for n, shp in W_SPECS}
    ident = nc.dram_tensor("ident", [128, 128], BF16, kind="ExternalInput").ap()
    ctab = nc.dram_tensor("ctab", [128, CTW], F32, kind="ExternalInput").ap()
    out = nc.dram_tensor("out", [S, D], F32, kind="ExternalOutput").ap()
    xres = nc.dram_tensor("xres", [S, D], F32, kind=SCRATCH_KIND).ap()
    c = Ctx()
    c.tr = Tracker()
    c.ident_dram = ident
    c.modrows = nc.dram_tensor("modrows", [DEPTH, 3, 3, D], F32, kind=SCRATCH_KIND).ap()
    c.modrows_b = [Buf(f"modrows{i}") for i in range(DEPTH)]
    alloc_tables(c, nc, S)
    alloc_ret_scratch(c, nc, S)
    alloc_mla_scratch(c, nc, S)
    with nc.sbuf_tensor("arena", [128, ARENA_BYTES // 4], F32) as ah, \
            nc.psum_tensor("psum", [128, 4096], F32) as ps:
        setup_common(c, nc, ah, ARENA_BYTES, ps)
        load_ctab(c, ctab)
        phase_setup_tables(c, pos, S)
        phase_mod(c, cvec, W["mod_w"], W["mod_b"], W["norm_g"], depth)
        xin_b = xbufs(S, "xin")
        xb = xbufs(S, "xres")
        ob = xbufs(S, "out")
        for i in (layers if layers is not None else range(depth)):
            rows = lambda sl: (c.modrows[i, sl], [c.modrows_b[i]])
            src = (x, xin_b) if i == (layers[0] if layers is not None else 0) else (xres, xb)
            phase_ffn(c, src, (xres, xb), W["ffn_w_in"][i, 0], W["ffn_w_out"][i, 0], rows(0), S)
            j = i // 2
            if i % 2 == 0:
                c.arena.mark()
                dt = ret_tables(c, W["ret_decay_fwd"][j], W["ret_decay_bwd"][j])
                phase_ret_in(c, (xres, xb), W["ret_w_in"][j], rows(1), S, dt)
                phase_ret_bwd(c, S, dt)
                phase_ret_fwd(c, (xres, xb), W["ret_w_out"][j], W["ret_gn_g"][j], W["ret_gn_b"][j], S, dt)
                c.arena.release()
            else:
                phase_mla_in(c, (xres, xb), W["mla_w_down"][j], W["mla_q_norm_g"][j], W["mla_kv_norm_g"][j],
                             W["mla_w_uq"][j], W["mla_w_ukv"][j], rows(1), S)
                phase_mla_attn(c, (xres, xb), W["mla_w_o"][j], S)
            phase_ffn(c, (xres, xb), (xres, xb), W["ffn_w_in"][i, 1], W["ffn_w_out"][i, 1], rows(2), S)
        phase_final(c, (xres, xb), out, ob, W["final_norm_g"], S)
        n = c.tr.emit(nc)
    _NC_CACHE["last_tr"] = c.tr
    return nc, n


SEQ = 4096
BATCH = 8


def kernel(**inputs):
    if "nc" not in _NC_CACHE:
        _NC_CACHE["nc"] = build_full(SEQ)[0]
    nc = _NC_CACHE["nc"]
    cst = host_consts()
    f32 = lambda a: np.ascontiguousarray(np.asarray(a), dtype=np.float32)
    shared = {n: f32(inputs[n]) for n, _ in W_SPECS}
    shared["ident"] = cst["ident"]
    shared["ctab"] = cst["ctab"]
    x = f32(inputs["x"])
    cc = f32(inputs["c"])
    pos = np.ascontiguousarray(np.asarray(inputs["positions"]), dtype=np.int32)
    in_maps = []
    for b in range(BATCH):
        m = dict(shared)
        m["x"] = x[b]
        m["c"] = cc[b]
        m["positions"] = pos[b]
        in_maps.append(m)
    res = run_bass_kernel_spmd(nc, in_maps, core_ids=list(range(BATCH)))
    return np.stack([np.asarray(res.results[b]["out"]) for b in range(BATCH)]).astype(np.float32)
```

```python
import contextlib
import numpy as np
import concourse.bass as bass
import concourse.mybir as mybir
from concourse.bass_utils import run_bass_kernel_spmd

F32 = mybir.dt.float32
BF16 = mybir.dt.bfloat16
I32 = mybir.dt.int32
AF = mybir.ActivationFunctionType
ALU = mybir.AluOpType
AX = mybir.AxisListType

D = 1024
DFF = 2816
KC = D // 128
FC = DFF // 128
RMS_EPS = 1e-6

ANNOTATE = False
ENGS = ["pe", "act", "dve", "pool", "sp"]
NDSEM = {"sp": 12, "pool": 8, "act": 4, "pe": 0, "dve": 0}


class Buf:
    __slots__ = ("name", "w", "rc", "rd")

    def __init__(self, name=""):
        self.name = name
        self.w = None
        self.rc = {}
        self.rd = set()
        reg = Arena.cur
        if reg is not None:
            for r in Arena.regions:
                if r is not reg and r[0] < reg[1] and reg[0] < r[1]:
                    for ob in r[2]:
                        self._inherit(ob)
            reg[2].append(self)

    def _inherit(self, ob):
        if ob.w is not None:
            if ob.w[0] == "d":
                self.rd.add(ob.w[1])
            elif self.rc.get(ob.w[1], -1) < ob.w[2]:
                self.rc[ob.w[1]] = ob.w[2]
        for e, i in ob.rc.items():
            if self.rc.get(e, -1) < i:
                self.rc[e] = i
        self.rd |= ob.rd


class Tracker:
    phase = ""

    def __init__(self):
        self.ops = {e: [] for e in ENGS}
        self.dmas = []
        self.ndma = {e: 0 for e in ENGS}
        self.dma_by_k = {e: [] for e in ENGS}

    def add(self, eng, method, reads, writes, *args, **kw):
        dma = method == "dma_start"
        fn = (method, args, kw)
        dc = {}
        dd = set()

        def dep(d):
            if d is None:
                return
            if d[0] == "d":
                dd.add(d[1])
            elif dc.get(d[1], -1) < d[2]:
                dc[d[1]] = d[2]

        for b in reads:
            dep(b.w)
        for b in writes:
            dep(b.w)
            for e, i in b.rc.items():
                dep(("c", e, i))
            for did in b.rd:
                dep(("d", did))
        idx = len(self.ops[eng])
        did = None
        if dma:
            did = len(self.dmas)
            k = self.ndma[eng]
            self.ndma[eng] += 1
            self.dmas.append((eng, k))
            n = NDSEM[eng]
            if k >= n:
                dd.add(self.dma_by_k[eng][k - n])
            self.dma_by_k[eng].append(did)
            me = ("d", did)
        else:
            me = ("c", eng, idx)
        self.ops[eng].append(dict(fn=fn, dc=dc, dd=dd, did=did, ph=Tracker.phase))
        for b in reads:
            if dma:
                b.rd.add(did)
            else:
                b.rc[eng] = idx
        for b in writes:
            b.w = me
            b.rc = {}
            b.rd = set()
        return me

    def emit(self, nc):
        ops = self.ops
        signal = {e: [False] * len(ops[e]) for e in ENGS}
        waits = {e: [None] * len(ops[e]) for e in ENGS}
        for e in ENGS:
            seen_c = {p: -1 for p in ENGS}
            seen_d = set()
            for i, op in enumerate(ops[e]):
                wl = []
                for p, j in op["dc"].items():
                    if p == "pe" and e == "pe" and op["did"] is None:
                        continue
                    if seen_c[p] >= j:
                        continue
                    seen_c[p] = j
                    signal[p][j] = True
                    wl.append(("c", p, j))
                for did in sorted(op["dd"]):
                    if did in seen_d:
                        continue
                    seen_d.add(did)
                    wl.append(("d", did))
                waits[e][i] = wl
        sigval = {}
        for e in ENGS:
            c = 0
            for i in range(len(ops[e])):
                if signal[e][i]:
                    c += 1
                    sigval[(e, i)] = c
        with contextlib.ExitStack() as st:
            csem = {e: st.enter_context(nc.semaphore(f"c_{e}")) for e in ENGS if e != "sp"}
            dsem = {e: [st.enter_context(nc.semaphore(f"d_{e}{k}")) for k in range(NDSEM[e])]
                    for e in ENGS if self.ndma[e] > 0}

            def dma_semval(did):
                q, k = self.dmas[did]
                n = NDSEM[q]
                return dsem[q][k % n], 16 * (k // n + 1)

            final = []
            for q in ENGS:
                nd = self.ndma[q]
                n = NDSEM[q]
                for s in range(min(n, nd)):
                    cnt = (nd - 1 - s) // n + 1
                    final.append((dsem[q][s], 16 * cnt))

            with nc.Block() as block:
                regs = {"pe": block.tensor, "act": block.scalar, "dve": block.vector,
                        "pool": block.gpsimd, "sp": block.sync}
                for e in ENGS:
                    def body(eng, e=e):
                        for i, op in enumerate(ops[e]):
                            for w in waits[e][i]:
                                if w[0] == "c":
                                    eng.wait_ge(csem[w[1]], sigval[(w[1], w[2])])
                                else:
                                    s, v = dma_semval(w[1])
                                    eng.wait_ge(s, v)
                            m, a, k = op["fn"]
                            ins = getattr(eng, m)(*a, **k)
                            if ANNOTATE and op["ph"]:
                                ins.annotate(op["ph"])
                            if op["did"] is not None:
                                s, v = dma_semval(op["did"])
                                ins.then_inc(s, 16)
                            elif signal[e][i]:
                                ins.then_inc(csem[e], 1)
                        if e == "sp":
                            for s, v in final:
                                eng.wait_ge(s, v)
                    regs[e](body)
        return {e: len(v) for e, v in ops.items()}


class Arena:
    cur = None
    regions = []

    def __init__(self, handle, nbytes):
        self.h = handle
        self.n = nbytes
        self.off = 0
        self.marks = []
        Arena.cur = None
        Arena.regions = []

    def alloc(self, nelem, dtype, shape=None):
        sz = 2 if dtype == BF16 else 4
        nb = (nelem * sz + 31) // 32 * 32
        assert self.off + nb <= self.n, f"arena overflow {self.off}+{nb}>{self.n}"
        a = self.h[:, self.off // 4:(self.off + nb) // 4]
        Arena.cur = [self.off, self.off + nb, []]
        Arena.regions.append(Arena.cur)
        self.off += nb
        if dtype != F32:
            a = a.bitcast(dtype)
        a = a[:, 0:nelem]
        return a

    def mark(self):
        self.marks.append(self.off)

    def release(self):
        self.off = self.marks.pop()


class Ctx:
    pass


def _mk(eng):
    def f(self, method, reads, writes, *a, **k):
        return self.tr.add(eng, method, reads, writes, *a, **k)
    return f


for _e in ENGS:
    setattr(Ctx, _e, _mk(_e))


def emit_front(c, x_src, x_bufs, hT, hT_buf, col0):
    slot = c.xslot
    c.xslot = (c.xslot + 1) % len(c.xt)
    xt, xb = c.xt[slot]
    hb_ap, hb_buf = c.hb[slot % len(c.hb)]
    ss_ap, ss_buf = c.ss[slot % len(c.ss)]
    c.sp("dma_start", list(x_bufs), [xb], out=xt, in_=x_src)
    c.dve("scalar_tensor_tensor", [xb], [hb_buf, ss_buf], out=hb_ap, in0=xt, scalar=1.0, in1=xt,
          op0=ALU.mult, op1=ALU.mult, accum_out=ss_ap[:, 0:1])
    c.act("activation", [ss_buf, c.eps_rms[1]], [ss_buf], out=ss_ap[:, 1:2], in_=ss_ap[:, 0:1], func=AF.Sqrt,
          scale=1.0 / D, bias=c.eps_rms[0])
    c.dve("reciprocal", [ss_buf], [ss_buf], out=ss_ap[:, 2:3], in_=ss_ap[:, 1:2])
    c.dve("scalar_tensor_tensor", [xb, ss_buf, c.A[1]], [xb], out=xt, in0=xt, scalar=ss_ap[:, 2:3],
          in1=c.A[0], op0=ALU.mult, op1=ALU.mult)
    c.dve("tensor_tensor", [xb, c.B[1]], [hb_buf], out=hb_ap, in0=xt, in1=c.B[0], op=ALU.add)
    pt_ap, pt_buf = c.ptr[c.ptr_i % len(c.ptr)]
    c.ptr_i += 1
    for kc in range(KC):
        c.pe("transpose", [hb_buf, c.ident[1]], [pt_buf], out=pt_ap[:, kc * 128:(kc + 1) * 128],
             in_=hb_ap[:, kc * 128:(kc + 1) * 128], identity=c.ident[0])
    c.act("activation", [pt_buf], [hT_buf], out=hT[:, :, col0:col0 + 128],
          in_=pt_ap.rearrange("p (k n) -> p k n", k=KC), func=AF.Copy)


def emit_resid(c, o_ap, o_buf, x_src, x_src_b, x_dst, x_dst_b, g_ap):
    r_ap, r_buf = c.xr[c.xr_i % len(c.xr)]
    t_ap, t_buf = c.ot[c.xr_i % len(c.ot)]
    c.xr_i += 1
    c.sp("dma_start", [x_src_b], [r_buf], out=r_ap, in_=x_src)
    c.dve("tensor_tensor", [o_buf, c.G[1]], [t_buf], out=t_ap, in0=o_ap, in1=g_ap, op=ALU.mult)
    c.dve("tensor_tensor", [t_buf, r_buf], [r_buf], out=r_ap, in0=r_ap, in1=t_ap, op=ALU.add)
    c.sp("dma_start", [r_buf], [x_dst_b], out=x_dst, in_=r_ap)


def load_bcast_rows(c, rows3):
    rb = []
    if isinstance(rows3, tuple):
        rows3, rb = rows3
    for i, (ap, buf) in enumerate((c.A, c.B, c.G)):
        c.sp("dma_start", list(rb), [buf], out=ap, in_=rows3[i, :].partition_broadcast(128))


def phase_ffn(c, xin, xout, w_in, w_out, rows3, S):
    Tracker.phase = "ffn"
    x_in, xin_b = xin
    x_out, xout_b = xout
    ar = c.arena
    ar.mark()
    win = ar.alloc(KC * 2 * DFF, BF16).rearrange("p (k n) -> p k n", k=KC)
    NJB = FC // 2
    wg_b = [Buf(f"wing{j}") for j in range(NJB)]
    wu_b = [Buf(f"winu{j}") for j in range(NJB)]
    wout = ar.alloc(FC * D, BF16).rearrange("p (k n) -> p k n", k=FC)
    wout_b = [Buf("wout0"), Buf("wout1")]
    NT = 512
    nsup = S // NT
    hT = [ar.alloc(KC * NT, BF16).rearrange("p (k n) -> p k n", k=KC) for _ in range(2)]
    hT_b = [Buf("hT0"), Buf("hT1")]
    aT = ar.alloc(FC * NT, BF16).rearrange("p (k n) -> p k n", k=FC)
    aT_b = [Buf(f"aT{j}") for j in range(FC)]
    sg = [(ar.alloc(NT, F32), Buf(f"sg{i}")) for i in range(2)]

    w_in_v = w_in.rearrange("(k p) n -> p k n", p=128)
    for jb in range(NJB):
        for (bb, c0) in ((wg_b, 0), (wu_b, DFF)):
            cs_ = slice(c0 + jb * 256, c0 + (jb + 1) * 256)
            c.pool("dma_start", [], [bb[jb]], out=win[:, :, cs_], in_=w_in_v[:, :, cs_])
    w_out_v = w_out.rearrange("(k p) n -> p k n", p=128)
    for hh in range(2):
        c.pool("dma_start", [], [wout_b[hh]], out=wout[:, hh * 11:(hh + 1) * 11, :],
               in_=w_out_v[:, hh * 11:(hh + 1) * 11, :])
    load_bcast_rows(c, rows3)

    xin_v = x_in.rearrange("(n p) d -> n p d", p=128)
    xout_v = x_out.rearrange("(n p) d -> n p d", p=128)
    gu = c.psum_gu
    gu_i = 0
    for su in range(nsup):
        hTs, hTb = hT[su % 2], hT_b[su % 2]
        for ts in range(4):
            emit_front(c, xin_v[su * 4 + ts], xin_b[su * 4 + ts], hTs, hTb, ts * 128)
        for j in range(FC):
            g_ap, g_buf = gu[gu_i % 4]
            u_ap, u_buf = gu[(gu_i + 1) % 4]
            gu_i += 2
            for k in range(KC):
                c.pe("matmul", [wg_b[j // 2], hTb], [g_buf], g_ap, lhsT=win[:, k, j * 128:(j + 1) * 128],
                     rhs=hTs[:, k, :], start=(k == 0), stop=(k == KC - 1))
            for k in range(KC):
                c.pe("matmul", [wu_b[j // 2], hTb], [u_buf], u_ap, lhsT=win[:, k, DFF + j * 128:DFF + (j + 1) * 128],
                     rhs=hTs[:, k, :], start=(k == 0), stop=(k == KC - 1))
            s_ap, s_buf = sg[j % 2]
            c.act("activation", [g_buf], [s_buf], out=s_ap, in_=g_ap, func=AF.Silu)
            c.dve("tensor_tensor", [s_buf, u_buf], [aT_b[j]], out=aT[:, j, :], in0=s_ap, in1=u_ap, op=ALU.mult)
        for ts in range(4):
            for nb in range(2):
                o_ap, o_buf = c.psum_o[c.po_i % len(c.psum_o)]
                c.po_i += 1
                for j in range(FC):
                    c.pe("matmul", [aT_b[j], wout_b[j // 11]], [o_buf], o_ap,
                         lhsT=aT[:, j, ts * 128:(ts + 1) * 128], rhs=wout[:, j, nb * 512:(nb + 1) * 512],
                         start=(j == 0), stop=(j == FC - 1))
                cs = slice(nb * 512, (nb + 1) * 512)
                emit_resid(c, o_ap, o_buf, xin_v[su * 4 + ts][:, cs], xin_b[su * 4 + ts][nb],
                           xout_v[su * 4 + ts][:, cs], xout_b[su * 4 + ts][nb], c.G[0][:, cs])
    ar.release()


DEBUG = False


def dbg(c, name, ap, buf):
    if not DEBUG:
        return
    shp = list(ap.shape)
    d = c.nc.dram_tensor("dbg_" + name, shp, ap.dtype, kind="ExternalOutput").ap()
    c.sp("dma_start", [buf], [], out=d, in_=ap)


def setup_common(c, nc, arena_handle, arena_bytes, psum):
    c.nc = nc
    c.arena = Arena(arena_handle, arena_bytes)
    ar = c.arena
    c.psum = psum
    c.ident = (ar.alloc(128, BF16), Buf("ident"))
    c.A = (ar.alloc(D, F32), Buf("A"))
    c.B = (ar.alloc(D, F32), Buf("B"))
    c.G = (ar.alloc(D, F32), Buf("G"))
    c.xt = [(ar.alloc(D, F32), Buf(f"xt{i}")) for i in range(2)]
    c.xslot = 0
    c.xr = [(ar.alloc(512, F32), Buf(f"xr{i}")) for i in range(2)]
    c.ot = [(ar.alloc(512, F32), Buf(f"ot{i}")) for i in range(2)]
    c.xr_i = 0
    c.hb = [(ar.alloc(D, BF16), Buf(f"hb{i}")) for i in range(2)]
    c.ss = [(ar.alloc(8, F32), Buf(f"ss{i}")) for i in range(6)]
    c.pb = [(psum[:, b * 512:(b + 1) * 512], Buf(f"pb{b}")) for b in range(8)]
    c.psum_gu = c.pb[0:4]
    c.psum_o = c.pb[4:7]
    c.po_i = 0
    c.ptr = [(c.pb[7][0].bitcast(BF16), c.pb[7][1])]
    c.ptr_i = 0
    c.sp("dma_start", [], [c.ident[1]], out=c.ident[0], in_=c.ident_dram)
    c.eps_rms = (ar.alloc(8, F32)[:, 0:1], Buf("eps"))
    c.dve("memset", [], [c.eps_rms[1]], c.eps_rms[0], RMS_EPS)


ARENA_BYTES = 212800

R_HEADS, R_DK, R_DV = 4, 256, 512
R_QK, R_VTOT = 1024, 2048
R_IN = 6144
M_HEADS, M_NOPE, M_ROPE, M_V = 8, 128, 64, 128
Q_LORA, KV_LORA = 384, 256
M_DOWN = 704
M_SCALE = (M_NOPE + M_ROPE) ** -0.5
GN_EPS = 1e-5
TWO_PI = 2.0 * np.pi
CW1 = 6.28125
CW2 = float(TWO_PI - 6.28125)
MAGIC = 12582912.0


CTW = 680


def host_consts():
    import ml_dtypes
    cst = {}
    cst["ident"] = np.eye(128, dtype=ml_dtypes.bfloat16)
    inv_r = np.power(np.float32(10000.0), -np.arange(0, R_DK, 2, dtype=np.float32) / np.float32(R_DK)).astype(np.float32)
    inv_m = np.power(np.float32(10000.0), -np.arange(0, M_ROPE, 2, dtype=np.float32) / np.float32(M_ROPE)).astype(np.float32)
    t = np.arange(128, dtype=np.float32)
    sI, tI = np.meshgrid(t, t, indexing="ij")
    tab = np.zeros((128, CTW), np.float32)
    tab[:, 552:680] = np.eye(128, dtype=np.float32)
    tab[:, 0:128] = np.maximum(tI - sI, 0)
    tab[:, 128:256] = (tI >= sI).astype(np.float32) / 16.0
    tab[:, 256:384] = np.maximum(sI - tI, 0)
    tab[:, 384:512] = (sI > tI).astype(np.float32) / 16.0
    tab[:, 512] = t + 1.0
    tab[:, 513] = 128.0 - t
    tab[:, 514] = 127.0 - t
    tab[:, 515] = t
    tab[:, 516] = 128.0
    tab[:, 517] = inv_r
    tab[:, 520:552] = inv_m[None, :]
    cst["ctab"] = tab
    return cst


def emit_sincos(c, ang, n, cos_out, sin_out, bufs):
    ang_b, cos_b, sin_b = bufs
    ar = c.arena
    ar.mark()
    k_ap, k_b = ar.alloc(n, F32), Buf("k")
    c.dve("tensor_scalar", [ang_b], [k_b], out=k_ap, in0=ang, scalar1=float(1.0 / TWO_PI), scalar2=MAGIC,
          op0=ALU.mult, op1=ALU.add)
    c.dve("tensor_scalar", [k_b], [k_b], out=k_ap, in0=k_ap, scalar1=MAGIC, scalar2=None, op0=ALU.subtract)
    c.dve("scalar_tensor_tensor", [k_b, ang_b], [ang_b], out=ang, in0=k_ap, scalar=-CW1, in1=ang,
          op0=ALU.mult, op1=ALU.add)
    c.dve("scalar_tensor_tensor", [k_b, ang_b], [ang_b], out=ang, in0=k_ap, scalar=-CW2, in1=ang,
          op0=ALU.mult, op1=ALU.add)
    c.dve("tensor_scalar", [ang_b], [ang_b], out=ang, in0=ang, scalar1=float(-np.pi), scalar2=float(np.pi),
          op0=ALU.max, op1=ALU.min)
    c.act("activation", [ang_b], [sin_b], out=sin_out, in_=ang, func=AF.Sin)
    c.act("activation", [ang_b], [k_b], out=k_ap, in_=ang, func=AF.Sin, scale=0.5)
    c.dve("tensor_tensor", [k_b], [k_b], out=k_ap, in0=k_ap, in1=k_ap, op=ALU.mult)
    c.dve("tensor_scalar", [k_b], [cos_b], out=cos_out, in0=k_ap, scalar1=-2.0, scalar2=1.0,
          op0=ALU.mult, op1=ALU.add)
    ar.release()


def phase_setup_tables(c, pos, S):
    Tracker.phase = "setup_tables"
    ar = c.arena
    ar.mark()
    ct = fetch_ctab(c)
    nt = S // 128
    pi_ap, pi_b = ar.alloc(S, I32), Buf("posi")
    c.sp("dma_start", [], [pi_b], out=pi_ap, in_=pos.partition_broadcast(128))
    ang, ang_b = ar.alloc(S, F32), Buf("ang")
    c.dve("tensor_copy", [pi_b], [ang_b], out=ang, in_=pi_ap)
    c.dve("tensor_scalar", [ang_b, ct[1]], [ang_b], out=ang, in0=ang, scalar1=ct[0][:, 517:518], scalar2=None,
          op0=ALU.mult)
    cs, cs_b = ar.alloc(S, F32), Buf("cos")
    sn, sn_b = ar.alloc(S, F32), Buf("sin")
    emit_sincos(c, ang, S, cs, sn, (ang_b, cs_b, sn_b))
    c.sp("dma_start", [cs_b], [c.CR_b], out=c.CR, in_=cs)
    c.sp("dma_start", [sn_b], [c.SR_b], out=c.SR, in_=sn)
    pf_ap, pf_b = ar.alloc(nt, F32), Buf("posf")
    pb2, pb2_b = ar.alloc(S, F32), Buf("posf_row")
    c.dve("tensor_copy", [pi_b], [pb2_b], out=pb2, in_=pi_ap)
    junk, junk_b = ar.alloc(128, F32), Buf("junk")
    for t in range(nt):
        c.dve("scalar_tensor_tensor", [pb2_b, ct[1]], [junk_b, pf_b], out=junk, in0=pb2[:, t * 128:(t + 1) * 128],
              scalar=1.0, in1=ct[0][:, 552:680], op0=ALU.mult, op1=ALU.mult, accum_out=pf_ap[:, t:t + 1])
    am, am_b = ar.alloc(nt * 32, F32), Buf("am")
    for t in range(nt):
        c.dve("tensor_scalar", [pf_b, ct[1]], [am_b], out=am[:, t * 32:(t + 1) * 32], in0=ct[0][:, 520:552],
              scalar1=pf_ap[:, t:t + 1], scalar2=None, op0=ALU.mult)
    cm, cm_b = ar.alloc(nt * 32, F32), Buf("cm")
    sm, sm_b = ar.alloc(nt * 32, F32), Buf("sm")
    emit_sincos(c, am, nt * 32, cm, sm, (am_b, cm_b, sm_b))
    c.sp("dma_start", [cm_b], [c.CM_b], out=c.CM.rearrange("(t p) j -> p t j", p=128),
         in_=cm.rearrange("p (t j) -> p t j", j=32))
    c.sp("dma_start", [sm_b], [c.SM_b], out=c.SM.rearrange("(t p) j -> p t j", p=128),
         in_=sm.rearrange("p (t j) -> p t j", j=32))
    ar.release()


def phase_mod(c, cvec, mod_w, mod_b, norm_g, depth):
    Tracker.phase = "mod"
    ar = c.arena
    ar.mark()
    cf, cf_b = ar.alloc(128, F32), Buf("cf")
    c.sp("dma_start", [], [cf_b], out=cf[0:KC, :], in_=cvec.rearrange("(k p) -> k p", p=128))
    cab, cab_b = ar.alloc(128, BF16), Buf("cab")
    c.act("activation", [cf_b], [cab_b], out=cab[0:KC, :], in_=cf[0:KC, :], func=AF.Silu)
    pt_ap, pt_buf = c.ptr[0]
    c.pe("transpose", [cab_b, c.ident[1]], [pt_buf], out=pt_ap[:, 0:KC], in_=cab[0:KC, :],
         identity=c.ident[0][0:KC, 0:KC])
    ca, ca_b = ar.alloc(KC, BF16), Buf("ca")
    c.act("activation", [pt_buf], [ca_b], out=ca, in_=pt_ap[:, 0:KC], func=AF.Copy)
    NB = 512
    nblk = 9 * D // NB
    wb = [(ar.alloc(KC * NB, BF16).rearrange("p (k n) -> p k n", k=KC), Buf(f"mw{i}")) for i in range(3)]
    row, row_b = ar.alloc(9 * D, F32), Buf("modrow")
    mb_ap, mb_b = ar.alloc(9 * D, F32), Buf("modb")
    ng_ap, ng_b = ar.alloc(3 * D, F32), Buf("normg")
    orow, orow_b = ar.alloc(9 * D, F32), Buf("orow")
    bi = 0
    for i in range(depth):
        c.sp("dma_start", [], [mb_b], out=mb_ap[0:1, :], in_=mod_b[i:i + 1, :])
        c.sp("dma_start", [], [ng_b], out=ng_ap[0:1, :], in_=norm_g[i:i + 1].rearrange("o s d -> o (s d)"))
        mw_v = mod_w[i].rearrange("(k p) n -> p k n", p=128)
        for b in range(nblk):
            w_ap, w_b = wb[bi % 3]
            p_ap, p_b = c.pb[bi % 2]
            bi += 1
            c.pool("dma_start", [], [w_b], out=w_ap, in_=mw_v[:, :, b * NB:(b + 1) * NB])
            for k in range(KC):
                c.pe("matmul", [ca_b, w_b], [p_b], p_ap[0:1, :], lhsT=ca[:, k:k + 1], rhs=w_ap[:, k, :],
                     start=(k == 0), stop=(k == KC - 1))
            c.dve("tensor_tensor", [p_b, mb_b], [row_b], out=row[0:1, b * NB:(b + 1) * NB], in0=p_ap[0:1, :],
                  in1=mb_ap[0:1, b * NB:(b + 1) * NB], op=ALU.add)
        for sl in range(3):
            sh = row[0:1, (3 * sl) * D:(3 * sl + 1) * D]
            sc = row[0:1, (3 * sl + 1) * D:(3 * sl + 2) * D]
            gt = row[0:1, (3 * sl + 2) * D:(3 * sl + 3) * D]
            oa = orow[0:1, (3 * sl) * D:(3 * sl + 1) * D]
            ob = orow[0:1, (3 * sl + 1) * D:(3 * sl + 2) * D]
            og = orow[0:1, (3 * sl + 2) * D:(3 * sl + 3) * D]
            c.dve("scalar_tensor_tensor", [row_b, ng_b], [orow_b], out=oa, in0=sc, scalar=1.0,
                  in1=ng_ap[0:1, sl * D:(sl + 1) * D], op0=ALU.add, op1=ALU.mult)
            c.dve("tensor_copy", [row_b], [orow_b], out=ob, in_=sh)
            c.dve("tensor_scalar", [row_b], [orow_b], out=og, in0=gt, scalar1=(1.0 if sl == 1 else 0.5),
                  scalar2=None, op0=ALU.mult)
        c.sp("dma_start", [orow_b], [c.modrows_b[i]], out=c.modrows[i:i + 1].rearrange("o s r d -> o (s r d)"),
             in_=orow[0:1, :])
    ar.release()


def ret_tables(c, dec_f, dec_b):
    ar = c.arena
    ct, ct_b = fetch_ctab(c)
    t = Ctx()
    raw, raw_b = ar.alloc(8, F32), Buf("decraw")
    c.sp("dma_start", [], [raw_b], out=raw[:, 0:4], in_=dec_f.partition_broadcast(128))
    c.sp("dma_start", [], [raw_b], out=raw[:, 4:8], in_=dec_b.partition_broadcast(128))
    lg, lg_b = ar.alloc(8, F32), Buf("lg")
    c.act("activation", [raw_b], [lg_b], out=lg, in_=raw, func=AF.Exp, scale=-1.0)
    c.act("activation", [lg_b], [lg_b], out=lg, in_=lg, func=AF.Ln, bias=1.0)
    c.dve("tensor_scalar", [lg_b], [lg_b], out=lg, in0=lg, scalar1=-1.0, scalar2=None, op0=ALU.mult)
    cols, cols_b = ar.alloc(24, F32), Buf("deccols")
    src = [(512, 0), (513, 4), (514, 0), (515, 4), (516, 0), (516, 4)]
    for kind, (ccol, lgo) in enumerate(src):
        for h in range(R_HEADS):
            c.act("activation", [lg_b, ct_b], [cols_b], out=cols[:, kind * 4 + h:kind * 4 + h + 1],
                  in_=ct[:, ccol:ccol + 1], func=AF.Exp, scale=lg[:, lgo + h:lgo + h + 1])
    t.cols, t.cols_b = cols, cols_b
    DT, DT_b = ar.alloc(4 * 128, F32), Buf("DT")
    e1, e1_b = ar.alloc(128, F32), Buf("e1")
    e2, e2_b = ar.alloc(128, F32), Buf("e2")
    for h in range(R_HEADS):
        c.act("activation", [lg_b, ct_b], [e1_b], out=e1, in_=ct[:, 0:128], func=AF.Exp, scale=lg[:, h:h + 1])
        c.dve("tensor_tensor", [e1_b, ct_b], [e1_b], out=e1, in0=e1, in1=ct[:, 128:256], op=ALU.mult)
        c.act("activation", [lg_b, ct_b], [e2_b], out=e2, in_=ct[:, 256:384], func=AF.Exp, scale=lg[:, 4 + h:5 + h])
        c.dve("tensor_tensor", [e2_b, ct_b], [e2_b], out=e2, in0=e2, in1=ct[:, 384:512], op=ALU.mult)
        c.dve("tensor_tensor", [e1_b, e2_b], [DT_b], out=DT[:, h * 128:(h + 1) * 128], in0=e1, in1=e2, op=ALU.add)
    t.DT, t.DT_b = DT, DT_b
    return t


def phase_ret_in(c, xin, w_in, rows3, S, dt):
    Tracker.phase = "ret_in"
    x_in, xin_b = xin
    ar = c.arena
    ar.mark()
    NKC = R_IN
    win = ar.alloc(KC * NKC, BF16).rearrange("p (k n) -> p k n", k=KC)
    win_b = [Buf(f"rwin{k}") for k in range(KC)]
    w_in_v = w_in.rearrange("(k p) n -> p k n", p=128)
    for k in range(KC):
        c.pool("dma_start", [], [win_b[k]], out=win[:, k, :], in_=w_in_v[:, k, :])
    load_bcast_rows(c, rows3)
    NT = 512
    nsup = S // NT
    hT = [ar.alloc(KC * NT, BF16).rearrange("p (k n) -> p k n", k=KC) for _ in range(2)]
    hT_b = [Buf("hT0"), Buf("hT1")]
    cs = [(ar.alloc(NT, F32), Buf(f"cs{i}")) for i in range(2)]
    sn = [(ar.alloc(NT, F32), Buf(f"sn{i}")) for i in range(2)]
    tmp = [(ar.alloc(NT, F32), Buf(f"rt{i}")) for i in range(4)]
    qst = [(ar.alloc(2 * NT, BF16).rearrange("p (a n) -> p a n", a=2), Buf(f"qst{i}")) for i in range(2)]
    KDf, KDf_b = ar.alloc(1024, F32), Buf("KDf")
    KDb, KDb_b = ar.alloc(1024, F32), Buf("KDb")
    for h in range(R_HEADS):
        for (KD, KD_b, kind) in ((KDf, KDf_b, 2), (KDb, KDb_b, 3)):
            c.dve("memset", [], [KD_b], KD[:, h * 256:(h + 1) * 256], 1.0 / 16.0)
            c.dve("tensor_scalar", [KD_b, dt.cols_b], [KD_b], out=KD[:, h * 256:(h + 1) * 256],
                  in0=KD[:, h * 256:(h + 1) * 256], scalar1=dt.cols[:, kind * 4 + h:kind * 4 + h + 1],
                  scalar2=None, op0=ALU.mult)
    kst = [(ar.alloc(1024, BF16), Buf(f"kst{i}")) for i in range(2)]
    vst = [(ar.alloc(512, BF16), Buf(f"vst{i}")) for i in range(2)]
    gst = [(ar.alloc(512, F32), Buf(f"gst{i}")) for i in range(2)]
    ktb = [(ar.alloc(8 * NT, BF16).rearrange("p (a n) -> p a n", a=8), Buf("ktb"))]
    xin_v = x_in.rearrange("(n p) d -> n p d", p=128)
    pbi = 0
    qi = 0
    vi = 0
    for su in range(nsup):
        hTs, hTb = hT[su % 2], hT_b[su % 2]
        tok = slice(su * NT, (su + 1) * NT)
        c_ap, c_b = cs[su % 2]
        s_ap, s_b = sn[su % 2]
        c.sp("dma_start", [c.CR_b], [c_b], out=c_ap, in_=c.CR[:, tok])
        c.sp("dma_start", [c.SR_b], [s_b], out=s_ap, in_=c.SR[:, tok])
        for ts in range(4):
            emit_front(c, xin_v[su * 4 + ts], xin_b[su * 4 + ts], hTs, hTb, ts * 128)
        kt_ap, kt_b = ktb[0]
        if su == 0:
            dbg(c, "hT", hTs, hTb)
            dbg(c, "win0", win[:, 0, :], win_b[0])
            dbg(c, "win7", win[:, 7, :], win_b[7])
        for which in range(2):
            for h in range(R_HEADS):
                base = which * R_QK + h * R_DK
                p1, p1_b = c.pb[pbi % 6]
                p2, p2_b = c.pb[(pbi + 1) % 6]
                pbi += 2
                for half, (pp, pp_b) in enumerate(((p1, p1_b), (p2, p2_b))):
                    for k in range(KC):
                        c.pe("matmul", [win_b[k], hTb], [pp_b], pp,
                             lhsT=win[:, k, base + half * 128:base + (half + 1) * 128], rhs=hTs[:, k, :],
                             start=(k == 0), stop=(k == KC - 1))
                t1, t1b = tmp[0]
                t2, t2b = tmp[1]
                t3, t3b = tmp[2]
                t4, t4b = tmp[3]
                c.dve("tensor_tensor", [p1_b, c_b], [t1b], out=t1, in0=p1, in1=c_ap, op=ALU.mult)
                c.dve("tensor_tensor", [p2_b, s_b], [t2b], out=t2, in0=p2, in1=s_ap, op=ALU.mult)
                c.dve("tensor_tensor", [p1_b, s_b], [t3b], out=t3, in0=p1, in1=s_ap, op=ALU.mult)
                c.dve("tensor_tensor", [p2_b, c_b], [t4b], out=t4, in0=p2, in1=c_ap, op=ALU.mult)
                if which == 0:
                    o_ap, o_b = qst[qi % 2]
                    qi += 1
                    o1, o2 = o_ap[:, 0, :], o_ap[:, 1, :]
                else:
                    o_ap, o_b = kt_ap, kt_b
                    o1, o2 = kt_ap[:, 2 * h, :], kt_ap[:, 2 * h + 1, :]
                c.dve("tensor_tensor", [t1b, t2b], [o_b], out=o1, in0=t1, in1=t2, op=ALU.subtract)
                c.dve("tensor_tensor", [t3b, t4b], [o_b], out=o2, in0=t3, in1=t4, op=ALU.add)
                if which == 0:
                    dst = c.QT[h * 256:(h + 1) * 256, tok].rearrange("(a p) n -> p a n", p=128)
                    c.sp("dma_start", [o_b], [c.QT_b[su][h]], out=dst, in_=o_ap)
            if which == 1:
                dst = c.KT[:, tok].rearrange("(a p) n -> p a n", p=128)
                c.sp("dma_start", [kt_b], [c.KT_b[su]], out=dst, in_=kt_ap)
        for ts in range(4):
            pt_ap, pt_buf = c.ptr[0]
            for a in range(8):
                c.pe("transpose", [kt_b, c.ident[1]], [pt_buf], out=pt_ap[:, a * 128:(a + 1) * 128],
                     in_=kt_ap[:, a, ts * 128:(ts + 1) * 128], identity=c.ident[0])
            for di, (KD, KD_b, dst, dst_b) in enumerate(((KDf, KDf_b, c.KF, c.KF_b), (KDb, KDb_b, c.KB, c.KB_b))):
                k_ap, k_b = kst[di]
                c.dve("tensor_tensor", [pt_buf, KD_b], [k_b], out=k_ap, in0=pt_ap, in1=KD, op=ALU.mult)
                c.sp("dma_start", [k_b], [dst_b[su * 4 + ts]], out=dst[(su * 4 + ts) * 128:(su * 4 + ts + 1) * 128, :],
                     in_=k_ap)
        for ts in range(4):
            rows = slice((su * 4 + ts) * 128, (su * 4 + ts + 1) * 128)
            for nb in range(8):
                pp, pp_b = c.pb[pbi % 6]
                pbi += 1
                col0 = 2 * R_QK + nb * 512
                for k in range(KC):
                    c.pe("matmul", [win_b[k], hTb], [pp_b], pp, lhsT=hTs[:, k, ts * 128:(ts + 1) * 128],
                         rhs=win[:, k, col0:col0 + 512], start=(k == 0), stop=(k == KC - 1))
                if nb < 4:
                    v_ap, v_b = vst[vi % 2]
                    vi += 1
                    c.act("activation", [pp_b], [v_b], out=v_ap, in_=pp, func=AF.Copy)
                    c.sp("dma_start", [v_b], [c.V_b[su * 4 + ts][nb]], out=c.V[rows, nb * 512:(nb + 1) * 512], in_=v_ap)
                else:
                    g_ap, g_b = gst[vi % 2]
                    vi += 1
                    c.act("activation", [pp_b], [g_b], out=g_ap, in_=pp, func=AF.Silu)
                    c.sp("dma_start", [g_b], [c.Gs_b[su * 4 + ts][nb - 4]], out=c.Gs[rows, (nb - 4) * 512:(nb - 3) * 512],
                         in_=g_ap)
    ar.release()


def phase_ret_bwd(c, S, dt):
    Tracker.phase = "ret_bwd"
    ar = c.arena
    ar.mark()
    nch = S // 128
    St = ar.alloc(R_HEADS * 1024, F32).rearrange("p (h n) -> p h n", h=R_HEADS)
    St_b = [Buf(f"St{h}") for h in range(R_HEADS)]
    Sb = ar.alloc(R_HEADS * 1024, BF16).rearrange("p (h n) -> p h n", h=R_HEADS)
    Sb_b = [Buf(f"Sb{h}") for h in range(R_HEADS)]
    for h in range(R_HEADS):
        c.dve("memset", [], [St_b[h]], St[:, h], 0.0)
        c.dve("memset", [], [Sb_b[h]], Sb[:, h], 0.0)
    kb = [(ar.alloc(1024, BF16), Buf(f"kbt{i}")) for i in range(2)]
    vt = [(ar.alloc(2048, BF16), Buf(f"vt{i}")) for i in range(2)]
    pbi = 0
    for i, ch in enumerate(range(nch - 1, -1, -1)):
        k_ap, k_b = kb[i % 2]
        v_ap, v_b = vt[i % 2]
        rows = slice(ch * 128, (ch + 1) * 128)
        c.sp("dma_start", [c.KB_b[ch]], [k_b], out=k_ap, in_=c.KB[rows, :])
        c.sp("dma_start", c.V_b[ch], [v_b], out=v_ap, in_=c.V[rows, :])
        for h in range(R_HEADS):
            c.sp("dma_start", [Sb_b[h]], [c.SB_b[ch][h]], out=c.SB[ch, h], in_=Sb[:, h])
            for a in range(2):
                pp, pp_b = c.pb[pbi % 8]
                pbi += 1
                c.pe("matmul", [k_b, v_b], [pp_b], pp, lhsT=k_ap[:, h * 256 + a * 128:h * 256 + (a + 1) * 128],
                     rhs=v_ap[:, h * 512:(h + 1) * 512], start=True, stop=True)
                c.dve("scalar_tensor_tensor", [St_b[h], dt.cols_b, pp_b], [St_b[h]], out=St[:, h, a * 512:(a + 1) * 512],
                      in0=St[:, h, a * 512:(a + 1) * 512], scalar=dt.cols[:, 5 * 4 + h:5 * 4 + h + 1], in1=pp,
                      op0=ALU.mult, op1=ALU.add)
            c.act("activation", [St_b[h]], [Sb_b[h]], out=Sb[:, h], in_=St[:, h], func=AF.Copy)
    ar.release()


def phase_ret_fwd(c, xio, w_out, gn_g, gn_b, S, dt):
    Tracker.phase = "ret_fwd"
    x_io, xio_b = xio
    ar = c.arena
    ar.mark()
    nch = S // 128
    wo = ar.alloc(16 * D, BF16).rearrange("p (k n) -> p k n", k=16)
    wo_b = Buf("rwo")
    c.pool("dma_start", [], [wo_b], out=wo, in_=w_out.rearrange("(k p) n -> p k n", p=128))
    gg, gg_b = ar.alloc(R_VTOT, F32), Buf("gng")
    gb, gb_b = ar.alloc(R_VTOT, F32), Buf("gnb")
    c.sp("dma_start", [], [gg_b], out=gg, in_=gn_g.partition_broadcast(128))
    c.sp("dma_start", [], [gb_b], out=gb, in_=gn_b.partition_broadcast(128))
    eps, eps_b = ar.alloc(8, F32)[:, 0:1], Buf("gneps")
    c.dve("memset", [], [eps_b], eps, GN_EPS)
    St = ar.alloc(R_HEADS * 1024, F32).rearrange("p (h n) -> p h n", h=R_HEADS)
    St_b = [Buf(f"Sf{h}") for h in range(R_HEADS)]
    Sb = ar.alloc(R_HEADS * 1024, BF16).rearrange("p (h n) -> p h n", h=R_HEADS)
    Sb_b = [Buf(f"Sfb{h}") for h in range(R_HEADS)]
    for h in range(R_HEADS):
        c.dve("memset", [], [St_b[h]], St[:, h], 0.0)
        c.dve("memset", [], [Sb_b[h]], Sb[:, h], 0.0)
    qt = [(ar.alloc(1024, BF16).rearrange("p (a n) -> p a n", a=8), Buf(f"qt{i}")) for i in range(2)]
    kt = [(ar.alloc(1024, BF16).rearrange("p (a n) -> p a n", a=8), Buf(f"kt{i}")) for i in range(2)]
    kf = [(ar.alloc(1024, BF16), Buf(f"kf{i}")) for i in range(2)]
    vt = [(ar.alloc(2048, BF16), Buf(f"vt{i}")) for i in range(2)]
    gt = [(ar.alloc(2048, F32), Buf(f"gt{i}")) for i in range(2)]
    sbt = [(ar.alloc(R_HEADS * 1024, BF16).rearrange("p (h n) -> p h n", h=R_HEADS), Buf(f"sbt{i}"))
           for i in range(2)]
    y, y_b = ar.alloc(R_VTOT, F32), [Buf(f"y{h}") for h in range(R_HEADS)]
    z, z_b = ar.alloc(R_VTOT, BF16), Buf("z")
    zT = ar.alloc(16 * 128, BF16).rearrange("p (k n) -> p k n", k=16)
    zT_b = Buf("zT")
    pT = [(ar.alloc(128, BF16), Buf(f"pT{i}")) for i in range(2)]
    st6, st6_b = ar.alloc(4 * 8, F32), [Buf(f"st6{h}") for h in range(R_HEADS)]
    xv = x_io.rearrange("(n p) d -> n p d", p=128)
    for ch in range(nch):
        i = ch
        q_ap, q_b = qt[i % 2]
        k_ap, k_b = kt[i % 2]
        f_ap, f_b = kf[i % 2]
        v_ap, v_b = vt[i % 2]
        g_ap, g_b = gt[i % 2]
        s_ap, s_b = sbt[i % 2]
        rows = slice(ch * 128, (ch + 1) * 128)
        su = ch // 4
        c.sp("dma_start", c.QT_b[su], [q_b], out=q_ap, in_=c.QT[:, rows].rearrange("(a p) n -> p a n", p=128))
        c.sp("dma_start", [c.KT_b[su]], [k_b], out=k_ap, in_=c.KT[:, rows].rearrange("(a p) n -> p a n", p=128))
        c.sp("dma_start", [c.KF_b[ch]], [f_b], out=f_ap, in_=c.KF[rows, :])
        c.sp("dma_start", c.V_b[ch], [v_b], out=v_ap, in_=c.V[rows, :])
        c.sp("dma_start", c.Gs_b[ch], [g_b], out=g_ap, in_=c.Gs[rows, :])
        c.sp("dma_start", c.SB_b[ch], [s_b], out=s_ap, in_=c.SB[ch].rearrange("h p n -> p h n"))
        for h in range(R_HEADS):
            ps_ap, ps_b = c.pb[0]
            st_ap = ps_ap[:, (h % 4) * 128:(h % 4 + 1) * 128]
            for a in range(2):
                c.pe("matmul", [k_b, q_b], [ps_b], st_ap, lhsT=k_ap[:, 2 * h + a, :], rhs=q_ap[:, 2 * h + a, :],
                     start=(a == 0), stop=(a == 1))
            p_ap, p_b = pT[h % 2]
            c.dve("tensor_tensor", [ps_b, dt.DT_b], [p_b], out=p_ap, in0=st_ap, in1=dt.DT[:, h * 128:(h + 1) * 128],
                  op=ALU.mult)
            y0, y0_b = c.pb[1]
            y1, y1_b = c.pb[2]
            y2, y2_b = c.pb[3]
            vh = v_ap[:, h * 512:(h + 1) * 512]
            c.pe("matmul", [p_b, v_b], [y0_b], y0, lhsT=p_ap, rhs=vh, start=True, stop=True)
            for a in range(2):
                c.pe("matmul", [q_b, Sb_b[h]], [y1_b], y1, lhsT=q_ap[:, 2 * h + a, :], rhs=Sb[:, h, a * 512:(a + 1) * 512],
                     start=(a == 0), stop=(a == 1))
            for a in range(2):
                c.pe("matmul", [q_b, s_b], [y2_b], y2, lhsT=q_ap[:, 2 * h + a, :], rhs=s_ap[:, h, a * 512:(a + 1) * 512],
                     start=(a == 0), stop=(a == 1))
            yh = y[:, h * 512:(h + 1) * 512]
            c.act("activation", [y0_b], [y_b[h]], out=yh, in_=y0, func=AF.Copy)
            c.dve("scalar_tensor_tensor", [y1_b, dt.cols_b, y_b[h]], [y_b[h]], out=yh, in0=y1,
                  scalar=dt.cols[:, 0 * 4 + h:0 * 4 + h + 1], in1=yh, op0=ALU.mult, op1=ALU.add)
            c.dve("scalar_tensor_tensor", [y2_b, dt.cols_b, y_b[h]], [y_b[h]], out=yh, in0=y2,
                  scalar=dt.cols[:, 1 * 4 + h:1 * 4 + h + 1], in1=yh, op0=ALU.mult, op1=ALU.add)
            for a in range(2):
                u, u_b = c.pb[4 + a]
                c.pe("matmul", [f_b, v_b], [u_b], u, lhsT=f_ap[:, h * 256 + a * 128:h * 256 + (a + 1) * 128], rhs=vh,
                     start=True, stop=True)
                c.dve("scalar_tensor_tensor", [St_b[h], dt.cols_b, u_b], [St_b[h]], out=St[:, h, a * 512:(a + 1) * 512],
                      in0=St[:, h, a * 512:(a + 1) * 512], scalar=dt.cols[:, 4 * 4 + h:4 * 4 + h + 1], in1=u,
                      op0=ALU.mult, op1=ALU.add)
            c.act("activation", [St_b[h]], [Sb_b[h]], out=Sb[:, h], in_=St[:, h], func=AF.Copy)
            s6 = st6[:, h * 8:h * 8 + 8]
            c.dve("bn_stats", [y_b[h]], [st6_b[h]], out=s6[:, 0:6], in_=yh)
            c.dve("bn_aggr", [st6_b[h]], [st6_b[h]], out=s6[:, 6:8], in_=s6[:, 0:6])
            c.act("activation", [st6_b[h], eps_b], [st6_b[h]], out=s6[:, 0:1], in_=s6[:, 7:8], func=AF.Sqrt,
                  bias=eps, scale=1.0)
            c.dve("reciprocal", [st6_b[h]], [st6_b[h]], out=s6[:, 1:2], in_=s6[:, 0:1])
            c.dve("tensor_scalar", [y_b[h], st6_b[h]], [y_b[h]], out=yh, in0=yh, scalar1=s6[:, 6:7],
                  scalar2=s6[:, 1:2], op0=ALU.subtract, op1=ALU.mult)
            hs = slice(h * 512, (h + 1) * 512)
            c.dve("tensor_tensor", [y_b[h], gg_b], [y_b[h]], out=yh, in0=yh, in1=gg[:, hs], op=ALU.mult)
            c.dve("tensor_tensor", [y_b[h], gb_b], [y_b[h]], out=yh, in0=yh, in1=gb[:, hs], op=ALU.add)
            c.dve("tensor_tensor", [y_b[h], g_b], [z_b], out=z[:, hs], in0=yh, in1=g_ap[:, hs], op=ALU.mult)
        pt_ap, pt_buf = c.ptr[0]
        for half in range(2):
            for a in range(8):
                kc = half * 8 + a
                c.pe("transpose", [z_b, c.ident[1]], [pt_buf], out=pt_ap[:, a * 128:(a + 1) * 128],
                     in_=z[:, kc * 128:(kc + 1) * 128], identity=c.ident[0])
            c.act("activation", [pt_buf], [zT_b], out=zT[:, half * 8:(half + 1) * 8, :],
                  in_=pt_ap.rearrange("p (k n) -> p k n", k=8), func=AF.Copy)
        for nb in range(2):
            o_ap, o_buf = c.pb[6]
            for kc in range(16):
                c.pe("matmul", [zT_b, wo_b], [o_buf], o_ap, lhsT=zT[:, kc, :], rhs=wo[:, kc, nb * 512:(nb + 1) * 512],
                     start=(kc == 0), stop=(kc == 15))
            csl = slice(nb * 512, (nb + 1) * 512)
            emit_resid(c, o_ap, o_buf, xv[ch][:, csl], xio_b[ch][nb], xv[ch][:, csl], xio_b[ch][nb], c.G[0][:, csl])
    ar.release()


def alloc_mla_scratch(c, nc, S):
    nt = S // 128
    c.QTN = nc.dram_tensor("QTN", [M_HEADS, 128, S], BF16, kind=SCRATCH_KIND).ap()
    c.QTR = nc.dram_tensor("QTR", [M_HEADS, 64, S], BF16, kind=SCRATCH_KIND).ap()
    c.KTN = nc.dram_tensor("KTN", [M_HEADS, 128, S], BF16, kind=SCRATCH_KIND).ap()
    c.KTR = nc.dram_tensor("KTR", [64, S], BF16, kind=SCRATCH_KIND).ap()
    c.VM = nc.dram_tensor("VM", [S, M_HEADS * M_V], BF16, kind=SCRATCH_KIND).ap()
    c.QTN_b = [Buf("QTN") for _ in range(nt)]
    c.QTR_b = [Buf("QTR") for _ in range(nt)]
    c.KTN_b = [[Buf("KTN") for _ in range(M_HEADS)] for _ in range(S // 512)]
    c.KTR_b = [Buf("KTR") for _ in range(S // 512)]
    c.VM_b = [Buf("VM") for _ in range(nt)]


def phase_mla_in(c, xin, w_down, qng, kvng, w_uq, w_ukv, rows3, S):
    Tracker.phase = "mla_in"
    x_in, xin_b = xin
    ar = c.arena
    ar.mark()
    nt = S // 128
    NT = 512
    nsup = S // NT
    wdn = ar.alloc(KC * M_DOWN, BF16).rearrange("p (k n) -> p k n", k=KC)
    wdn_b = Buf("wdn")
    c.pool("dma_start", [], [wdn_b], out=wdn, in_=w_down.rearrange("(k p) n -> p k n", p=128))
    wuq = ar.alloc(3 * 1536, BF16).rearrange("p (k n) -> p k n", k=3)
    wuq_b = Buf("wuq")
    c.pool("dma_start", [], [wuq_b], out=wuq, in_=w_uq.rearrange("(k p) n -> p k n", p=128))
    wuk = ar.alloc(2 * 1024, BF16).rearrange("p (k n) -> p k n", k=2)
    wuk_b = Buf("wuk")
    wuv = ar.alloc(2 * 1024, BF16).rearrange("p (k n) -> p k n", k=2)
    wuv_b = Buf("wuv")
    for kc in range(2):
        src = w_ukv[kc * 128:(kc + 1) * 128, :].rearrange("p (h two d) -> p h two d", two=2, d=128)
        c.pool("dma_start", [], [wuk_b], out=wuk[:, kc, :].rearrange("p (h d) -> p h d", d=128), in_=src[:, :, 0, :])
        c.pool("dma_start", [], [wuv_b], out=wuv[:, kc, :].rearrange("p (h d) -> p h d", d=128), in_=src[:, :, 1, :])
    qg, qg_b = ar.alloc(Q_LORA, F32), Buf("qg")
    kg, kg_b = ar.alloc(KV_LORA, F32), Buf("kg")
    c.sp("dma_start", [], [qg_b], out=qg, in_=qng.partition_broadcast(128))
    c.sp("dma_start", [], [kg_b], out=kg, in_=kvng.partition_broadcast(128))
    cm, cm_b = ar.alloc(nt * 32, F32), Buf("cmr")
    sm, sm_b = ar.alloc(nt * 32, F32), Buf("smr")
    c.sp("dma_start", [c.CM_b], [cm_b], out=cm.rearrange("p (t j) -> p t j", j=32),
         in_=c.CM.rearrange("(t p) j -> p t j", p=128))
    c.sp("dma_start", [c.SM_b], [sm_b], out=sm.rearrange("p (t j) -> p t j", j=32),
         in_=c.SM.rearrange("(t p) j -> p t j", p=128))
    load_bcast_rows(c, rows3)
    hT = [ar.alloc(KC * NT, BF16).rearrange("p (k n) -> p k n", k=KC) for _ in range(2)]
    hT_b = [Buf("hT0"), Buf("hT1")]
    cd = [(ar.alloc(M_DOWN, F32), Buf(f"cd{i}")) for i in range(2)]
    cqn, cqn_b = ar.alloc(Q_LORA, BF16), Buf("cqn")
    ckn, ckn_b = ar.alloc(KV_LORA, BF16), Buf("ckn")
    krr, krr_b = ar.alloc(64, BF16), Buf("krr")
    st, st_b = ar.alloc(8, F32), Buf("mst")
    cqT = ar.alloc(3 * NT, BF16).rearrange("p (k n) -> p k n", k=3)
    cqT_b = Buf("cqT")
    ckT = ar.alloc(2 * NT, BF16).rearrange("p (k n) -> p k n", k=2)
    ckT_b = Buf("ckT")
    krT, krT_b = ar.alloc(NT, BF16), Buf("krT")
    kts = [(ar.alloc(NT, BF16), Buf(f"kts{i}")) for i in range(2)]
    vst = [(ar.alloc(1024, BF16), Buf(f"mvst{i}")) for i in range(2)]
    qf, qf_b = ar.alloc(1536, F32), Buf("qf")
    qb, qb_b = ar.alloc(1536, BF16), Buf("qb")
    rt = [(ar.alloc(256, F32), Buf(f"mrt{i}")) for i in range(4)]
    kt4 = [(ar.alloc(32, F32), Buf(f"kt4{i}")) for i in range(4)]
    qtn = [(ar.alloc(1024, BF16).rearrange("p (h n) -> p h n", h=8), Buf(f"qtn{i}")) for i in range(2)]
    qtr = [(ar.alloc(1024, BF16).rearrange("p (h n) -> p h n", h=8), Buf(f"qtr{i}")) for i in range(2)]
    eq, eq_b = ar.alloc(8, F32)[:, 0:1], Buf("meps")
    c.dve("memset", [], [eq_b], eq, RMS_EPS)
    xin_v = x_in.rearrange("(n p) d -> n p d", p=128)
    ki = 0
    for su in range(nsup):
        hTs, hTb = hT[su % 2], hT_b[su % 2]
        tok = slice(su * NT, (su + 1) * NT)
        for ts in range(4):
            emit_front(c, xin_v[su * 4 + ts], xin_b[su * 4 + ts], hTs, hTb, ts * 128)
        for ts in range(4):
            tile_i = su * 4 + ts
            tcs = slice(ts * 128, (ts + 1) * 128)
            d0, d0_b = c.pb[0]
            d1, d1_b = c.pb[1]
            for k in range(KC):
                c.pe("matmul", [hTb, wdn_b], [d0_b], d0, lhsT=hTs[:, k, tcs], rhs=wdn[:, k, 0:512],
                     start=(k == 0), stop=(k == KC - 1))
            for k in range(KC):
                c.pe("matmul", [hTb, wdn_b], [d1_b], d1[:, 0:192], lhsT=hTs[:, k, tcs], rhs=wdn[:, k, 512:704],
                     start=(k == 0), stop=(k == KC - 1))
            cd_ap, cd_b = cd[tile_i % 2]
            c.act("activation", [d0_b], [cd_b], out=cd_ap[:, 0:512], in_=d0, func=AF.Copy)
            c.act("activation", [d1_b], [cd_b], out=cd_ap[:, 512:704], in_=d1[:, 0:192], func=AF.Copy)
            c.dve("scalar_tensor_tensor", [cd_b], [cqn_b, st_b], out=cqn, in0=cd_ap[:, 0:384], scalar=1.0,
                  in1=cd_ap[:, 0:384], op0=ALU.mult, op1=ALU.mult, accum_out=st[:, 0:1])
            c.dve("scalar_tensor_tensor", [cd_b], [ckn_b, st_b], out=ckn, in0=cd_ap[:, 384:640], scalar=1.0,
                  in1=cd_ap[:, 384:640], op0=ALU.mult, op1=ALU.mult, accum_out=st[:, 1:2])
            c.act("activation", [st_b, eq_b], [st_b], out=st[:, 2:3], in_=st[:, 0:1], func=AF.Sqrt,
                  scale=1.0 / Q_LORA, bias=eq)
            c.act("activation", [st_b, eq_b], [st_b], out=st[:, 3:4], in_=st[:, 1:2], func=AF.Sqrt,
                  scale=1.0 / KV_LORA, bias=eq)
            c.dve("reciprocal", [st_b], [st_b], out=st[:, 4:6], in_=st[:, 2:4])
            c.dve("scalar_tensor_tensor", [cd_b, st_b, qg_b], [cqn_b], out=cqn, in0=cd_ap[:, 0:384],
                  scalar=st[:, 4:5], in1=qg, op0=ALU.mult, op1=ALU.mult)
            c.dve("scalar_tensor_tensor", [cd_b, st_b, kg_b], [ckn_b], out=ckn, in0=cd_ap[:, 384:640],
                  scalar=st[:, 5:6], in1=kg, op0=ALU.mult, op1=ALU.mult)
            cs_t = cm[:, tile_i * 32:(tile_i + 1) * 32]
            sn_t = sm[:, tile_i * 32:(tile_i + 1) * 32]
            x1, x2 = cd_ap[:, 640:672], cd_ap[:, 672:704]
            (a1, a1b), (a2, a2b), (a3, a3b), (a4, a4b) = kt4
            c.dve("tensor_tensor", [cd_b, cm_b], [a1b], out=a1, in0=x1, in1=cs_t, op=ALU.mult)
            c.dve("tensor_tensor", [cd_b, sm_b], [a2b], out=a2, in0=x2, in1=sn_t, op=ALU.mult)
            c.dve("tensor_tensor", [cd_b, sm_b], [a3b], out=a3, in0=x1, in1=sn_t, op=ALU.mult)
            c.dve("tensor_tensor", [cd_b, cm_b], [a4b], out=a4, in0=x2, in1=cs_t, op=ALU.mult)
            c.dve("tensor_tensor", [a1b, a2b], [krr_b], out=krr[:, 0:32], in0=a1, in1=a2, op=ALU.subtract)
            c.dve("tensor_tensor", [a3b, a4b], [krr_b], out=krr[:, 32:64], in0=a3, in1=a4, op=ALU.add)
            pt_ap, pt_buf = c.ptr[0]
            for a in range(3):
                c.pe("transpose", [cqn_b, c.ident[1]], [pt_buf], out=pt_ap[:, a * 128:(a + 1) * 128],
                     in_=cqn[:, a * 128:(a + 1) * 128], identity=c.ident[0])
            for a in range(2):
                c.pe("transpose", [ckn_b, c.ident[1]], [pt_buf], out=pt_ap[:, 384 + a * 128:384 + (a + 1) * 128],
                     in_=ckn[:, a * 128:(a + 1) * 128], identity=c.ident[0])
            c.pe("transpose", [krr_b, c.ident[1]], [pt_buf], out=pt_ap[0:64, 640:768], in_=krr,
                 identity=c.ident[0])
            c.act("activation", [pt_buf], [cqT_b], out=cqT[:, :, tcs],
                  in_=pt_ap[:, 0:384].rearrange("p (k n) -> p k n", k=3), func=AF.Copy)
            c.act("activation", [pt_buf], [ckT_b], out=ckT[:, :, tcs],
                  in_=pt_ap[:, 384:640].rearrange("p (k n) -> p k n", k=2), func=AF.Copy)
            c.act("activation", [pt_buf], [krT_b], out=krT[0:64, tcs], in_=pt_ap[0:64, 640:768], func=AF.Copy)
        c.sp("dma_start", [krT_b], [c.KTR_b[su]], out=c.KTR[:, tok], in_=krT[0:64, :])
        for h in range(M_HEADS):
            pp, pp_b = c.pb[2 + h % 2]
            for kc in range(2):
                c.pe("matmul", [wuk_b, ckT_b], [pp_b], pp, lhsT=wuk[:, kc, h * 128:(h + 1) * 128], rhs=ckT[:, kc, :],
                     start=(kc == 0), stop=(kc == 1))
            k_ap, k_b = kts[ki % 2]
            ki += 1
            c.act("activation", [pp_b], [k_b], out=k_ap, in_=pp, func=AF.Copy)
            c.sp("dma_start", [k_b], [c.KTN_b[su][h]], out=c.KTN[h][:, tok], in_=k_ap)
        for ts in range(4):
            tile_i = su * 4 + ts
            tcs = slice(ts * 128, (ts + 1) * 128)
            rows = slice(tile_i * 128, (tile_i + 1) * 128)
            v_ap, v_b = vst[tile_i % 2]
            for nb in range(2):
                pp, pp_b = c.pb[2 + nb]
                for kc in range(2):
                    c.pe("matmul", [wuv_b, ckT_b], [pp_b], pp, lhsT=ckT[:, kc, tcs], rhs=wuv[:, kc, nb * 512:(nb + 1) * 512],
                         start=(kc == 0), stop=(kc == 1))
                c.act("activation", [pp_b], [v_b], out=v_ap[:, nb * 512:(nb + 1) * 512], in_=pp, func=AF.Copy)
            c.sp("dma_start", [v_b], [c.VM_b[tile_i]], out=c.VM[rows, :], in_=v_ap)
            for nb in range(3):
                pp, pp_b = c.pb[(4, 5, 1)[nb]]
                for kc in range(3):
                    c.pe("matmul", [wuq_b, cqT_b], [pp_b], pp, lhsT=cqT[:, kc, tcs], rhs=wuq[:, kc, nb * 512:(nb + 1) * 512],
                         start=(kc == 0), stop=(kc == 2))
                c.act("activation", [pp_b], [qf_b], out=qf[:, nb * 512:(nb + 1) * 512], in_=pp, func=AF.Copy)
            qf3 = qf.rearrange("p (h d) -> p h d", h=8)
            qb3 = qb.rearrange("p (h d) -> p h d", h=8)
            cs_t = cm[:, tile_i * 32:(tile_i + 1) * 32].unsqueeze(1).broadcast_to([128, 8, 32])
            sn_t = sm[:, tile_i * 32:(tile_i + 1) * 32].unsqueeze(1).broadcast_to([128, 8, 32])
            x1, x2 = qf3[:, :, 128:160], qf3[:, :, 160:192]
            rr = [(r[0].rearrange("p (h j) -> p h j", h=8), r[1]) for r in rt]
            (a1, a1b), (a2, a2b), (a3, a3b), (a4, a4b) = rr
            c.dve("tensor_tensor", [qf_b, cm_b], [a1b], out=a1, in0=x1, in1=cs_t, op=ALU.mult)
            c.dve("tensor_tensor", [qf_b, sm_b], [a2b], out=a2, in0=x2, in1=sn_t, op=ALU.mult)
            c.dve("tensor_tensor", [qf_b, sm_b], [a3b], out=a3, in0=x1, in1=sn_t, op=ALU.mult)
            c.dve("tensor_tensor", [qf_b, cm_b], [a4b], out=a4, in0=x2, in1=cs_t, op=ALU.mult)
            c.dve("tensor_tensor", [a1b, a2b], [qb_b], out=qb3[:, :, 128:160], in0=a1, in1=a2, op=ALU.subtract)
            c.dve("tensor_tensor", [a3b, a4b], [qb_b], out=qb3[:, :, 160:192], in0=a3, in1=a4, op=ALU.add)
            c.dve("tensor_copy", [qf_b], [qb_b], out=qb3[:, :, 0:128], in_=qf3[:, :, 0:128])
            p6, p6_b = c.pb[6][0].bitcast(BF16), c.pb[6][1]
            p7, p7_b = c.ptr[0]
            for h in range(M_HEADS):
                c.pe("transpose", [qb_b, c.ident[1]], [p6_b], out=p6[:, h * 128:(h + 1) * 128], in_=qb3[:, h, 0:128],
                     identity=c.ident[0])
            for h in range(M_HEADS):
                c.pe("transpose", [qb_b, c.ident[1]], [p7_b], out=p7[0:64, h * 128:(h + 1) * 128],
                     in_=qb3[:, h, 128:192], identity=c.ident[0])
            n_ap, n_b = qtn[tile_i % 2]
            r_ap, r_b = qtr[tile_i % 2]
            c.act("activation", [p6_b], [n_b], out=n_ap, in_=p6.rearrange("p (h n) -> p h n", h=8), func=AF.Copy)
            c.act("activation", [p7_b], [r_b], out=r_ap[0:64], in_=p7[0:64, :].rearrange("p (h n) -> p h n", h=8),
                  func=AF.Copy)
            c.sp("dma_start", [n_b], [c.QTN_b[tile_i]], out=c.QTN[:, :, rows].rearrange("h d s -> d h s"), in_=n_ap)
            c.sp("dma_start", [r_b], [c.QTR_b[tile_i]], out=c.QTR[:, :, rows].rearrange("h d s -> d h s"),
                 in_=r_ap[0:64])
    ar.release()


def phase_mla_attn(c, xio, w_o, S):
    Tracker.phase = "mla_attn"
    x_io, xio_b = xio
    ar = c.arena
    ar.mark()
    nt = S // 128
    nq = S // 512
    OT = ar.alloc(M_HEADS * S, BF16).rearrange("p (h s) -> p h s", h=M_HEADS)
    OT_b = [Buf(f"OT{h}") for h in range(M_HEADS)]
    ar.mark()
    ktr, ktr_b = ar.alloc(S, BF16), Buf("ktr")
    c.dve("memset", [], [ktr_b], ktr[64:128, :], 0.0)
    c.sp("dma_start", c.KTR_b, [ktr_b], out=ktr[0:64, :], in_=c.KTR)
    hd = []
    for i in range(2):
        hd.append(dict(qn=(ar.alloc(S, BF16), Buf(f"qn{i}")), qr=(ar.alloc(S, BF16), Buf(f"qr{i}")),
                       kn=(ar.alloc(S, BF16), Buf(f"kn{i}")),
                       vh=(ar.alloc(S, BF16).rearrange("p (t e) -> p t e", e=128), Buf(f"vh{i}"))))
    for i in range(2):
        c.dve("memset", [], [hd[i]["qr"][1]], hd[i]["qr"][0][64:128, :], 0.0)
    NST = 4
    st_banks = [c.pb[0], c.pb[1], c.pb[2], c.pb[6]]
    pTl = [(ar.alloc(512, BF16), Buf(f"pT{i}")) for i in range(NST)]
    Lacc = [(ar.alloc(512, F32), Buf(f"Lacc{i}")) for i in range(2)]
    RLs = [(ar.alloc(512, F32), Buf(f"RLs{i}")) for i in range(2)]
    ones_f, ones_fb = ar.alloc(128, F32), Buf("ones_f")
    c.dve("memset", [], [ones_fb], ones_f, 1.0)
    cnt = 0
    for h in range(M_HEADS):
        H = hd[h % 2]
        qn, qn_b = H["qn"]
        qr, qr_b = H["qr"]
        kn, kn_b = H["kn"]
        vh, vh_b = H["vh"]
        c.sp("dma_start", c.QTN_b, [qn_b], out=qn, in_=c.QTN[h])
        c.sp("dma_start", c.QTR_b, [qr_b], out=qr[0:64, :], in_=c.QTR[h])
        c.sp("dma_start", [b[h] for b in c.KTN_b], [kn_b], out=kn, in_=c.KTN[h])
        c.sp("dma_start", c.VM_b, [vh_b], out=vh,
             in_=c.VM[:, h * 128:(h + 1) * 128].rearrange("(t p) e -> p t e", p=128))
        iters = [(qt, kt) for qt in range(nq) for kt in range(nt)]
        slots = {}

        def emit_st(i):
            nonlocal cnt
            qt, kt = iters[i]
            qs = slice(qt * 512, (qt + 1) * 512)
            ks = slice(kt * 128, (kt + 1) * 128)
            sT, sT_b = st_banks[cnt % NST]
            p_ap, p_b = pTl[cnt % NST]
            cnt += 1
            slots[i] = (sT, sT_b, p_ap, p_b)
            c.pe("matmul", [kn_b, qn_b], [sT_b], sT, lhsT=kn[:, ks], rhs=qn[:, qs], start=True, stop=False)
            c.pe("matmul", [ktr_b, qr_b], [sT_b], sT, lhsT=ktr[:, ks], rhs=qr[:, qs], start=False, stop=True)

        AHEAD = 3
        for i in range(min(AHEAD, len(iters))):
            emit_st(i)
        for i, (qt, kt) in enumerate(iters):
            if i + AHEAD < len(iters):
                emit_st(i + AHEAD)
            qs = slice(qt * 512, (qt + 1) * 512)
            oT, oT_b = c.pb[3 + qt % 2]
            la, la_b = Lacc[qt % 2]
            sT, sT_b, p_ap, p_b = slots.pop(i)
            c.act("activation", [sT_b], [p_b], out=p_ap, in_=sT, func=AF.Exp, scale=float(M_SCALE))
            c.pe("matmul", [vh_b, p_b], [oT_b], oT, lhsT=vh[:, kt, :], rhs=p_ap, start=(kt == 0), stop=(kt == nt - 1))
            if kt == 0:
                c.dve("tensor_copy", [p_b], [la_b], out=la, in_=p_ap)
            else:
                c.dve("tensor_tensor", [p_b, la_b], [la_b], out=la, in0=la, in1=p_ap, op=ALU.add)
            if kt == nt - 1:
                RB, RB_b = c.pb[5]
                c.pe("matmul", [ones_fb, la_b], [RB_b], RB, lhsT=ones_f, rhs=la, start=True, stop=True)
                R_ap, R_b = RLs[qt % 2]
                c.dve("reciprocal", [RB_b], [R_b], out=R_ap, in_=RB)
                c.dve("tensor_tensor", [oT_b, R_b], [OT_b[h]], out=OT[:, h, qs], in0=oT, in1=R_ap, op=ALU.mult)
    ar.release()
    Tracker.phase = "mla_out"
    wo = ar.alloc(M_HEADS * D, BF16).rearrange("p (k n) -> p k n", k=M_HEADS)
    wo_b = Buf("mwo")
    c.pool("dma_start", [], [wo_b], out=wo, in_=w_o.rearrange("(k p) n -> p k n", p=128))
    xv = x_io.rearrange("(n p) d -> n p d", p=128)
    for t in range(nt):
        for nb in range(2):
            o_ap, o_buf = c.psum_o[c.po_i % len(c.psum_o)]
            c.po_i += 1
            for h in range(M_HEADS):
                c.pe("matmul", [OT_b[h], wo_b], [o_buf], o_ap, lhsT=OT[:, h, t * 128:(t + 1) * 128],
                     rhs=wo[:, h, nb * 512:(nb + 1) * 512], start=(h == 0), stop=(h == M_HEADS - 1))
            csl = slice(nb * 512, (nb + 1) * 512)
            emit_resid(c, o_ap, o_buf, xv[t][:, csl], xio_b[t][nb], xv[t][:, csl], xio_b[t][nb], c.G[0][:, csl])
    ar.release()


def phase_final(c, xin, out, out_b, fg, S):
    Tracker.phase = "final"
    x_in, xin_b = xin
    c.sp("dma_start", [], [c.A[1]], out=c.A[0], in_=fg.partition_broadcast(128))
    xv = x_in.rearrange("(n p) d -> n p d", p=128)
    ov = out.rearrange("(n p) d -> n p d", p=128)
    for t in range(S // 128):
        slot = c.xslot
        c.xslot = (c.xslot + 1) % len(c.xt)
        xt, xb = c.xt[slot]
        hb_ap, hb_buf = c.hb[slot % len(c.hb)]
        ss_ap, ss_buf = c.ss[slot % len(c.ss)]
        c.sp("dma_start", list(xin_b[t]), [xb], out=xt, in_=xv[t])
        c.dve("scalar_tensor_tensor", [xb], [hb_buf, ss_buf], out=hb_ap, in0=xt, scalar=1.0, in1=xt,
              op0=ALU.mult, op1=ALU.mult, accum_out=ss_ap[:, 0:1])
        c.act("activation", [ss_buf, c.eps_rms[1]], [ss_buf], out=ss_ap[:, 1:2], in_=ss_ap[:, 0:1], func=AF.Sqrt,
              scale=1.0 / D, bias=c.eps_rms[0])
        c.dve("reciprocal", [ss_buf], [ss_buf], out=ss_ap[:, 2:3], in_=ss_ap[:, 1:2])
        c.dve("scalar_tensor_tensor", [xb, ss_buf, c.A[1]], [xb], out=xt, in0=xt, scalar=ss_ap[:, 2:3],
              in1=c.A[0], op0=ALU.mult, op1=ALU.mult)
        c.sp("dma_start", [xb], list(out_b[t]), out=ov[t], in_=xt)


def xbufs(S, name):
    return [[Buf(f"{name}{t}_{h}") for h in range(2)] for t in range(S // 128)]


SCRATCH_KIND = "Internal"


def alloc_ret_scratch(c, nc, S):
    nt = S // 128
    c.QT = nc.dram_tensor("QT", [R_QK, S], BF16, kind=SCRATCH_KIND).ap()
    c.KT = nc.dram_tensor("KT", [R_QK, S], BF16, kind=SCRATCH_KIND).ap()
    c.KF = nc.dram_tensor("KF", [S, R_QK], BF16, kind=SCRATCH_KIND).ap()
    c.KB = nc.dram_tensor("KB", [S, R_QK], BF16, kind=SCRATCH_KIND).ap()
    c.V = nc.dram_tensor("Vr", [S, R_VTOT], BF16, kind=SCRATCH_KIND).ap()
    c.Gs = nc.dram_tensor("Gs", [S, R_VTOT], F32, kind=SCRATCH_KIND).ap()
    c.SB = nc.dram_tensor("SBs", [nt, R_HEADS, 128, 1024], BF16, kind=SCRATCH_KIND).ap()
    c.QT_b = [[Buf("QT") for h in range(R_HEADS)] for _ in range(S // 512)]
    c.KT_b = [Buf("KT") for _ in range(S // 512)]
    c.KF_b = [Buf("KF") for _ in range(nt)]
    c.KB_b = [Buf("KB") for _ in range(nt)]
    c.V_b = [[Buf("V") for _ in range(4)] for _ in range(nt)]
    c.Gs_b = [[Buf("Gs") for _ in range(4)] for _ in range(nt)]
    c.SB_b = [[Buf("SB") for _ in range(R_HEADS)] for _ in range(nt)]


def alloc_tables(c, nc, S):
    c.CR = nc.dram_tensor("CR", [128, S], F32, kind=SCRATCH_KIND).ap()
    c.SR = nc.dram_tensor("SR", [128, S], F32, kind=SCRATCH_KIND).ap()
    c.CM = nc.dram_tensor("CM", [S, 32], F32, kind=SCRATCH_KIND).ap()
    c.SM = nc.dram_tensor("SM", [S, 32], F32, kind=SCRATCH_KIND).ap()
    c.CR_b, c.SR_b, c.CM_b, c.SM_b = Buf("CR"), Buf("SR"), Buf("CM"), Buf("SM")


def load_ctab(c, ctab_dram):
    c.ctab_dram = ctab_dram


def fetch_ctab(c):
    ar = c.arena
    ap, b = ar.alloc(CTW, F32), Buf("ctab")
    c.sp("dma_start", [], [b], out=ap, in_=c.ctab_dram)
    c.ctab = (ap, b)
    return c.ctab


def emit_copy_x(c, src, dst, dst_b, S):
    for t in range(S // 128):
        for hf in range(2):
            r_ap, r_buf = c.xr[c.xr_i % 2]
            c.xr_i += 1
            sl = (slice(t * 128, (t + 1) * 128), slice(hf * 512, (hf + 1) * 512))
            c.sp("dma_start", [], [r_buf], out=r_ap, in_=src[sl])
            c.sp("dma_start", [r_buf], [dst_b[t][hf]], out=dst[sl], in_=r_ap)


def build_ret_test(S):
    nc = bass.Bass("TRN2", target_bir_lowering=False)
    x = nc.dram_tensor("x", [S, D], F32, kind="ExternalInput").ap()
    pos = nc.dram_tensor("pos", [S], I32, kind="ExternalInput").ap()
    w_in = nc.dram_tensor("w_in", [D, R_IN], F32, kind="ExternalInput").ap()
    w_out = nc.dram_tensor("w_out", [R_VTOT, D], F32, kind="ExternalInput").ap()
    gn_g = nc.dram_tensor("gn_g", [R_VTOT], F32, kind="ExternalInput").ap()
    gn_b = nc.dram_tensor("gn_b", [R_VTOT], F32, kind="ExternalInput").ap()
    dec_f = nc.dram_tensor("dec_f", [4], F32, kind="ExternalInput").ap()
    dec_b = nc.dram_tensor("dec_b", [4], F32, kind="ExternalInput").ap()
    rows3 = nc.dram_tensor("rows3", [3, D], F32, kind="ExternalInput").ap()
    ident = nc.dram_tensor("ident", [128, 128], BF16, kind="ExternalInput").ap()
    ctab = nc.dram_tensor("ctab", [128, CTW], F32, kind="ExternalInput").ap()
    out = nc.dram_tensor("out", [S, D], F32, kind="ExternalOutput").ap()
    c = Ctx()
    c.tr = Tracker()
    c.ident_dram = ident
    alloc_tables(c, nc, S)
    alloc_ret_scratch(c, nc, S)
    with nc.sbuf_tensor("arena", [128, ARENA_BYTES // 4], F32) as ah, \
            nc.psum_tensor("psum", [128, 4096], F32) as ps:
        setup_common(c, nc, ah, ARENA_BYTES, ps)
        load_ctab(c, ctab)
        phase_setup_tables(c, pos, S)
        xb = xbufs(S, "x")
        ob = xbufs(S, "o")
        emit_copy_x(c, x, out, ob, S)
        c.arena.mark()
        dt = ret_tables(c, dec_f, dec_b)
        phase_ret_in(c, (x, xb), w_in, rows3, S, dt)
        phase_ret_bwd(c, S, dt)
        phase_ret_fwd(c, (out, ob), w_out, gn_g, gn_b, S, dt)
        c.arena.release()
        n = c.tr.emit(nc)
    print("ops", n)
    return nc


def build_ffn_test(S):
    nc = bass.Bass("TRN2", target_bir_lowering=False)
    x = nc.dram_tensor("x", [S, D], F32, kind="ExternalInput").ap()
    w_in = nc.dram_tensor("w_in", [D, 2 * DFF], F32, kind="ExternalInput").ap()
    w_out = nc.dram_tensor("w_out", [DFF, D], F32, kind="ExternalInput").ap()
    rows3 = nc.dram_tensor("rows3", [3, D], F32, kind="ExternalInput").ap()
    ident = nc.dram_tensor("ident", [128, 128], BF16, kind="ExternalInput").ap()
    out = nc.dram_tensor("out", [S, D], F32, kind="ExternalOutput").ap()
    c = Ctx()
    c.tr = Tracker()
    c.ident_dram = ident
    with nc.sbuf_tensor("arena", [128, ARENA_BYTES // 4], F32) as ah, \
            nc.psum_tensor("psum", [128, 4096], F32) as ps:
        setup_common(c, nc, ah, ARENA_BYTES, ps)
        phase_ffn(c, (x, xbufs(S, "x")), (out, xbufs(S, "o")), w_in, w_out, rows3, S)
        n = c.tr.emit(nc)
    print("ops", n)
    return nc


def build_mla_test(S):
    nc = bass.Bass("TRN2", target_bir_lowering=False)
    x = nc.dram_tensor("x", [S, D], F32, kind="ExternalInput").ap()
    pos = nc.dram_tensor("pos", [S], I32, kind="ExternalInput").ap()
    w_down = nc.dram_tensor("w_down", [D, M_DOWN], F32, kind="ExternalInput").ap()
    qng = nc.dram_tensor("qng", [Q_LORA], F32, kind="ExternalInput").ap()
    kvng = nc.dram_tensor("kvng", [KV_LORA], F32, kind="ExternalInput").ap()
    w_uq = nc.dram_tensor("w_uq", [Q_LORA, 1536], F32, kind="ExternalInput").ap()
    w_ukv = nc.dram_tensor("w_ukv", [KV_LORA, 2048], F32, kind="ExternalInput").ap()
    w_o = nc.dram_tensor("w_o", [1024, D], F32, kind="ExternalInput").ap()
    rows3 = nc.dram_tensor("rows3", [3, D], F32, kind="ExternalInput").ap()
    ident = nc.dram_tensor("ident", [128, 128], BF16, kind="ExternalInput").ap()
    ctab = nc.dram_tensor("ctab", [128, CTW], F32, kind="ExternalInput").ap()
    out = nc.dram_tensor("out", [S, D], F32, kind="ExternalOutput").ap()
    c = Ctx()
    c.tr = Tracker()
    c.ident_dram = ident
    alloc_tables(c, nc, S)
    alloc_mla_scratch(c, nc, S)
    with nc.sbuf_tensor("arena", [128, ARENA_BYTES // 4], F32) as ah, \
            nc.psum_tensor("psum", [128, 4096], F32) as ps:
        setup_common(c, nc, ah, ARENA_BYTES, ps)
        load_ctab(c, ctab)
        phase_setup_tables(c, pos, S)
        xb = xbufs(S, "x")
        ob = xbufs(S, "o")
        emit_copy_x(c, x, out, ob, S)
        phase_mla_in(c, (x, xb), w_down, qng, kvng, w_uq, w_ukv, rows3, S)
        phase_mla_attn(c, (out, ob), w_o, S)
        n = c.tr.emit(nc)
    print("ops", n)
    return nc


DEPTH = 4
_NC_CACHE = {}
W_SPECS = [
    ("norm_g", [DEPTH, 3, D]), ("final_norm_g", [D]), ("mod_w", [DEPTH, D, 9 * D]), ("mod_b", [DEPTH, 9 * D]),
    ("ffn_w_in", [DEPTH, 2, D, 2 * DFF]), ("ffn_w_out", [DEPTH, 2, DFF, D]),
    ("ret_w_in", [2, D, R_IN]), ("ret_w_out", [2, R_VTOT, D]), ("ret_gn_g", [2, R_VTOT]), ("ret_gn_b", [2, R_VTOT]),
    ("ret_decay_fwd", [2, 4]), ("ret_decay_bwd", [2, 4]),
    ("mla_w_down", [2, D, M_DOWN]), ("mla_q_norm_g", [2, Q_LORA]), ("mla_kv_norm_g", [2, KV_LORA]),
    ("mla_w_uq", [2, Q_LORA, 1536]), ("mla_w_ukv", [2, KV_LORA, 2048]), ("mla_w_o", [2, 1024, D]),
]


def build_full(S, depth=DEPTH, layers=None):
    nc = bass.Bass("TRN2", target_bir_lowering=False)
    x = nc.dram_tensor("x", [S, D], F32, kind="ExternalInput").ap()
    cvec = nc.dram_tensor("c", [D], F32, kind="ExternalInput").ap()
    pos = nc.dram_tensor("positions", [S], I32, kind="ExternalInput").ap()
    W = {n: nc.dram_tensor(n, shp, F32, kind="ExternalInput").ap() for n, shp in W_SPECS}
    ident = nc.dram_tensor("ident", [128, 128], BF16, kind="ExternalInput").ap()
    ctab = nc.dram_tensor("ctab", [128, CTW], F32, kind="ExternalInput").ap()
    out = nc.dram_tensor("out", [S, D], F32, kind="ExternalOutput").ap()
    xres = nc.dram_tensor("xres", [S, D], F32, kind=SCRATCH_KIND).ap()
    c = Ctx()
    c.tr = Tracker()
    c.ident_dram = ident
    c.modrows = nc.dram_tensor("modrows", [DEPTH, 3, 3, D], F32, kind=SCRATCH_KIND).ap()
    c.modrows_b = [Buf(f"modrows{i}") for i in range(DEPTH)]
    alloc_tables(c, nc, S)
    alloc_ret_scratch(c, nc, S)
    alloc_mla_scratch(c, nc, S)
    with nc.sbuf_tensor("arena", [128, ARENA_BYTES // 4], F32) as ah, \
            nc.psum_tensor("psum", [128, 4096], F32) as ps:
        setup_common(c, nc, ah, ARENA_BYTES, ps)
        load_ctab(c, ctab)
        phase_setup_tables(c, pos, S)
        phase_mod(c, cvec, W["mod_w"], W["mod_b"], W["norm_g"], depth)
        xin_b = xbufs(S, "xin")
        xb = xbufs(S, "xres")
        ob = xbufs(S, "out")
        for i in (layers if layers is not None else range(depth)):
            rows = lambda sl: (c.modrows[i, sl], [c.modrows_b[i]])
            src = (x, xin_b) if i == (layers[0] if layers is not None else 0) else (xres, xb)
            phase_ffn(c, src, (xres, xb), W["ffn_w_in"][i, 0], W["ffn_w_out"][i, 0], rows(0), S)
            j = i // 2
            if i % 2 == 0:
                c.arena.mark()
                dt = ret_tables(c, W["ret_decay_fwd"][j], W["ret_decay_bwd"][j])
                phase_ret_in(c, (xres, xb), W["ret_w_in"][j], rows(1), S, dt)
                phase_ret_bwd(c, S, dt)
                phase_ret_fwd(c, (xres, xb), W["ret_w_out"][j], W["ret_gn_g"][j], W["ret_gn_b"][j], S, dt)
                c.arena.release()
            else:
                phase_mla_in(c, (xres, xb), W["mla_w_down"][j], W["mla_q_norm_g"][j], W["mla_kv_norm_g"][j],
                             W["mla_w_uq"][j], W["mla_w_ukv"][j], rows(1), S)
                phase_mla_attn(c, (xres, xb), W["mla_w_o"][j], S)
            phase_ffn(c, (xres, xb), (xres, xb), W["ffn_w_in"][i, 1], W["ffn_w_out"][i, 1], rows(2), S)
        phase_final(c, (xres, xb), out, ob, W["final_norm_g"], S)
        n = c.tr.emit(nc)
    _NC_CACHE["last_tr"] = c.tr
    return nc, n


SEQ = 4096
BATCH = 8


def kernel(**inputs):
    if "nc" not in _NC_CACHE:
        _NC_CACHE["nc"] = build_full(SEQ)[0]
    nc = _NC_CACHE["nc"]
    cst = host_consts()
    f32 = lambda a: np.ascontiguousarray(np.asarray(a), dtype=np.float32)
    shared = {n: f32(inputs[n]) for n, _ in W_SPECS}
    shared["ident"] = cst["ident"]
    shared["ctab"] = cst["ctab"]
    x = f32(inputs["x"])
    cc = f32(inputs["c"])
    pos = np.ascontiguousarray(np.asarray(inputs["positions"]), dtype=np.int32)
    in_maps = []
    for b in range(BATCH):
        m = dict(shared)
        m["x"] = x[b]
        m["c"] = cc[b]
        m["positions"] = pos[b]
        in_maps.append(m)
    res = run_bass_kernel_spmd(nc, in_maps, core_ids=list(range(BATCH)))
    return np.stack([np.asarray(res.results[b]["out"]) for b in range(BATCH)]).astype(np.float32)
```

```python
import contextlib
import numpy as np
import concourse.bass as bass
import concourse.mybir as mybir
from concourse.bass_utils import run_bass_kernel_spmd

F32 = mybir.dt.float32
BF16 = mybir.dt.bfloat16
I32 = mybir.dt.int32
AF = mybir.ActivationFunctionType
ALU = mybir.AluOpType
AX = mybir.AxisListType

D = 1024
DFF = 2816
KC = D // 128
FC = DFF // 128
RMS_EPS = 1e-6

ANNOTATE = False
ENGS = ["pe", "act", "dve", "pool", "sp"]
NDSEM = {"sp": 12, "pool": 8, "act": 4, "pe": 0, "dve": 0}


class Buf:
    __slots__ = ("name", "w", "rc", "rd")

    def __init__(self, name=""):
        self.name = name
        self.w = None
        self.rc = {}
        self.rd = set()
        reg = Arena.cur
        if reg is not None:
            for r in Arena.regions:
                if r is not reg and r[0] < reg[1] and reg[0] < r[1]:
                    for ob in r[2]:
                        self._inherit(ob)
            reg[2].append(self)

    def _inherit(self, ob):
        if ob.w is not None:
            if ob.w[0] == "d":
                self.rd.add(ob.w[1])
            elif self.rc.get(ob.w[1], -1) < ob.w[2]:
                self.rc[ob.w[1]] = ob.w[2]
        for e, i in ob.rc.items():
            if self.rc.get(e, -1) < i:
                self.rc[e] = i
        self.rd |= ob.rd


class Tracker:
    phase = ""

    def __init__(self):
        self.ops = {e: [] for e in ENGS}
        self.dmas = []
        self.ndma = {e: 0 for e in ENGS}
        self.dma_by_k = {e: [] for e in ENGS}

    def add(self, eng, method, reads, writes, *args, **kw):
        dma = method == "dma_start"
        fn = (method, args, kw)
        dc = {}
        dd = set()

        def dep(d):
            if d is None:
                return
            if d[0] == "d":
                dd.add(d[1])
            elif dc.get(d[1], -1) < d[2]:
                dc[d[1]] = d[2]

        for b in reads:
            dep(b.w)
        for b in writes:
            dep(b.w)
            for e, i in b.rc.items():
                dep(("c", e, i))
            for did in b.rd:
                dep(("d", did))
        idx = len(self.ops[eng])
        did = None
        if dma:
            did = len(self.dmas)
            k = self.ndma[eng]
            self.ndma[eng] += 1
            self.dmas.append((eng, k))
            n = NDSEM[eng]
            if k >= n:
                dd.add(self.dma_by_k[eng][k - n])
            self.dma_by_k[eng].append(did)
            me = ("d", did)
        else:
            me = ("c", eng, idx)
        self.ops[eng].append(dict(fn=fn, dc=dc, dd=dd, did=did, ph=Tracker.phase))
        for b in reads:
            if dma:
                b.rd.add(did)
            else:
                b.rc[eng] = idx
        for b in writes:
            b.w = me
            b.rc = {}
            b.rd = set()
        return me

    def emit(self, nc):
        ops = self.ops
        signal = {e: [False] * len(ops[e]) for e in ENGS}
        waits = {e: [None] * len(ops[e]) for e in ENGS}
        for e in ENGS:
            seen_c = {p: -1 for p in ENGS}
            seen_d = set()
            for i, op in enumerate(ops[e]):
                wl = []
                for p, j in op["dc"].items():
                    if p == "pe" and e == "pe" and op["did"] is None:
                        continue
                    if seen_c[p] >= j:
                        continue
                    seen_c[p] = j
                    signal[p][j] = True
                    wl.append(("c", p, j))
                for did in sorted(op["dd"]):
                    if did in seen_d:
                        continue
                    seen_d.add(did)
                    wl.append(("d", did))
                waits[e][i] = wl
        sigval = {}
        for e in ENGS:
            c = 0
            for i in range(len(ops[e])):
                if signal[e][i]:
                    c += 1
                    sigval[(e, i)] = c
        with contextlib.ExitStack() as st:
            csem = {e: st.enter_context(nc.semaphore(f"c_{e}")) for e in ENGS if e != "sp"}
            dsem = {e: [st.enter_context(nc.semaphore(f"d_{e}{k}")) for k in range(NDSEM[e])]
                    for e in ENGS if self.ndma[e] > 0}

            def dma_semval(did):
                q, k = self.dmas[did]
                n = NDSEM[q]
                return dsem[q][k % n], 16 * (k // n + 1)

            final = []
            for q in ENGS:
                nd = self.ndma[q]
                n = NDSEM[q]
                for s in range(min(n, nd)):
                    cnt = (nd - 1 - s) // n + 1
                    final.append((dsem[q][s], 16 * cnt))

            with nc.Block() as block:
                regs = {"pe": block.tensor, "act": block.scalar, "dve": block.vector,
                        "pool": block.gpsimd, "sp": block.sync}
                for e in ENGS:
                    def body(eng, e=e):
                        for i, op in enumerate(ops[e]):
                            for w in waits[e][i]:
                                if w[0] == "c":
                                    eng.wait_ge(csem[w[1]], sigval[(w[1], w[2])])
                                else:
                                    s, v = dma_semval(w[1])
                                    eng.wait_ge(s, v)
                            m, a, k = op["fn"]
                            ins = getattr(eng, m)(*a, **k)
                            if ANNOTATE and op["ph"]:
                                ins.annotate(op["ph"])
                            if op["did"] is not None:
                                s, v = dma_semval(op["did"])
                                ins.then_inc(s, 16)
                            elif signal[e][i]:
                                ins.then_inc(csem[e], 1)
                        if e == "sp":
                            for s, v in final:
                                eng.wait_ge(s, v)
                    regs[e](body)
        return {e: len(v) for e, v in ops.items()}


class Arena:
    cur = None
    regions = []

    def __init__(self, handle, nbytes):
        self.h = handle
        self.n = nbytes
        self.off = 0
        self.marks = []
        Arena.cur = None
        Arena.regions = []

    def alloc(self, nelem, dtype, shape=None):
        sz = 2 if dtype == BF16 else 4
        nb = (nelem * sz + 31) // 32 * 32
        assert self.off + nb <= self.n, f"arena overflow {self.off}+{nb}>{self.n}"
        a = self.h[:, self.off // 4:(self.off + nb) // 4]
        Arena.cur = [self.off, self.off + nb, []]
        Arena.regions.append(Arena.cur)
        self.off += nb
        if dtype != F32:
            a = a.bitcast(dtype)
        a = a[:, 0:nelem]
        return a

    def mark(self):
        self.marks.append(self.off)

    def release(self):
        self.off = self.marks.pop()


class Ctx:
    pass


def _mk(eng):
    def f(self, method, reads, writes, *a, **k):
        return self.tr.add(eng, method, reads, writes, *a, **k)
    return f


for _e in ENGS:
    setattr(Ctx, _e, _mk(_e))


def emit_front(c, x_src, x_bufs, hT, hT_buf, col0):
    slot = c.xslot
    c.xslot = (c.xslot + 1) % len(c.xt)
    xt, xb = c.xt[slot]
    hb_ap, hb_buf = c.hb[slot % len(c.hb)]
    ss_ap, ss_buf = c.ss[slot % len(c.ss)]
    c.sp("dma_start", list(x_bufs), [xb], out=xt, in_=x_src)
    c.dve("scalar_tensor_tensor", [xb], [hb_buf, ss_buf], out=hb_ap, in0=xt, scalar=1.0, in1=xt,
          op0=ALU.mult, op1=ALU.mult, accum_out=ss_ap[:, 0:1])
    c.act("activation", [ss_buf, c.eps_rms[1]], [ss_buf], out=ss_ap[:, 1:2], in_=ss_ap[:, 0:1], func=AF.Sqrt,
          scale=1.0 / D, bias=c.eps_rms[0])
    c.dve("reciprocal", [ss_buf], [ss_buf], out=ss_ap[:, 2:3], in_=ss_ap[:, 1:2])
    c.dve("scalar_tensor_tensor", [xb, ss_buf, c.A[1]], [xb], out=xt, in0=xt, scalar=ss_ap[:, 2:3],
          in1=c.A[0], op0=ALU.mult, op1=ALU.mult)
    c.dve("tensor_tensor", [xb, c.B[1]], [hb_buf], out=hb_ap, in0=xt, in1=c.B[0], op=ALU.add)
    pt_ap, pt_buf = c.ptr[c.ptr_i % len(c.ptr)]
    c.ptr_i += 1
    for kc in range(KC):
        c.pe("transpose", [hb_buf, c.ident[1]], [pt_buf], out=pt_ap[:, kc * 128:(kc + 1) * 128],
             in_=hb_ap[:, kc * 128:(kc + 1) * 128], identity=c.ident[0])
    c.act("activation", [pt_buf], [hT_buf], out=hT[:, :, col0:col0 + 128],
          in_=pt_ap.rearrange("p (k n) -> p k n", k=KC), func=AF.Copy)


def emit_resid(c, o_ap, o_buf, x_src, x_src_b, x_dst, x_dst_b, g_ap):
    r_ap, r_buf = c.xr[c.xr_i % len(c.xr)]
    t_ap, t_buf = c.ot[c.xr_i % len(c.ot)]
    c.xr_i += 1
    c.sp("dma_start", [x_src_b], [r_buf], out=r_ap, in_=x_src)
    c.dve("tensor_tensor", [o_buf, c.G[1]], [t_buf], out=t_ap, in0=o_ap, in1=g_ap, op=ALU.mult)
    c.dve("tensor_tensor", [t_buf, r_buf], [r_buf], out=r_ap, in0=r_ap, in1=t_ap, op=ALU.add)
    c.sp("dma_start", [r_buf], [x_dst_b], out=x_dst, in_=r_ap)


def load_bcast_rows(c, rows3):
    rb = []
    if isinstance(rows3, tuple):
        rows3, rb = rows3
    for i, (ap, buf) in enumerate((c.A, c.B, c.G)):
        c.sp("dma_start", list(rb), [buf], out=ap, in_=rows3[i, :].partition_broadcast(128))


def phase_ffn(c, xin, xout, w_in, w_out, rows3, S):
    Tracker.phase = "ffn"
    x_in, xin_b = xin
    x_out, xout_b = xout
    ar = c.arena
    ar.mark()
    win = ar.alloc(KC * 2 * DFF, BF16).rearrange("p (k n) -> p k n", k=KC)
    NJB = FC // 2
    wg_b = [Buf(f"wing{j}") for j in range(NJB)]
    wu_b = [Buf(f"winu{j}") for j in range(NJB)]
    wout = ar.alloc(FC * D, BF16).rearrange("p (k n) -> p k n", k=FC)
    wout_b = [Buf("wout0"), Buf("wout1")]
    NT = 512
    nsup = S // NT
    hT = [ar.alloc(KC * NT, BF16).rearrange("p (k n) -> p k n", k=KC) for _ in range(2)]
    hT_b = [Buf("hT0"), Buf("hT1")]
    aT = ar.alloc(FC * NT, BF16).rearrange("p (k n) -> p k n", k=FC)
    aT_b = [Buf(f"aT{j}") for j in range(FC)]
    sg = [(ar.alloc(NT, F32), Buf(f"sg{i}")) for i in range(2)]

    w_in_v = w_in.rearrange("(k p) n -> p k n", p=128)
    for jb in range(NJB):
        for (bb, c0) in ((wg_b, 0), (wu_b, DFF)):
            cs_ = slice(c0 + jb * 256, c0 + (jb + 1) * 256)
            c.pool("dma_start", [], [bb[jb]], out=win[:, :, cs_], in_=w_in_v[:, :, cs_])
    w_out_v = w_out.rearrange("(k p) n -> p k n", p=128)
    for hh in range(2):
        c.pool("dma_start", [], [wout_b[hh]], out=wout[:, hh * 11:(hh + 1) * 11, :],
               in_=w_out_v[:, hh * 11:(hh + 1) * 11, :])
    load_bcast_rows(c, rows3)

    xin_v = x_in.rearrange("(n p) d -> n p d", p=128)
    xout_v = x_out.rearrange("(n p) d -> n p d", p=128)
    gu = c.psum_gu
    gu_i = 0
    for su in range(nsup):
        hTs, hTb = hT[su % 2], hT_b[su % 2]
        for ts in range(4):
            emit_front(c, xin_v[su * 4 + ts], xin_b[su * 4 + ts], hTs, hTb, ts * 128)
        for j in range(FC):
            g_ap, g_buf = gu[gu_i % 4]
            u_ap, u_buf = gu[(gu_i + 1) % 4]
            gu_i += 2
            for k in range(KC):
                c.pe("matmul", [wg_b[j // 2], hTb], [g_buf], g_ap, lhsT=win[:, k, j * 128:(j + 1) * 128],
                     rhs=hTs[:, k, :], start=(k == 0), stop=(k == KC - 1))
            for k in range(KC):
                c.pe("matmul", [wu_b[j // 2], hTb], [u_buf], u_ap, lhsT=win[:, k, DFF + j * 128:DFF + (j + 1) * 128],
                     rhs=hTs[:, k, :], start=(k == 0), stop=(k == KC - 1))
            s_ap, s_buf = sg[j % 2]
            c.act("activation", [g_buf], [s_buf], out=s_ap, in_=g_ap, func=AF.Silu)
            c.dve("tensor_tensor", [s_buf, u_buf], [aT_b[j]], out=aT[:, j, :], in0=s_ap, in1=u_ap, op=ALU.mult)
        for ts in range(4):
            for nb in range(2):
                o_ap, o_buf = c.psum_o[c.po_i % len(c.psum_o)]
                c.po_i += 1
                for j in range(FC):
                    c.pe("matmul", [aT_b[j], wout_b[j // 11]], [o_buf], o_ap,
                         lhsT=aT[:, j, ts * 128:(ts + 1) * 128], rhs=wout[:, j, nb * 512:(nb + 1) * 512],
                         start=(j == 0), stop=(j == FC - 1))
                cs = slice(nb * 512, (nb + 1) * 512)
                emit_resid(c, o_ap, o_buf, xin_v[su * 4 + ts][:, cs], xin_b[su * 4 + ts][nb],
                           xout_v[su * 4 + ts][:, cs], xout_b[su * 4 + ts][nb], c.G[0][:, cs])
    ar.release()


DEBUG = False


def dbg(c, name, ap, buf):
    if not DEBUG:
        return
    shp = list(ap.shape)
    d = c.nc.dram_tensor("dbg_" + name, shp, ap.dtype, kind="ExternalOutput").ap()
    c.sp("dma_start", [buf], [], out=d, in_=ap)


def setup_common(c, nc, arena_handle, arena_bytes, psum):
    c.nc = nc
    c.arena = Arena(arena_handle, arena_bytes)
    ar = c.arena
    c.psum = psum
    c.ident = (ar.alloc(128, BF16), Buf("ident"))
    c.A = (ar.alloc(D, F32), Buf("A"))
    c.B = (ar.alloc(D, F32), Buf("B"))
    c.G = (ar.alloc(D, F32), Buf("G"))
    c.xt = [(ar.alloc(D, F32), Buf(f"xt{i}")) for i in range(2)]
    c.xslot = 0
    c.xr = [(ar.alloc(512, F32), Buf(f"xr{i}")) for i in range(2)]
    c.ot = [(ar.alloc(512, F32), Buf(f"ot{i}")) for i in range(2)]
    c.xr_i = 0
    c.hb = [(ar.alloc(D, BF16), Buf(f"hb{i}")) for i in range(2)]
    c.ss = [(ar.alloc(8, F32), Buf(f"ss{i}")) for i in range(6)]
    c.pb = [(psum[:, b * 512:(b + 1) * 512], Buf(f"pb{b}")) for b in range(8)]
    c.psum_gu = c.pb[0:4]
    c.psum_o = c.pb[4:7]
    c.po_i = 0
    c.ptr = [(c.pb[7][0].bitcast(BF16), c.pb[7][1])]
    c.ptr_i = 0
    c.sp("dma_start", [], [c.ident[1]], out=c.ident[0], in_=c.ident_dram)
    c.eps_rms = (ar.alloc(8, F32)[:, 0:1], Buf("eps"))
    c.dve("memset", [], [c.eps_rms[1]], c.eps_rms[0], RMS_EPS)


ARENA_BYTES = 212800

R_HEADS, R_DK, R_DV = 4, 256, 512
R_QK, R_VTOT = 1024, 2048
R_IN = 6144
M_HEADS, M_NOPE, M_ROPE, M_V = 8, 128, 64, 128
Q_LORA, KV_LORA = 384, 256
M_DOWN = 704
M_SCALE = (M_NOPE + M_ROPE) ** -0.5
GN_EPS = 1e-5
TWO_PI = 2.0 * np.pi
CW1 = 6.28125
CW2 = float(TWO_PI - 6.28125)
MAGIC = 12582912.0


CTW = 680


def host_consts():
    import ml_dtypes
    cst = {}
    cst["ident"] = np.eye(128, dtype=ml_dtypes.bfloat16)
    inv_r = np.power(np.float32(10000.0), -np.arange(0, R_DK, 2, dtype=np.float32) / np.float32(R_DK)).astype(np.float32)
    inv_m = np.power(np.float32(10000.0), -np.arange(0, M_ROPE, 2, dtype=np.float32) / np.float32(M_ROPE)).astype(np.float32)
    t = np.arange(128, dtype=np.float32)
    sI, tI = np.meshgrid(t, t, indexing="ij")
    tab = np.zeros((128, CTW), np.float32)
    tab[:, 552:680] = np.eye(128, dtype=np.float32)
    tab[:, 0:128] = np.maximum(tI - sI, 0)
    tab[:, 128:256] = (tI >= sI).astype(np.float32) / 16.0
    tab[:, 256:384] = np.maximum(sI - tI, 0)
    tab[:, 384:512] = (sI > tI).astype(np.float32) / 16.0
    tab[:, 512] = t + 1.0
    tab[:, 513] = 128.0 - t
    tab[:, 514] = 127.0 - t
    tab[:, 515] = t
    tab[:, 516] = 128.0
    tab[:, 517] = inv_r
    tab[:, 520:552] = inv_m[None, :]
    cst["ctab"] = tab
    return cst


def emit_sincos(c, ang, n, cos_out, sin_out, bufs):
    ang_b, cos_b, sin_b = bufs
    ar = c.arena
    ar.mark()
    k_ap, k_b = ar.alloc(n, F32), Buf("k")
    c.dve("tensor_scalar", [ang_b], [k_b], out=k_ap, in0=ang, scalar1=float(1.0 / TWO_PI), scalar2=MAGIC,
          op0=ALU.mult, op1=ALU.add)
    c.dve("tensor_scalar", [k_b], [k_b], out=k_ap, in0=k_ap, scalar1=MAGIC, scalar2=None, op0=ALU.subtract)
    c.dve("scalar_tensor_tensor", [k_b, ang_b], [ang_b], out=ang, in0=k_ap, scalar=-CW1, in1=ang,
          op0=ALU.mult, op1=ALU.add)
    c.dve("scalar_tensor_tensor", [k_b, ang_b], [ang_b], out=ang, in0=k_ap, scalar=-CW2, in1=ang,
          op0=ALU.mult, op1=ALU.add)
    c.dve("tensor_scalar", [ang_b], [ang_b], out=ang, in0=ang, scalar1=float(-np.pi), scalar2=float(np.pi),
          op0=ALU.max, op1=ALU.min)
    c.act("activation", [ang_b], [sin_b], out=sin_out, in_=ang, func=AF.Sin)
    c.act("activation", [ang_b], [k_b], out=k_ap, in_=ang, func=AF.Sin, scale=0.5)
    c.dve("tensor_tensor", [k_b], [k_b], out=k_ap, in0=k_ap, in1=k_ap, op=ALU.mult)
    c.dve("tensor_scalar", [k_b], [cos_b], out=cos_out, in0=k_ap, scalar1=-2.0, scalar2=1.0,
          op0=ALU.mult, op1=ALU.add)
    ar.release()


def phase_setup_tables(c, pos, S):
    Tracker.phase = "setup_tables"
    ar = c.arena
    ar.mark()
    ct = fetch_ctab(c)
    nt = S // 128
    pi_ap, pi_b = ar.alloc(S, I32), Buf("posi")
    c.sp("dma_start", [], [pi_b], out=pi_ap, in_=pos.partition_broadcast(128))
    ang, ang_b = ar.alloc(S, F32), Buf("ang")
    c.dve("tensor_copy", [pi_b], [ang_b], out=ang, in_=pi_ap)
    c.dve("tensor_scalar", [ang_b, ct[1]], [ang_b], out=ang, in0=ang, scalar1=ct[0][:, 517:518], scalar2=None,
          op0=ALU.mult)
    cs, cs_b = ar.alloc(S, F32), Buf("cos")
    sn, sn_b = ar.alloc(S, F32), Buf("sin")
    emit_sincos(c, ang, S, cs, sn, (ang_b, cs_b, sn_b))
    c.sp("dma_start", [cs_b], [c.CR_b], out=c.CR, in_=cs)
    c.sp("dma_start", [sn_b], [c.SR_b], out=c.SR, in_=sn)
    pf_ap, pf_b = ar.alloc(nt, F32), Buf("posf")
    pb2, pb2_b = ar.alloc(S, F32), Buf("posf_row")
    c.dve("tensor_copy", [pi_b], [pb2_b], out=pb2, in_=pi_ap)
    junk, junk_b = ar.alloc(128, F32), Buf("junk")
    for t in range(nt):
        c.dve("scalar_tensor_tensor", [pb2_b, ct[1]], [junk_b, pf_b], out=junk, in0=pb2[:, t * 128:(t + 1) * 128],
              scalar=1.0, in1=ct[0][:, 552:680], op0=ALU.mult, op1=ALU.mult, accum_out=pf_ap[:, t:t + 1])
    am, am_b = ar.alloc(nt * 32, F32), Buf("am")
    for t in range(nt):
        c.dve("tensor_scalar", [pf_b, ct[1]], [am_b], out=am[:, t * 32:(t + 1) * 32], in0=ct[0][:, 520:552],
              scalar1=pf_ap[:, t:t + 1], scalar2=None, op0=ALU.mult)
    cm, cm_b = ar.alloc(nt * 32, F32), Buf("cm")
    sm, sm_b = ar.alloc(nt * 32, F32), Buf("sm")
    emit_sincos(c, am, nt * 32, cm, sm, (am_b, cm_b, sm_b))
    c.sp("dma_start", [cm_b], [c.CM_b], out=c.CM.rearrange("(t p) j -> p t j", p=128),
         in_=cm.rearrange("p (t j) -> p t j", j=32))
    c.sp("dma_start", [sm_b], [c.SM_b], out=c.SM.rearrange("(t p) j -> p t j", p=128),
         in_=sm.rearrange("p (t j) -> p t j", j=32))
    ar.release()


def phase_mod(c, cvec, mod_w, mod_b, norm_g, depth):
    Tracker.phase = "mod"
    ar = c.arena
    ar.mark()
    cf, cf_b = ar.alloc(128, F32), Buf("cf")
    c.sp("dma_start", [], [cf_b], out=cf[0:KC, :], in_=cvec.rearrange("(k p) -> k p", p=128))
    cab, cab_b = ar.alloc(128, BF16), Buf("cab")
    c.act("activation", [cf_b], [cab_b], out=cab[0:KC, :], in_=cf[0:KC, :], func=AF.Silu)
    pt_ap, pt_buf = c.ptr[0]
    c.pe("transpose", [cab_b, c.ident[1]], [pt_buf], out=pt_ap[:, 0:KC], in_=cab[0:KC, :],
         identity=c.ident[0][0:KC, 0:KC])
    ca, ca_b = ar.alloc(KC, BF16), Buf("ca")
    c.act("activation", [pt_buf], [ca_b], out=ca, in_=pt_ap[:, 0:KC], func=AF.Copy)
    NB = 512
    nblk = 9 * D // NB
    wb = [(ar.alloc(KC * NB, BF16).rearrange("p (k n) -> p k n", k=KC), Buf(f"mw{i}")) for i in range(3)]
    row, row_b = ar.alloc(9 * D, F32), Buf("modrow")
    mb_ap, mb_b = ar.alloc(9 * D, F32), Buf("modb")
    ng_ap, ng_b = ar.alloc(3 * D, F32), Buf("normg")
    orow, orow_b = ar.alloc(9 * D, F32), Buf("orow")
    bi = 0
    for i in range(depth):
        c.sp("dma_start", [], [mb_b], out=mb_ap[0:1, :], in_=mod_b[i:i + 1, :])
        c.sp("dma_start", [], [ng_b], out=ng_ap[0:1, :], in_=norm_g[i:i + 1].rearrange("o s d -> o (s d)"))
        mw_v = mod_w[i].rearrange("(k p) n -> p k n", p=128)
        for b in range(nblk):
            w_ap, w_b = wb[bi % 3]
            p_ap, p_b = c.pb[bi % 2]
            bi += 1
            c.pool("dma_start", [], [w_b], out=w_ap, in_=mw_v[:, :, b * NB:(b + 1) * NB])
            for k in range(KC):
                c.pe("matmul", [ca_b, w_b], [p_b], p_ap[0:1, :], lhsT=ca[:, k:k + 1], rhs=w_ap[:, k, :],
                     start=(k == 0), stop=(k == KC - 1))
            c.dve("tensor_tensor", [p_b, mb_b], [row_b], out=row[0:1, b * NB:(b + 1) * NB], in0=p_ap[0:1, :],
                  in1=mb_ap[0:1, b * NB:(b + 1) * NB], op=ALU.add)
        for sl in range(3):
            sh = row[0:1, (3 * sl) * D:(3 * sl + 1) * D]
            sc = row[0:1, (3 * sl + 1) * D:(3 * sl + 2) * D]
            gt = row[0:1, (3 * sl + 2) * D:(3 * sl + 3) * D]
            oa = orow[0:1, (3 * sl) * D:(3 * sl + 1) * D]
            ob = orow[0:1, (3 * sl + 1) * D:(3 * sl + 2) * D]
            og = orow[0:1, (3 * sl + 2) * D:(3 * sl + 3) * D]
            c.dve("scalar_tensor_tensor", [row_b, ng_b], [orow_b], out=oa, in0=sc, scalar=1.0,
                  in1=ng_ap[0:1, sl * D:(sl + 1) * D], op0=ALU.add, op1=ALU.mult)
            c.dve("tensor_copy", [row_b], [orow_b], out=ob, in_=sh)
            c.dve("tensor_scalar", [row_b], [orow_b], out=og, in0=gt, scalar1=(1.0 if sl == 1 else 0.5),
                  scalar2=None, op0=ALU.mult)
        c.sp("dma_start", [orow_b], [c.modrows_b[i]], out=c.modrows[i:i + 1].rearrange("o s r d -> o (s r d)"),
             in_=orow[0:1, :])
    ar.release()


def ret_tables(c, dec_f, dec_b):
    ar = c.arena
    ct, ct_b = fetch_ctab(c)
    t = Ctx()
    raw, raw_b = ar.alloc(8, F32), Buf("decraw")
    c.sp("dma_start", [], [raw_b], out=raw[:, 0:4], in_=dec_f.partition_broadcast(128))
    c.sp("dma_start", [], [raw_b], out=raw[:, 4:8], in_=dec_b.partition_broadcast(128))
    lg, lg_b = ar.alloc(8, F32), Buf("lg")
    c.act("activation", [raw_b], [lg_b], out=lg, in_=raw, func=AF.Exp, scale=-1.0)
    c.act("activation", [lg_b], [lg_b], out=lg, in_=lg, func=AF.Ln, bias=1.0)
    c.dve("tensor_scalar", [lg_b], [lg_b], out=lg, in0=lg, scalar1=-1.0, scalar2=None, op0=ALU.mult)
    cols, cols_b = ar.alloc(24, F32), Buf("deccols")
    src = [(512, 0), (513, 4), (514, 0), (515, 4), (516, 0), (516, 4)]
    for kind, (ccol, lgo) in enumerate(src):
        for h in range(R_HEADS):
            c.act("activation", [lg_b, ct_b], [cols_b], out=cols[:, kind * 4 + h:kind * 4 + h + 1],
                  in_=ct[:, ccol:ccol + 1], func=AF.Exp, scale=lg[:, lgo + h:lgo + h + 1])
    t.cols, t.cols_b = cols, cols_b
    DT, DT_b = ar.alloc(4 * 128, F32), Buf("DT")
    e1, e1_b = ar.alloc(128, F32), Buf("e1")
    e2, e2_b = ar.alloc(128, F32), Buf("e2")
    for h in range(R_HEADS):
        c.act("activation", [lg_b, ct_b], [e1_b], out=e1, in_=ct[:, 0:128], func=AF.Exp, scale=lg[:, h:h + 1])
        c.dve("tensor_tensor", [e1_b, ct_b], [e1_b], out=e1, in0=e1, in1=ct[:, 128:256], op=ALU.mult)
        c.act("activation", [lg_b, ct_b], [e2_b], out=e2, in_=ct[:, 256:384], func=AF.Exp, scale=lg[:, 4 + h:5 + h])
        c.dve("tensor_tensor", [e2_b, ct_b], [e2_b], out=e2, in0=e2, in1=ct[:, 384:512], op=ALU.mult)
        c.dve("tensor_tensor", [e1_b, e2_b], [DT_b], out=DT[:, h * 128:(h + 1) * 128], in0=e1, in1=e2, op=ALU.add)
    t.DT, t.DT_b = DT, DT_b
    return t


def phase_ret_in(c, xin, w_in, rows3, S, dt):
    Tracker.phase = "ret_in"
    x_in, xin_b = xin
    ar = c.arena
    ar.mark()
    NKC = R_IN
    win = ar.alloc(KC * NKC, BF16).rearrange("p (k n) -> p k n", k=KC)
    win_b = [Buf(f"rwin{k}") for k in range(KC)]
    w_in_v = w_in.rearrange("(k p) n -> p k n", p=128)
    for k in range(KC):
        c.pool("dma_start", [], [win_b[k]], out=win[:, k, :], in_=w_in_v[:, k, :])
    load_bcast_rows(c, rows3)
    NT = 512
    nsup = S // NT
    hT = [ar.alloc(KC * NT, BF16).rearrange("p (k n) -> p k n", k=KC) for _ in range(2)]
    hT_b = [Buf("hT0"), Buf("hT1")]
    cs = [(ar.alloc(NT, F32), Buf(f"cs{i}")) for i in range(2)]
    sn = [(ar.alloc(NT, F32), Buf(f"sn{i}")) for i in range(2)]
    tmp = [(ar.alloc(NT, F32), Buf(f"rt{i}")) for i in range(4)]
    qst = [(ar.alloc(2 * NT, BF16).rearrange("p (a n) -> p a n", a=2), Buf(f"qst{i}")) for i in range(2)]
    KDf, KDf_b = ar.alloc(1024, F32), Buf("KDf")
    KDb, KDb_b = ar.alloc(1024, F32), Buf("KDb")
    for h in range(R_HEADS):
        for (KD, KD_b, kind) in ((KDf, KDf_b, 2), (KDb, KDb_b, 3)):
            c.dve("memset", [], [KD_b], KD[:, h * 256:(h + 1) * 256], 1.0 / 16.0)
            c.dve("tensor_scalar", [KD_b, dt.cols_b], [KD_b], out=KD[:, h * 256:(h + 1) * 256],
                  in0=KD[:, h * 256:(h + 1) * 256], scalar1=dt.cols[:, kind * 4 + h:kind * 4 + h + 1],
                  scalar2=None, op0=ALU.mult)
    kst = [(ar.alloc(1024, BF16), Buf(f"kst{i}")) for i in range(2)]
    vst = [(ar.alloc(512, BF16), Buf(f"vst{i}")) for i in range(2)]
    gst = [(ar.alloc(512, F32), Buf(f"gst{i}")) for i in range(2)]
    ktb = [(ar.alloc(8 * NT, BF16).rearrange("p (a n) -> p a n", a=8), Buf("ktb"))]
    xin_v = x_in.rearrange("(n p) d -> n p d", p=128)
    pbi = 0
    qi = 0
    vi = 0
    for su in range(nsup):
        hTs, hTb = hT[su % 2], hT_b[su % 2]
        tok = slice(su * NT, (su + 1) * NT)
        c_ap, c_b = cs[su % 2]
        s_ap, s_b = sn[su % 2]
        c.sp("dma_start", [c.CR_b], [c_b], out=c_ap, in_=c.CR[:, tok])
        c.sp("dma_start", [c.SR_b], [s_b], out=s_ap, in_=c.SR[:, tok])
        for ts in range(4):
            emit_front(c, xin_v[su * 4 + ts], xin_b[su * 4 + ts], hTs, hTb, ts * 128)
        kt_ap, kt_b = ktb[0]
        if su == 0:
            dbg(c, "hT", hTs, hTb)
            dbg(c, "win0", win[:, 0, :], win_b[0])
            dbg(c, "win7", win[:, 7, :], win_b[7])
        for which in range(2):
            for h in range(R_HEADS):
                base = which * R_QK + h * R_DK
                p1, p1_b = c.pb[pbi % 6]
                p2, p2_b = c.pb[(pbi + 1) % 6]
                pbi += 2
                for half, (pp, pp_b) in enumerate(((p1, p1_b), (p2, p2_b))):
                    for k in range(KC):
                        c.pe("matmul", [win_b[k], hTb], [pp_b], pp,
                             lhsT=win[:, k, base + half * 128:base + (half + 1) * 128], rhs=hTs[:, k, :],
                             start=(k == 0), stop=(k == KC - 1))
                t1, t1b = tmp[0]
                t2, t2b = tmp[1]
                t3, t3b = tmp[2]
                t4, t4b = tmp[3]
                c.dve("tensor_tensor", [p1_b, c_b], [t1b], out=t1, in0=p1, in1=c_ap, op=ALU.mult)
                c.dve("tensor_tensor", [p2_b, s_b], [t2b], out=t2, in0=p2, in1=s_ap, op=ALU.mult)
                c.dve("tensor_tensor", [p1_b, s_b], [t3b], out=t3, in0=p1, in1=s_ap, op=ALU.mult)
                c.dve("tensor_tensor", [p2_b, c_b], [t4b], out=t4, in0=p2, in1=c_ap, op=ALU.mult)
                if which == 0:
                    o_ap, o_b = qst[qi % 2]
                    qi += 1
                    o1, o2 = o_ap[:, 0, :], o_ap[:, 1, :]
                else:
                    o_ap, o_b = kt_ap, kt_b
                    o1, o2 = kt_ap[:, 2 * h, :], kt_ap[:, 2 * h + 1, :]
                c.dve("tensor_tensor", [t1b, t2b], [o_b], out=o1, in0=t1, in1=t2, op=ALU.subtract)
                c.dve("tensor_tensor", [t3b, t4b], [o_b], out=o2, in0=t3, in1=t4, op=ALU.add)
                if which == 0:
                    dst = c.QT[h * 256:(h + 1) * 256, tok].rearrange("(a p) n -> p a n", p=128)
                    c.sp("dma_start", [o_b], [c.QT_b[su][h]], out=dst, in_=o_ap)
            if which == 1:
                dst = c.KT[:, tok].rearrange("(a p) n -> p a n", p=128)
                c.sp("dma_start", [kt_b], [c.KT_b[su]], out=dst, in_=kt_ap)
        for ts in range(4):
            pt_ap, pt_buf = c.ptr[0]
            for a in range(8):
                c.pe("transpose", [kt_b, c.ident[1]], [pt_buf], out=pt_ap[:, a * 128:(a + 1) * 128],
                     in_=kt_ap[:, a, ts * 128:(ts + 1) * 128], identity=c.ident[0])
            for di, (KD, KD_b, dst, dst_b) in enumerate(((KDf, KDf_b, c.KF, c.KF_b), (KDb, KDb_b, c.KB, c.KB_b))):
                k_ap, k_b = kst[di]
                c.dve("tensor_tensor", [pt_buf, KD_b], [k_b], out=k_ap, in0=pt_ap, in1=KD, op=ALU.mult)
                c.sp("dma_start", [k_b], [dst_b[su * 4 + ts]], out=dst[(su * 4 + ts) * 128:(su * 4 + ts + 1) * 128, :],
                     in_=k_ap)
        for ts in range(4):
            rows = slice((su * 4 + ts) * 128, (su * 4 + ts + 1) * 128)
            for nb in range(8):
                pp, pp_b = c.pb[pbi % 6]
                pbi += 1
                col0 = 2 * R_QK + nb * 512
                for k in range(KC):
                    c.pe("matmul", [win_b[k], hTb], [pp_b], pp, lhsT=hTs[:, k, ts * 128:(ts + 1) * 128],
                         rhs=win[:, k, col0:col0 + 512], start=(k == 0), stop=(k == KC - 1))
                if nb < 4:
                    v_ap, v_b = vst[vi % 2]
                    vi += 1
                    c.act("activation", [pp_b], [v_b], out=v_ap, in_=pp, func=AF.Copy)
                    c.sp("dma_start", [v_b], [c.V_b[su * 4 + ts][nb]], out=c.V[rows, nb * 512:(nb + 1) * 512], in_=v_ap)
                else:
                    g_ap, g_b = gst[vi % 2]
                    vi += 1
                    c.act("activation", [pp_b], [g_b], out=g_ap, in_=pp, func=AF.Silu)
                    c.sp("dma_start", [g_b], [c.Gs_b[su * 4 + ts][nb - 4]], out=c.Gs[rows, (nb - 4) * 512:(nb - 3) * 512],
                         in_=g_ap)
    ar.release()


def phase_ret_bwd(c, S, dt):
    Tracker.phase = "ret_bwd"
    ar = c.arena
    ar.mark()
    nch = S // 128
    St = ar.alloc(R_HEADS * 1024, F32).rearrange("p (h n) -> p h n", h=R_HEADS)
    St_b = [Buf(f"St{h}") for h in range(R_HEADS)]
    Sb = ar.alloc(R_HEADS * 1024, BF16).rearrange("p (h n) -> p h n", h=R_HEADS)
    Sb_b = [Buf(f"Sb{h}") for h in range(R_HEADS)]
    for h in range(R_HEADS):
        c.dve("memset", [], [St_b[h]], St[:, h], 0.0)
        c.dve("memset", [], [Sb_b[h]], Sb[:, h], 0.0)
    kb = [(ar.alloc(1024, BF16), Buf(f"kbt{i}")) for i in range(2)]
    vt = [(ar.alloc(2048, BF16), Buf(f"vt{i}")) for i in range(2)]
    pbi = 0
    for i, ch in enumerate(range(nch - 1, -1, -1)):
        k_ap, k_b = kb[i % 2]
        v_ap, v_b = vt[i % 2]
        rows = slice(ch * 128, (ch + 1) * 128)
        c.sp("dma_start", [c.KB_b[ch]], [k_b], out=k_ap, in_=c.KB[rows, :])
        c.sp("dma_start", c.V_b[ch], [v_b], out=v_ap, in_=c.V[rows, :])
        for h in range(R_HEADS):
            c.sp("dma_start", [Sb_b[h]], [c.SB_b[ch][h]], out=c.SB[ch, h], in_=Sb[:, h])
            for a in range(2):
                pp, pp_b = c.pb[pbi % 8]
                pbi += 1
                c.pe("matmul", [k_b, v_b], [pp_b], pp, lhsT=k_ap[:, h * 256 + a * 128:h * 256 + (a + 1) * 128],
                     rhs=v_ap[:, h * 512:(h + 1) * 512], start=True, stop=True)
                c.dve("scalar_tensor_tensor", [St_b[h], dt.cols_b, pp_b], [St_b[h]], out=St[:, h, a * 512:(a + 1) * 512],
                      in0=St[:, h, a * 512:(a + 1) * 512], scalar=dt.cols[:, 5 * 4 + h:5 * 4 + h + 1], in1=pp,
                      op0=ALU.mult, op1=ALU.add)
            c.act("activation", [St_b[h]], [Sb_b[h]], out=Sb[:, h], in_=St[:, h], func=AF.Copy)
    ar.release()


def phase_ret_fwd(c, xio, w_out, gn_g, gn_b, S, dt):
    Tracker.phase = "ret_fwd"
    x_io, xio_b = xio
    ar = c.arena
    ar.mark()
    nch = S // 128
    wo = ar.alloc(16 * D, BF16).rearrange("p (k n) -> p k n", k=16)
    wo_b = Buf("rwo")
    c.pool("dma_start", [], [wo_b], out=wo, in_=w_out.rearrange("(k p) n -> p k n", p=128))
    gg, gg_b = ar.alloc(R_VTOT, F32), Buf("gng")
    gb, gb_b = ar.alloc(R_VTOT, F32), Buf("gnb")
    c.sp("dma_start", [], [gg_b], out=gg, in_=gn_g.partition_broadcast(128))
    c.sp("dma_start", [], [gb_b], out=gb, in_=gn_b.partition_broadcast(128))
    eps, eps_b = ar.alloc(8, F32)[:, 0:1], Buf("gneps")
    c.dve("memset", [], [eps_b], eps, GN_EPS)
    St = ar.alloc(R_HEADS * 1024, F32).rearrange("p (h n) -> p h n", h=R_HEADS)
    St_b = [Buf(f"Sf{h}") for h in range(R_HEADS)]
    Sb = ar.alloc(R_HEADS * 1024, BF16).rearrange("p (h n) -> p h n", h=R_HEADS)
    Sb_b = [Buf(f"Sfb{h}") for h in range(R_HEADS)]
    for h in range(R_HEADS):
        c.dve("memset", [], [St_b[h]], St[:, h], 0.0)
        c.dve("memset", [], [Sb_b[h]], Sb[:, h], 0.0)
    qt = [(ar.alloc(1024, BF16).rearrange("p (a n) -> p a n", a=8), Buf(f"qt{i}")) for i in range(2)]
    kt = [(ar.alloc(1024, BF16).rearrange("p (a n) -> p a n", a=8), Buf(f"kt{i}")) for i in range(2)]
    kf = [(ar.alloc(1024, BF16), Buf(f"kf{i}")) for i in range(2)]
    vt = [(ar.alloc(2048, BF16), Buf(f"vt{i}")) for i in range(2)]
    gt = [(ar.alloc(2048, F32), Buf(f"gt{i}")) for i in range(2)]
    sbt = [(ar.alloc(R_HEADS * 1024, BF16).rearrange("p (h n) -> p h n", h=R_HEADS), Buf(f"sbt{i}"))
           for i in range(2)]
    y, y_b = ar.alloc(R_VTOT, F32), [Buf(f"y{h}") for h in range(R_HEADS)]
    z, z_b = ar.alloc(R_VTOT, BF16), Buf("z")
    zT = ar.alloc(16 * 128, BF16).rearrange("p (k n) -> p k n", k=16)
    zT_b = Buf("zT")
    pT = [(ar.alloc(128, BF16), Buf(f"pT{i}")) for i in range(R_HEADS)]
    st6, st6_b = ar.alloc(4 * 16, F32), [Buf(f"st6{h}") for h in range(R_HEADS)]
    ps_bufs = [Buf(f"psS{h}") for h in range(R_HEADS)]
    xv = x_io.rearrange("(n p) d -> n p d", p=128)
    for ch in range(nch):
        i = ch
        q_ap, q_b = qt[i % 2]
        k_ap, k_b = kt[i % 2]
        f_ap, f_b = kf[i % 2]
        v_ap, v_b = vt[i % 2]
        g_ap, g_b = gt[i % 2]
        s_ap, s_b = sbt[i % 2]
        rows = slice(ch * 128, (ch + 1) * 128)
        su = ch // 4
        c.sp("dma_start", c.QT_b[su], [q_b], out=q_ap, in_=c.QT[:, rows].rearrange("(a p) n -> p a n", p=128))
        c.sp("dma_start", [c.KT_b[su]], [k_b], out=k_ap, in_=c.KT[:, rows].rearrange("(a p) n -> p a n", p=128))
        c.sp("dma_start", [c.KF_b[ch]], [f_b], out=f_ap, in_=c.KF[rows, :])
        c.sp("dma_start", c.V_b[ch], [v_b], out=v_ap, in_=c.V[rows, :])
        c.sp("dma_start", c.Gs_b[ch], [g_b], out=g_ap, in_=c.Gs[rows, :])
        c.sp("dma_start", c.SB_b[ch], [s_b], out=s_ap, in_=c.SB[ch].rearrange("h p n -> p h n"))
        for h in range(R_HEADS):
            ps_ap, ps_b = c.pb[0]
            st_ap = ps_ap[:, h * 128:(h + 1) * 128]
            for a in range(2):
                c.pe("matmul", [k_b, q_b], [ps_b], st_ap, lhsT=k_ap[:, 2 * h + a, :], rhs=q_ap[:, 2 * h + a, :],
                     start=(a == 0), stop=(a == 1))
            p_ap, p_b = pT[h]
            c.dve("tensor_tensor", [ps_b, dt.DT_b], [p_b], out=p_ap, in0=st_ap,
                  in1=dt.DT[:, h * 128:(h + 1) * 128], op=ALU.mult)
        for h in range(R_HEADS):
            p_ap, p_b = pT[h]
            y0, y0_b = c.pb[1]
            y1, y1_b = c.pb[2]
            y2, y2_b = c.pb[3]
            vh = v_ap[:, h * 512:(h + 1) * 512]
            c.pe("matmul", [p_b, v_b], [y0_b], y0, lhsT=p_ap, rhs=vh, start=True, stop=True)
            for a in range(2):
                c.pe("matmul", [q_b, Sb_b[h]], [y1_b], y1, lhsT=q_ap[:, 2 * h + a, :], rhs=Sb[:, h, a * 512:(a + 1) * 512],
                     start=(a == 0), stop=(a == 1))
            for a in range(2):
                c.pe("matmul", [q_b, s_b], [y2_b], y2, lhsT=q_ap[:, 2 * h + a, :], rhs=s_ap[:, h, a * 512:(a + 1) * 512],
                     start=(a == 0), stop=(a == 1))
            yh = y[:, h * 512:(h + 1) * 512]
            c.act("activation", [y0_b], [y_b[h]], out=yh, in_=y0, func=AF.Copy)
            c.dve("scalar_tensor_tensor", [y1_b, dt.cols_b, y_b[h]], [y_b[h]], out=yh, in0=y1,
                  scalar=dt.cols[:, 0 * 4 + h:0 * 4 + h + 1], in1=yh, op0=ALU.mult, op1=ALU.add)
            c.dve("scalar_tensor_tensor", [y2_b, dt.cols_b, y_b[h]], [y_b[h]], out=yh, in0=y2,
                  scalar=dt.cols[:, 1 * 4 + h:1 * 4 + h + 1], in1=yh, op0=ALU.mult, op1=ALU.add)
            for a in range(2):
                u, u_b = c.pb[4 + a]
                c.pe("matmul", [f_b, v_b], [u_b], u, lhsT=f_ap[:, h * 256 + a * 128:h * 256 + (a + 1) * 128], rhs=vh,
                     start=True, stop=True)
                c.dve("scalar_tensor_tensor", [St_b[h], dt.cols_b, u_b], [St_b[h]], out=St[:, h, a * 512:(a + 1) * 512],
                      in0=St[:, h, a * 512:(a + 1) * 512], scalar=dt.cols[:, 4 * 4 + h:4 * 4 + h + 1], in1=u,
                      op0=ALU.mult, op1=ALU.add)
            c.act("activation", [St_b[h]], [Sb_b[h]], out=Sb[:, h], in_=St[:, h], func=AF.Copy)
            s6 = st6[:, h * 16:h * 16 + 16]
            c.dve("bn_stats", [y_b[h]], [st6_b[h]], out=s6[:, 0:6], in_=yh)
            c.dve("bn_aggr", [st6_b[h]], [st6_b[h]], out=s6[:, 6:8], in_=s6[:, 0:6])
            c.act("activation", [st6_b[h], eps_b], [st6_b[h]], out=s6[:, 8:9], in_=s6[:, 7:8], func=AF.Sqrt,
                  bias=eps, scale=1.0)
            c.dve("reciprocal", [st6_b[h]], [st6_b[h]], out=s6[:, 9:10], in_=s6[:, 8:9])
            c.dve("tensor_tensor", [st6_b[h]], [st6_b[h]], out=s6[:, 11:12], in0=s6[:, 6:7], in1=s6[:, 9:10], op=ALU.mult)
            c.dve("tensor_scalar", [st6_b[h]], [st6_b[h]], out=s6[:, 10:11], in0=s6[:, 11:12], scalar1=-1.0,
                  scalar2=None, op0=ALU.mult)
            c.act("activation", [y_b[h], st6_b[h]], [y_b[h]], out=yh, in_=yh, func=AF.Identity,
                  scale=s6[:, 9:10], bias=s6[:, 10:11])
            hs = slice(h * 512, (h + 1) * 512)
            c.dve("tensor_tensor", [y_b[h], gg_b], [y_b[h]], out=yh, in0=yh, in1=gg[:, hs], op=ALU.mult)
            c.dve("tensor_tensor", [y_b[h], gb_b], [y_b[h]], out=yh, in0=yh, in1=gb[:, hs], op=ALU.add)
            c.dve("tensor_tensor", [y_b[h], g_b], [z_b], out=z[:, hs], in0=yh, in1=g_ap[:, hs], op=ALU.mult)
        pt_ap, pt_buf = c.ptr[0]
        for half in range(2):
            for a in range(8):
                kc = half * 8 + a
                c.pe("transpose", [z_b, c.ident[1]], [pt_buf], out=pt_ap[:, a * 128:(a + 1) * 128],
                     in_=z[:, kc * 128:(kc + 1) * 128], identity=c.ident[0])
            c.act("activation", [pt_buf], [zT_b], out=zT[:, half * 8:(half + 1) * 8, :],
                  in_=pt_ap.rearrange("p (k n) -> p k n", k=8), func=AF.Copy)
        for nb in range(2):
            o_ap, o_buf = c.pb[6]
            for kc in range(16):
                c.pe("matmul", [zT_b, wo_b], [o_buf], o_ap, lhsT=zT[:, kc, :], rhs=wo[:, kc, nb * 512:(nb + 1) * 512],
                     start=(kc == 0), stop=(kc == 15))
            csl = slice(nb * 512, (nb + 1) * 512)
            emit_resid(c, o_ap, o_buf, xv[ch][:, csl], xio_b[ch][nb], xv[ch][:, csl], xio_b[ch][nb], c.G[0][:, csl])
    ar.release()


def alloc_mla_scratch(c, nc, S):
    nt = S // 128
    c.QTN = nc.dram_tensor("QTN", [M_HEADS, 128, S], BF16, kind=SCRATCH_KIND).ap()
    c.QTR = nc.dram_tensor("QTR", [M_HEADS, 64, S], BF16, kind=SCRATCH_KIND).ap()
    c.KTN = nc.dram_tensor("KTN", [M_HEADS, 128, S], BF16, kind=SCRATCH_KIND).ap()
    c.KTR = nc.dram_tensor("KTR", [64, S], BF16, kind=SCRATCH_KIND).ap()
    c.VM = nc.dram_tensor("VM", [S, M_HEADS * M_V], BF16, kind=SCRATCH_KIND).ap()
    c.QTN_b = [Buf("QTN") for _ in range(nt)]
    c.QTR_b = [Buf("QTR") for _ in range(nt)]
    c.KTN_b = [[Buf("KTN") for _ in range(M_HEADS)] for _ in range(S // 512)]
    c.KTR_b = [Buf("KTR") for _ in range(S // 512)]
    c.VM_b = [Buf("VM") for _ in range(nt)]


def phase_mla_in(c, xin, w_down, qng, kvng, w_uq, w_ukv, rows3, S):
    Tracker.phase = "mla_in"
    x_in, xin_b = xin
    ar = c.arena
    ar.mark()
    nt = S // 128
    NT = 512
    nsup = S // NT
    wdn = ar.alloc(KC * M_DOWN, BF16).rearrange("p (k n) -> p k n", k=KC)
    wdn_b = Buf("wdn")
    c.pool("dma_start", [], [wdn_b], out=wdn, in_=w_down.rearrange("(k p) n -> p k n", p=128))
    wuq = ar.alloc(3 * 1536, BF16).rearrange("p (k n) -> p k n", k=3)
    wuq_b = Buf("wuq")
    c.pool("dma_start", [], [wuq_b], out=wuq, in_=w_uq.rearrange("(k p) n -> p k n", p=128))
    wuk = ar.alloc(2 * 1024, BF16).rearrange("p (k n) -> p k n", k=2)
    wuk_b = Buf("wuk")
    wuv = ar.alloc(2 * 1024, BF16).rearrange("p (k n) -> p k n", k=2)
    wuv_b = Buf("wuv")
    for kc in range(2):
        src = w_ukv[kc * 128:(kc + 1) * 128, :].rearrange("p (h two d) -> p h two d", two=2, d=128)
        c.pool("dma_start", [], [wuk_b], out=wuk[:, kc, :].rearrange("p (h d) -> p h d", d=128), in_=src[:, :, 0, :])
        c.pool("dma_start", [], [wuv_b], out=wuv[:, kc, :].rearrange("p (h d) -> p h d", d=128), in_=src[:, :, 1, :])
    qg, qg_b = ar.alloc(Q_LORA, F32), Buf("qg")
    kg, kg_b = ar.alloc(KV_LORA, F32), Buf("kg")
    c.sp("dma_start", [], [qg_b], out=qg, in_=qng.partition_broadcast(128))
    c.sp("dma_start", [], [kg_b], out=kg, in_=kvng.partition_broadcast(128))
    cm, cm_b = ar.alloc(nt * 32, F32), Buf("cmr")
    sm, sm_b = ar.alloc(nt * 32, F32), Buf("smr")
    c.sp("dma_start", [c.CM_b], [cm_b], out=cm.rearrange("p (t j) -> p t j", j=32),
         in_=c.CM.rearrange("(t p) j -> p t j", p=128))
    c.sp("dma_start", [c.SM_b], [sm_b], out=sm.rearrange("p (t j) -> p t j", j=32),
         in_=c.SM.rearrange("(t p) j -> p t j", p=128))
    load_bcast_rows(c, rows3)
    hT = [ar.alloc(KC * NT, BF16).rearrange("p (k n) -> p k n", k=KC) for _ in range(2)]
    hT_b = [Buf("hT0"), Buf("hT1")]
    cd = [(ar.alloc(M_DOWN, F32), Buf(f"cd{i}")) for i in range(2)]
    cqn, cqn_b = ar.alloc(Q_LORA, BF16), Buf("cqn")
    ckn, ckn_b = ar.alloc(KV_LORA, BF16), Buf("ckn")
    krr, krr_b = ar.alloc(64, BF16), Buf("krr")
    st, st_b = ar.alloc(8, F32), Buf("mst")
    cqT = ar.alloc(3 * NT, BF16).rearrange("p (k n) -> p k n", k=3)
    cqT_b = Buf("cqT")
    ckT = ar.alloc(2 * NT, BF16).rearrange("p (k n) -> p k n", k=2)
    ckT_b = Buf("ckT")
    krT, krT_b = ar.alloc(NT, BF16), Buf("krT")
    kts = [(ar.alloc(NT, BF16), Buf(f"kts{i}")) for i in range(2)]
    vst = [(ar.alloc(1024, BF16), Buf(f"mvst{i}")) for i in range(2)]
    qf, qf_b = ar.alloc(1536, F32), Buf("qf")
    qb, qb_b = ar.alloc(1536, BF16), Buf("qb")
    rt = [(ar.alloc(256, F32), Buf(f"mrt{i}")) for i in range(4)]
    kt4 = [(ar.alloc(32, F32), Buf(f"kt4{i}")) for i in range(4)]
    qtn = [(ar.alloc(1024, BF16).rearrange("p (h n) -> p h n", h=8), Buf(f"qtn{i}")) for i in range(2)]
    qtr = [(ar.alloc(1024, BF16).rearrange("p (h n) -> p h n", h=8), Buf(f"qtr{i}")) for i in range(2)]
    eq, eq_b = ar.alloc(8, F32)[:, 0:1], Buf("meps")
    c.dve("memset", [], [eq_b], eq, RMS_EPS)
    xin_v = x_in.rearrange("(n p) d -> n p d", p=128)
    ki = 0
    for su in range(nsup):
        hTs, hTb = hT[su % 2], hT_b[su % 2]
        tok = slice(su * NT, (su + 1) * NT)
        for ts in range(4):
            emit_front(c, xin_v[su * 4 + ts], xin_b[su * 4 + ts], hTs, hTb, ts * 128)
        for ts in range(4):
            tile_i = su * 4 + ts
            tcs = slice(ts * 128, (ts + 1) * 128)
            d0, d0_b = c.pb[0]
            d1, d1_b = c.pb[1]
            for k in range(KC):
                c.pe("matmul", [hTb, wdn_b], [d0_b], d0, lhsT=hTs[:, k, tcs], rhs=wdn[:, k, 0:512],
                     start=(k == 0), stop=(k == KC - 1))
            for k in range(KC):
                c.pe("matmul", [hTb, wdn_b], [d1_b], d1[:, 0:192], lhsT=hTs[:, k, tcs], rhs=wdn[:, k, 512:704],
                     start=(k == 0), stop=(k == KC - 1))
            cd_ap, cd_b = cd[tile_i % 2]
            c.act("activation", [d0_b], [cd_b], out=cd_ap[:, 0:512], in_=d0, func=AF.Copy)
            c.act("activation", [d1_b], [cd_b], out=cd_ap[:, 512:704], in_=d1[:, 0:192], func=AF.Copy)
            c.dve("scalar_tensor_tensor", [cd_b], [cqn_b, st_b], out=cqn, in0=cd_ap[:, 0:384], scalar=1.0,
                  in1=cd_ap[:, 0:384], op0=ALU.mult, op1=ALU.mult, accum_out=st[:, 0:1])
            c.dve("scalar_tensor_tensor", [cd_b], [ckn_b, st_b], out=ckn, in0=cd_ap[:, 384:640], scalar=1.0,
                  in1=cd_ap[:, 384:640], op0=ALU.mult, op1=ALU.mult, accum_out=st[:, 1:2])
            c.act("activation", [st_b, eq_b], [st_b], out=st[:, 2:3], in_=st[:, 0:1], func=AF.Sqrt,
                  scale=1.0 / Q_LORA, bias=eq)
            c.act("activation", [st_b, eq_b], [st_b], out=st[:, 3:4], in_=st[:, 1:2], func=AF.Sqrt,
                  scale=1.0 / KV_LORA, bias=eq)
            c.dve("reciprocal", [st_b], [st_b], out=st[:, 4:6], in_=st[:, 2:4])
            c.dve("scalar_tensor_tensor", [cd_b, st_b, qg_b], [cqn_b], out=cqn, in0=cd_ap[:, 0:384],
                  scalar=st[:, 4:5], in1=qg, op0=ALU.mult, op1=ALU.mult)
            c.dve("scalar_tensor_tensor", [cd_b, st_b, kg_b], [ckn_b], out=ckn, in0=cd_ap[:, 384:640],
                  scalar=st[:, 5:6], in1=kg, op0=ALU.mult, op1=ALU.mult)
            cs_t = cm[:, tile_i * 32:(tile_i + 1) * 32]
            sn_t = sm[:, tile_i * 32:(tile_i + 1) * 32]
            x1, x2 = cd_ap[:, 640:672], cd_ap[:, 672:704]
            (a1, a1b), (a2, a2b), (a3, a3b), (a4, a4b) = kt4
            c.dve("tensor_tensor", [cd_b, cm_b], [a1b], out=a1, in0=x1, in1=cs_t, op=ALU.mult)
            c.dve("tensor_tensor", [cd_b, sm_b], [a2b], out=a2, in0=x2, in1=sn_t, op=ALU.mult)
            c.dve("tensor_tensor", [cd_b, sm_b], [a3b], out=a3, in0=x1, in1=sn_t, op=ALU.mult)
            c.dve("tensor_tensor", [cd_b, cm_b], [a4b], out=a4, in0=x2, in1=cs_t, op=ALU.mult)
            c.dve("tensor_tensor", [a1b, a2b], [krr_b], out=krr[:, 0:32], in0=a1, in1=a2, op=ALU.subtract)
            c.dve("tensor_tensor", [a3b, a4b], [krr_b], out=krr[:, 32:64], in0=a3, in1=a4, op=ALU.add)
            pt_ap, pt_buf = c.ptr[0]
            for a in range(3):
                c.pe("transpose", [cqn_b, c.ident[1]], [pt_buf], out=pt_ap[:, a * 128:(a + 1) * 128],
                     in_=cqn[:, a * 128:(a + 1) * 128], identity=c.ident[0])
            for a in range(2):
                c.pe("transpose", [ckn_b, c.ident[1]], [pt_buf], out=pt_ap[:, 384 + a * 128:384 + (a + 1) * 128],
                     in_=ckn[:, a * 128:(a + 1) * 128], identity=c.ident[0])
            c.pe("transpose", [krr_b, c.ident[1]], [pt_buf], out=pt_ap[0:64, 640:768], in_=krr,
                 identity=c.ident[0])
            c.act("activation", [pt_buf], [cqT_b], out=cqT[:, :, tcs],
                  in_=pt_ap[:, 0:384].rearrange("p (k n) -> p k n", k=3), func=AF.Copy)
            c.act("activation", [pt_buf], [ckT_b], out=ckT[:, :, tcs],
                  in_=pt_ap[:, 384:640].rearrange("p (k n) -> p k n", k=2), func=AF.Copy)
            c.act("activation", [pt_buf], [krT_b], out=krT[0:64, tcs], in_=pt_ap[0:64, 640:768], func=AF.Copy)
        c.sp("dma_start", [krT_b], [c.KTR_b[su]], out=c.KTR[:, tok], in_=krT[0:64, :])
        for h in range(M_HEADS):
            pp, pp_b = c.pb[2 + h % 2]
            for kc in range(2):
                c.pe("matmul", [wuk_b, ckT_b], [pp_b], pp, lhsT=wuk[:, kc, h * 128:(h + 1) * 128], rhs=ckT[:, kc, :],
                     start=(kc == 0), stop=(kc == 1))
            k_ap, k_b = kts[ki % 2]
            ki += 1
            c.act("activation", [pp_b], [k_b], out=k_ap, in_=pp, func=AF.Copy)
            c.sp("dma_start", [k_b], [c.KTN_b[su][h]], out=c.KTN[h][:, tok], in_=k_ap)
        for ts in range(4):
            tile_i = su * 4 + ts
            tcs = slice(ts * 128, (ts + 1) * 128)
            rows = slice(tile_i * 128, (tile_i + 1) * 128)
            v_ap, v_b = vst[tile_i % 2]
            for nb in range(2):
                pp, pp_b = c.pb[2 + nb]
                for kc in range(2):
                    c.pe("matmul", [wuv_b, ckT_b], [pp_b], pp, lhsT=ckT[:, kc, tcs], rhs=wuv[:, kc, nb * 512:(nb + 1) * 512],
                         start=(kc == 0), stop=(kc == 1))
                c.act("activation", [pp_b], [v_b], out=v_ap[:, nb * 512:(nb + 1) * 512], in_=pp, func=AF.Copy)
            c.sp("dma_start", [v_b], [c.VM_b[tile_i]], out=c.VM[rows, :], in_=v_ap)
            for nb in range(3):
                pp, pp_b = c.pb[(4, 5, 1)[nb]]
                for kc in range(3):
                    c.pe("matmul", [wuq_b, cqT_b], [pp_b], pp, lhsT=cqT[:, kc, tcs], rhs=wuq[:, kc, nb * 512:(nb + 1) * 512],
                         start=(kc == 0), stop=(kc == 2))
                c.act("activation", [pp_b], [qf_b], out=qf[:, nb * 512:(nb + 1) * 512], in_=pp, func=AF.Copy)
            qf3 = qf.rearrange("p (h d) -> p h d", h=8)
            qb3 = qb.rearrange("p (h d) -> p h d", h=8)
            cs_t = cm[:, tile_i * 32:(tile_i + 1) * 32].unsqueeze(1).broadcast_to([128, 8, 32])
            sn_t = sm[:, tile_i * 32:(tile_i + 1) * 32].unsqueeze(1).broadcast_to([128, 8, 32])
            x1, x2 = qf3[:, :, 128:160], qf3[:, :, 160:192]
            rr = [(r[0].rearrange("p (h j) -> p h j", h=8), r[1]) for r in rt]
            (a1, a1b), (a2, a2b), (a3, a3b), (a4, a4b) = rr
            c.dve("tensor_tensor", [qf_b, cm_b], [a1b], out=a1, in0=x1, in1=cs_t, op=ALU.mult)
            c.dve("tensor_tensor", [qf_b, sm_b], [a2b], out=a2, in0=x2, in1=sn_t, op=ALU.mult)
            c.dve("tensor_tensor", [qf_b, sm_b], [a3b], out=a3, in0=x1, in1=sn_t, op=ALU.mult)
            c.dve("tensor_tensor", [qf_b, cm_b], [a4b], out=a4, in0=x2, in1=cs_t, op=ALU.mult)
            c.dve("tensor_tensor", [a1b, a2b], [qb_b], out=qb3[:, :, 128:160], in0=a1, in1=a2, op=ALU.subtract)
            c.dve("tensor_tensor", [a3b, a4b], [qb_b], out=qb3[:, :, 160:192], in0=a3, in1=a4, op=ALU.add)
            c.dve("tensor_copy", [qf_b], [qb_b], out=qb3[:, :, 0:128], in_=qf3[:, :, 0:128])
            p6, p6_b = c.pb[6][0].bitcast(BF16), c.pb[6][1]
            p7, p7_b = c.ptr[0]
            for h in range(M_HEADS):
                c.pe("transpose", [qb_b, c.ident[1]], [p6_b], out=p6[:, h * 128:(h + 1) * 128], in_=qb3[:, h, 0:128],
                     identity=c.ident[0])
            for h in range(M_HEADS):
                c.pe("transpose", [qb_b, c.ident[1]], [p7_b], out=p7[0:64, h * 128:(h + 1) * 128],
                     in_=qb3[:, h, 128:192], identity=c.ident[0])
            n_ap, n_b = qtn[tile_i % 2]
            r_ap, r_b = qtr[tile_i % 2]
            c.act("activation", [p6_b], [n_b], out=n_ap, in_=p6.rearrange("p (h n) -> p h n", h=8), func=AF.Copy)
            c.act("activation", [p7_b], [r_b], out=r_ap[0:64], in_=p7[0:64, :].rearrange("p (h n) -> p h n", h=8),
                  func=AF.Copy)
            c.sp("dma_start", [n_b], [c.QTN_b[tile_i]], out=c.QTN[:, :, rows].rearrange("h d s -> d h s"), in_=n_ap)
            c.sp("dma_start", [r_b], [c.QTR_b[tile_i]], out=c.QTR[:, :, rows].rearrange("h d s -> d h s"),
                 in_=r_ap[0:64])
    ar.release()


def phase_mla_attn(c, xio, w_o, S):
    Tracker.phase = "mla_attn"
    x_io, xio_b = xio
    ar = c.arena
    ar.mark()
    nt = S // 128
    nq = S // 512
    OT = ar.alloc(M_HEADS * S, BF16).rearrange("p (h s) -> p h s", h=M_HEADS)
    OT_b = [Buf(f"OT{h}") for h in range(M_HEADS)]
    ar.mark()
    ktr, ktr_b = ar.alloc(S, BF16), Buf("ktr")
    c.dve("memset", [], [ktr_b], ktr[64:128, :], 0.0)
    c.sp("dma_start", c.KTR_b, [ktr_b], out=ktr[0:64, :], in_=c.KTR)
    hd = []
    for i in range(2):
        hd.append(dict(qn=(ar.alloc(S, BF16), Buf(f"qn{i}")), qr=(ar.alloc(S, BF16), Buf(f"qr{i}")),
                       kn=(ar.alloc(S, BF16), Buf(f"kn{i}")),
                       vh=(ar.alloc(S, BF16).rearrange("p (t e) -> p t e", e=128), Buf(f"vh{i}"))))
    for i in range(2):
        c.dve("memset", [], [hd[i]["qr"][1]], hd[i]["qr"][0][64:128, :], 0.0)
    NST = 4
    st_banks = [c.pb[0], c.pb[1], c.pb[2], c.pb[6]]
    pTl = [(ar.alloc(512, BF16), Buf(f"pT{i}")) for i in range(NST)]
    Lacc = [(ar.alloc(512, F32), Buf(f"Lacc{i}")) for i in range(2)]
    RLs = [(ar.alloc(512, F32), Buf(f"RLs{i}")) for i in range(2)]
    ones_f, ones_fb = ar.alloc(128, F32), Buf("ones_f")
    c.dve("memset", [], [ones_fb], ones_f, 1.0)
    cnt = 0
    for h in range(M_HEADS):
        H = hd[h % 2]
        qn, qn_b = H["qn"]
        qr, qr_b = H["qr"]
        kn, kn_b = H["kn"]
        vh, vh_b = H["vh"]
        c.sp("dma_start", c.QTN_b, [qn_b], out=qn, in_=c.QTN[h])
        c.sp("dma_start", c.QTR_b, [qr_b], out=qr[0:64, :], in_=c.QTR[h])
        c.sp("dma_start", [b[h] for b in c.KTN_b], [kn_b], out=kn, in_=c.KTN[h])
        c.sp("dma_start", c.VM_b, [vh_b], out=vh,
             in_=c.VM[:, h * 128:(h + 1) * 128].rearrange("(t p) e -> p t e", p=128))
        iters = [(qt, kt) for qt in range(nq) for kt in range(nt)]
        slots = {}

        def emit_st(i):
            nonlocal cnt
            qt, kt = iters[i]
            qs = slice(qt * 512, (qt + 1) * 512)
            ks = slice(kt * 128, (kt + 1) * 128)
            sT, sT_b = st_banks[cnt % NST]
            p_ap, p_b = pTl[cnt % NST]
            cnt += 1
            slots[i] = (sT, sT_b, p_ap, p_b)
            c.pe("matmul", [kn_b, qn_b], [sT_b], sT, lhsT=kn[:, ks], rhs=qn[:, qs], start=True, stop=False)
            c.pe("matmul", [ktr_b, qr_b], [sT_b], sT, lhsT=ktr[:, ks], rhs=qr[:, qs], start=False, stop=True)

        AHEAD = 3
        for i in range(min(AHEAD, len(iters))):
            emit_st(i)
        for i, (qt, kt) in enumerate(iters):
            if i + AHEAD < len(iters):
                emit_st(i + AHEAD)
            qs = slice(qt * 512, (qt + 1) * 512)
            oT, oT_b = c.pb[3 + qt % 2]
            la, la_b = Lacc[qt % 2]
            sT, sT_b, p_ap, p_b = slots.pop(i)
            c.act("activation", [sT_b], [p_b], out=p_ap, in_=sT, func=AF.Exp, scale=float(M_SCALE))
            c.pe("matmul", [vh_b, p_b], [oT_b], oT, lhsT=vh[:, kt, :], rhs=p_ap, start=(kt == 0), stop=(kt == nt - 1))
            if kt == 0:
                c.dve("tensor_copy", [p_b], [la_b], out=la, in_=p_ap)
            else:
                c.dve("tensor_tensor", [p_b, la_b], [la_b], out=la, in0=la, in1=p_ap, op=ALU.add)
            if kt == nt - 1:
                RB, RB_b = c.pb[5]
                c.pe("matmul", [ones_fb, la_b], [RB_b], RB, lhsT=ones_f, rhs=la, start=True, stop=True)
                R_ap, R_b = RLs[qt % 2]
                c.dve("reciprocal", [RB_b], [R_b], out=R_ap, in_=RB)
                c.dve("tensor_tensor", [oT_b, R_b], [OT_b[h]], out=OT[:, h, qs], in0=oT, in1=R_ap, op=ALU.mult)
    ar.release()
    Tracker.phase = "mla_out"
    wo = ar.alloc(M_HEADS * D, BF16).rearrange("p (k n) -> p k n", k=M_HEADS)
    wo_b = Buf("mwo")
    c.pool("dma_start", [], [wo_b], out=wo, in_=w_o.rearrange("(k p) n -> p k n", p=128))
    xv = x_io.rearrange("(n p) d -> n p d", p=128)
    for t in range(nt):
        for nb in range(2):
            o_ap, o_buf = c.psum_o[c.po_i % len(c.psum_o)]
            c.po_i += 1
            for h in range(M_HEADS):
                c.pe("matmul", [OT_b[h], wo_b], [o_buf], o_ap, lhsT=OT[:, h, t * 128:(t + 1) * 128],
                     rhs=wo[:, h, nb * 512:(nb + 1) * 512], start=(h == 0), stop=(h == M_HEADS - 1))
            csl = slice(nb * 512, (nb + 1) * 512)
            emit_resid(c, o_ap, o_buf, xv[t][:, csl], xio_b[t][nb], xv[t][:, csl], xio_b[t][nb], c.G[0][:, csl])
    ar.release()


def phase_final(c, xin, out, out_b, fg, S):
    Tracker.phase = "final"
    x_in, xin_b = xin
    c.sp("dma_start", [], [c.A[1]], out=c.A[0], in_=fg.partition_broadcast(128))
    xv = x_in.rearrange("(n p) d -> n p d", p=128)
    ov = out.rearrange("(n p) d -> n p d", p=128)
    for t in range(S // 128):
        slot = c.xslot
        c.xslot = (c.xslot + 1) % len(c.xt)
        xt, xb = c.xt[slot]
        hb_ap, hb_buf = c.hb[slot % len(c.hb)]
        ss_ap, ss_buf = c.ss[slot % len(c.ss)]
        c.sp("dma_start", list(xin_b[t]), [xb], out=xt, in_=xv[t])
        c.dve("scalar_tensor_tensor", [xb], [hb_buf, ss_buf], out=hb_ap, in0=xt, scalar=1.0, in1=xt,
              op0=ALU.mult, op1=ALU.mult, accum_out=ss_ap[:, 0:1])
        c.act("activation", [ss_buf, c.eps_rms[1]], [ss_buf], out=ss_ap[:, 1:2], in_=ss_ap[:, 0:1], func=AF.Sqrt,
              scale=1.0 / D, bias=c.eps_rms[0])
        c.dve("reciprocal", [ss_buf], [ss_buf], out=ss_ap[:, 2:3], in_=ss_ap[:, 1:2])
        c.dve("scalar_tensor_tensor", [xb, ss_buf, c.A[1]], [xb], out=xt, in0=xt, scalar=ss_ap[:, 2:3],
              in1=c.A[0], op0=ALU.mult, op1=ALU.mult)
        c.sp("dma_start", [xb], list(out_b[t]), out=ov[t], in_=xt)


def xbufs(S, name):
    return [[Buf(f"{name}{t}_{h}") for h in range(2)] for t in range(S // 128)]


SCRATCH_KIND = "Internal"


def alloc_ret_scratch(c, nc, S):
    nt = S // 128
    c.QT = nc.dram_tensor("QT", [R_QK, S], BF16, kind=SCRATCH_KIND).ap()
    c.KT = nc.dram_tensor("KT", [R_QK, S], BF16, kind=SCRATCH_KIND).ap()
    c.KF = nc.dram_tensor("KF", [S, R_QK], BF16, kind=SCRATCH_KIND).ap()
    c.KB = nc.dram_tensor("KB", [S, R_QK], BF16, kind=SCRATCH_KIND).ap()
    c.V = nc.dram_tensor("Vr", [S, R_VTOT], BF16, kind=SCRATCH_KIND).ap()
    c.Gs = nc.dram_tensor("Gs", [S, R_VTOT], F32, kind=SCRATCH_KIND).ap()
    c.SB = nc.dram_tensor("SBs", [nt, R_HEADS, 128, 1024], BF16, kind=SCRATCH_KIND).ap()
    c.QT_b = [[Buf("QT") for h in range(R_HEADS)] for _ in range(S // 512)]
    c.KT_b = [Buf("KT") for _ in range(S // 512)]
    c.KF_b = [Buf("KF") for _ in range(nt)]
    c.KB_b = [Buf("KB") for _ in range(nt)]
    c.V_b = [[Buf("V") for _ in range(4)] for _ in range(nt)]
    c.Gs_b = [[Buf("Gs") for _ in range(4)] for _ in range(nt)]
    c.SB_b = [[Buf("SB") for _ in range(R_HEADS)] for _ in range(nt)]


def alloc_tables(c, nc, S):
    c.CR = nc.dram_tensor("CR", [128, S], F32, kind=SCRATCH_KIND).ap()
    c.SR = nc.dram_tensor("SR", [128, S], F32, kind=SCRATCH_KIND).ap()
    c.CM = nc.dram_tensor("CM", [S, 32], F32, kind=SCRATCH_KIND).ap()
    c.SM = nc.dram_tensor("SM", [S, 32], F32, kind=SCRATCH_KIND).ap()
    c.CR_b, c.SR_b, c.CM_b, c.SM_b = Buf("CR"), Buf("SR"), Buf("CM"), Buf("SM")


def load_ctab(c, ctab_dram):
    c.ctab_dram = ctab_dram


def fetch_ctab(c):
    ar = c.arena
    ap, b = ar.alloc(CTW, F32), Buf("ctab")
    c.sp("dma_start", [], [b], out=ap, in_=c.ctab_dram)
    c.ctab = (ap, b)
    return c.ctab


def emit_copy_x(c, src, dst, dst_b, S):
    for t in range(S // 128):
        for hf in range(2):
            r_ap, r_buf = c.xr[c.xr_i % 2]
            c.xr_i += 1
            sl = (slice(t * 128, (t + 1) * 128), slice(hf * 512, (hf + 1) * 512))
            c.sp("dma_start", [], [r_buf], out=r_ap, in_=src[sl])
            c.sp("dma_start", [r_buf], [dst_b[t][hf]], out=dst[sl], in_=r_ap)


def build_ret_test(S):
    nc = bass.Bass("TRN2", target_bir_lowering=False)
    x = nc.dram_tensor("x", [S, D], F32, kind="ExternalInput").ap()
    pos = nc.dram_tensor("pos", [S], I32, kind="ExternalInput").ap()
    w_in = nc.dram_tensor("w_in", [D, R_IN], F32, kind="ExternalInput").ap()
    w_out = nc.dram_tensor("w_out", [R_VTOT, D], F32, kind="ExternalInput").ap()
    gn_g = nc.dram_tensor("gn_g", [R_VTOT], F32, kind="ExternalInput").ap()
    gn_b = nc.dram_tensor("gn_b", [R_VTOT], F32, kind="ExternalInput").ap()
    dec_f = nc.dram_tensor("dec_f", [4], F32, kind="ExternalInput").ap()
    dec_b = nc.dram_tensor("dec_b", [4], F32, kind="ExternalInput").ap()
    rows3 = nc.dram_tensor("rows3", [3, D], F32, kind="ExternalInput").ap()
    ident = nc.dram_tensor("ident", [128, 128], BF16, kind="ExternalInput").ap()
    ctab = nc.dram_tensor("ctab", [128, CTW], F32, kind="ExternalInput").ap()
    out = nc.dram_tensor("out", [S, D], F32, kind="ExternalOutput").ap()
    c = Ctx()
    c.tr = Tracker()
    c.ident_dram = ident
    alloc_tables(c, nc, S)
    alloc_ret_scratch(c, nc, S)
    with nc.sbuf_tensor("arena", [128, ARENA_BYTES // 4], F32) as ah, \
            nc.psum_tensor("psum", [128, 4096], F32) as ps:
        setup_common(c, nc, ah, ARENA_BYTES, ps)
        load_ctab(c, ctab)
        phase_setup_tables(c, pos, S)
        xb = xbufs(S, "x")
        ob = xbufs(S, "o")
        emit_copy_x(c, x, out, ob, S)
        c.arena.mark()
        dt = ret_tables(c, dec_f, dec_b)
        phase_ret_in(c, (x, xb), w_in, rows3, S, dt)
        phase_ret_bwd(c, S, dt)
        phase_ret_fwd(c, (out, ob), w_out, gn_g, gn_b, S, dt)
        c.arena.release()
        n = c.tr.emit(nc)
    print("ops", n)
    return nc


def build_ffn_test(S):
    nc = bass.Bass("TRN2", target_bir_lowering=False)
    x = nc.dram_tensor("x", [S, D], F32, kind="ExternalInput").ap()
    w_in = nc.dram_tensor("w_in", [D, 2 * DFF], F32, kind="ExternalInput").ap()
    w_out = nc.dram_tensor("w_out", [DFF, D], F32, kind="ExternalInput").ap()
    rows3 = nc.dram_tensor("rows3", [3, D], F32, kind="ExternalInput").ap()
    ident = nc.dram_tensor("ident", [128, 128], BF16, kind="ExternalInput").ap()
    out = nc.dram_tensor("out", [S, D], F32, kind="ExternalOutput").ap()
    c = Ctx()
    c.tr = Tracker()
    c.ident_dram = ident
    with nc.sbuf_tensor("arena", [128, ARENA_BYTES // 4], F32) as ah, \
            nc.psum_tensor("psum", [128, 4096], F32) as ps:
        setup_common(c, nc, ah, ARENA_BYTES, ps)
        phase_ffn(c, (x, xbufs(S, "x")), (out, xbufs(S, "o")), w_in, w_out, rows3, S)
        n = c.tr.emit(nc)
    print("ops", n)
    return nc


def build_mla_test(S):
    nc = bass.Bass("TRN2", target_bir_lowering=False)
    x = nc.dram_tensor("x", [S, D], F32, kind="ExternalInput").ap()
    pos = nc.dram_tensor("pos", [S], I32, kind="ExternalInput").ap()
    w_down = nc.dram_tensor("w_down", [D, M_DOWN], F32, kind="ExternalInput").ap()
    qng = nc.dram_tensor("qng", [Q_LORA], F32, kind="ExternalInput").ap()
    kvng = nc.dram_tensor("kvng", [KV_LORA], F32, kind="ExternalInput").ap()
    w_uq = nc.dram_tensor("w_uq", [Q_LORA, 1536], F32, kind="ExternalInput").ap()
    w_ukv = nc.dram_tensor("w_ukv", [KV_LORA, 2048], F32, kind="ExternalInput").ap()
    w_o = nc.dram_tensor("w_o", [1024, D], F32, kind="ExternalInput").ap()
    rows3 = nc.dram_tensor("rows3", [3, D], F32, kind="ExternalInput").ap()
    ident = nc.dram_tensor("ident", [128, 128], BF16, kind="ExternalInput").ap()
    ctab = nc.dram_tensor("ctab", [128, CTW], F32, kind="ExternalInput").ap()
    out = nc.dram_tensor("out", [S, D], F32, kind="ExternalOutput").ap()
    c = Ctx()
    c.tr = Tracker()
    c.ident_dram = ident
    alloc_tables(c, nc, S)
    alloc_mla_scratch(c, nc, S)
    with nc.sbuf_tensor("arena", [128, ARENA_BYTES // 4], F32) as ah, \
            nc.psum_tensor("psum", [128, 4096], F32) as ps:
        setup_common(c, nc, ah, ARENA_BYTES, ps)
        load_ctab(c, ctab)
        phase_setup_tables(c, pos, S)
        xb = xbufs(S, "x")
        ob = xbufs(S, "o")
        emit_copy_x(c, x, out, ob, S)
        phase_mla_in(c, (x, xb), w_down, qng, kvng, w_uq, w_ukv, rows3, S)
        phase_mla_attn(c, (out, ob), w_o, S)
        n = c.tr.emit(nc)
    print("ops", n)
    return nc


DEPTH = 4
_NC_CACHE = {}
W_SPECS = [
    ("norm_g", [DEPTH, 3, D]), ("final_norm_g", [D]), ("mod_w", [DEPTH, D, 9 * D]), ("mod_b", [DEPTH, 9 * D]),
    ("ffn_w_in", [DEPTH, 2, D, 2 * DFF]), ("ffn_w_out", [DEPTH, 2, DFF, D]),
    ("ret_w_in", [2, D, R_IN]), ("ret_w_out", [2, R_VTOT, D]), ("ret_gn_g", [2, R_VTOT]), ("ret_gn_b", [2, R_VTOT]),
    ("ret_decay_fwd", [2, 4]), ("ret_decay_bwd", [2, 4]),
    ("mla_w_down", [2, D, M_DOWN]), ("mla_q_norm_g", [2, Q_LORA]), ("mla_kv_norm_g", [2, KV_LORA]),
    ("mla_w_uq", [2, Q_LORA, 1536]), ("mla_w_ukv", [2, KV_LORA, 2048]), ("mla_w_o", [2, 1024, D]),
]


def build_full(S, depth=DEPTH, layers=None):
    nc = bass.Bass("TRN2", target_bir_lowering=False)
    x = nc.dram_tensor("x", [S, D], F32, kind="ExternalInput").ap()
    cvec = nc.dram_tensor("c", [D], F32, kind="ExternalInput").ap()
    pos = nc.dram_tensor("positions", [S], I32, kind="ExternalInput").ap()
    W = {n: nc.dram_tensor(n, shp, F32, kind="ExternalInput").ap() for n, shp in W_SPECS}
    ident = nc.dram_tensor("ident", [128, 128], BF16, kind="ExternalInput").ap()
    ctab = nc.dram_tensor("ctab", [128, CTW], F32, kind="ExternalInput").ap()
    out = nc.dram_tensor("out", [S, D], F32, kind="ExternalOutput").ap()
    xres = nc.dram_tensor("xres", [S, D], F32, kind=SCRATCH_KIND).ap()
    c = Ctx()
    c.tr = Tracker()
    c.ident_dram = ident
    c.modrows = nc.dram_tensor("modrows", [DEPTH, 3, 3, D], F32, kind=SCRATCH_KIND).ap()
    c.modrows_b = [Buf(f"modrows{i}") for i in range(DEPTH)]
    alloc_tables(c, nc, S)
    alloc_ret_scratch(c, nc, S)
    alloc_mla_scratch(c, nc, S)
    with nc.sbuf_tensor("arena", [128, ARENA_BYTES // 4], F32) as ah, \
            nc.psum_tensor("psum", [128, 4096], F32) as ps:
        setup_common(c, nc, ah, ARENA_BYTES, ps)
        load_ctab(c, ctab)
        phase_setup_tables(c, pos, S)
        phase_mod(c, cvec, W["mod_w"], W["mod_b"], W["norm_g"], depth)
        xin_b = xbufs(S, "xin")
        xb = xbufs(S, "xres")
        ob = xbufs(S, "out")
        for i in (layers if layers is not None else range(depth)):
            rows = lambda sl: (c.modrows[i, sl], [c.modrows_b[i]])
            src = (x, xin_b) if i == (layers[0] if layers is not None else 0) else (xres, xb)
            phase_ffn(c, src, (xres, xb), W["ffn_w_in"][i, 0], W["ffn_w_out"][i, 0], rows(0), S)
            j = i // 2
            if i % 2 == 0:
                c.arena.mark()
                dt = ret_tables(c, W["ret_decay_fwd"][j], W["ret_decay_bwd"][j])
                phase_ret_in(c, (xres, xb), W["ret_w_in"][j], rows(1), S, dt)
                phase_ret_bwd(c, S, dt)
                phase_ret_fwd(c, (xres, xb), W["ret_w_out"][j], W["ret_gn_g"][j], W["ret_gn_b"][j], S, dt)
                c.arena.release()
            else:
                phase_mla_in(c, (xres, xb), W["mla_w_down"][j], W["mla_q_norm_g"][j], W["mla_kv_norm_g"][j],
                             W["mla_w_uq"][j], W["mla_w_ukv"][j], rows(1), S)
                phase_mla_attn(c, (xres, xb), W["mla_w_o"][j], S)
            phase_ffn(c, (xres, xb), (xres, xb), W["ffn_w_in"][i, 1], W["ffn_w_out"][i, 1], rows(2), S)
        phase_final(c, (xres, xb), out, ob, W["final_norm_g"], S)
        n = c.tr.emit(nc)
    _NC_CACHE["last_tr"] = c.tr
    return nc, n


SEQ = 4096
BATCH = 8


def kernel(**inputs):
    if "nc" not in _NC_CACHE:
        _NC_CACHE["nc"] = build_full(SEQ)[0]
    nc = _NC_CACHE["nc"]
    cst = host_consts()
    f32 = lambda a: np.ascontiguousarray(np.asarray(a), dtype=np.float32)
    shared = {n: f32(inputs[n]) for n, _ in W_SPECS}
    shared["ident"] = cst["ident"]
    shared["ctab"] = cst["ctab"]
    x = f32(inputs["x"])
    cc = f32(inputs["c"])
    pos = np.ascontiguousarray(np.asarray(inputs["positions"]), dtype=np.int32)
    in_maps = []
    for b in range(BATCH):
        m = dict(shared)
        m["x"] = x[b]
        m["c"] = cc[b]
        m["positions"] = pos[b]
        in_maps.append(m)
    res = run_bass_kernel_spmd(nc, in_maps, core_ids=list(range(BATCH)))
    return np.stack([np.asarray(res.results[b]["out"]) for b in range(BATCH)]).astype(np.float32)
```

```python
import contextlib
import numpy as np
import concourse.bass as bass
import concourse.mybir as mybir
from concourse.bass_utils import run_bass_kernel_spmd

F32 = mybir.dt.float32
BF16 = mybir.dt.bfloat16
I32 = mybir.dt.int32
AF = mybir.ActivationFunctionType
ALU = mybir.AluOpType
AX = mybir.AxisListType

D = 1024
DFF = 2816
KC = D // 128
FC = DFF // 128
RMS_EPS = 1e-6

ANNOTATE = False
ENGS = ["pe", "act", "dve", "pool", "sp"]
NDSEM = {"sp": 12, "pool": 8, "act": 4, "pe": 0, "dve": 0}


class Buf:
    __slots__ = ("name", "w", "rc", "rd")

    def __init__(self, name=""):
        self.name = name
        self.w = None
        self.rc = {}
        self.rd = set()
        reg = Arena.cur
        if reg is not None:
            for r in Arena.regions:
                if r is not reg and r[0] < reg[1] and reg[0] < r[1]:
                    for ob in r[2]:
                        self._inherit(ob)
            reg[2].append(self)

    def _inherit(self, ob):
        if ob.w is not None:
            if ob.w[0] == "d":
                self.rd.add(ob.w[1])
            elif self.rc.get(ob.w[1], -1) < ob.w[2]:
                self.rc[ob.w[1]] = ob.w[2]
        for e, i in ob.rc.items():
            if self.rc.get(e, -1) < i:
                self.rc[e] = i
        self.rd |= ob.rd


class Tracker:
    phase = ""

    def __init__(self):
        self.ops = {e: [] for e in ENGS}
        self.dmas = []
        self.ndma = {e: 0 for e in ENGS}
        self.dma_by_k = {e: [] for e in ENGS}

    def add(self, eng, method, reads, writes, *args, **kw):
        dma = method == "dma_start"
        fn = (method, args, kw)
        dc = {}
        dd = set()

        def dep(d):
            if d is None:
                return
            if d[0] == "d":
                dd.add(d[1])
            elif dc.get(d[1], -1) < d[2]:
                dc[d[1]] = d[2]

        for b in reads:
            dep(b.w)
        for b in writes:
            dep(b.w)
            for e, i in b.rc.items():
                dep(("c", e, i))
            for did in b.rd:
                dep(("d", did))
        idx = len(self.ops[eng])
        did = None
        if dma:
            did = len(self.dmas)
            k = self.ndma[eng]
            self.ndma[eng] += 1
            self.dmas.append((eng, k))
            n = NDSEM[eng]
            if k >= n:
                dd.add(self.dma_by_k[eng][k - n])
            self.dma_by_k[eng].append(did)
            me = ("d", did)
        else:
            me = ("c", eng, idx)
        self.ops[eng].append(dict(fn=fn, dc=dc, dd=dd, did=did, ph=Tracker.phase))
        for b in reads:
            if dma:
                b.rd.add(did)
            else:
                b.rc[eng] = idx
        for b in writes:
            b.w = me
            b.rc = {}
            b.rd = set()
        return me

    def emit(self, nc):
        ops = self.ops
        signal = {e: [False] * len(ops[e]) for e in ENGS}
        waits = {e: [None] * len(ops[e]) for e in ENGS}
        for e in ENGS:
            seen_c = {p: -1 for p in ENGS}
            seen_d = set()
            for i, op in enumerate(ops[e]):
                wl = []
                for p, j in op["dc"].items():
                    if p == "pe" and e == "pe" and op["did"] is None:
                        continue
                    if seen_c[p] >= j:
                        continue
                    seen_c[p] = j
                    signal[p][j] = True
                    wl.append(("c", p, j))
                for did in sorted(op["dd"]):
                    if did in seen_d:
                        continue
                    seen_d.add(did)
                    wl.append(("d", did))
                waits[e][i] = wl
        sigval = {}
        for e in ENGS:
            c = 0
            for i in range(len(ops[e])):
                if signal[e][i]:
                    c += 1
                    sigval[(e, i)] = c
        with contextlib.ExitStack() as st:
            csem = {e: st.enter_context(nc.semaphore(f"c_{e}")) for e in ENGS if e != "sp"}
            dsem = {e: [st.enter_context(nc.semaphore(f"d_{e}{k}")) for k in range(NDSEM[e])]
                    for e in ENGS if self.ndma[e] > 0}

            def dma_semval(did):
                q, k = self.dmas[did]
                n = NDSEM[q]
                return dsem[q][k % n], 16 * (k // n + 1)

            final = []
            for q in ENGS:
                nd = self.ndma[q]
                n = NDSEM[q]
                for s in range(min(n, nd)):
                    cnt = (nd - 1 - s) // n + 1
                    final.append((dsem[q][s], 16 * cnt))

            with nc.Block() as block:
                regs = {"pe": block.tensor, "act": block.scalar, "dve": block.vector,
                        "pool": block.gpsimd, "sp": block.sync}
                for e in ENGS:
                    def body(eng, e=e):
                        for i, op in enumerate(ops[e]):
                            for w in waits[e][i]:
                                if w[0] == "c":
                                    eng.wait_ge(csem[w[1]], sigval[(w[1], w[2])])
                                else:
                                    s, v = dma_semval(w[1])
                                    eng.wait_ge(s, v)
                            m, a, k = op["fn"]
                            ins = getattr(eng, m)(*a, **k)
                            if ANNOTATE and op["ph"]:
                                ins.annotate(op["ph"])
                            if op["did"] is not None:
                                s, v = dma_semval(op["did"])
                                ins.then_inc(s, 16)
                            elif signal[e][i]:
                                ins.then_inc(csem[e], 1)
                        if e == "sp":
                            for s, v in final:
                                eng.wait_ge(s, v)
                    regs[e](body)
        return {e: len(v) for e, v in ops.items()}


class Arena:
    cur = None
    regions = []

    def __init__(self, handle, nbytes):
        self.h = handle
        self.n = nbytes
        self.off = 0
        self.marks = []
        Arena.cur = None
        Arena.regions = []

    def alloc(self, nelem, dtype, shape=None):
        sz = 2 if dtype == BF16 else 4
        nb = (nelem * sz + 31) // 32 * 32
        assert self.off + nb <= self.n, f"arena overflow {self.off}+{nb}>{self.n}"
        a = self.h[:, self.off // 4:(self.off + nb) // 4]
        Arena.cur = [self.off, self.off + nb, []]
        Arena.regions.append(Arena.cur)
        self.off += nb
        if dtype != F32:
            a = a.bitcast(dtype)
        a = a[:, 0:nelem]
        return a

    def mark(self):
        self.marks.append(self.off)

    def release(self):
        self.off = self.marks.pop()


class Ctx:
    pass


def _mk(eng):
    def f(self, method, reads, writes, *a, **k):
        return self.tr.add(eng, method, reads, writes, *a, **k)
    return f


for _e in ENGS:
    setattr(Ctx, _e, _mk(_e))


def emit_front(c, x_src, x_bufs, hT, hT_buf, col0):
    slot = c.xslot
    c.xslot = (c.xslot + 1) % len(c.xt)
    xt, xb = c.xt[slot]
    hb_ap, hb_buf = c.hb[slot % len(c.hb)]
    ss_ap, ss_buf = c.ss[slot % len(c.ss)]
    c.sp("dma_start", list(x_bufs), [xb], out=xt, in_=x_src)
    c.dve("scalar_tensor_tensor", [xb], [hb_buf, ss_buf], out=hb_ap, in0=xt, scalar=1.0, in1=xt,
          op0=ALU.mult, op1=ALU.mult, accum_out=ss_ap[:, 0:1])
    c.act("activation", [ss_buf, c.eps_rms[1]], [ss_buf], out=ss_ap[:, 1:2], in_=ss_ap[:, 0:1], func=AF.Sqrt,
          scale=1.0 / D, bias=c.eps_rms[0])
    c.dve("reciprocal", [ss_buf], [ss_buf], out=ss_ap[:, 2:3], in_=ss_ap[:, 1:2])
    c.dve("scalar_tensor_tensor", [xb, ss_buf, c.A[1]], [xb], out=xt, in0=xt, scalar=ss_ap[:, 2:3],
          in1=c.A[0], op0=ALU.mult, op1=ALU.mult)
    c.dve("tensor_tensor", [xb, c.B[1]], [hb_buf], out=hb_ap, in0=xt, in1=c.B[0], op=ALU.add)
    pt_ap, pt_buf = c.ptr[c.ptr_i % len(c.ptr)]
    c.ptr_i += 1
    for kc in range(KC):
        c.pe("transpose", [hb_buf, c.ident[1]], [pt_buf], out=pt_ap[:, kc * 128:(kc + 1) * 128],
             in_=hb_ap[:, kc * 128:(kc + 1) * 128], identity=c.ident[0])
    c.act("activation", [pt_buf], [hT_buf], out=hT[:, :, col0:col0 + 128],
          in_=pt_ap.rearrange("p (k n) -> p k n", k=KC), func=AF.Copy)


def emit_resid(c, o_ap, o_buf, x_src, x_src_b, x_dst, x_dst_b, g_ap):
    r_ap, r_buf = c.xr[c.xr_i % len(c.xr)]
    t_ap, t_buf = c.ot[c.xr_i % len(c.ot)]
    c.xr_i += 1
    c.sp("dma_start", [x_src_b], [r_buf], out=r_ap, in_=x_src)
    c.dve("tensor_tensor", [o_buf, c.G[1]], [t_buf], out=t_ap, in0=o_ap, in1=g_ap, op=ALU.mult)
    c.dve("tensor_tensor", [t_buf, r_buf], [r_buf], out=r_ap, in0=r_ap, in1=t_ap, op=ALU.add)
    c.sp("dma_start", [r_buf], [x_dst_b], out=x_dst, in_=r_ap)


def load_bcast_rows(c, rows3):
    rb = []
    if isinstance(rows3, tuple):
        rows3, rb = rows3
    for i, (ap, buf) in enumerate((c.A, c.B, c.G)):
        c.sp("dma_start", list(rb), [buf], out=ap, in_=rows3[i, :].partition_broadcast(128))


def phase_ffn(c, xin, xout, w_in, w_out, rows3, S):
    Tracker.phase = "ffn"
    x_in, xin_b = xin
    x_out, xout_b = xout
    ar = c.arena
    ar.mark()
    win = ar.alloc(KC * 2 * DFF, BF16).rearrange("p (k n) -> p k n", k=KC)
    NJB = FC // 2
    wg_b = [Buf(f"wing{j}") for j in range(NJB)]
    wu_b = [Buf(f"winu{j}") for j in range(NJB)]
    wout = ar.alloc(FC * D, BF16).rearrange("p (k n) -> p k n", k=FC)
    wout_b = [Buf("wout0"), Buf("wout1")]
    NT = 512
    nsup = S // NT
    hT = [ar.alloc(KC * NT, BF16).rearrange("p (k n) -> p k n", k=KC) for _ in range(2)]
    hT_b = [Buf("hT0"), Buf("hT1")]
    aT = ar.alloc(FC * NT, BF16).rearrange("p (k n) -> p k n", k=FC)
    aT_b = [Buf(f"aT{j}") for j in range(FC)]
    sg = [(ar.alloc(NT, F32), Buf(f"sg{i}")) for i in range(2)]

    w_in_v = w_in.rearrange("(k p) n -> p k n", p=128)
    for jb in range(NJB):
        for (bb, c0) in ((wg_b, 0), (wu_b, DFF)):
            cs_ = slice(c0 + jb * 256, c0 + (jb + 1) * 256)
            c.pool("dma_start", [], [bb[jb]], out=win[:, :, cs_], in_=w_in_v[:, :, cs_])
    w_out_v = w_out.rearrange("(k p) n -> p k n", p=128)
    for hh in range(2):
        c.pool("dma_start", [], [wout_b[hh]], out=wout[:, hh * 11:(hh + 1) * 11, :],
               in_=w_out_v[:, hh * 11:(hh + 1) * 11, :])
    load_bcast_rows(c, rows3)

    xin_v = x_in.rearrange("(n p) d -> n p d", p=128)
    xout_v = x_out.rearrange("(n p) d -> n p d", p=128)
    gu = c.psum_gu
    gu_i = 0
    def do_front(su_):
        for ts_ in range(4):
            emit_front(c, xin_v[su_ * 4 + ts_], xin_b[su_ * 4 + ts_], hT[su_ % 2], hT_b[su_ % 2], ts_ * 128)

    do_front(0)
    for su in range(nsup):
        hTs, hTb = hT[su % 2], hT_b[su % 2]
        for j in range(FC):
            g_ap, g_buf = gu[gu_i % 4]
            u_ap, u_buf = gu[(gu_i + 1) % 4]
            gu_i += 2
            for k in range(KC):
                c.pe("matmul", [wg_b[j // 2], hTb], [g_buf], g_ap, lhsT=win[:, k, j * 128:(j + 1) * 128],
                     rhs=hTs[:, k, :], start=(k == 0), stop=(k == KC - 1))
            for k in range(KC):
                c.pe("matmul", [wu_b[j // 2], hTb], [u_buf], u_ap, lhsT=win[:, k, DFF + j * 128:DFF + (j + 1) * 128],
                     rhs=hTs[:, k, :], start=(k == 0), stop=(k == KC - 1))
            s_ap, s_buf = sg[j % 2]
            c.act("activation", [g_buf], [s_buf], out=s_ap, in_=g_ap, func=AF.Silu)
            c.dve("tensor_tensor", [s_buf, u_buf], [aT_b[j]], out=aT[:, j, :], in0=s_ap, in1=u_ap, op=ALU.mult)
        if su + 1 < nsup:
            do_front(su + 1)
        for ts in range(4):
            for nb in range(2):
                o_ap, o_buf = c.psum_o[c.po_i % len(c.psum_o)]
                c.po_i += 1
                for j in range(FC):
                    c.pe("matmul", [aT_b[j], wout_b[j // 11]], [o_buf], o_ap,
                         lhsT=aT[:, j, ts * 128:(ts + 1) * 128], rhs=wout[:, j, nb * 512:(nb + 1) * 512],
                         start=(j == 0), stop=(j == FC - 1))
                cs = slice(nb * 512, (nb + 1) * 512)
                emit_resid(c, o_ap, o_buf, xin_v[su * 4 + ts][:, cs], xin_b[su * 4 + ts][nb],
                           xout_v[su * 4 + ts][:, cs], xout_b[su * 4 + ts][nb], c.G[0][:, cs])
    ar.release()


DEBUG = False


def dbg(c, name, ap, buf):
    if not DEBUG:
        return
    shp = list(ap.shape)
    d = c.nc.dram_tensor("dbg_" + name, shp, ap.dtype, kind="ExternalOutput").ap()
    c.sp("dma_start", [buf], [], out=d, in_=ap)


def setup_common(c, nc, arena_handle, arena_bytes, psum):
    c.nc = nc
    c.arena = Arena(arena_handle, arena_bytes)
    ar = c.arena
    c.psum = psum
    c.ident = (ar.alloc(128, BF16), Buf("ident"))
    c.A = (ar.alloc(D, F32), Buf("A"))
    c.B = (ar.alloc(D, F32), Buf("B"))
    c.G = (ar.alloc(D, F32), Buf("G"))
    c.xt = [(ar.alloc(D, F32), Buf(f"xt{i}")) for i in range(2)]
    c.xslot = 0
    c.xr = [(ar.alloc(512, F32), Buf(f"xr{i}")) for i in range(2)]
    c.ot = [(ar.alloc(512, F32), Buf(f"ot{i}")) for i in range(2)]
    c.xr_i = 0
    c.hb = [(ar.alloc(D, BF16), Buf(f"hb{i}")) for i in range(2)]
    c.ss = [(ar.alloc(8, F32), Buf(f"ss{i}")) for i in range(6)]
    c.pb = [(psum[:, b * 512:(b + 1) * 512], Buf(f"pb{b}")) for b in range(8)]
    c.psum_gu = c.pb[0:4]
    c.psum_o = c.pb[4:7]
    c.po_i = 0
    c.ptr = [(c.pb[7][0].bitcast(BF16), c.pb[7][1])]
    c.ptr_i = 0
    c.sp("dma_start", [], [c.ident[1]], out=c.ident[0], in_=c.ident_dram)
    c.eps_rms = (ar.alloc(8, F32)[:, 0:1], Buf("eps"))
    c.dve("memset", [], [c.eps_rms[1]], c.eps_rms[0], RMS_EPS)


ARENA_BYTES = 212800

R_HEADS, R_DK, R_DV = 4, 256, 512
R_QK, R_VTOT = 1024, 2048
R_IN = 6144
M_HEADS, M_NOPE, M_ROPE, M_V = 8, 128, 64, 128
Q_LORA, KV_LORA = 384, 256
M_DOWN = 704
M_SCALE = (M_NOPE + M_ROPE) ** -0.5
GN_EPS = 1e-5
TWO_PI = 2.0 * np.pi
CW1 = 6.28125
CW2 = float(TWO_PI - 6.28125)
MAGIC = 12582912.0


CTW = 680


def host_consts():
    import ml_dtypes
    cst = {}
    cst["ident"] = np.eye(128, dtype=ml_dtypes.bfloat16)
    inv_r = np.power(np.float32(10000.0), -np.arange(0, R_DK, 2, dtype=np.float32) / np.float32(R_DK)).astype(np.float32)
    inv_m = np.power(np.float32(10000.0), -np.arange(0, M_ROPE, 2, dtype=np.float32) / np.float32(M_ROPE)).astype(np.float32)
    t = np.arange(128, dtype=np.float32)
    sI, tI = np.meshgrid(t, t, indexing="ij")
    tab = np.zeros((128, CTW), np.float32)
    tab[:, 552:680] = np.eye(128, dtype=np.float32)
    tab[:, 0:128] = np.maximum(tI - sI, 0)
    tab[:, 128:256] = (tI >= sI).astype(np.float32) / 16.0
    tab[:, 256:384] = np.maximum(sI - tI, 0)
    tab[:, 384:512] = (sI > tI).astype(np.float32) / 16.0
    tab[:, 512] = t + 1.0
    tab[:, 513] = 128.0 - t
    tab[:, 514] = 127.0 - t
    tab[:, 515] = t
    tab[:, 516] = 128.0
    tab[:, 517] = inv_r
    tab[:, 520:552] = inv_m[None, :]
    cst["ctab"] = tab
    return cst


def emit_sincos(c, ang, n, cos_out, sin_out, bufs):
    ang_b, cos_b, sin_b = bufs
    ar = c.arena
    ar.mark()
    k_ap, k_b = ar.alloc(n, F32), Buf("k")
    c.dve("tensor_scalar", [ang_b], [k_b], out=k_ap, in0=ang, scalar1=float(1.0 / TWO_PI), scalar2=MAGIC,
          op0=ALU.mult, op1=ALU.add)
    c.dve("tensor_scalar", [k_b], [k_b], out=k_ap, in0=k_ap, scalar1=MAGIC, scalar2=None, op0=ALU.subtract)
    c.dve("scalar_tensor_tensor", [k_b, ang_b], [ang_b], out=ang, in0=k_ap, scalar=-CW1, in1=ang,
          op0=ALU.mult, op1=ALU.add)
    c.dve("scalar_tensor_tensor", [k_b, ang_b], [ang_b], out=ang, in0=k_ap, scalar=-CW2, in1=ang,
          op0=ALU.mult, op1=ALU.add)
    c.dve("tensor_scalar", [ang_b], [ang_b], out=ang, in0=ang, scalar1=float(-np.pi), scalar2=float(np.pi),
          op0=ALU.max, op1=ALU.min)
    c.act("activation", [ang_b], [sin_b], out=sin_out, in_=ang, func=AF.Sin)
    c.act("activation", [ang_b], [k_b], out=k_ap, in_=ang, func=AF.Sin, scale=0.5)
    c.dve("tensor_tensor", [k_b], [k_b], out=k_ap, in0=k_ap, in1=k_ap, op=ALU.mult)
    c.dve("tensor_scalar", [k_b], [cos_b], out=cos_out, in0=k_ap, scalar1=-2.0, scalar2=1.0,
          op0=ALU.mult, op1=ALU.add)
    ar.release()


def phase_setup_tables(c, pos, S):
    Tracker.phase = "setup_tables"
    ar = c.arena
    ar.mark()
    ct = fetch_ctab(c)
    nt = S // 128
    pi_ap, pi_b = ar.alloc(S, I32), Buf("posi")
    c.sp("dma_start", [], [pi_b], out=pi_ap, in_=pos.partition_broadcast(128))
    ang, ang_b = ar.alloc(S, F32), Buf("ang")
    c.dve("tensor_copy", [pi_b], [ang_b], out=ang, in_=pi_ap)
    c.dve("tensor_scalar", [ang_b, ct[1]], [ang_b], out=ang, in0=ang, scalar1=ct[0][:, 517:518], scalar2=None,
          op0=ALU.mult)
    cs, cs_b = ar.alloc(S, F32), Buf("cos")
    sn, sn_b = ar.alloc(S, F32), Buf("sin")
    emit_sincos(c, ang, S, cs, sn, (ang_b, cs_b, sn_b))
    c.sp("dma_start", [cs_b], [c.CR_b], out=c.CR, in_=cs)
    c.sp("dma_start", [sn_b], [c.SR_b], out=c.SR, in_=sn)
    pf_ap, pf_b = ar.alloc(nt, F32), Buf("posf")
    pb2, pb2_b = ar.alloc(S, F32), Buf("posf_row")
    c.dve("tensor_copy", [pi_b], [pb2_b], out=pb2, in_=pi_ap)
    junk, junk_b = ar.alloc(128, F32), Buf("junk")
    for t in range(nt):
        c.dve("scalar_tensor_tensor", [pb2_b, ct[1]], [junk_b, pf_b], out=junk, in0=pb2[:, t * 128:(t + 1) * 128],
              scalar=1.0, in1=ct[0][:, 552:680], op0=ALU.mult, op1=ALU.mult, accum_out=pf_ap[:, t:t + 1])
    am, am_b = ar.alloc(nt * 32, F32), Buf("am")
    for t in range(nt):
        c.dve("tensor_scalar", [pf_b, ct[1]], [am_b], out=am[:, t * 32:(t + 1) * 32], in0=ct[0][:, 520:552],
              scalar1=pf_ap[:, t:t + 1], scalar2=None, op0=ALU.mult)
    cm, cm_b = ar.alloc(nt * 32, F32), Buf("cm")
    sm, sm_b = ar.alloc(nt * 32, F32), Buf("sm")
    emit_sincos(c, am, nt * 32, cm, sm, (am_b, cm_b, sm_b))
    c.sp("dma_start", [cm_b], [c.CM_b], out=c.CM.rearrange("(t p) j -> p t j", p=128),
         in_=cm.rearrange("p (t j) -> p t j", j=32))
    c.sp("dma_start", [sm_b], [c.SM_b], out=c.SM.rearrange("(t p) j -> p t j", p=128),
         in_=sm.rearrange("p (t j) -> p t j", j=32))
    ar.release()


def phase_mod(c, cvec, mod_w, mod_b, norm_g, depth):
    Tracker.phase = "mod"
    ar = c.arena
    ar.mark()
    cf, cf_b = ar.alloc(128, F32), Buf("cf")
    c.sp("dma_start", [], [cf_b], out=cf[0:KC, :], in_=cvec.rearrange("(k p) -> k p", p=128))
    cab, cab_b = ar.alloc(128, BF16), Buf("cab")
    c.act("activation", [cf_b], [cab_b], out=cab[0:KC, :], in_=cf[0:KC, :], func=AF.Silu)
    pt_ap, pt_buf = c.ptr[0]
    c.pe("transpose", [cab_b, c.ident[1]], [pt_buf], out=pt_ap[:, 0:KC], in_=cab[0:KC, :],
         identity=c.ident[0][0:KC, 0:KC])
    ca, ca_b = ar.alloc(KC, BF16), Buf("ca")
    c.act("activation", [pt_buf], [ca_b], out=ca, in_=pt_ap[:, 0:KC], func=AF.Copy)
    NB = 512
    nblk = 9 * D // NB
    wb = [(ar.alloc(KC * NB, BF16).rearrange("p (k n) -> p k n", k=KC), Buf(f"mw{i}")) for i in range(3)]
    row, row_b = ar.alloc(9 * D, F32), Buf("modrow")
    mb_ap, mb_b = ar.alloc(9 * D, F32), Buf("modb")
    ng_ap, ng_b = ar.alloc(3 * D, F32), Buf("normg")
    orow, orow_b = ar.alloc(9 * D, F32), Buf("orow")
    bi = 0
    for i in range(depth):
        c.sp("dma_start", [], [mb_b], out=mb_ap[0:1, :], in_=mod_b[i:i + 1, :])
        c.sp("dma_start", [], [ng_b], out=ng_ap[0:1, :], in_=norm_g[i:i + 1].rearrange("o s d -> o (s d)"))
        mw_v = mod_w[i].rearrange("(k p) n -> p k n", p=128)
        for b in range(nblk):
            w_ap, w_b = wb[bi % 3]
            p_ap, p_b = c.pb[bi % 2]
            bi += 1
            c.pool("dma_start", [], [w_b], out=w_ap, in_=mw_v[:, :, b * NB:(b + 1) * NB])
            for k in range(KC):
                c.pe("matmul", [ca_b, w_b], [p_b], p_ap[0:1, :], lhsT=ca[:, k:k + 1], rhs=w_ap[:, k, :],
                     start=(k == 0), stop=(k == KC - 1))
            c.dve("tensor_tensor", [p_b, mb_b], [row_b], out=row[0:1, b * NB:(b + 1) * NB], in0=p_ap[0:1, :],
                  in1=mb_ap[0:1, b * NB:(b + 1) * NB], op=ALU.add)
        for sl in range(3):
            sh = row[0:1, (3 * sl) * D:(3 * sl + 1) * D]
            sc = row[0:1, (3 * sl + 1) * D:(3 * sl + 2) * D]
            gt = row[0:1, (3 * sl + 2) * D:(3 * sl + 3) * D]
            oa = orow[0:1, (3 * sl) * D:(3 * sl + 1) * D]
            ob = orow[0:1, (3 * sl + 1) * D:(3 * sl + 2) * D]
            og = orow[0:1, (3 * sl + 2) * D:(3 * sl + 3) * D]
            c.dve("scalar_tensor_tensor", [row_b, ng_b], [orow_b], out=oa, in0=sc, scalar=1.0,
                  in1=ng_ap[0:1, sl * D:(sl + 1) * D], op0=ALU.add, op1=ALU.mult)
            c.dve("tensor_copy", [row_b], [orow_b], out=ob, in_=sh)
            c.dve("tensor_scalar", [row_b], [orow_b], out=og, in0=gt, scalar1=(1.0 if sl == 1 else 0.5),
                  scalar2=None, op0=ALU.mult)
        c.sp("dma_start", [orow_b], [c.modrows_b[i]], out=c.modrows[i:i + 1].rearrange("o s r d -> o (s r d)"),
             in_=orow[0:1, :])
    ar.release()


def ret_tables(c, dec_f, dec_b):
    ar = c.arena
    ct, ct_b = fetch_ctab(c)
    t = Ctx()
    raw, raw_b = ar.alloc(8, F32), Buf("decraw")
    c.sp("dma_start", [], [raw_b], out=raw[:, 0:4], in_=dec_f.partition_broadcast(128))
    c.sp("dma_start", [], [raw_b], out=raw[:, 4:8], in_=dec_b.partition_broadcast(128))
    lg, lg_b = ar.alloc(8, F32), Buf("lg")
    c.act("activation", [raw_b], [lg_b], out=lg, in_=raw, func=AF.Exp, scale=-1.0)
    c.act("activation", [lg_b], [lg_b], out=lg, in_=lg, func=AF.Ln, bias=1.0)
    c.dve("tensor_scalar", [lg_b], [lg_b], out=lg, in0=lg, scalar1=-1.0, scalar2=None, op0=ALU.mult)
    cols, cols_b = ar.alloc(24, F32), Buf("deccols")
    src = [(512, 0), (513, 4), (514, 0), (515, 4), (516, 0), (516, 4)]
    for kind, (ccol, lgo) in enumerate(src):
        for h in range(R_HEADS):
            c.act("activation", [lg_b, ct_b], [cols_b], out=cols[:, kind * 4 + h:kind * 4 + h + 1],
                  in_=ct[:, ccol:ccol + 1], func=AF.Exp, scale=lg[:, lgo + h:lgo + h + 1])
    t.cols, t.cols_b = cols, cols_b
    DT, DT_b = ar.alloc(4 * 128, F32), Buf("DT")
    e1, e1_b = ar.alloc(128, F32), Buf("e1")
    e2, e2_b = ar.alloc(128, F32), Buf("e2")
    for h in range(R_HEADS):
        c.act("activation", [lg_b, ct_b], [e1_b], out=e1, in_=ct[:, 0:128], func=AF.Exp, scale=lg[:, h:h + 1])
        c.dve("tensor_tensor", [e1_b, ct_b], [e1_b], out=e1, in0=e1, in1=ct[:, 128:256], op=ALU.mult)
        c.act("activation", [lg_b, ct_b], [e2_b], out=e2, in_=ct[:, 256:384], func=AF.Exp, scale=lg[:, 4 + h:5 + h])
        c.dve("tensor_tensor", [e2_b, ct_b], [e2_b], out=e2, in0=e2, in1=ct[:, 384:512], op=ALU.mult)
        c.dve("tensor_tensor", [e1_b, e2_b], [DT_b], out=DT[:, h * 128:(h + 1) * 128], in0=e1, in1=e2, op=ALU.add)
    t.DT, t.DT_b = DT, DT_b
    return t


def phase_ret_in(c, xin, w_in, rows3, S, dt):
    Tracker.phase = "ret_in"
    x_in, xin_b = xin
    ar = c.arena
    ar.mark()
    NKC = R_IN
    win = ar.alloc(KC * NKC, BF16).rearrange("p (k n) -> p k n", k=KC)
    win_b = [Buf(f"rwin{k}") for k in range(KC)]
    w_in_v = w_in.rearrange("(k p) n -> p k n", p=128)
    for k in range(KC):
        c.pool("dma_start", [], [win_b[k]], out=win[:, k, :], in_=w_in_v[:, k, :])
    load_bcast_rows(c, rows3)
    NT = 512
    nsup = S // NT
    hT = [ar.alloc(KC * NT, BF16).rearrange("p (k n) -> p k n", k=KC) for _ in range(2)]
    hT_b = [Buf("hT0"), Buf("hT1")]
    cs = [(ar.alloc(NT, F32), Buf(f"cs{i}")) for i in range(2)]
    sn = [(ar.alloc(NT, F32), Buf(f"sn{i}")) for i in range(2)]
    tmp = [(ar.alloc(NT, F32), Buf(f"rt{i}")) for i in range(4)]
    qst = [(ar.alloc(2 * NT, BF16).rearrange("p (a n) -> p a n", a=2), Buf(f"qst{i}")) for i in range(2)]
    KDf, KDf_b = ar.alloc(1024, F32), Buf("KDf")
    KDb, KDb_b = ar.alloc(1024, F32), Buf("KDb")
    for h in range(R_HEADS):
        for (KD, KD_b, kind) in ((KDf, KDf_b, 2), (KDb, KDb_b, 3)):
            c.dve("memset", [], [KD_b], KD[:, h * 256:(h + 1) * 256], 1.0 / 16.0)
            c.dve("tensor_scalar", [KD_b, dt.cols_b], [KD_b], out=KD[:, h * 256:(h + 1) * 256],
                  in0=KD[:, h * 256:(h + 1) * 256], scalar1=dt.cols[:, kind * 4 + h:kind * 4 + h + 1],
                  scalar2=None, op0=ALU.mult)
    kst = [(ar.alloc(1024, BF16), Buf(f"kst{i}")) for i in range(2)]
    vst = [(ar.alloc(512, BF16), Buf(f"vst{i}")) for i in range(2)]
    gst = [(ar.alloc(512, F32), Buf(f"gst{i}")) for i in range(2)]
    ktb = [(ar.alloc(8 * NT, BF16).rearrange("p (a n) -> p a n", a=8), Buf("ktb"))]
    xin_v = x_in.rearrange("(n p) d -> n p d", p=128)
    pbi = 0
    qi = 0
    vi = 0

    def ri_front(su_):
        tok_ = slice(su_ * NT, (su_ + 1) * NT)
        c.sp("dma_start", [c.CR_b], [cs[su_ % 2][1]], out=cs[su_ % 2][0], in_=c.CR[:, tok_])
        c.sp("dma_start", [c.SR_b], [sn[su_ % 2][1]], out=sn[su_ % 2][0], in_=c.SR[:, tok_])
        for ts_ in range(4):
            emit_front(c, xin_v[su_ * 4 + ts_], xin_b[su_ * 4 + ts_], hT[su_ % 2], hT_b[su_ % 2], ts_ * 128)

    for su in range(nsup):
        hTs, hTb = hT[su % 2], hT_b[su % 2]
        tok = slice(su * NT, (su + 1) * NT)
        c_ap, c_b = cs[su % 2]
        s_ap, s_b = sn[su % 2]
        if su == 0:
            ri_front(0)
        kt_ap, kt_b = ktb[0]
        if su == 0:
            dbg(c, "hT", hTs, hTb)
            dbg(c, "win0", win[:, 0, :], win_b[0])
            dbg(c, "win7", win[:, 7, :], win_b[7])
        for which in range(2):
            for h in range(R_HEADS):
                base = which * R_QK + h * R_DK
                p1, p1_b = c.pb[pbi % 6]
                p2, p2_b = c.pb[(pbi + 1) % 6]
                pbi += 2
                for half, (pp, pp_b) in enumerate(((p1, p1_b), (p2, p2_b))):
                    for k in range(KC):
                        c.pe("matmul", [win_b[k], hTb], [pp_b], pp,
                             lhsT=win[:, k, base + half * 128:base + (half + 1) * 128], rhs=hTs[:, k, :],
                             start=(k == 0), stop=(k == KC - 1))
                t1, t1b = tmp[0]
                t2, t2b = tmp[1]
                t3, t3b = tmp[2]
                t4, t4b = tmp[3]
                c.dve("tensor_tensor", [p1_b, c_b], [t1b], out=t1, in0=p1, in1=c_ap, op=ALU.mult)
                c.dve("tensor_tensor", [p2_b, s_b], [t2b], out=t2, in0=p2, in1=s_ap, op=ALU.mult)
                c.dve("tensor_tensor", [p1_b, s_b], [t3b], out=t3, in0=p1, in1=s_ap, op=ALU.mult)
                c.dve("tensor_tensor", [p2_b, c_b], [t4b], out=t4, in0=p2, in1=c_ap, op=ALU.mult)
                if which == 0:
                    o_ap, o_b = qst[qi % 2]
                    qi += 1
                    o1, o2 = o_ap[:, 0, :], o_ap[:, 1, :]
                else:
                    o_ap, o_b = kt_ap, kt_b
                    o1, o2 = kt_ap[:, 2 * h, :], kt_ap[:, 2 * h + 1, :]
                c.dve("tensor_tensor", [t1b, t2b], [o_b], out=o1, in0=t1, in1=t2, op=ALU.subtract)
                c.dve("tensor_tensor", [t3b, t4b], [o_b], out=o2, in0=t3, in1=t4, op=ALU.add)
                if which == 0:
                    dst = c.QT[h * 256:(h + 1) * 256, tok].rearrange("(a p) n -> p a n", p=128)
                    c.sp("dma_start", [o_b], [c.QT_b[su][h]], out=dst, in_=o_ap)
            if which == 1:
                dst = c.KT[:, tok].rearrange("(a p) n -> p a n", p=128)
                c.sp("dma_start", [kt_b], [c.KT_b[su]], out=dst, in_=kt_ap)
        for ts in range(4):
            pt_ap, pt_buf = c.ptr[0]
            for a in range(8):
                c.pe("transpose", [kt_b, c.ident[1]], [pt_buf], out=pt_ap[:, a * 128:(a + 1) * 128],
                     in_=kt_ap[:, a, ts * 128:(ts + 1) * 128], identity=c.ident[0])
            for di, (KD, KD_b, dst, dst_b) in enumerate(((KDf, KDf_b, c.KF, c.KF_b), (KDb, KDb_b, c.KB, c.KB_b))):
                k_ap, k_b = kst[di]
                c.dve("tensor_tensor", [pt_buf, KD_b], [k_b], out=k_ap, in0=pt_ap, in1=KD, op=ALU.mult)
                c.sp("dma_start", [k_b], [dst_b[su * 4 + ts]], out=dst[(su * 4 + ts) * 128:(su * 4 + ts + 1) * 128, :],
                     in_=k_ap)
        if su + 1 < nsup:
            ri_front(su + 1)
        for ts in range(4):
            rows = slice((su * 4 + ts) * 128, (su * 4 + ts + 1) * 128)
            for nb in range(8):
                pp, pp_b = c.pb[pbi % 6]
                pbi += 1
                col0 = 2 * R_QK + nb * 512
                for k in range(KC):
                    c.pe("matmul", [win_b[k], hTb], [pp_b], pp, lhsT=hTs[:, k, ts * 128:(ts + 1) * 128],
                         rhs=win[:, k, col0:col0 + 512], start=(k == 0), stop=(k == KC - 1))
                if nb < 4:
                    v_ap, v_b = vst[vi % 2]
                    vi += 1
                    c.act("activation", [pp_b], [v_b], out=v_ap, in_=pp, func=AF.Copy)
                    c.sp("dma_start", [v_b], [c.V_b[su * 4 + ts][nb]], out=c.V[rows, nb * 512:(nb + 1) * 512], in_=v_ap)
                else:
                    g_ap, g_b = gst[vi % 2]
                    vi += 1
                    c.act("activation", [pp_b], [g_b], out=g_ap, in_=pp, func=AF.Silu)
                    c.sp("dma_start", [g_b], [c.Gs_b[su * 4 + ts][nb - 4]], out=c.Gs[rows, (nb - 4) * 512:(nb - 3) * 512],
                         in_=g_ap)
    ar.release()


def phase_ret_bwd(c, S, dt):
    Tracker.phase = "ret_bwd"
    ar = c.arena
    ar.mark()
    nch = S // 128
    St = ar.alloc(R_HEADS * 1024, F32).rearrange("p (h n) -> p h n", h=R_HEADS)
    St_b = [Buf(f"St{h}") for h in range(R_HEADS)]
    Sb = ar.alloc(R_HEADS * 1024, BF16).rearrange("p (h n) -> p h n", h=R_HEADS)
    Sb_b = [Buf(f"Sb{h}") for h in range(R_HEADS)]
    for h in range(R_HEADS):
        c.dve("memset", [], [St_b[h]], St[:, h], 0.0)
        c.dve("memset", [], [Sb_b[h]], Sb[:, h], 0.0)
    kb = [(ar.alloc(1024, BF16), Buf(f"kbt{i}")) for i in range(2)]
    vt = [(ar.alloc(2048, BF16), Buf(f"vt{i}")) for i in range(2)]
    pbi = 0
    for i, ch in enumerate(range(nch - 1, -1, -1)):
        k_ap, k_b = kb[i % 2]
        v_ap, v_b = vt[i % 2]
        rows = slice(ch * 128, (ch + 1) * 128)
        c.sp("dma_start", [c.KB_b[ch]], [k_b], out=k_ap, in_=c.KB[rows, :])
        c.sp("dma_start", c.V_b[ch], [v_b], out=v_ap, in_=c.V[rows, :])
        for h in range(R_HEADS):
            c.sp("dma_start", [Sb_b[h]], [c.SB_b[ch][h]], out=c.SB[ch, h], in_=Sb[:, h])
            for a in range(2):
                pp, pp_b = c.pb[pbi % 8]
                pbi += 1
                c.pe("matmul", [k_b, v_b], [pp_b], pp, lhsT=k_ap[:, h * 256 + a * 128:h * 256 + (a + 1) * 128],
                     rhs=v_ap[:, h * 512:(h + 1) * 512], start=True, stop=True)
                c.dve("scalar_tensor_tensor", [St_b[h], dt.cols_b, pp_b], [St_b[h]], out=St[:, h, a * 512:(a + 1) * 512],
                      in0=St[:, h, a * 512:(a + 1) * 512], scalar=dt.cols[:, 5 * 4 + h:5 * 4 + h + 1], in1=pp,
                      op0=ALU.mult, op1=ALU.add)
            c.act("activation", [St_b[h]], [Sb_b[h]], out=Sb[:, h], in_=St[:, h], func=AF.Copy)
    ar.release()


def phase_ret_fwd(c, xio, w_out, gn_g, gn_b, S, dt):
    Tracker.phase = "ret_fwd"
    x_io, xio_b = xio
    ar = c.arena
    ar.mark()
    nch = S // 128
    wo = ar.alloc(16 * D, BF16).rearrange("p (k n) -> p k n", k=16)
    wo_b = Buf("rwo")
    c.pool("dma_start", [], [wo_b], out=wo, in_=w_out.rearrange("(k p) n -> p k n", p=128))
    gg, gg_b = ar.alloc(R_VTOT, F32), Buf("gng")
    gb, gb_b = ar.alloc(R_VTOT, F32), Buf("gnb")
    c.sp("dma_start", [], [gg_b], out=gg, in_=gn_g.partition_broadcast(128))
    c.sp("dma_start", [], [gb_b], out=gb, in_=gn_b.partition_broadcast(128))
    eps, eps_b = ar.alloc(8, F32)[:, 0:1], Buf("gneps")
    c.dve("memset", [], [eps_b], eps, GN_EPS)
    St = ar.alloc(R_HEADS * 1024, F32).rearrange("p (h n) -> p h n", h=R_HEADS)
    St_b = [Buf(f"Sf{h}") for h in range(R_HEADS)]
    Sb = ar.alloc(R_HEADS * 1024, BF16).rearrange("p (h n) -> p h n", h=R_HEADS)
    Sb_b = [Buf(f"Sfb{h}") for h in range(R_HEADS)]
    for h in range(R_HEADS):
        c.dve("memset", [], [St_b[h]], St[:, h], 0.0)
        c.dve("memset", [], [Sb_b[h]], Sb[:, h], 0.0)
    qt = [(ar.alloc(1024, BF16).rearrange("p (a n) -> p a n", a=8), Buf(f"qt{i}")) for i in range(2)]
    kt = [(ar.alloc(1024, BF16).rearrange("p (a n) -> p a n", a=8), Buf(f"kt{i}")) for i in range(2)]
    kf = [(ar.alloc(1024, BF16), Buf(f"kf{i}")) for i in range(2)]
    vt = [(ar.alloc(2048, BF16), Buf(f"vt{i}")) for i in range(2)]
    gt = [(ar.alloc(2048, F32), Buf(f"gt{i}")) for i in range(2)]
    sbt = [(ar.alloc(R_HEADS * 1024, BF16).rearrange("p (h n) -> p h n", h=R_HEADS), Buf(f"sbt{i}"))
           for i in range(2)]
    y, y_b = ar.alloc(R_VTOT, F32), [Buf(f"y{h}") for h in range(R_HEADS)]
    z, z_b = ar.alloc(R_VTOT, BF16), Buf("z")
    zT = ar.alloc(16 * 128, BF16).rearrange("p (k n) -> p k n", k=16)
    zT_b = Buf("zT")
    pT = [(ar.alloc(128, BF16), Buf(f"pT{i}")) for i in range(R_HEADS)]
    st6, st6_b = ar.alloc(4 * 16, F32), [Buf(f"st6{h}") for h in range(R_HEADS)]
    ps_bufs = [Buf(f"psS{h}") for h in range(R_HEADS)]
    xv = x_io.rearrange("(n p) d -> n p d", p=128)
    def issue_loads(ch_):
        q_ap_, q_b_ = qt[ch_ % 2]
        k_ap_, k_b_ = kt[ch_ % 2]
        f_ap_, f_b_ = kf[ch_ % 2]
        v_ap_, v_b_ = vt[ch_ % 2]
        g_ap_, g_b_ = gt[ch_ % 2]
        s_ap_, s_b_ = sbt[ch_ % 2]
        rows_ = slice(ch_ * 128, (ch_ + 1) * 128)
        su_ = ch_ // 4
        c.sp("dma_start", c.QT_b[su_], [q_b_], out=q_ap_, in_=c.QT[:, rows_].rearrange("(a p) n -> p a n", p=128))
        c.sp("dma_start", [c.KT_b[su_]], [k_b_], out=k_ap_, in_=c.KT[:, rows_].rearrange("(a p) n -> p a n", p=128))
        c.sp("dma_start", [c.KF_b[ch_]], [f_b_], out=f_ap_, in_=c.KF[rows_, :])
        c.sp("dma_start", c.V_b[ch_], [v_b_], out=v_ap_, in_=c.V[rows_, :])
        c.sp("dma_start", c.Gs_b[ch_], [g_b_], out=g_ap_, in_=c.Gs[rows_, :])
        c.sp("dma_start", c.SB_b[ch_], [s_b_], out=s_ap_, in_=c.SB[ch_].rearrange("h p n -> p h n"))

    issue_loads(0)
    for ch in range(nch):
        i = ch
        q_ap, q_b = qt[i % 2]
        k_ap, k_b = kt[i % 2]
        f_ap, f_b = kf[i % 2]
        v_ap, v_b = vt[i % 2]
        g_ap, g_b = gt[i % 2]
        s_ap, s_b = sbt[i % 2]
        rows = slice(ch * 128, (ch + 1) * 128)
        su = ch // 4
        if ch + 1 < nch:
            issue_loads(ch + 1)
        for h in range(R_HEADS):
            ps_ap, ps_b = c.pb[0]
            st_ap = ps_ap[:, h * 128:(h + 1) * 128]
            for a in range(2):
                c.pe("matmul", [k_b, q_b], [ps_b], st_ap, lhsT=k_ap[:, 2 * h + a, :], rhs=q_ap[:, 2 * h + a, :],
                     start=(a == 0), stop=(a == 1))
            p_ap, p_b = pT[h]
            c.dve("tensor_tensor", [ps_b, dt.DT_b], [p_b], out=p_ap, in0=st_ap,
                  in1=dt.DT[:, h * 128:(h + 1) * 128], op=ALU.mult)
        for h in range(R_HEADS):
            p_ap, p_b = pT[h]
            y0, y0_b = c.pb[1]
            y1, y1_b = c.pb[2]
            y2, y2_b = c.pb[3]
            vh = v_ap[:, h * 512:(h + 1) * 512]
            c.pe("matmul", [p_b, v_b], [y0_b], y0, lhsT=p_ap, rhs=vh, start=True, stop=True)
            for a in range(2):
                c.pe("matmul", [q_b, Sb_b[h]], [y1_b], y1, lhsT=q_ap[:, 2 * h + a, :], rhs=Sb[:, h, a * 512:(a + 1) * 512],
                     start=(a == 0), stop=(a == 1))
            for a in range(2):
                c.pe("matmul", [q_b, s_b], [y2_b], y2, lhsT=q_ap[:, 2 * h + a, :], rhs=s_ap[:, h, a * 512:(a + 1) * 512],
                     start=(a == 0), stop=(a == 1))
            yh = y[:, h * 512:(h + 1) * 512]
            c.act("activation", [y0_b], [y_b[h]], out=yh, in_=y0, func=AF.Copy)
            c.dve("scalar_tensor_tensor", [y1_b, dt.cols_b, y_b[h]], [y_b[h]], out=yh, in0=y1,
                  scalar=dt.cols[:, 0 * 4 + h:0 * 4 + h + 1], in1=yh, op0=ALU.mult, op1=ALU.add)
            c.dve("scalar_tensor_tensor", [y2_b, dt.cols_b, y_b[h]], [y_b[h]], out=yh, in0=y2,
                  scalar=dt.cols[:, 1 * 4 + h:1 * 4 + h + 1], in1=yh, op0=ALU.mult, op1=ALU.add)
            for a in range(2):
                u, u_b = c.pb[4 + a]
                c.pe("matmul", [f_b, v_b], [u_b], u, lhsT=f_ap[:, h * 256 + a * 128:h * 256 + (a + 1) * 128], rhs=vh,
                     start=True, stop=True)
                c.dve("scalar_tensor_tensor", [St_b[h], dt.cols_b, u_b], [St_b[h]], out=St[:, h, a * 512:(a + 1) * 512],
                      in0=St[:, h, a * 512:(a + 1) * 512], scalar=dt.cols[:, 4 * 4 + h:4 * 4 + h + 1], in1=u,
                      op0=ALU.mult, op1=ALU.add)
            c.act("activation", [St_b[h]], [Sb_b[h]], out=Sb[:, h], in_=St[:, h], func=AF.Copy)
            s6 = st6[:, h * 16:h * 16 + 16]
            c.dve("bn_stats", [y_b[h]], [st6_b[h]], out=s6[:, 0:6], in_=yh)
            c.dve("bn_aggr", [st6_b[h]], [st6_b[h]], out=s6[:, 6:8], in_=s6[:, 0:6])
            c.act("activation", [st6_b[h], eps_b], [st6_b[h]], out=s6[:, 8:9], in_=s6[:, 7:8], func=AF.Sqrt,
                  bias=eps, scale=1.0)
            c.dve("reciprocal", [st6_b[h]], [st6_b[h]], out=s6[:, 9:10], in_=s6[:, 8:9])
            c.dve("tensor_tensor", [st6_b[h]], [st6_b[h]], out=s6[:, 11:12], in0=s6[:, 6:7], in1=s6[:, 9:10], op=ALU.mult)
            c.dve("tensor_scalar", [st6_b[h]], [st6_b[h]], out=s6[:, 10:11], in0=s6[:, 11:12], scalar1=-1.0,
                  scalar2=None, op0=ALU.mult)
            c.act("activation", [y_b[h], st6_b[h]], [y_b[h]], out=yh, in_=yh, func=AF.Identity,
                  scale=s6[:, 9:10], bias=s6[:, 10:11])
            hs = slice(h * 512, (h + 1) * 512)
            c.dve("tensor_tensor", [y_b[h], gg_b], [y_b[h]], out=yh, in0=yh, in1=gg[:, hs], op=ALU.mult)
            c.dve("tensor_tensor", [y_b[h], gb_b], [y_b[h]], out=yh, in0=yh, in1=gb[:, hs], op=ALU.add)
            c.dve("tensor_tensor", [y_b[h], g_b], [z_b], out=z[:, hs], in0=yh, in1=g_ap[:, hs], op=ALU.mult)
        pt_ap, pt_buf = c.ptr[0]
        for half in range(2):
            for a in range(8):
                kc = half * 8 + a
                c.pe("transpose", [z_b, c.ident[1]], [pt_buf], out=pt_ap[:, a * 128:(a + 1) * 128],
                     in_=z[:, kc * 128:(kc + 1) * 128], identity=c.ident[0])
            c.act("activation", [pt_buf], [zT_b], out=zT[:, half * 8:(half + 1) * 8, :],
                  in_=pt_ap.rearrange("p (k n) -> p k n", k=8), func=AF.Copy)
        for nb in range(2):
            o_ap, o_buf = c.pb[6]
            for kc in range(16):
                c.pe("matmul", [zT_b, wo_b], [o_buf], o_ap, lhsT=zT[:, kc, :], rhs=wo[:, kc, nb * 512:(nb + 1) * 512],
                     start=(kc == 0), stop=(kc == 15))
            csl = slice(nb * 512, (nb + 1) * 512)
            emit_resid(c, o_ap, o_buf, xv[ch][:, csl], xio_b[ch][nb], xv[ch][:, csl], xio_b[ch][nb], c.G[0][:, csl])
    ar.release()


def alloc_mla_scratch(c, nc, S):
    nt = S // 128
    c.QTN = nc.dram_tensor("QTN", [M_HEADS, 128, S], BF16, kind=SCRATCH_KIND).ap()
    c.QTR = nc.dram_tensor("QTR", [M_HEADS, 64, S], BF16, kind=SCRATCH_KIND).ap()
    c.KTN = nc.dram_tensor("KTN", [M_HEADS, 128, S], BF16, kind=SCRATCH_KIND).ap()
    c.KTR = nc.dram_tensor("KTR", [64, S], BF16, kind=SCRATCH_KIND).ap()
    c.VM = nc.dram_tensor("VM", [S, M_HEADS * M_V], BF16, kind=SCRATCH_KIND).ap()
    c.QTN_b = [Buf("QTN") for _ in range(nt)]
    c.QTR_b = [Buf("QTR") for _ in range(nt)]
    c.KTN_b = [[Buf("KTN") for _ in range(M_HEADS)] for _ in range(S // 512)]
    c.KTR_b = [Buf("KTR") for _ in range(S // 512)]
    c.VM_b = [Buf("VM") for _ in range(nt)]


def phase_mla_in(c, xin, w_down, qng, kvng, w_uq, w_ukv, rows3, S):
    Tracker.phase = "mla_in"
    x_in, xin_b = xin
    ar = c.arena
    ar.mark()
    nt = S // 128
    NT = 512
    nsup = S // NT
    wdn = ar.alloc(KC * M_DOWN, BF16).rearrange("p (k n) -> p k n", k=KC)
    wdn_b = Buf("wdn")
    c.pool("dma_start", [], [wdn_b], out=wdn, in_=w_down.rearrange("(k p) n -> p k n", p=128))
    wuq = ar.alloc(3 * 1536, BF16).rearrange("p (k n) -> p k n", k=3)
    wuq_b = Buf("wuq")
    c.pool("dma_start", [], [wuq_b], out=wuq, in_=w_uq.rearrange("(k p) n -> p k n", p=128))
    wuk = ar.alloc(2 * 1024, BF16).rearrange("p (k n) -> p k n", k=2)
    wuk_b = Buf("wuk")
    wuv = ar.alloc(2 * 1024, BF16).rearrange("p (k n) -> p k n", k=2)
    wuv_b = Buf("wuv")
    for kc in range(2):
        src = w_ukv[kc * 128:(kc + 1) * 128, :].rearrange("p (h two d) -> p h two d", two=2, d=128)
        c.pool("dma_start", [], [wuk_b], out=wuk[:, kc, :].rearrange("p (h d) -> p h d", d=128), in_=src[:, :, 0, :])
        c.pool("dma_start", [], [wuv_b], out=wuv[:, kc, :].rearrange("p (h d) -> p h d", d=128), in_=src[:, :, 1, :])
    qg, qg_b = ar.alloc(Q_LORA, F32), Buf("qg")
    kg, kg_b = ar.alloc(KV_LORA, F32), Buf("kg")
    c.sp("dma_start", [], [qg_b], out=qg, in_=qng.partition_broadcast(128))
    c.sp("dma_start", [], [kg_b], out=kg, in_=kvng.partition_broadcast(128))
    cm, cm_b = ar.alloc(nt * 32, F32), Buf("cmr")
    sm, sm_b = ar.alloc(nt * 32, F32), Buf("smr")
    c.sp("dma_start", [c.CM_b], [cm_b], out=cm.rearrange("p (t j) -> p t j", j=32),
         in_=c.CM.rearrange("(t p) j -> p t j", p=128))
    c.sp("dma_start", [c.SM_b], [sm_b], out=sm.rearrange("p (t j) -> p t j", j=32),
         in_=c.SM.rearrange("(t p) j -> p t j", p=128))
    load_bcast_rows(c, rows3)
    hT = [ar.alloc(KC * NT, BF16).rearrange("p (k n) -> p k n", k=KC) for _ in range(2)]
    hT_b = [Buf("hT0"), Buf("hT1")]
    cd = [(ar.alloc(M_DOWN, F32), Buf(f"cd{i}")) for i in range(2)]
    cqn, cqn_b = ar.alloc(Q_LORA, BF16), Buf("cqn")
    ckn, ckn_b = ar.alloc(KV_LORA, BF16), Buf("ckn")
    krr, krr_b = ar.alloc(64, BF16), Buf("krr")
    st, st_b = ar.alloc(8, F32), Buf("mst")
    cqT = ar.alloc(3 * NT, BF16).rearrange("p (k n) -> p k n", k=3)
    cqT_b = Buf("cqT")
    ckT = ar.alloc(2 * NT, BF16).rearrange("p (k n) -> p k n", k=2)
    ckT_b = Buf("ckT")
    krT, krT_b = ar.alloc(NT, BF16), Buf("krT")
    kts = [(ar.alloc(NT, BF16), Buf(f"kts{i}")) for i in range(2)]
    vst = [(ar.alloc(1024, BF16), Buf(f"mvst{i}")) for i in range(2)]
    qf, qf_b = ar.alloc(1536, F32), Buf("qf")
    qb, qb_b = ar.alloc(1536, BF16), Buf("qb")
    rt = [(ar.alloc(256, F32), Buf(f"mrt{i}")) for i in range(4)]
    kt4 = [(ar.alloc(32, F32), Buf(f"kt4{i}")) for i in range(4)]
    qtn = [(ar.alloc(1024, BF16).rearrange("p (h n) -> p h n", h=8), Buf(f"qtn{i}")) for i in range(2)]
    qtr = [(ar.alloc(1024, BF16).rearrange("p (h n) -> p h n", h=8), Buf(f"qtr{i}")) for i in range(2)]
    eq, eq_b = ar.alloc(8, F32)[:, 0:1], Buf("meps")
    c.dve("memset", [], [eq_b], eq, RMS_EPS)
    xin_v = x_in.rearrange("(n p) d -> n p d", p=128)
    ki = 0
    def mi_front(su_):
        for ts_ in range(4):
            emit_front(c, xin_v[su_ * 4 + ts_], xin_b[su_ * 4 + ts_], hT[su_ % 2], hT_b[su_ % 2], ts_ * 128)

    mi_front(0)
    for su in range(nsup):
        hTs, hTb = hT[su % 2], hT_b[su % 2]
        tok = slice(su * NT, (su + 1) * NT)
        for ts in range(4):
            tile_i = su * 4 + ts
            tcs = slice(ts * 128, (ts + 1) * 128)
            d0, d0_b = c.pb[0]
            d1, d1_b = c.pb[1]
            for k in range(KC):
                c.pe("matmul", [hTb, wdn_b], [d0_b], d0, lhsT=hTs[:, k, tcs], rhs=wdn[:, k, 0:512],
                     start=(k == 0), stop=(k == KC - 1))
            for k in range(KC):
                c.pe("matmul", [hTb, wdn_b], [d1_b], d1[:, 0:192], lhsT=hTs[:, k, tcs], rhs=wdn[:, k, 512:704],
                     start=(k == 0), stop=(k == KC - 1))
            cd_ap, cd_b = cd[tile_i % 2]
            c.act("activation", [d0_b], [cd_b], out=cd_ap[:, 0:512], in_=d0, func=AF.Copy)
            c.act("activation", [d1_b], [cd_b], out=cd_ap[:, 512:704], in_=d1[:, 0:192], func=AF.Copy)
            c.dve("scalar_tensor_tensor", [cd_b], [cqn_b, st_b], out=cqn, in0=cd_ap[:, 0:384], scalar=1.0,
                  in1=cd_ap[:, 0:384], op0=ALU.mult, op1=ALU.mult, accum_out=st[:, 0:1])
            c.dve("scalar_tensor_tensor", [cd_b], [ckn_b, st_b], out=ckn, in0=cd_ap[:, 384:640], scalar=1.0,
                  in1=cd_ap[:, 384:640], op0=ALU.mult, op1=ALU.mult, accum_out=st[:, 1:2])
            c.act("activation", [st_b, eq_b], [st_b], out=st[:, 2:3], in_=st[:, 0:1], func=AF.Sqrt,
                  scale=1.0 / Q_LORA, bias=eq)
            c.act("activation", [st_b, eq_b], [st_b], out=st[:, 3:4], in_=st[:, 1:2], func=AF.Sqrt,
                  scale=1.0 / KV_LORA, bias=eq)
            c.dve("reciprocal", [st_b], [st_b], out=st[:, 4:6], in_=st[:, 2:4])
            c.dve("scalar_tensor_tensor", [cd_b, st_b, qg_b], [cqn_b], out=cqn, in0=cd_ap[:, 0:384],
                  scalar=st[:, 4:5], in1=qg, op0=ALU.mult, op1=ALU.mult)
            c.dve("scalar_tensor_tensor", [cd_b, st_b, kg_b], [ckn_b], out=ckn, in0=cd_ap[:, 384:640],
                  scalar=st[:, 5:6], in1=kg, op0=ALU.mult, op1=ALU.mult)
            cs_t = cm[:, tile_i * 32:(tile_i + 1) * 32]
            sn_t = sm[:, tile_i * 32:(tile_i + 1) * 32]
            x1, x2 = cd_ap[:, 640:672], cd_ap[:, 672:704]
            (a1, a1b), (a2, a2b), (a3, a3b), (a4, a4b) = kt4
            c.dve("tensor_tensor", [cd_b, cm_b], [a1b], out=a1, in0=x1, in1=cs_t, op=ALU.mult)
            c.dve("tensor_tensor", [cd_b, sm_b], [a2b], out=a2, in0=x2, in1=sn_t, op=ALU.mult)
            c.dve("tensor_tensor", [cd_b, sm_b], [a3b], out=a3, in0=x1, in1=sn_t, op=ALU.mult)
            c.dve("tensor_tensor", [cd_b, cm_b], [a4b], out=a4, in0=x2, in1=cs_t, op=ALU.mult)
            c.dve("tensor_tensor", [a1b, a2b], [krr_b], out=krr[:, 0:32], in0=a1, in1=a2, op=ALU.subtract)
            c.dve("tensor_tensor", [a3b, a4b], [krr_b], out=krr[:, 32:64], in0=a3, in1=a4, op=ALU.add)
            pt_ap, pt_buf = c.ptr[0]
            for a in range(3):
                c.pe("transpose", [cqn_b, c.ident[1]], [pt_buf], out=pt_ap[:, a * 128:(a + 1) * 128],
                     in_=cqn[:, a * 128:(a + 1) * 128], identity=c.ident[0])
            for a in range(2):
                c.pe("transpose", [ckn_b, c.ident[1]], [pt_buf], out=pt_ap[:, 384 + a * 128:384 + (a + 1) * 128],
                     in_=ckn[:, a * 128:(a + 1) * 128], identity=c.ident[0])
            c.pe("transpose", [krr_b, c.ident[1]], [pt_buf], out=pt_ap[0:64, 640:768], in_=krr,
                 identity=c.ident[0])
            c.act("activation", [pt_buf], [cqT_b], out=cqT[:, :, tcs],
                  in_=pt_ap[:, 0:384].rearrange("p (k n) -> p k n", k=3), func=AF.Copy)
            c.act("activation", [pt_buf], [ckT_b], out=ckT[:, :, tcs],
                  in_=pt_ap[:, 384:640].rearrange("p (k n) -> p k n", k=2), func=AF.Copy)
            c.act("activation", [pt_buf], [krT_b], out=krT[0:64, tcs], in_=pt_ap[0:64, 640:768], func=AF.Copy)
        if su + 1 < nsup:
            mi_front(su + 1)
        c.sp("dma_start", [krT_b], [c.KTR_b[su]], out=c.KTR[:, tok], in_=krT[0:64, :])
        for h in range(M_HEADS):
            pp, pp_b = c.pb[2 + h % 2]
            for kc in range(2):
                c.pe("matmul", [wuk_b, ckT_b], [pp_b], pp, lhsT=wuk[:, kc, h * 128:(h + 1) * 128], rhs=ckT[:, kc, :],
                     start=(kc == 0), stop=(kc == 1))
            k_ap, k_b = kts[ki % 2]
            ki += 1
            c.act("activation", [pp_b], [k_b], out=k_ap, in_=pp, func=AF.Copy)
            c.sp("dma_start", [k_b], [c.KTN_b[su][h]], out=c.KTN[h][:, tok], in_=k_ap)
        for ts in range(4):
            tile_i = su * 4 + ts
            tcs = slice(ts * 128, (ts + 1) * 128)
            rows = slice(tile_i * 128, (tile_i + 1) * 128)
            v_ap, v_b = vst[tile_i % 2]
            for nb in range(2):
                pp, pp_b = c.pb[2 + nb]
                for kc in range(2):
                    c.pe("matmul", [wuv_b, ckT_b], [pp_b], pp, lhsT=ckT[:, kc, tcs], rhs=wuv[:, kc, nb * 512:(nb + 1) * 512],
                         start=(kc == 0), stop=(kc == 1))
                c.act("activation", [pp_b], [v_b], out=v_ap[:, nb * 512:(nb + 1) * 512], in_=pp, func=AF.Copy)
            c.sp("dma_start", [v_b], [c.VM_b[tile_i]], out=c.VM[rows, :], in_=v_ap)
            for nb in range(3):
                pp, pp_b = c.pb[(4, 5, 1)[nb]]
                for kc in range(3):
                    c.pe("matmul", [wuq_b, cqT_b], [pp_b], pp, lhsT=cqT[:, kc, tcs], rhs=wuq[:, kc, nb * 512:(nb + 1) * 512],
                         start=(kc == 0), stop=(kc == 2))
                c.act("activation", [pp_b], [qf_b], out=qf[:, nb * 512:(nb + 1) * 512], in_=pp, func=AF.Copy)
            qf3 = qf.rearrange("p (h d) -> p h d", h=8)
            qb3 = qb.rearrange("p (h d) -> p h d", h=8)
            cs_t = cm[:, tile_i * 32:(tile_i + 1) * 32].unsqueeze(1).broadcast_to([128, 8, 32])
            sn_t = sm[:, tile_i * 32:(tile_i + 1) * 32].unsqueeze(1).broadcast_to([128, 8, 32])
            x1, x2 = qf3[:, :, 128:160], qf3[:, :, 160:192]
            rr = [(r[0].rearrange("p (h j) -> p h j", h=8), r[1]) for r in rt]
            (a1, a1b), (a2, a2b), (a3, a3b), (a4, a4b) = rr
            c.dve("tensor_tensor", [qf_b, cm_b], [a1b], out=a1, in0=x1, in1=cs_t, op=ALU.mult)
            c.dve("tensor_tensor", [qf_b, sm_b], [a2b], out=a2, in0=x2, in1=sn_t, op=ALU.mult)
            c.dve("tensor_tensor", [qf_b, sm_b], [a3b], out=a3, in0=x1, in1=sn_t, op=ALU.mult)
            c.dve("tensor_tensor", [qf_b, cm_b], [a4b], out=a4, in0=x2, in1=cs_t, op=ALU.mult)
            c.dve("tensor_tensor", [a1b, a2b], [qb_b], out=qb3[:, :, 128:160], in0=a1, in1=a2, op=ALU.subtract)
            c.dve("tensor_tensor", [a3b, a4b], [qb_b], out=qb3[:, :, 160:192], in0=a3, in1=a4, op=ALU.add)
            c.dve("tensor_copy", [qf_b], [qb_b], out=qb3[:, :, 0:128], in_=qf3[:, :, 0:128])
            p6, p6_b = c.pb[6][0].bitcast(BF16), c.pb[6][1]
            p7, p7_b = c.ptr[0]
            for h in range(M_HEADS):
                c.pe("transpose", [qb_b, c.ident[1]], [p6_b], out=p6[:, h * 128:(h + 1) * 128], in_=qb3[:, h, 0:128],
                     identity=c.ident[0])
            for h in range(M_HEADS):
                c.pe("transpose", [qb_b, c.ident[1]], [p7_b], out=p7[0:64, h * 128:(h + 1) * 128],
                     in_=qb3[:, h, 128:192], identity=c.ident[0])
            n_ap, n_b = qtn[tile_i % 2]
            r_ap, r_b = qtr[tile_i % 2]
            c.act("activation", [p6_b], [n_b], out=n_ap, in_=p6.rearrange("p (h n) -> p h n", h=8), func=AF.Copy)
            c.act("activation", [p7_b], [r_b], out=r_ap[0:64], in_=p7[0:64, :].rearrange("p (h n) -> p h n", h=8),
                  func=AF.Copy)
            c.sp("dma_start", [n_b], [c.QTN_b[tile_i]], out=c.QTN[:, :, rows].rearrange("h d s -> d h s"), in_=n_ap)
            c.sp("dma_start", [r_b], [c.QTR_b[tile_i]], out=c.QTR[:, :, rows].rearrange("h d s -> d h s"),
                 in_=r_ap[0:64])
    ar.release()


def phase_mla_attn(c, xio, w_o, S):
    Tracker.phase = "mla_attn"
    x_io, xio_b = xio
    ar = c.arena
    ar.mark()
    nt = S // 128
    nq = S // 512
    OT = ar.alloc(M_HEADS * S, BF16).rearrange("p (h s) -> p h s", h=M_HEADS)
    OT_b = [Buf(f"OT{h}") for h in range(M_HEADS)]
    ar.mark()
    ktr, ktr_b = ar.alloc(S, BF16), Buf("ktr")
    c.dve("memset", [], [ktr_b], ktr[64:128, :], 0.0)
    c.sp("dma_start", c.KTR_b, [ktr_b], out=ktr[0:64, :], in_=c.KTR)
    hd = []
    for i in range(2):
        hd.append(dict(qn=(ar.alloc(S, BF16), Buf(f"qn{i}")), qr=(ar.alloc(S, BF16), Buf(f"qr{i}")),
                       kn=(ar.alloc(S, BF16), Buf(f"kn{i}")),
                       vh=(ar.alloc(S, BF16).rearrange("p (t e) -> p t e", e=128), Buf(f"vh{i}"))))
    for i in range(2):
        c.dve("memset", [], [hd[i]["qr"][1]], hd[i]["qr"][0][64:128, :], 0.0)
    NST = 4
    st_banks = [c.pb[0], c.pb[1], c.pb[2], c.pb[6]]
    pTl = [(ar.alloc(512, BF16), Buf(f"pT{i}")) for i in range(NST)]
    Lacc = [(ar.alloc(512, F32), Buf(f"Lacc{i}")) for i in range(2)]
    RLs = [(ar.alloc(512, F32), Buf(f"RLs{i}")) for i in range(2)]
    ones_f, ones_fb = ar.alloc(128, F32), Buf("ones_f")
    c.dve("memset", [], [ones_fb], ones_f, 1.0)
    cnt = 0
    for h in range(M_HEADS):
        H = hd[h % 2]
        qn, qn_b = H["qn"]
        qr, qr_b = H["qr"]
        kn, kn_b = H["kn"]
        vh, vh_b = H["vh"]
        c.sp("dma_start", c.QTN_b, [qn_b], out=qn, in_=c.QTN[h])
        c.sp("dma_start", c.QTR_b, [qr_b], out=qr[0:64, :], in_=c.QTR[h])
        c.sp("dma_start", [b[h] for b in c.KTN_b], [kn_b], out=kn, in_=c.KTN[h])
        c.sp("dma_start", c.VM_b, [vh_b], out=vh,
             in_=c.VM[:, h * 128:(h + 1) * 128].rearrange("(t p) e -> p t e", p=128))
        iters = [(qt, kt) for qt in range(nq) for kt in range(nt)]
        slots = {}

        def emit_st(i):
            nonlocal cnt
            qt, kt = iters[i]
            qs = slice(qt * 512, (qt + 1) * 512)
            ks = slice(kt * 128, (kt + 1) * 128)
            sT, sT_b = st_banks[cnt % NST]
            p_ap, p_b = pTl[cnt % NST]
            cnt += 1
            slots[i] = (sT, sT_b, p_ap, p_b)
            c.pe("matmul", [kn_b, qn_b], [sT_b], sT, lhsT=kn[:, ks], rhs=qn[:, qs], start=True, stop=False)
            c.pe("matmul", [ktr_b, qr_b], [sT_b], sT, lhsT=ktr[:, ks], rhs=qr[:, qs], start=False, stop=True)

        AHEAD = 3
        for i in range(min(AHEAD, len(iters))):
            emit_st(i)
        for i, (qt, kt) in enumerate(iters):
            if i + AHEAD < len(iters):
                emit_st(i + AHEAD)
            qs = slice(qt * 512, (qt + 1) * 512)
            oT, oT_b = c.pb[3 + qt % 2]
            la, la_b = Lacc[qt % 2]
            sT, sT_b, p_ap, p_b = slots.pop(i)
            c.act("activation", [sT_b], [p_b], out=p_ap, in_=sT, func=AF.Exp, scale=float(M_SCALE))
            c.pe("matmul", [vh_b, p_b], [oT_b], oT, lhsT=vh[:, kt, :], rhs=p_ap, start=(kt == 0), stop=(kt == nt - 1))
            if kt == 0:
                c.dve("tensor_copy", [p_b], [la_b], out=la, in_=p_ap)
            else:
                c.dve("tensor_tensor", [p_b, la_b], [la_b], out=la, in0=la, in1=p_ap, op=ALU.add)
            if kt == nt - 1:
                RB, RB_b = c.pb[5]
                c.pe("matmul", [ones_fb, la_b], [RB_b], RB, lhsT=ones_f, rhs=la, start=True, stop=True)
                R_ap, R_b = RLs[qt % 2]
                c.dve("reciprocal", [RB_b], [R_b], out=R_ap, in_=RB)
                c.dve("tensor_tensor", [oT_b, R_b], [OT_b[h]], out=OT[:, h, qs], in0=oT, in1=R_ap, op=ALU.mult)
    ar.release()
    Tracker.phase = "mla_out"
    wo = ar.alloc(M_HEADS * D, BF16).rearrange("p (k n) -> p k n", k=M_HEADS)
    wo_b = Buf("mwo")
    c.pool("dma_start", [], [wo_b], out=wo, in_=w_o.rearrange("(k p) n -> p k n", p=128))
    xv = x_io.rearrange("(n p) d -> n p d", p=128)
    for t in range(nt):
        for nb in range(2):
            o_ap, o_buf = c.psum_o[c.po_i % len(c.psum_o)]
            c.po_i += 1
            for h in range(M_HEADS):
                c.pe("matmul", [OT_b[h], wo_b], [o_buf], o_ap, lhsT=OT[:, h, t * 128:(t + 1) * 128],
                     rhs=wo[:, h, nb * 512:(nb + 1) * 512], start=(h == 0), stop=(h == M_HEADS - 1))
            csl = slice(nb * 512, (nb + 1) * 512)
            emit_resid(c, o_ap, o_buf, xv[t][:, csl], xio_b[t][nb], xv[t][:, csl], xio_b[t][nb], c.G[0][:, csl])
    ar.release()


def phase_final(c, xin, out, out_b, fg, S):
    Tracker.phase = "final"
    x_in, xin_b = xin
    c.sp("dma_start", [], [c.A[1]], out=c.A[0], in_=fg.partition_broadcast(128))
    xv = x_in.rearrange("(n p) d -> n p d", p=128)
    ov = out.rearrange("(n p) d -> n p d", p=128)
    for t in range(S // 128):
        slot = c.xslot
        c.xslot = (c.xslot + 1) % len(c.xt)
        xt, xb = c.xt[slot]
        hb_ap, hb_buf = c.hb[slot % len(c.hb)]
        ss_ap, ss_buf = c.ss[slot % len(c.ss)]
        c.sp("dma_start", list(xin_b[t]), [xb], out=xt, in_=xv[t])
        c.dve("scalar_tensor_tensor", [xb], [hb_buf, ss_buf], out=hb_ap, in0=xt, scalar=1.0, in1=xt,
              op0=ALU.mult, op1=ALU.mult, accum_out=ss_ap[:, 0:1])
        c.act("activation", [ss_buf, c.eps_rms[1]], [ss_buf], out=ss_ap[:, 1:2], in_=ss_ap[:, 0:1], func=AF.Sqrt,
              scale=1.0 / D, bias=c.eps_rms[0])
        c.dve("reciprocal", [ss_buf], [ss_buf], out=ss_ap[:, 2:3], in_=ss_ap[:, 1:2])
        c.dve("scalar_tensor_tensor", [xb, ss_buf, c.A[1]], [xb], out=xt, in0=xt, scalar=ss_ap[:, 2:3],
              in1=c.A[0], op0=ALU.mult, op1=ALU.mult)
        c.sp("dma_start", [xb], list(out_b[t]), out=ov[t], in_=xt)


def xbufs(S, name):
    return [[Buf(f"{name}{t}_{h}") for h in range(2)] for t in range(S // 128)]


SCRATCH_KIND = "Internal"


def alloc_ret_scratch(c, nc, S):
    nt = S // 128
    c.QT = nc.dram_tensor("QT", [R_QK, S], BF16, kind=SCRATCH_KIND).ap()
    c.KT = nc.dram_tensor("KT", [R_QK, S], BF16, kind=SCRATCH_KIND).ap()
    c.KF = nc.dram_tensor("KF", [S, R_QK], BF16, kind=SCRATCH_KIND).ap()
    c.KB = nc.dram_tensor("KB", [S, R_QK], BF16, kind=SCRATCH_KIND).ap()
    c.V = nc.dram_tensor("Vr", [S, R_VTOT], BF16, kind=SCRATCH_KIND).ap()
    c.Gs = nc.dram_tensor("Gs", [S, R_VTOT], F32, kind=SCRATCH_KIND).ap()
    c.SB = nc.dram_tensor("SBs", [nt, R_HEADS, 128, 1024], BF16, kind=SCRATCH_KIND).ap()
    c.QT_b = [[Buf("QT") for h in range(R_HEADS)] for _ in range(S // 512)]
    c.KT_b = [Buf("KT") for _ in range(S // 512)]
    c.KF_b = [Buf("KF") for _ in range(nt)]
    c.KB_b = [Buf("KB") for _ in range(nt)]
    c.V_b = [[Buf("V") for _ in range(4)] for _ in range(nt)]
    c.Gs_b = [[Buf("Gs") for _ in range(4)] for _ in range(nt)]
    c.SB_b = [[Buf("SB") for _ in range(R_HEADS)] for _ in range(nt)]


def alloc_tables(c, nc, S):
    c.CR = nc.dram_tensor("CR", [128, S], F32, kind=SCRATCH_KIND).ap()
    c.SR = nc.dram_tensor("SR", [128, S], F32, kind=SCRATCH_KIND).ap()
    c.CM = nc.dram_tensor("CM", [S, 32], F32, kind=SCRATCH_KIND).ap()
    c.SM = nc.dram_tensor("SM", [S, 32], F32, kind=SCRATCH_KIND).ap()
    c.CR_b, c.SR_b, c.CM_b, c.SM_b = Buf("CR"), Buf("SR"), Buf("CM"), Buf("SM")


def load_ctab(c, ctab_dram):
    c.ctab_dram = ctab_dram


def fetch_ctab(c):
    ar = c.arena
    ap, b = ar.alloc(CTW, F32), Buf("ctab")
    c.sp("dma_start", [], [b], out=ap, in_=c.ctab_dram)
    c.ctab = (ap, b)
    return c.ctab


def emit_copy_x(c, src, dst, dst_b, S):
    for t in range(S // 128):
        for hf in range(2):
            r_ap, r_buf = c.xr[c.xr_i % 2]
            c.xr_i += 1
            sl = (slice(t * 128, (t + 1) * 128), slice(hf * 512, (hf + 1) * 512))
            c.sp("dma_start", [], [r_buf], out=r_ap, in_=src[sl])
            c.sp("dma_start", [r_buf], [dst_b[t][hf]], out=dst[sl], in_=r_ap)


def build_ret_test(S):
    nc = bass.Bass("TRN2", target_bir_lowering=False)
    x = nc.dram_tensor("x", [S, D], F32, kind="ExternalInput").ap()
    pos = nc.dram_tensor("pos", [S], I32, kind="ExternalInput").ap()
    w_in = nc.dram_tensor("w_in", [D, R_IN], F32, kind="ExternalInput").ap()
    w_out = nc.dram_tensor("w_out", [R_VTOT, D], F32, kind="ExternalInput").ap()
    gn_g = nc.dram_tensor("gn_g", [R_VTOT], F32, kind="ExternalInput").ap()
    gn_b = nc.dram_tensor("gn_b", [R_VTOT], F32, kind="ExternalInput").ap()
    dec_f = nc.dram_tensor("dec_f", [4], F32, kind="ExternalInput").ap()
    dec_b = nc.dram_tensor("dec_b", [4], F32, kind="ExternalInput").ap()
    rows3 = nc.dram_tensor("rows3", [3, D], F32, kind="ExternalInput").ap()
    ident = nc.dram_tensor("ident", [128, 128], BF16, kind="ExternalInput").ap()
    ctab = nc.dram_tensor("ctab", [128, CTW], F32, kind="ExternalInput").ap()
    out = nc.dram_tensor("out", [S, D], F32, kind="ExternalOutput").ap()
    c = Ctx()
    c.tr = Tracker()
    c.ident_dram = ident
    alloc_tables(c, nc, S)
    alloc_ret_scratch(c, nc, S)
    with nc.sbuf_tensor("arena", [128, ARENA_BYTES // 4], F32) as ah, \
            nc.psum_tensor("psum", [128, 4096], F32) as ps:
        setup_common(c, nc, ah, ARENA_BYTES, ps)
        load_ctab(c, ctab)
        phase_setup_tables(c, pos, S)
        xb = xbufs(S, "x")
        ob = xbufs(S, "o")
        emit_copy_x(c, x, out, ob, S)
        c.arena.mark()
        dt = ret_tables(c, dec_f, dec_b)
        phase_ret_in(c, (x, xb), w_in, rows3, S, dt)
        phase_ret_bwd(c, S, dt)
        phase_ret_fwd(c, (out, ob), w_out, gn_g, gn_b, S, dt)
        c.arena.release()
        n = c.tr.emit(nc)
    print("ops", n)
    return nc


def build_ffn_test(S):
    nc = bass.Bass("TRN2", target_bir_lowering=False)
    x = nc.dram_tensor("x", [S, D], F32, kind="ExternalInput").ap()
    w_in = nc.dram_tensor("w_in", [D, 2 * DFF], F32, kind="ExternalInput").ap()
    w_out = nc.dram_tensor("w_out", [DFF, D], F32, kind="ExternalInput").ap()
    rows3 = nc.dram_tensor("rows3", [3, D], F32, kind="ExternalInput").ap()
    ident = nc.dram_tensor("ident", [128, 128], BF16, kind="ExternalInput").ap()
    out = nc.dram_tensor("out", [S, D], F32, kind="ExternalOutput").ap()
    c = Ctx()
    c.tr = Tracker()
    c.ident_dram = ident
    with nc.sbuf_tensor("arena", [128, ARENA_BYTES // 4], F32) as ah, \
            nc.psum_tensor("psum", [128, 4096], F32) as ps:
        setup_common(c, nc, ah, ARENA_BYTES, ps)
        phase_ffn(c, (x, xbufs(S, "x")), (out, xbufs(S, "o")), w_in, w_out, rows3, S)
        n = c.tr.emit(nc)
    print("ops", n)
    return nc


def build_mla_test(S):
    nc = bass.Bass("TRN2", target_bir_lowering=False)
    x = nc.dram_tensor("x", [S, D], F32, kind="ExternalInput").ap()
    pos = nc.dram_tensor("pos", [S], I32, kind="ExternalInput").ap()
    w_down = nc.dram_tensor("w_down", [D, M_DOWN], F32, kind="ExternalInput").ap()
    qng = nc.dram_tensor("qng", [Q_LORA], F32, kind="ExternalInput").ap()
    kvng = nc.dram_tensor("kvng", [KV_LORA], F32, kind="ExternalInput").ap()
    w_uq = nc.dram_tensor("w_uq", [Q_LORA, 1536], F32, kind="ExternalInput").ap()
    w_ukv = nc.dram_tensor("w_ukv", [KV_LORA, 2048], F32, kind="ExternalInput").ap()
    w_o = nc.dram_tensor("w_o", [1024, D], F32, kind="ExternalInput").ap()
    rows3 = nc.dram_tensor("rows3", [3, D], F32, kind="ExternalInput").ap()
    ident = nc.dram_tensor("ident", [128, 128], BF16, kind="ExternalInput").ap()
    ctab = nc.dram_tensor("ctab", [128, CTW], F32, kind="ExternalInput").ap()
    out = nc.dram_tensor("out", [S, D], F32, kind="ExternalOutput").ap()
    c = Ctx()
    c.tr = Tracker()
    c.ident_dram = ident
    alloc_tables(c, nc, S)
    alloc_mla_scratch(c, nc, S)
    with nc.sbuf_tensor("arena", [128, ARENA_BYTES // 4], F32) as ah, \
            nc.psum_tensor("psum", [128, 4096], F32) as ps:
        setup_common(c, nc, ah, ARENA_BYTES, ps)
        load_ctab(c, ctab)
        phase_setup_tables(c, pos, S)
        xb = xbufs(S, "x")
        ob = xbufs(S, "o")
        emit_copy_x(c, x, out, ob, S)
        phase_mla_in(c, (x, xb), w_down, qng, kvng, w_uq, w_ukv, rows3, S)
        phase_mla_attn(c, (out, ob), w_o, S)
        n = c.tr.emit(nc)
    print("ops", n)
    return nc


DEPTH = 4
_NC_CACHE = {}
W_SPECS = [
    ("norm_g", [DEPTH, 3, D]), ("final_norm_g", [D]), ("mod_w", [DEPTH, D, 9 * D]), ("mod_b", [DEPTH, 9 * D]),
    ("ffn_w_in", [DEPTH, 2, D, 2 * DFF]), ("ffn_w_out", [DEPTH, 2, DFF, D]),
    ("ret_w_in", [2, D, R_IN]), ("ret_w_out", [2, R_VTOT, D]), ("ret_gn_g", [2, R_VTOT]), ("ret_gn_b", [2, R_VTOT]),
    ("ret_decay_fwd", [2, 4]), ("ret_decay_bwd", [2, 4]),
    ("mla_w_down", [2, D, M_DOWN]), ("mla_q_norm_g", [2, Q_LORA]), ("mla_kv_norm_g", [2, KV_LORA]),
    ("mla_w_uq", [2, Q_LORA, 1536]), ("mla_w_ukv", [2, KV_LORA, 2048]), ("mla_w_o", [2, 1024, D]),
]


def build_full(S, depth=DEPTH, layers=None):
    nc = bass.Bass("TRN2", target_bir_lowering=False)
    x = nc.dram_tensor("x", [S, D], F32, kind="ExternalInput").ap()
    cvec = nc.dram_tensor("c", [D], F32, kind="ExternalInput").ap()
    pos = nc.dram_tensor("positions", [S], I32, kind="ExternalInput").ap()
    W = {n: nc.dram_tensor(n, shp, F32, kind="ExternalInput").ap() for n, shp in W_SPECS}
    ident = nc.dram_tensor("ident", [128, 128], BF16, kind="ExternalInput").ap()
    ctab = nc.dram_tensor("ctab", [128, CTW], F32, kind="ExternalInput").ap()
    out = nc.dram_tensor("out", [S, D], F32, kind="ExternalOutput").ap()
    xres = nc.dram_tensor("xres", [S, D], F32, kind=SCRATCH_KIND).ap()
    c = Ctx()
    c.tr = Tracker()
    c.ident_dram = ident
    c.modrows = nc.dram_tensor("modrows", [DEPTH, 3, 3, D], F32, kind=SCRATCH_KIND).ap()
    c.modrows_b = [Buf(f"modrows{i}") for i in range(DEPTH)]
    alloc_tables(c, nc, S)
    alloc_ret_scratch(c, nc, S)
    alloc_mla_scratch(c, nc, S)
    with nc.sbuf_tensor("arena", [128, ARENA_BYTES // 4], F32) as ah, \
            nc.psum_tensor("psum", [128, 4096], F32) as ps:
        setup_common(c, nc, ah, ARENA_BYTES, ps)
        load_ctab(c, ctab)
        phase_setup_tables(c, pos, S)
        phase_mod(c, cvec, W["mod_w"], W["mod_b"], W["norm_g"], depth)
        xin_b = xbufs(S, "xin")
        xb = xbufs(S, "xres")
        ob = xbufs(S, "out")
        for i in (layers if layers is not None else range(depth)):
            rows = lambda sl: (c.modrows[i, sl], [c.modrows_b[i]])
            src = (x, xin_b) if i == (layers[0] if layers is not None else 0) else (xres, xb)
            phase_ffn(c, src, (xres, xb), W["ffn_w_in"][i, 0], W["ffn_w_out"][i, 0], rows(0), S)
            j = i // 2
            if i % 2 == 0:
                c.arena.mark()
                dt = ret_tables(c, W["ret_decay_fwd"][j], W["ret_decay_bwd"][j])
                phase_ret_in(c, (xres, xb), W["ret_w_in"][j], rows(1), S, dt)
                phase_ret_bwd(c, S, dt)
                phase_ret_fwd(c, (xres, xb), W["ret_w_out"][j], W["ret_gn_g"][j], W["ret_gn_b"][j], S, dt)
                c.arena.release()
            else:
                phase_mla_in(c, (xres, xb), W["mla_w_down"][j], W["mla_q_norm_g"][j], W["mla_kv_norm_g"][j],
                             W["mla_w_uq"][j], W["mla_w_ukv"][j], rows(1), S)
                phase_mla_attn(c, (xres, xb), W["mla_w_o"][j], S)
            phase_ffn(c, (xres, xb), (xres, xb), W["ffn_w_in"][i, 1], W["ffn_w_out"][i, 1], rows(2), S)
        phase_final(c, (xres, xb), out, ob, W["final_norm_g"], S)
        n = c.tr.emit(nc)
    _NC_CACHE["last_tr"] = c.tr
    return nc, n


SEQ = 4096
BATCH = 8


def kernel(**inputs):
    if "nc" not in _NC_CACHE:
        _NC_CACHE["nc"] = build_full(SEQ)[0]
    nc = _NC_CACHE["nc"]
    cst = host_consts()
    f32 = lambda a: np.ascontiguousarray(np.asarray(a), dtype=np.float32)
    shared = {n: f32(inputs[n]) for n, _ in W_SPECS}
    shared["ident"] = cst["ident"]
    shared["ctab"] = cst["ctab"]
    x = f32(inputs["x"])
    cc = f32(inputs["c"])
    pos = np.ascontiguousarray(np.asarray(inputs["positions"]), dtype=np.int32)
    in_maps = []
    for b in range(BATCH):
        m = dict(shared)
        m["x"] = x[b]
        m["c"] = cc[b]
        m["positions"] = pos[b]
        in_maps.append(m)
    res = run_bass_kernel_spmd(nc, in_maps, core_ids=list(range(BATCH)))
    return np.stack([np.asarray(res.results[b]["out"]) for b in range(BATCH)]).astype(np.float32)
```

```python
import contextlib
import numpy as np
import concourse.bass as bass
import concourse.mybir as mybir
from concourse.bass_utils import run_bass_kernel_spmd

F32 = mybir.dt.float32
BF16 = mybir.dt.bfloat16
I32 = mybir.dt.int32
AF = mybir.ActivationFunctionType
ALU = mybir.AluOpType
AX = mybir.AxisListType

D = 1024
DFF = 2816
KC = D // 128
FC = DFF // 128
RMS_EPS = 1e-6

ANNOTATE = False
ENGS = ["pe", "act", "dve", "pool", "sp"]
NDSEM = {"sp": 12, "pool": 8, "act": 4, "pe": 0, "dve": 0}


class Buf:
    __slots__ = ("name", "w", "rc", "rd")

    def __init__(self, name=""):
        self.name = name
        self.w = None
        self.rc = {}
        self.rd = set()
        reg = Arena.cur
        if reg is not None:
            for r in Arena.regions:
                if r is not reg and r[0] < reg[1] and reg[0] < r[1]:
                    for ob in r[2]:
                        self._inherit(ob)
            reg[2].append(self)

    def _inherit(self, ob):
        if ob.w is not None:
            if ob.w[0] == "d":
                self.rd.add(ob.w[1])
            elif self.rc.get(ob.w[1], -1) < ob.w[2]:
                self.rc[ob.w[1]] = ob.w[2]
        for e, i in ob.rc.items():
            if self.rc.get(e, -1) < i:
                self.rc[e] = i
        self.rd |= ob.rd


class Tracker:
    phase = ""

    def __init__(self):
        self.ops = {e: [] for e in ENGS}
        self.dmas = []
        self.ndma = {e: 0 for e in ENGS}
        self.dma_by_k = {e: [] for e in ENGS}

    def add(self, eng, method, reads, writes, *args, **kw):
        dma = method == "dma_start"
        fn = (method, args, kw)
        dc = {}
        dd = set()

        def dep(d):
            if d is None:
                return
            if d[0] == "d":
                dd.add(d[1])
            elif dc.get(d[1], -1) < d[2]:
                dc[d[1]] = d[2]

        for b in reads:
            dep(b.w)
        for b in writes:
            dep(b.w)
            for e, i in b.rc.items():
                dep(("c", e, i))
            for did in b.rd:
                dep(("d", did))
        idx = len(self.ops[eng])
        did = None
        if dma:
            did = len(self.dmas)
            k = self.ndma[eng]
            self.ndma[eng] += 1
            self.dmas.append((eng, k))
            n = NDSEM[eng]
            if k >= n:
                dd.add(self.dma_by_k[eng][k - n])
            self.dma_by_k[eng].append(did)
            me = ("d", did)
        else:
            me = ("c", eng, idx)
        self.ops[eng].append(dict(fn=fn, dc=dc, dd=dd, did=did, ph=Tracker.phase))
        for b in reads:
            if dma:
                b.rd.add(did)
            else:
                b.rc[eng] = idx
        for b in writes:
            b.w = me
            b.rc = {}
            b.rd = set()
        return me

    def emit(self, nc):
        ops = self.ops
        signal = {e: [False] * len(ops[e]) for e in ENGS}
        waits = {e: [None] * len(ops[e]) for e in ENGS}
        for e in ENGS:
            seen_c = {p: -1 for p in ENGS}
            seen_d = set()
            for i, op in enumerate(ops[e]):
                wl = []
                for p, j in op["dc"].items():
                    if p == "pe" and e == "pe" and op["did"] is None:
                        continue
                    if seen_c[p] >= j:
                        continue
                    seen_c[p] = j
                    signal[p][j] = True
                    wl.append(("c", p, j))
                for did in sorted(op["dd"]):
                    if did in seen_d:
                        continue
                    seen_d.add(did)
                    wl.append(("d", did))
                waits[e][i] = wl
        sigval = {}
        for e in ENGS:
            c = 0
            for i in range(len(ops[e])):
                if signal[e][i]:
                    c += 1
                    sigval[(e, i)] = c
        with contextlib.ExitStack() as st:
            csem = {e: st.enter_context(nc.semaphore(f"c_{e}")) for e in ENGS if e != "sp"}
            dsem = {e: [st.enter_context(nc.semaphore(f"d_{e}{k}")) for k in range(NDSEM[e])]
                    for e in ENGS if self.ndma[e] > 0}

            def dma_semval(did):
                q, k = self.dmas[did]
                n = NDSEM[q]
                return dsem[q][k % n], 16 * (k // n + 1)

            final = []
            for q in ENGS:
                nd = self.ndma[q]
                n = NDSEM[q]
                for s in range(min(n, nd)):
                    cnt = (nd - 1 - s) // n + 1
                    final.append((dsem[q][s], 16 * cnt))

            with nc.Block() as block:
                regs = {"pe": block.tensor, "act": block.scalar, "dve": block.vector,
                        "pool": block.gpsimd, "sp": block.sync}
                for e in ENGS:
                    def body(eng, e=e):
                        for i, op in enumerate(ops[e]):
                            for w in waits[e][i]:
                                if w[0] == "c":
                                    eng.wait_ge(csem[w[1]], sigval[(w[1], w[2])])
                                else:
                                    s, v = dma_semval(w[1])
                                    eng.wait_ge(s, v)
                            m, a, k = op["fn"]
                            ins = getattr(eng, m)(*a, **k)
                            if ANNOTATE and op["ph"]:
                                ins.annotate(op["ph"])
                            if op["did"] is not None:
                                s, v = dma_semval(op["did"])
                                ins.then_inc(s, 16)
                            elif signal[e][i]:
                                ins.then_inc(csem[e], 1)
                        if e == "sp":
                            for s, v in final:
                                eng.wait_ge(s, v)
                    regs[e](body)
        return {e: len(v) for e, v in ops.items()}


class Arena:
    cur = None
    regions = []

    def __init__(self, handle, nbytes):
        self.h = handle
        self.n = nbytes
        self.off = 0
        self.marks = []
        Arena.cur = None
        Arena.regions = []

    def alloc(self, nelem, dtype, shape=None):
        sz = 2 if dtype == BF16 else 4
        nb = (nelem * sz + 31) // 32 * 32
        assert self.off + nb <= self.n, f"arena overflow {self.off}+{nb}>{self.n}"
        a = self.h[:, self.off // 4:(self.off + nb) // 4]
        Arena.cur = [self.off, self.off + nb, []]
        Arena.regions.append(Arena.cur)
        self.off += nb
        if dtype != F32:
            a = a.bitcast(dtype)
        a = a[:, 0:nelem]
        return a

    def mark(self):
        self.marks.append(self.off)

    def release(self):
        self.off = self.marks.pop()


class Ctx:
    pass


def _mk(eng):
    def f(self, method, reads, writes, *a, **k):
        return self.tr.add(eng, method, reads, writes, *a, **k)
    return f


for _e in ENGS:
    setattr(Ctx, _e, _mk(_e))


def emit_front(c, x_src, x_bufs, hT, hT_buf, col0):
    slot = c.xslot
    c.xslot = (c.xslot + 1) % len(c.xt)
    xt, xb = c.xt[slot]
    hb_ap, hb_buf = c.hb[slot % len(c.hb)]
    ss_ap, ss_buf = c.ss[slot % len(c.ss)]
    c.sp("dma_start", list(x_bufs), [xb], out=xt, in_=x_src)
    c.dve("scalar_tensor_tensor", [xb], [hb_buf, ss_buf], out=hb_ap, in0=xt, scalar=1.0, in1=xt,
          op0=ALU.mult, op1=ALU.mult, accum_out=ss_ap[:, 0:1])
    c.act("activation", [ss_buf, c.eps_rms[1]], [ss_buf], out=ss_ap[:, 1:2], in_=ss_ap[:, 0:1], func=AF.Sqrt,
          scale=1.0 / D, bias=c.eps_rms[0])
    c.dve("reciprocal", [ss_buf], [ss_buf], out=ss_ap[:, 2:3], in_=ss_ap[:, 1:2])
    c.dve("scalar_tensor_tensor", [xb, ss_buf, c.A[1]], [xb], out=xt, in0=xt, scalar=ss_ap[:, 2:3],
          in1=c.A[0], op0=ALU.mult, op1=ALU.mult)
    c.dve("tensor_tensor", [xb, c.B[1]], [hb_buf], out=hb_ap, in0=xt, in1=c.B[0], op=ALU.add)
    pt_ap, pt_buf = c.ptr[c.ptr_i % len(c.ptr)]
    c.ptr_i += 1
    for kc in range(KC):
        c.pe("transpose", [hb_buf, c.ident[1]], [pt_buf], out=pt_ap[:, kc * 128:(kc + 1) * 128],
             in_=hb_ap[:, kc * 128:(kc + 1) * 128], identity=c.ident[0])
    c.act("activation", [pt_buf], [hT_buf], out=hT[:, :, col0:col0 + 128],
          in_=pt_ap.rearrange("p (k n) -> p k n", k=KC), func=AF.Copy)


def emit_resid(c, o_ap, o_buf, x_src, x_src_b, x_dst, x_dst_b, g_ap):
    r_ap, r_buf = c.xr[c.xr_i % len(c.xr)]
    t_ap, t_buf = c.ot[c.xr_i % len(c.ot)]
    c.xr_i += 1
    c.sp("dma_start", [x_src_b], [r_buf], out=r_ap, in_=x_src)
    c.dve("tensor_tensor", [o_buf, c.G[1]], [t_buf], out=t_ap, in0=o_ap, in1=g_ap, op=ALU.mult)
    c.dve("tensor_tensor", [t_buf, r_buf], [r_buf], out=r_ap, in0=r_ap, in1=t_ap, op=ALU.add)
    c.sp("dma_start", [r_buf], [x_dst_b], out=x_dst, in_=r_ap)


def load_bcast_rows(c, rows3):
    rb = []
    if isinstance(rows3, tuple):
        rows3, rb = rows3
    for i, (ap, buf) in enumerate((c.A, c.B, c.G)):
        c.sp("dma_start", list(rb), [buf], out=ap, in_=rows3[i, :].partition_broadcast(128))


def phase_ffn(c, xin, xout, w_in, w_out, rows3, S):
    Tracker.phase = "ffn"
    x_in, xin_b = xin
    x_out, xout_b = xout
    ar = c.arena
    ar.mark()
    win = ar.alloc(KC * 2 * DFF, BF16).rearrange("p (k n) -> p k n", k=KC)
    NJB = FC // 2
    wg_b = [Buf(f"wing{j}") for j in range(NJB)]
    wu_b = [Buf(f"winu{j}") for j in range(NJB)]
    wout = ar.alloc(FC * D, BF16).rearrange("p (k n) -> p k n", k=FC)
    wout_b = [Buf("wout0"), Buf("wout1")]
    NT = 512
    nsup = S // NT
    hT = [ar.alloc(KC * NT, BF16).rearrange("p (k n) -> p k n", k=KC) for _ in range(2)]
    hT_b = [Buf("hT0"), Buf("hT1")]
    aT = ar.alloc(FC * NT, BF16).rearrange("p (k n) -> p k n", k=FC)
    aT_b = [Buf(f"aT{j}") for j in range(FC)]
    sg = [(ar.alloc(NT, F32), Buf(f"sg{i}")) for i in range(2)]

    w_in_v = w_in.rearrange("(k p) n -> p k n", p=128)
    for jb in range(NJB):
        for (bb, c0) in ((wg_b, 0), (wu_b, DFF)):
            cs_ = slice(c0 + jb * 256, c0 + (jb + 1) * 256)
            c.pool("dma_start", [], [bb[jb]], out=win[:, :, cs_], in_=w_in_v[:, :, cs_])
    w_out_v = w_out.rearrange("(k p) n -> p k n", p=128)
    for hh in range(2):
        c.pool("dma_start", [], [wout_b[hh]], out=wout[:, hh * 11:(hh + 1) * 11, :],
               in_=w_out_v[:, hh * 11:(hh + 1) * 11, :])
    load_bcast_rows(c, rows3)

    xin_v = x_in.rearrange("(n p) d -> n p d", p=128)
    xout_v = x_out.rearrange("(n p) d -> n p d", p=128)
    gu = c.psum_gu
    gu_i = 0
    def do_front(su_):
        for ts_ in range(4):
            emit_front(c, xin_v[su_ * 4 + ts_], xin_b[su_ * 4 + ts_], hT[su_ % 2], hT_b[su_ % 2], ts_ * 128)

    do_front(0)
    for su in range(nsup):
        hTs, hTb = hT[su % 2], hT_b[su % 2]
        for j in range(FC):
            g_ap, g_buf = gu[gu_i % 4]
            u_ap, u_buf = gu[(gu_i + 1) % 4]
            gu_i += 2
            for k in range(KC):
                c.pe("matmul", [wg_b[j // 2], hTb], [g_buf], g_ap, lhsT=win[:, k, j * 128:(j + 1) * 128],
                     rhs=hTs[:, k, :], start=(k == 0), stop=(k == KC - 1))
            for k in range(KC):
                c.pe("matmul", [wu_b[j // 2], hTb], [u_buf], u_ap, lhsT=win[:, k, DFF + j * 128:DFF + (j + 1) * 128],
                     rhs=hTs[:, k, :], start=(k == 0), stop=(k == KC - 1))
            s_ap, s_buf = sg[j % 2]
            c.act("activation", [g_buf], [s_buf], out=s_ap, in_=g_ap, func=AF.Silu)
            c.dve("tensor_tensor", [s_buf, u_buf], [aT_b[j]], out=aT[:, j, :], in0=s_ap, in1=u_ap, op=ALU.mult)
        if su + 1 < nsup:
            do_front(su + 1)
        for ts in range(4):
            for nb in range(2):
                o_ap, o_buf = c.psum_o[c.po_i % len(c.psum_o)]
                c.po_i += 1
                for j in range(FC):
                    c.pe("matmul", [aT_b[j], wout_b[j // 11]], [o_buf], o_ap,
                         lhsT=aT[:, j, ts * 128:(ts + 1) * 128], rhs=wout[:, j, nb * 512:(nb + 1) * 512],
                         start=(j == 0), stop=(j == FC - 1))
                cs = slice(nb * 512, (nb + 1) * 512)
                emit_resid(c, o_ap, o_buf, xin_v[su * 4 + ts][:, cs], xin_b[su * 4 + ts][nb],
                           xout_v[su * 4 + ts][:, cs], xout_b[su * 4 + ts][nb], c.G[0][:, cs])
    ar.release()


DEBUG = False


def dbg(c, name, ap, buf):
    if not DEBUG:
        return
    shp = list(ap.shape)
    d = c.nc.dram_tensor("dbg_" + name, shp, ap.dtype, kind="ExternalOutput").ap()
    c.sp("dma_start", [buf], [], out=d, in_=ap)


def setup_common(c, nc, arena_handle, arena_bytes, psum):
    c.nc = nc
    c.arena = Arena(arena_handle, arena_bytes)
    ar = c.arena
    c.psum = psum
    c.ident = (ar.alloc(128, BF16), Buf("ident"))
    c.A = (ar.alloc(D, F32), Buf("A"))
    c.B = (ar.alloc(D, F32), Buf("B"))
    c.G = (ar.alloc(D, F32), Buf("G"))
    c.xt = [(ar.alloc(D, F32), Buf(f"xt{i}")) for i in range(2)]
    c.xslot = 0
    c.xr = [(ar.alloc(512, F32), Buf(f"xr{i}")) for i in range(2)]
    c.ot = [(ar.alloc(512, F32), Buf(f"ot{i}")) for i in range(2)]
    c.xr_i = 0
    c.hb = [(ar.alloc(D, BF16), Buf(f"hb{i}")) for i in range(2)]
    c.ss = [(ar.alloc(8, F32), Buf(f"ss{i}")) for i in range(6)]
    c.pb = [(psum[:, b * 512:(b + 1) * 512], Buf(f"pb{b}")) for b in range(8)]
    c.psum_gu = c.pb[0:4]
    c.psum_o = c.pb[4:7]
    c.po_i = 0
    c.ptr = [(c.pb[7][0].bitcast(BF16), c.pb[7][1])]
    c.ptr_i = 0
    c.sp("dma_start", [], [c.ident[1]], out=c.ident[0], in_=c.ident_dram)
    c.eps_rms = (ar.alloc(8, F32)[:, 0:1], Buf("eps"))
    c.dve("memset", [], [c.eps_rms[1]], c.eps_rms[0], RMS_EPS)


ARENA_BYTES = 212800

R_HEADS, R_DK, R_DV = 4, 256, 512
R_QK, R_VTOT = 1024, 2048
R_IN = 6144
M_HEADS, M_NOPE, M_ROPE, M_V = 8, 128, 64, 128
Q_LORA, KV_LORA = 384, 256
M_DOWN = 704
M_SCALE = (M_NOPE + M_ROPE) ** -0.5
GN_EPS = 1e-5
TWO_PI = 2.0 * np.pi
CW1 = 6.28125
CW2 = float(TWO_PI - 6.28125)
MAGIC = 12582912.0


CTW = 680


def host_consts():
    import ml_dtypes
    cst = {}
    cst["ident"] = np.eye(128, dtype=ml_dtypes.bfloat16)
    inv_r = np.power(np.float32(10000.0), -np.arange(0, R_DK, 2, dtype=np.float32) / np.float32(R_DK)).astype(np.float32)
    inv_m = np.power(np.float32(10000.0), -np.arange(0, M_ROPE, 2, dtype=np.float32) / np.float32(M_ROPE)).astype(np.float32)
    t = np.arange(128, dtype=np.float32)
    sI, tI = np.meshgrid(t, t, indexing="ij")
    tab = np.zeros((128, CTW), np.float32)
    tab[:, 552:680] = np.eye(128, dtype=np.float32)
    tab[:, 0:128] = np.maximum(tI - sI, 0)
    tab[:, 128:256] = (tI >= sI).astype(np.float32) / 16.0
    tab[:, 256:384] = np.maximum(sI - tI, 0)
    tab[:, 384:512] = (sI > tI).astype(np.float32) / 16.0
    tab[:, 512] = t + 1.0
    tab[:, 513] = 128.0 - t
    tab[:, 514] = 127.0 - t
    tab[:, 515] = t
    tab[:, 516] = 128.0
    tab[:, 517] = inv_r
    tab[:, 520:552] = inv_m[None, :]
    cst["ctab"] = tab
    return cst


def emit_sincos(c, ang, n, cos_out, sin_out, bufs):
    ang_b, cos_b, sin_b = bufs
    ar = c.arena
    ar.mark()
    k_ap, k_b = ar.alloc(n, F32), Buf("k")
    c.dve("tensor_scalar", [ang_b], [k_b], out=k_ap, in0=ang, scalar1=float(1.0 / TWO_PI), scalar2=MAGIC,
          op0=ALU.mult, op1=ALU.add)
    c.dve("tensor_scalar", [k_b], [k_b], out=k_ap, in0=k_ap, scalar1=MAGIC, scalar2=None, op0=ALU.subtract)
    c.dve("scalar_tensor_tensor", [k_b, ang_b], [ang_b], out=ang, in0=k_ap, scalar=-CW1, in1=ang,
          op0=ALU.mult, op1=ALU.add)
    c.dve("scalar_tensor_tensor", [k_b, ang_b], [ang_b], out=ang, in0=k_ap, scalar=-CW2, in1=ang,
          op0=ALU.mult, op1=ALU.add)
    c.dve("tensor_scalar", [ang_b], [ang_b], out=ang, in0=ang, scalar1=float(-np.pi), scalar2=float(np.pi),
          op0=ALU.max, op1=ALU.min)
    c.act("activation", [ang_b], [sin_b], out=sin_out, in_=ang, func=AF.Sin)
    c.act("activation", [ang_b], [k_b], out=k_ap, in_=ang, func=AF.Sin, scale=0.5)
    c.dve("tensor_tensor", [k_b], [k_b], out=k_ap, in0=k_ap, in1=k_ap, op=ALU.mult)
    c.dve("tensor_scalar", [k_b], [cos_b], out=cos_out, in0=k_ap, scalar1=-2.0, scalar2=1.0,
          op0=ALU.mult, op1=ALU.add)
    ar.release()


def phase_setup_tables(c, pos, S):
    Tracker.phase = "setup_tables"
    ar = c.arena
    ar.mark()
    ct = fetch_ctab(c)
    nt = S // 128
    pi_ap, pi_b = ar.alloc(S, I32), Buf("posi")
    c.sp("dma_start", [], [pi_b], out=pi_ap, in_=pos.partition_broadcast(128))
    ang, ang_b = ar.alloc(S, F32), Buf("ang")
    c.dve("tensor_copy", [pi_b], [ang_b], out=ang, in_=pi_ap)
    c.dve("tensor_scalar", [ang_b, ct[1]], [ang_b], out=ang, in0=ang, scalar1=ct[0][:, 517:518], scalar2=None,
          op0=ALU.mult)
    cs, cs_b = ar.alloc(S, F32), Buf("cos")
    sn, sn_b = ar.alloc(S, F32), Buf("sin")
    emit_sincos(c, ang, S, cs, sn, (ang_b, cs_b, sn_b))
    c.sp("dma_start", [cs_b], [c.CR_b], out=c.CR, in_=cs)
    c.sp("dma_start", [sn_b], [c.SR_b], out=c.SR, in_=sn)
    pf_ap, pf_b = ar.alloc(nt, F32), Buf("posf")
    pb2, pb2_b = ar.alloc(S, F32), Buf("posf_row")
    c.dve("tensor_copy", [pi_b], [pb2_b], out=pb2, in_=pi_ap)
    junk, junk_b = ar.alloc(128, F32), Buf("junk")
    for t in range(nt):
        c.dve("scalar_tensor_tensor", [pb2_b, ct[1]], [junk_b, pf_b], out=junk, in0=pb2[:, t * 128:(t + 1) * 128],
              scalar=1.0, in1=ct[0][:, 552:680], op0=ALU.mult, op1=ALU.mult, accum_out=pf_ap[:, t:t + 1])
    am, am_b = ar.alloc(nt * 32, F32), Buf("am")
    for t in range(nt):
        c.dve("tensor_scalar", [pf_b, ct[1]], [am_b], out=am[:, t * 32:(t + 1) * 32], in0=ct[0][:, 520:552],
              scalar1=pf_ap[:, t:t + 1], scalar2=None, op0=ALU.mult)
    cm, cm_b = ar.alloc(nt * 32, F32), Buf("cm")
    sm, sm_b = ar.alloc(nt * 32, F32), Buf("sm")
    emit_sincos(c, am, nt * 32, cm, sm, (am_b, cm_b, sm_b))
    c.sp("dma_start", [cm_b], [c.CM_b], out=c.CM.rearrange("(t p) j -> p t j", p=128),
         in_=cm.rearrange("p (t j) -> p t j", j=32))
    c.sp("dma_start", [sm_b], [c.SM_b], out=c.SM.rearrange("(t p) j -> p t j", p=128),
         in_=sm.rearrange("p (t j) -> p t j", j=32))
    ar.release()


def phase_mod(c, cvec, mod_w, mod_b, norm_g, depth):
    Tracker.phase = "mod"
    ar = c.arena
    ar.mark()
    cf, cf_b = ar.alloc(128, F32), Buf("cf")
    c.sp("dma_start", [], [cf_b], out=cf[0:KC, :], in_=cvec.rearrange("(k p) -> k p", p=128))
    cab, cab_b = ar.alloc(128, BF16), Buf("cab")
    c.act("activation", [cf_b], [cab_b], out=cab[0:KC, :], in_=cf[0:KC, :], func=AF.Silu)
    pt_ap, pt_buf = c.ptr[0]
    c.pe("transpose", [cab_b, c.ident[1]], [pt_buf], out=pt_ap[:, 0:KC], in_=cab[0:KC, :],
         identity=c.ident[0][0:KC, 0:KC])
    ca, ca_b = ar.alloc(KC, BF16), Buf("ca")
    c.act("activation", [pt_buf], [ca_b], out=ca, in_=pt_ap[:, 0:KC], func=AF.Copy)
    NB = 512
    nblk = 9 * D // NB
    wb = [(ar.alloc(KC * NB, BF16).rearrange("p (k n) -> p k n", k=KC), Buf(f"mw{i}")) for i in range(3)]
    row, row_b = ar.alloc(9 * D, F32), Buf("modrow")
    mb_ap, mb_b = ar.alloc(9 * D, F32), Buf("modb")
    ng_ap, ng_b = ar.alloc(3 * D, F32), Buf("normg")
    orow, orow_b = ar.alloc(9 * D, F32), Buf("orow")
    bi = 0
    for i in range(depth):
        c.sp("dma_start", [], [mb_b], out=mb_ap[0:1, :], in_=mod_b[i:i + 1, :])
        c.sp("dma_start", [], [ng_b], out=ng_ap[0:1, :], in_=norm_g[i:i + 1].rearrange("o s d -> o (s d)"))
        mw_v = mod_w[i].rearrange("(k p) n -> p k n", p=128)
        for b in range(nblk):
            w_ap, w_b = wb[bi % 3]
            p_ap, p_b = c.pb[bi % 2]
            bi += 1
            c.pool("dma_start", [], [w_b], out=w_ap, in_=mw_v[:, :, b * NB:(b + 1) * NB])
            for k in range(KC):
                c.pe("matmul", [ca_b, w_b], [p_b], p_ap[0:1, :], lhsT=ca[:, k:k + 1], rhs=w_ap[:, k, :],
                     start=(k == 0), stop=(k == KC - 1))
            c.dve("tensor_tensor", [p_b, mb_b], [row_b], out=row[0:1, b * NB:(b + 1) * NB], in0=p_ap[0:1, :],
                  in1=mb_ap[0:1, b * NB:(b + 1) * NB], op=ALU.add)
        for sl in range(3):
            sh = row[0:1, (3 * sl) * D:(3 * sl + 1) * D]
            sc = row[0:1, (3 * sl + 1) * D:(3 * sl + 2) * D]
            gt = row[0:1, (3 * sl + 2) * D:(3 * sl + 3) * D]
            oa = orow[0:1, (3 * sl) * D:(3 * sl + 1) * D]
            ob = orow[0:1, (3 * sl + 1) * D:(3 * sl + 2) * D]
            og = orow[0:1, (3 * sl + 2) * D:(3 * sl + 3) * D]
            c.dve("scalar_tensor_tensor", [row_b, ng_b], [orow_b], out=oa, in0=sc, scalar=1.0,
                  in1=ng_ap[0:1, sl * D:(sl + 1) * D], op0=ALU.add, op1=ALU.mult)
            c.dve("tensor_copy", [row_b], [orow_b], out=ob, in_=sh)
            c.dve("tensor_scalar", [row_b], [orow_b], out=og, in0=gt, scalar1=(1.0 if sl == 1 else 0.5),
                  scalar2=None, op0=ALU.mult)
        c.sp("dma_start", [orow_b], [c.modrows_b[i]], out=c.modrows[i:i + 1].rearrange("o s r d -> o (s r d)"),
             in_=orow[0:1, :])
    ar.release()


def ret_tables(c, dec_f, dec_b):
    ar = c.arena
    ct, ct_b = fetch_ctab(c)
    t = Ctx()
    raw, raw_b = ar.alloc(8, F32), Buf("decraw")
    c.sp("dma_start", [], [raw_b], out=raw[:, 0:4], in_=dec_f.partition_broadcast(128))
    c.sp("dma_start", [], [raw_b], out=raw[:, 4:8], in_=dec_b.partition_broadcast(128))
    lg, lg_b = ar.alloc(8, F32), Buf("lg")
    c.act("activation", [raw_b], [lg_b], out=lg, in_=raw, func=AF.Exp, scale=-1.0)
    c.act("activation", [lg_b], [lg_b], out=lg, in_=lg, func=AF.Ln, bias=1.0)
    c.dve("tensor_scalar", [lg_b], [lg_b], out=lg, in0=lg, scalar1=-1.0, scalar2=None, op0=ALU.mult)
    cols, cols_b = ar.alloc(24, F32), Buf("deccols")
    src = [(512, 0), (513, 4), (514, 0), (515, 4), (516, 0), (516, 4)]
    for kind, (ccol, lgo) in enumerate(src):
        for h in range(R_HEADS):
            c.act("activation", [lg_b, ct_b], [cols_b], out=cols[:, kind * 4 + h:kind * 4 + h + 1],
                  in_=ct[:, ccol:ccol + 1], func=AF.Exp, scale=lg[:, lgo + h:lgo + h + 1])
    t.cols, t.cols_b = cols, cols_b
    DT, DT_b = ar.alloc(4 * 128, F32), Buf("DT")
    e1, e1_b = ar.alloc(128, F32), Buf("e1")
    e2, e2_b = ar.alloc(128, F32), Buf("e2")
    for h in range(R_HEADS):
        c.act("activation", [lg_b, ct_b], [e1_b], out=e1, in_=ct[:, 0:128], func=AF.Exp, scale=lg[:, h:h + 1])
        c.dve("tensor_tensor", [e1_b, ct_b], [e1_b], out=e1, in0=e1, in1=ct[:, 128:256], op=ALU.mult)
        c.act("activation", [lg_b, ct_b], [e2_b], out=e2, in_=ct[:, 256:384], func=AF.Exp, scale=lg[:, 4 + h:5 + h])
        c.dve("tensor_tensor", [e2_b, ct_b], [e2_b], out=e2, in0=e2, in1=ct[:, 384:512], op=ALU.mult)
        c.dve("tensor_tensor", [e1_b, e2_b], [DT_b], out=DT[:, h * 128:(h + 1) * 128], in0=e1, in1=e2, op=ALU.add)
    t.DT, t.DT_b = DT, DT_b
    return t


def phase_ret_in(c, xin, w_in, rows3, S, dt):
    Tracker.phase = "ret_in"
    x_in, xin_b = xin
    ar = c.arena
    ar.mark()
    NKC = R_IN
    win = ar.alloc(KC * NKC, BF16).rearrange("p (k n) -> p k n", k=KC)
    win_b = [Buf(f"rwin{k}") for k in range(KC)]
    w_in_v = w_in.rearrange("(k p) n -> p k n", p=128)
    for k in range(KC):
        c.pool("dma_start", [], [win_b[k]], out=win[:, k, :], in_=w_in_v[:, k, :])
    load_bcast_rows(c, rows3)
    NT = 512
    nsup = S // NT
    hT = [ar.alloc(KC * NT, BF16).rearrange("p (k n) -> p k n", k=KC) for _ in range(2)]
    hT_b = [Buf("hT0"), Buf("hT1")]
    cs = [(ar.alloc(NT, F32), Buf(f"cs{i}")) for i in range(2)]
    sn = [(ar.alloc(NT, F32), Buf(f"sn{i}")) for i in range(2)]
    tmp = [(ar.alloc(NT, F32), Buf(f"rt{i}")) for i in range(4)]
    qst = [(ar.alloc(2 * NT, BF16).rearrange("p (a n) -> p a n", a=2), Buf(f"qst{i}")) for i in range(2)]
    KDf, KDf_b = ar.alloc(1024, F32), Buf("KDf")
    KDb, KDb_b = ar.alloc(1024, F32), Buf("KDb")
    for h in range(R_HEADS):
        for (KD, KD_b, kind) in ((KDf, KDf_b, 2), (KDb, KDb_b, 3)):
            c.dve("memset", [], [KD_b], KD[:, h * 256:(h + 1) * 256], 1.0 / 16.0)
            c.dve("tensor_scalar", [KD_b, dt.cols_b], [KD_b], out=KD[:, h * 256:(h + 1) * 256],
                  in0=KD[:, h * 256:(h + 1) * 256], scalar1=dt.cols[:, kind * 4 + h:kind * 4 + h + 1],
                  scalar2=None, op0=ALU.mult)
    kst = [(ar.alloc(1024, BF16), Buf(f"kst{i}")) for i in range(2)]
    vst = [(ar.alloc(512, BF16), Buf(f"vst{i}")) for i in range(2)]
    gst = [(ar.alloc(512, F32), Buf(f"gst{i}")) for i in range(2)]
    ktb = [(ar.alloc(8 * NT, BF16).rearrange("p (a n) -> p a n", a=8), Buf("ktb"))]
    xin_v = x_in.rearrange("(n p) d -> n p d", p=128)
    pbi = 0
    qi = 0
    vi = 0

    def ri_front(su_):
        tok_ = slice(su_ * NT, (su_ + 1) * NT)
        c.sp("dma_start", [c.CR_b], [cs[su_ % 2][1]], out=cs[su_ % 2][0], in_=c.CR[:, tok_])
        c.sp("dma_start", [c.SR_b], [sn[su_ % 2][1]], out=sn[su_ % 2][0], in_=c.SR[:, tok_])
        for ts_ in range(4):
            emit_front(c, xin_v[su_ * 4 + ts_], xin_b[su_ * 4 + ts_], hT[su_ % 2], hT_b[su_ % 2], ts_ * 128)

    for su in range(nsup):
        hTs, hTb = hT[su % 2], hT_b[su % 2]
        tok = slice(su * NT, (su + 1) * NT)
        c_ap, c_b = cs[su % 2]
        s_ap, s_b = sn[su % 2]
        if su == 0:
            ri_front(0)
        kt_ap, kt_b = ktb[0]
        if su == 0:
            dbg(c, "hT", hTs, hTb)
            dbg(c, "win0", win[:, 0, :], win_b[0])
            dbg(c, "win7", win[:, 7, :], win_b[7])
        for which in range(2):
            for h in range(R_HEADS):
                base = which * R_QK + h * R_DK
                p1, p1_b = c.pb[pbi % 6]
                p2, p2_b = c.pb[(pbi + 1) % 6]
                pbi += 2
                for half, (pp, pp_b) in enumerate(((p1, p1_b), (p2, p2_b))):
                    for k in range(KC):
                        c.pe("matmul", [win_b[k], hTb], [pp_b], pp,
                             lhsT=win[:, k, base + half * 128:base + (half + 1) * 128], rhs=hTs[:, k, :],
                             start=(k == 0), stop=(k == KC - 1))
                t1, t1b = tmp[0]
                t2, t2b = tmp[1]
                t3, t3b = tmp[2]
                t4, t4b = tmp[3]
                c.dve("tensor_tensor", [p1_b, c_b], [t1b], out=t1, in0=p1, in1=c_ap, op=ALU.mult)
                c.dve("tensor_tensor", [p2_b, s_b], [t2b], out=t2, in0=p2, in1=s_ap, op=ALU.mult)
                c.dve("tensor_tensor", [p1_b, s_b], [t3b], out=t3, in0=p1, in1=s_ap, op=ALU.mult)
                c.dve("tensor_tensor", [p2_b, c_b], [t4b], out=t4, in0=p2, in1=c_ap, op=ALU.mult)
                if which == 0:
                    o_ap, o_b = qst[qi % 2]
                    qi += 1
                    o1, o2 = o_ap[:, 0, :], o_ap[:, 1, :]
                else:
                    o_ap, o_b = kt_ap, kt_b
                    o1, o2 = kt_ap[:, 2 * h, :], kt_ap[:, 2 * h + 1, :]
                c.dve("tensor_tensor", [t1b, t2b], [o_b], out=o1, in0=t1, in1=t2, op=ALU.subtract)
                c.dve("tensor_tensor", [t3b, t4b], [o_b], out=o2, in0=t3, in1=t4, op=ALU.add)
                if which == 0:
                    dst = c.QT[h * 256:(h + 1) * 256, tok].rearrange("(a p) n -> p a n", p=128)
                    c.sp("dma_start", [o_b], [c.QT_b[su][h]], out=dst, in_=o_ap)
            if which == 1:
                dst = c.KT[:, tok].rearrange("(a p) n -> p a n", p=128)
                c.sp("dma_start", [kt_b], [c.KT_b[su]], out=dst, in_=kt_ap)
        for ts in range(4):
            pt_ap, pt_buf = c.ptr[0]
            for a in range(8):
                c.pe("transpose", [kt_b, c.ident[1]], [pt_buf], out=pt_ap[:, a * 128:(a + 1) * 128],
                     in_=kt_ap[:, a, ts * 128:(ts + 1) * 128], identity=c.ident[0])
            for di, (KD, KD_b, dst, dst_b) in enumerate(((KDf, KDf_b, c.KF, c.KF_b), (KDb, KDb_b, c.KB, c.KB_b))):
                k_ap, k_b = kst[di]
                c.dve("tensor_tensor", [pt_buf, KD_b], [k_b], out=k_ap, in0=pt_ap, in1=KD, op=ALU.mult)
                c.sp("dma_start", [k_b], [dst_b[su * 4 + ts]], out=dst[(su * 4 + ts) * 128:(su * 4 + ts + 1) * 128, :],
                     in_=k_ap)
        if su + 1 < nsup:
            ri_front(su + 1)
        for ts in range(4):
            rows = slice((su * 4 + ts) * 128, (su * 4 + ts + 1) * 128)
            for nb in range(8):
                pp, pp_b = c.pb[pbi % 6]
                pbi += 1
                col0 = 2 * R_QK + nb * 512
                for k in range(KC):
                    c.pe("matmul", [win_b[k], hTb], [pp_b], pp, lhsT=hTs[:, k, ts * 128:(ts + 1) * 128],
                         rhs=win[:, k, col0:col0 + 512], start=(k == 0), stop=(k == KC - 1))
                if nb < 4:
                    v_ap, v_b = vst[vi % 2]
                    vi += 1
                    c.act("activation", [pp_b], [v_b], out=v_ap, in_=pp, func=AF.Copy)
                    c.sp("dma_start", [v_b], [c.V_b[su * 4 + ts][nb]], out=c.V[rows, nb * 512:(nb + 1) * 512], in_=v_ap)
                else:
                    g_ap, g_b = gst[vi % 2]
                    vi += 1
                    c.act("activation", [pp_b], [g_b], out=g_ap, in_=pp, func=AF.Silu)
                    c.sp("dma_start", [g_b], [c.Gs_b[su * 4 + ts][nb - 4]], out=c.Gs[rows, (nb - 4) * 512:(nb - 3) * 512],
                         in_=g_ap)
    ar.release()


def phase_ret_bwd(c, S, dt):
    Tracker.phase = "ret_bwd"
    ar = c.arena
    ar.mark()
    nch = S // 128
    St = ar.alloc(R_HEADS * 1024, F32).rearrange("p (h n) -> p h n", h=R_HEADS)
    St_b = [Buf(f"St{h}") for h in range(R_HEADS)]
    Sb = ar.alloc(R_HEADS * 1024, BF16).rearrange("p (h n) -> p h n", h=R_HEADS)
    Sb_b = [Buf(f"Sb{h}") for h in range(R_HEADS)]
    for h in range(R_HEADS):
        c.dve("memset", [], [St_b[h]], St[:, h], 0.0)
        c.dve("memset", [], [Sb_b[h]], Sb[:, h], 0.0)
    kb = [(ar.alloc(1024, BF16), Buf(f"kbt{i}")) for i in range(2)]
    vt = [(ar.alloc(2048, BF16), Buf(f"vt{i}")) for i in range(2)]
    pbi = 0
    for i, ch in enumerate(range(nch - 1, -1, -1)):
        k_ap, k_b = kb[i % 2]
        v_ap, v_b = vt[i % 2]
        rows = slice(ch * 128, (ch + 1) * 128)
        c.sp("dma_start", [c.KB_b[ch]], [k_b], out=k_ap, in_=c.KB[rows, :])
        c.sp("dma_start", c.V_b[ch], [v_b], out=v_ap, in_=c.V[rows, :])
        for h in range(R_HEADS):
            c.sp("dma_start", [Sb_b[h]], [c.SB_b[ch][h]], out=c.SB[ch, h], in_=Sb[:, h])
            for a in range(2):
                pp, pp_b = c.pb[pbi % 8]
                pbi += 1
                c.pe("matmul", [k_b, v_b], [pp_b], pp, lhsT=k_ap[:, h * 256 + a * 128:h * 256 + (a + 1) * 128],
                     rhs=v_ap[:, h * 512:(h + 1) * 512], start=True, stop=True)
                c.dve("scalar_tensor_tensor", [St_b[h], dt.cols_b, pp_b], [St_b[h]], out=St[:, h, a * 512:(a + 1) * 512],
                      in0=St[:, h, a * 512:(a + 1) * 512], scalar=dt.cols[:, 5 * 4 + h:5 * 4 + h + 1], in1=pp,
                      op0=ALU.mult, op1=ALU.add)
            c.act("activation", [St_b[h]], [Sb_b[h]], out=Sb[:, h], in_=St[:, h], func=AF.Copy)
    ar.release()


def phase_ret_fwd(c, xio, w_out, gn_g, gn_b, S, dt):
    Tracker.phase = "ret_fwd"
    x_io, xio_b = xio
    ar = c.arena
    ar.mark()
    nch = S // 128
    wo = ar.alloc(16 * D, BF16).rearrange("p (k n) -> p k n", k=16)
    wo_b = Buf("rwo")
    c.pool("dma_start", [], [wo_b], out=wo, in_=w_out.rearrange("(k p) n -> p k n", p=128))
    gg, gg_b = ar.alloc(R_VTOT, F32), Buf("gng")
    gb, gb_b = ar.alloc(R_VTOT, F32), Buf("gnb")
    c.sp("dma_start", [], [gg_b], out=gg, in_=gn_g.partition_broadcast(128))
    c.sp("dma_start", [], [gb_b], out=gb, in_=gn_b.partition_broadcast(128))
    eps, eps_b = ar.alloc(8, F32)[:, 0:1], Buf("gneps")
    c.dve("memset", [], [eps_b], eps, GN_EPS)
    St = ar.alloc(R_HEADS * 1024, F32).rearrange("p (h n) -> p h n", h=R_HEADS)
    St_b = [Buf(f"Sf{h}") for h in range(R_HEADS)]
    Sb = ar.alloc(R_HEADS * 1024, BF16).rearrange("p (h n) -> p h n", h=R_HEADS)
    Sb_b = [Buf(f"Sfb{h}") for h in range(R_HEADS)]
    for h in range(R_HEADS):
        c.dve("memset", [], [St_b[h]], St[:, h], 0.0)
        c.dve("memset", [], [Sb_b[h]], Sb[:, h], 0.0)
    qt = [(ar.alloc(1024, BF16).rearrange("p (a n) -> p a n", a=8), Buf(f"qt{i}")) for i in range(2)]
    kt = [(ar.alloc(1024, BF16).rearrange("p (a n) -> p a n", a=8), Buf(f"kt{i}")) for i in range(2)]
    kf = [(ar.alloc(1024, BF16), Buf(f"kf{i}")) for i in range(2)]
    vt = [(ar.alloc(2048, BF16), Buf(f"vt{i}")) for i in range(2)]
    gt = [(ar.alloc(2048, F32), Buf(f"gt{i}")) for i in range(2)]
    sbt = [(ar.alloc(R_HEADS * 1024, BF16).rearrange("p (h n) -> p h n", h=R_HEADS), Buf(f"sbt{i}"))
           for i in range(2)]
    y, y_b = ar.alloc(R_VTOT, F32), [Buf(f"y{h}") for h in range(R_HEADS)]
    z, z_b = ar.alloc(R_VTOT, BF16), Buf("z")
    zT = ar.alloc(16 * 128, BF16).rearrange("p (k n) -> p k n", k=16)
    zT_b = Buf("zT")
    pT = [(ar.alloc(128, BF16), Buf(f"pT{i}")) for i in range(R_HEADS)]
    st6, st6_b = ar.alloc(4 * 16, F32), [Buf(f"st6{h}") for h in range(R_HEADS)]
    ps_bufs = [Buf(f"psS{h}") for h in range(R_HEADS)]
    xv = x_io.rearrange("(n p) d -> n p d", p=128)
    def issue_loads(ch_):
        q_ap_, q_b_ = qt[ch_ % 2]
        k_ap_, k_b_ = kt[ch_ % 2]
        f_ap_, f_b_ = kf[ch_ % 2]
        v_ap_, v_b_ = vt[ch_ % 2]
        g_ap_, g_b_ = gt[ch_ % 2]
        s_ap_, s_b_ = sbt[ch_ % 2]
        rows_ = slice(ch_ * 128, (ch_ + 1) * 128)
        su_ = ch_ // 4
        c.sp("dma_start", c.QT_b[su_], [q_b_], out=q_ap_, in_=c.QT[:, rows_].rearrange("(a p) n -> p a n", p=128))
        c.sp("dma_start", [c.KT_b[su_]], [k_b_], out=k_ap_, in_=c.KT[:, rows_].rearrange("(a p) n -> p a n", p=128))
        c.sp("dma_start", [c.KF_b[ch_]], [f_b_], out=f_ap_, in_=c.KF[rows_, :])
        c.sp("dma_start", c.V_b[ch_], [v_b_], out=v_ap_, in_=c.V[rows_, :])
        c.sp("dma_start", c.Gs_b[ch_], [g_b_], out=g_ap_, in_=c.Gs[rows_, :])
        c.sp("dma_start", c.SB_b[ch_], [s_b_], out=s_ap_, in_=c.SB[ch_].rearrange("h p n -> p h n"))

    issue_loads(0)
    for ch in range(nch):
        i = ch
        q_ap, q_b = qt[i % 2]
        k_ap, k_b = kt[i % 2]
        f_ap, f_b = kf[i % 2]
        v_ap, v_b = vt[i % 2]
        g_ap, g_b = gt[i % 2]
        s_ap, s_b = sbt[i % 2]
        rows = slice(ch * 128, (ch + 1) * 128)
        su = ch // 4
        if ch + 1 < nch:
            issue_loads(ch + 1)
        for h in range(R_HEADS):
            ps_ap, ps_b = c.pb[0]
            st_ap = ps_ap[:, h * 128:(h + 1) * 128]
            for a in range(2):
                c.pe("matmul", [k_b, q_b], [ps_b], st_ap, lhsT=k_ap[:, 2 * h + a, :], rhs=q_ap[:, 2 * h + a, :],
                     start=(a == 0), stop=(a == 1))
            p_ap, p_b = pT[h]
            c.dve("tensor_tensor", [ps_b, dt.DT_b], [p_b], out=p_ap, in0=st_ap,
                  in1=dt.DT[:, h * 128:(h + 1) * 128], op=ALU.mult)
        for h in range(R_HEADS):
            p_ap, p_b = pT[h]
            y0, y0_b = c.pb[1]
            y1, y1_b = c.pb[2]
            y2, y2_b = c.pb[3]
            vh = v_ap[:, h * 512:(h + 1) * 512]
            c.pe("matmul", [p_b, v_b], [y0_b], y0, lhsT=p_ap, rhs=vh, start=True, stop=True)
            for a in range(2):
                c.pe("matmul", [q_b, Sb_b[h]], [y1_b], y1, lhsT=q_ap[:, 2 * h + a, :], rhs=Sb[:, h, a * 512:(a + 1) * 512],
                     start=(a == 0), stop=(a == 1))
            for a in range(2):
                c.pe("matmul", [q_b, s_b], [y2_b], y2, lhsT=q_ap[:, 2 * h + a, :], rhs=s_ap[:, h, a * 512:(a + 1) * 512],
                     start=(a == 0), stop=(a == 1))
            yh = y[:, h * 512:(h + 1) * 512]
            c.act("activation", [y0_b], [y_b[h]], out=yh, in_=y0, func=AF.Copy)
            c.dve("scalar_tensor_tensor", [y1_b, dt.cols_b, y_b[h]], [y_b[h]], out=yh, in0=y1,
                  scalar=dt.cols[:, 0 * 4 + h:0 * 4 + h + 1], in1=yh, op0=ALU.mult, op1=ALU.add)
            c.dve("scalar_tensor_tensor", [y2_b, dt.cols_b, y_b[h]], [y_b[h]], out=yh, in0=y2,
                  scalar=dt.cols[:, 1 * 4 + h:1 * 4 + h + 1], in1=yh, op0=ALU.mult, op1=ALU.add)
            for a in range(2):
                u, u_b = c.pb[4 + a]
                c.pe("matmul", [f_b, v_b], [u_b], u, lhsT=f_ap[:, h * 256 + a * 128:h * 256 + (a + 1) * 128], rhs=vh,
                     start=True, stop=True)
                c.dve("scalar_tensor_tensor", [St_b[h], dt.cols_b, u_b], [St_b[h]], out=St[:, h, a * 512:(a + 1) * 512],
                      in0=St[:, h, a * 512:(a + 1) * 512], scalar=dt.cols[:, 4 * 4 + h:4 * 4 + h + 1], in1=u,
                      op0=ALU.mult, op1=ALU.add)
            c.act("activation", [St_b[h]], [Sb_b[h]], out=Sb[:, h], in_=St[:, h], func=AF.Copy)
        HS = range(R_HEADS)
        s6s = [st6[:, h * 16:h * 16 + 16] for h in HS]
        yhs = [y[:, h * 512:(h + 1) * 512] for h in HS]
        for h in HS:
            c.dve("bn_stats", [y_b[h]], [st6_b[h]], out=s6s[h][:, 0:6], in_=yhs[h])
        for h in HS:
            c.dve("bn_aggr", [st6_b[h]], [st6_b[h]], out=s6s[h][:, 6:8], in_=s6s[h][:, 0:6])
        for h in HS:
            c.act("activation", [st6_b[h], eps_b], [st6_b[h]], out=s6s[h][:, 8:9], in_=s6s[h][:, 7:8], func=AF.Sqrt,
                  bias=eps, scale=1.0)
        for h in HS:
            c.dve("reciprocal", [st6_b[h]], [st6_b[h]], out=s6s[h][:, 9:10], in_=s6s[h][:, 8:9])
        for h in HS:
            c.dve("tensor_tensor", [st6_b[h]], [st6_b[h]], out=s6s[h][:, 11:12], in0=s6s[h][:, 6:7],
                  in1=s6s[h][:, 9:10], op=ALU.mult)
        for h in HS:
            c.dve("tensor_scalar", [st6_b[h]], [st6_b[h]], out=s6s[h][:, 10:11], in0=s6s[h][:, 11:12], scalar1=-1.0,
                  scalar2=None, op0=ALU.mult)
        for h in HS:
            c.act("activation", [y_b[h], st6_b[h]], [y_b[h]], out=yhs[h], in_=yhs[h], func=AF.Identity,
                  scale=s6s[h][:, 9:10], bias=s6s[h][:, 10:11])
        for h in HS:
            hs = slice(h * 512, (h + 1) * 512)
            c.dve("tensor_tensor", [y_b[h], gg_b], [y_b[h]], out=yhs[h], in0=yhs[h], in1=gg[:, hs], op=ALU.mult)
        for h in HS:
            hs = slice(h * 512, (h + 1) * 512)
            c.dve("tensor_tensor", [y_b[h], gb_b], [y_b[h]], out=yhs[h], in0=yhs[h], in1=gb[:, hs], op=ALU.add)
        for h in HS:
            hs = slice(h * 512, (h + 1) * 512)
            c.dve("tensor_tensor", [y_b[h], g_b], [z_b], out=z[:, hs], in0=yhs[h], in1=g_ap[:, hs], op=ALU.mult)
        pt_ap, pt_buf = c.ptr[0]
        for half in range(2):
            for a in range(8):
                kc = half * 8 + a
                c.pe("transpose", [z_b, c.ident[1]], [pt_buf], out=pt_ap[:, a * 128:(a + 1) * 128],
                     in_=z[:, kc * 128:(kc + 1) * 128], identity=c.ident[0])
            c.act("activation", [pt_buf], [zT_b], out=zT[:, half * 8:(half + 1) * 8, :],
                  in_=pt_ap.rearrange("p (k n) -> p k n", k=8), func=AF.Copy)
        for nb in range(2):
            o_ap, o_buf = c.pb[6]
            for kc in range(16):
                c.pe("matmul", [zT_b, wo_b], [o_buf], o_ap, lhsT=zT[:, kc, :], rhs=wo[:, kc, nb * 512:(nb + 1) * 512],
                     start=(kc == 0), stop=(kc == 15))
            csl = slice(nb * 512, (nb + 1) * 512)
            emit_resid(c, o_ap, o_buf, xv[ch][:, csl], xio_b[ch][nb], xv[ch][:, csl], xio_b[ch][nb], c.G[0][:, csl])
    ar.release()


def alloc_mla_scratch(c, nc, S):
    nt = S // 128
    c.QTN = nc.dram_tensor("QTN", [M_HEADS, 128, S], BF16, kind=SCRATCH_KIND).ap()
    c.QTR = nc.dram_tensor("QTR", [M_HEADS, 64, S], BF16, kind=SCRATCH_KIND).ap()
    c.KTN = nc.dram_tensor("KTN", [M_HEADS, 128, S], BF16, kind=SCRATCH_KIND).ap()
    c.KTR = nc.dram_tensor("KTR", [64, S], BF16, kind=SCRATCH_KIND).ap()
    c.VM = nc.dram_tensor("VM", [S, M_HEADS * M_V], BF16, kind=SCRATCH_KIND).ap()
    c.QTN_b = [Buf("QTN") for _ in range(nt)]
    c.QTR_b = [Buf("QTR") for _ in range(nt)]
    c.KTN_b = [[Buf("KTN") for _ in range(M_HEADS)] for _ in range(S // 512)]
    c.KTR_b = [Buf("KTR") for _ in range(S // 512)]
    c.VM_b = [Buf("VM") for _ in range(nt)]


def phase_mla_in(c, xin, w_down, qng, kvng, w_uq, w_ukv, rows3, S):
    Tracker.phase = "mla_in"
    x_in, xin_b = xin
    ar = c.arena
    ar.mark()
    nt = S // 128
    NT = 512
    nsup = S // NT
    wdn = ar.alloc(KC * M_DOWN, BF16).rearrange("p (k n) -> p k n", k=KC)
    wdn_b = Buf("wdn")
    c.pool("dma_start", [], [wdn_b], out=wdn, in_=w_down.rearrange("(k p) n -> p k n", p=128))
    wuq = ar.alloc(3 * 1536, BF16).rearrange("p (k n) -> p k n", k=3)
    wuq_b = Buf("wuq")
    c.pool("dma_start", [], [wuq_b], out=wuq, in_=w_uq.rearrange("(k p) n -> p k n", p=128))
    wuk = ar.alloc(2 * 1024, BF16).rearrange("p (k n) -> p k n", k=2)
    wuk_b = Buf("wuk")
    wuv = ar.alloc(2 * 1024, BF16).rearrange("p (k n) -> p k n", k=2)
    wuv_b = Buf("wuv")
    for kc in range(2):
        src = w_ukv[kc * 128:(kc + 1) * 128, :].rearrange("p (h two d) -> p h two d", two=2, d=128)
        c.pool("dma_start", [], [wuk_b], out=wuk[:, kc, :].rearrange("p (h d) -> p h d", d=128), in_=src[:, :, 0, :])
        c.pool("dma_start", [], [wuv_b], out=wuv[:, kc, :].rearrange("p (h d) -> p h d", d=128), in_=src[:, :, 1, :])
    qg, qg_b = ar.alloc(Q_LORA, F32), Buf("qg")
    kg, kg_b = ar.alloc(KV_LORA, F32), Buf("kg")
    c.sp("dma_start", [], [qg_b], out=qg, in_=qng.partition_broadcast(128))
    c.sp("dma_start", [], [kg_b], out=kg, in_=kvng.partition_broadcast(128))
    cm, cm_b = ar.alloc(nt * 32, F32), Buf("cmr")
    sm, sm_b = ar.alloc(nt * 32, F32), Buf("smr")
    c.sp("dma_start", [c.CM_b], [cm_b], out=cm.rearrange("p (t j) -> p t j", j=32),
         in_=c.CM.rearrange("(t p) j -> p t j", p=128))
    c.sp("dma_start", [c.SM_b], [sm_b], out=sm.rearrange("p (t j) -> p t j", j=32),
         in_=c.SM.rearrange("(t p) j -> p t j", p=128))
    load_bcast_rows(c, rows3)
    hT = [ar.alloc(KC * NT, BF16).rearrange("p (k n) -> p k n", k=KC) for _ in range(2)]
    hT_b = [Buf("hT0"), Buf("hT1")]
    cd = [(ar.alloc(M_DOWN, F32), Buf(f"cd{i}")) for i in range(2)]
    cqn, cqn_b = ar.alloc(Q_LORA, BF16), Buf("cqn")
    ckn, ckn_b = ar.alloc(KV_LORA, BF16), Buf("ckn")
    krr, krr_b = ar.alloc(64, BF16), Buf("krr")
    st, st_b = ar.alloc(8, F32), Buf("mst")
    cqT = ar.alloc(3 * NT, BF16).rearrange("p (k n) -> p k n", k=3)
    cqT_b = Buf("cqT")
    ckT = ar.alloc(2 * NT, BF16).rearrange("p (k n) -> p k n", k=2)
    ckT_b = Buf("ckT")
    krT, krT_b = ar.alloc(NT, BF16), Buf("krT")
    kts = [(ar.alloc(NT, BF16), Buf(f"kts{i}")) for i in range(2)]
    vst = [(ar.alloc(1024, BF16), Buf(f"mvst{i}")) for i in range(2)]
    qf, qf_b = ar.alloc(1536, F32), Buf("qf")
    qb, qb_b = ar.alloc(1536, BF16), Buf("qb")
    rt = [(ar.alloc(256, F32), Buf(f"mrt{i}")) for i in range(4)]
    kt4 = [(ar.alloc(32, F32), Buf(f"kt4{i}")) for i in range(4)]
    qtn = [(ar.alloc(1024, BF16).rearrange("p (h n) -> p h n", h=8), Buf(f"qtn{i}")) for i in range(2)]
    qtr = [(ar.alloc(1024, BF16).rearrange("p (h n) -> p h n", h=8), Buf(f"qtr{i}")) for i in range(2)]
    eq, eq_b = ar.alloc(8, F32)[:, 0:1], Buf("meps")
    c.dve("memset", [], [eq_b], eq, RMS_EPS)
    xin_v = x_in.rearrange("(n p) d -> n p d", p=128)
    ki = 0
    def mi_front(su_):
        for ts_ in range(4):
            emit_front(c, xin_v[su_ * 4 + ts_], xin_b[su_ * 4 + ts_], hT[su_ % 2], hT_b[su_ % 2], ts_ * 128)

    mi_front(0)
    for su in range(nsup):
        hTs, hTb = hT[su % 2], hT_b[su % 2]
        tok = slice(su * NT, (su + 1) * NT)
        for ts in range(4):
            tile_i = su * 4 + ts
            tcs = slice(ts * 128, (ts + 1) * 128)
            d0, d0_b = c.pb[0]
            d1, d1_b = c.pb[1]
            for k in range(KC):
                c.pe("matmul", [hTb, wdn_b], [d0_b], d0, lhsT=hTs[:, k, tcs], rhs=wdn[:, k, 0:512],
                     start=(k == 0), stop=(k == KC - 1))
            for k in range(KC):
                c.pe("matmul", [hTb, wdn_b], [d1_b], d1[:, 0:192], lhsT=hTs[:, k, tcs], rhs=wdn[:, k, 512:704],
                     start=(k == 0), stop=(k == KC - 1))
            cd_ap, cd_b = cd[tile_i % 2]
            c.act("activation", [d0_b], [cd_b], out=cd_ap[:, 0:512], in_=d0, func=AF.Copy)
            c.act("activation", [d1_b], [cd_b], out=cd_ap[:, 512:704], in_=d1[:, 0:192], func=AF.Copy)
            c.dve("scalar_tensor_tensor", [cd_b], [cqn_b, st_b], out=cqn, in0=cd_ap[:, 0:384], scalar=1.0,
                  in1=cd_ap[:, 0:384], op0=ALU.mult, op1=ALU.mult, accum_out=st[:, 0:1])
            c.dve("scalar_tensor_tensor", [cd_b], [ckn_b, st_b], out=ckn, in0=cd_ap[:, 384:640], scalar=1.0,
                  in1=cd_ap[:, 384:640], op0=ALU.mult, op1=ALU.mult, accum_out=st[:, 1:2])
            c.act("activation", [st_b, eq_b], [st_b], out=st[:, 2:3], in_=st[:, 0:1], func=AF.Sqrt,
                  scale=1.0 / Q_LORA, bias=eq)
            c.act("activation", [st_b, eq_b], [st_b], out=st[:, 3:4], in_=st[:, 1:2], func=AF.Sqrt,
                  scale=1.0 / KV_LORA, bias=eq)
            c.dve("reciprocal", [st_b], [st_b], out=st[:, 4:6], in_=st[:, 2:4])
            c.dve("scalar_tensor_tensor", [cd_b, st_b, qg_b], [cqn_b], out=cqn, in0=cd_ap[:, 0:384],
                  scalar=st[:, 4:5], in1=qg, op0=ALU.mult, op1=ALU.mult)
            c.dve("scalar_tensor_tensor", [cd_b, st_b, kg_b], [ckn_b], out=ckn, in0=cd_ap[:, 384:640],
                  scalar=st[:, 5:6], in1=kg, op0=ALU.mult, op1=ALU.mult)
            cs_t = cm[:, tile_i * 32:(tile_i + 1) * 32]
            sn_t = sm[:, tile_i * 32:(tile_i + 1) * 32]
            x1, x2 = cd_ap[:, 640:672], cd_ap[:, 672:704]
            (a1, a1b), (a2, a2b), (a3, a3b), (a4, a4b) = kt4
            c.dve("tensor_tensor", [cd_b, cm_b], [a1b], out=a1, in0=x1, in1=cs_t, op=ALU.mult)
            c.dve("tensor_tensor", [cd_b, sm_b], [a2b], out=a2, in0=x2, in1=sn_t, op=ALU.mult)
            c.dve("tensor_tensor", [cd_b, sm_b], [a3b], out=a3, in0=x1, in1=sn_t, op=ALU.mult)
            c.dve("tensor_tensor", [cd_b, cm_b], [a4b], out=a4, in0=x2, in1=cs_t, op=ALU.mult)
            c.dve("tensor_tensor", [a1b, a2b], [krr_b], out=krr[:, 0:32], in0=a1, in1=a2, op=ALU.subtract)
            c.dve("tensor_tensor", [a3b, a4b], [krr_b], out=krr[:, 32:64], in0=a3, in1=a4, op=ALU.add)
            pt_ap, pt_buf = c.ptr[0]
            for a in range(3):
                c.pe("transpose", [cqn_b, c.ident[1]], [pt_buf], out=pt_ap[:, a * 128:(a + 1) * 128],
                     in_=cqn[:, a * 128:(a + 1) * 128], identity=c.ident[0])
            for a in range(2):
                c.pe("transpose", [ckn_b, c.ident[1]], [pt_buf], out=pt_ap[:, 384 + a * 128:384 + (a + 1) * 128],
                     in_=ckn[:, a * 128:(a + 1) * 128], identity=c.ident[0])
            c.pe("transpose", [krr_b, c.ident[1]], [pt_buf], out=pt_ap[0:64, 640:768], in_=krr,
                 identity=c.ident[0])
            c.act("activation", [pt_buf], [cqT_b], out=cqT[:, :, tcs],
                  in_=pt_ap[:, 0:384].rearrange("p (k n) -> p k n", k=3), func=AF.Copy)
            c.act("activation", [pt_buf], [ckT_b], out=ckT[:, :, tcs],
                  in_=pt_ap[:, 384:640].rearrange("p (k n) -> p k n", k=2), func=AF.Copy)
            c.act("activation", [pt_buf], [krT_b], out=krT[0:64, tcs], in_=pt_ap[0:64, 640:768], func=AF.Copy)
        if su + 1 < nsup:
            mi_front(su + 1)
        c.sp("dma_start", [krT_b], [c.KTR_b[su]], out=c.KTR[:, tok], in_=krT[0:64, :])
        for h in range(M_HEADS):
            pp, pp_b = c.pb[2 + h % 2]
            for kc in range(2):
                c.pe("matmul", [wuk_b, ckT_b], [pp_b], pp, lhsT=wuk[:, kc, h * 128:(h + 1) * 128], rhs=ckT[:, kc, :],
                     start=(kc == 0), stop=(kc == 1))
            k_ap, k_b = kts[ki % 2]
            ki += 1
            c.act("activation", [pp_b], [k_b], out=k_ap, in_=pp, func=AF.Copy)
            c.sp("dma_start", [k_b], [c.KTN_b[su][h]], out=c.KTN[h][:, tok], in_=k_ap)
        for ts in range(4):
            tile_i = su * 4 + ts
            tcs = slice(ts * 128, (ts + 1) * 128)
            rows = slice(tile_i * 128, (tile_i + 1) * 128)
            v_ap, v_b = vst[tile_i % 2]
            for nb in range(2):
                pp, pp_b = c.pb[2 + nb]
                for kc in range(2):
                    c.pe("matmul", [wuv_b, ckT_b], [pp_b], pp, lhsT=ckT[:, kc, tcs], rhs=wuv[:, kc, nb * 512:(nb + 1) * 512],
                         start=(kc == 0), stop=(kc == 1))
                c.act("activation", [pp_b], [v_b], out=v_ap[:, nb * 512:(nb + 1) * 512], in_=pp, func=AF.Copy)
            c.sp("dma_start", [v_b], [c.VM_b[tile_i]], out=c.VM[rows, :], in_=v_ap)
            for nb in range(3):
                pp, pp_b = c.pb[(4, 5, 1)[nb]]
                for kc in range(3):
                    c.pe("matmul", [wuq_b, cqT_b], [pp_b], pp, lhsT=cqT[:, kc, tcs], rhs=wuq[:, kc, nb * 512:(nb + 1) * 512],
                         start=(kc == 0), stop=(kc == 2))
                c.act("activation", [pp_b], [qf_b], out=qf[:, nb * 512:(nb + 1) * 512], in_=pp, func=AF.Copy)
            qf3 = qf.rearrange("p (h d) -> p h d", h=8)
            qb3 = qb.rearrange("p (h d) -> p h d", h=8)
            cs_t = cm[:, tile_i * 32:(tile_i + 1) * 32].unsqueeze(1).broadcast_to([128, 8, 32])
            sn_t = sm[:, tile_i * 32:(tile_i + 1) * 32].unsqueeze(1).broadcast_to([128, 8, 32])
            x1, x2 = qf3[:, :, 128:160], qf3[:, :, 160:192]
            rr = [(r[0].rearrange("p (h j) -> p h j", h=8), r[1]) for r in rt]
            (a1, a1b), (a2, a2b), (a3, a3b), (a4, a4b) = rr
            c.dve("tensor_tensor", [qf_b, cm_b], [a1b], out=a1, in0=x1, in1=cs_t, op=ALU.mult)
            c.dve("tensor_tensor", [qf_b, sm_b], [a2b], out=a2, in0=x2, in1=sn_t, op=ALU.mult)
            c.dve("tensor_tensor", [qf_b, sm_b], [a3b], out=a3, in0=x1, in1=sn_t, op=ALU.mult)
            c.dve("tensor_tensor", [qf_b, cm_b], [a4b], out=a4, in0=x2, in1=cs_t, op=ALU.mult)
            c.dve("tensor_tensor", [a1b, a2b], [qb_b], out=qb3[:, :, 128:160], in0=a1, in1=a2, op=ALU.subtract)
            c.dve("tensor_tensor", [a3b, a4b], [qb_b], out=qb3[:, :, 160:192], in0=a3, in1=a4, op=ALU.add)
            c.dve("tensor_copy", [qf_b], [qb_b], out=qb3[:, :, 0:128], in_=qf3[:, :, 0:128])
            p6, p6_b = c.pb[6][0].bitcast(BF16), c.pb[6][1]
            p7, p7_b = c.ptr[0]
            for h in range(M_HEADS):
                c.pe("transpose", [qb_b, c.ident[1]], [p6_b], out=p6[:, h * 128:(h + 1) * 128], in_=qb3[:, h, 0:128],
                     identity=c.ident[0])
            for h in range(M_HEADS):
                c.pe("transpose", [qb_b, c.ident[1]], [p7_b], out=p7[0:64, h * 128:(h + 1) * 128],
                     in_=qb3[:, h, 128:192], identity=c.ident[0])
            n_ap, n_b = qtn[tile_i % 2]
            r_ap, r_b = qtr[tile_i % 2]
            c.act("activation", [p6_b], [n_b], out=n_ap, in_=p6.rearrange("p (h n) -> p h n", h=8), func=AF.Copy)
            c.act("activation", [p7_b], [r_b], out=r_ap[0:64], in_=p7[0:64, :].rearrange("p (h n) -> p h n", h=8),
                  func=AF.Copy)
            c.sp("dma_start", [n_b], [c.QTN_b[tile_i]], out=c.QTN[:, :, rows].rearrange("h d s -> d h s"), in_=n_ap)
            c.sp("dma_start", [r_b], [c.QTR_b[tile_i]], out=c.QTR[:, :, rows].rearrange("h d s -> d h s"),
                 in_=r_ap[0:64])
    ar.release()


def phase_mla_attn(c, xio, w_o, S):
    Tracker.phase = "mla_attn"
    x_io, xio_b = xio
    ar = c.arena
    ar.mark()
    nt = S // 128
    nq = S // 512
    OT = ar.alloc(M_HEADS * S, BF16).rearrange("p (h s) -> p h s", h=M_HEADS)
    OT_b = [Buf(f"OT{h}") for h in range(M_HEADS)]
    ar.mark()
    ktr, ktr_b = ar.alloc(S, BF16), Buf("ktr")
    c.dve("memset", [], [ktr_b], ktr[64:128, :], 0.0)
    c.sp("dma_start", c.KTR_b, [ktr_b], out=ktr[0:64, :], in_=c.KTR)
    hd = []
    for i in range(2):
        hd.append(dict(qn=(ar.alloc(S, BF16), Buf(f"qn{i}")), qr=(ar.alloc(S, BF16), Buf(f"qr{i}")),
                       kn=(ar.alloc(S, BF16), Buf(f"kn{i}")),
                       vh=(ar.alloc(S, BF16).rearrange("p (t e) -> p t e", e=128), Buf(f"vh{i}"))))
    for i in range(2):
        c.dve("memset", [], [hd[i]["qr"][1]], hd[i]["qr"][0][64:128, :], 0.0)
    NST = 4
    st_banks = [c.pb[0], c.pb[1], c.pb[2], c.pb[6]]
    pTl = [(ar.alloc(512, BF16), Buf(f"pT{i}")) for i in range(NST)]
    Lacc = [(ar.alloc(512, F32), Buf(f"Lacc{i}")) for i in range(2)]
    RLs = [(ar.alloc(512, F32), Buf(f"RLs{i}")) for i in range(2)]
    ones_f, ones_fb = ar.alloc(128, F32), Buf("ones_f")
    c.dve("memset", [], [ones_fb], ones_f, 1.0)
    cnt = 0
    for h in range(M_HEADS):
        H = hd[h % 2]
        qn, qn_b = H["qn"]
        qr, qr_b = H["qr"]
        kn, kn_b = H["kn"]
        vh, vh_b = H["vh"]
        c.sp("dma_start", c.QTN_b, [qn_b], out=qn, in_=c.QTN[h])
        c.sp("dma_start", c.QTR_b, [qr_b], out=qr[0:64, :], in_=c.QTR[h])
        c.sp("dma_start", [b[h] for b in c.KTN_b], [kn_b], out=kn, in_=c.KTN[h])
        c.sp("dma_start", c.VM_b, [vh_b], out=vh,
             in_=c.VM[:, h * 128:(h + 1) * 128].rearrange("(t p) e -> p t e", p=128))
        iters = [(qt, kt) for qt in range(nq) for kt in range(nt)]
        slots = {}

        def emit_st(i):
            nonlocal cnt
            qt, kt = iters[i]
            qs = slice(qt * 512, (qt + 1) * 512)
            ks = slice(kt * 128, (kt + 1) * 128)
            sT, sT_b = st_banks[cnt % NST]
            p_ap, p_b = pTl[cnt % NST]
            cnt += 1
            slots[i] = (sT, sT_b, p_ap, p_b)
            c.pe("matmul", [kn_b, qn_b], [sT_b], sT, lhsT=kn[:, ks], rhs=qn[:, qs], start=True, stop=False)
            c.pe("matmul", [ktr_b, qr_b], [sT_b], sT, lhsT=ktr[:, ks], rhs=qr[:, qs], start=False, stop=True)

        AHEAD = 3
        for i in range(min(AHEAD, len(iters))):
            emit_st(i)
        for i, (qt, kt) in enumerate(iters):
            if i + AHEAD < len(iters):
                emit_st(i + AHEAD)
            qs = slice(qt * 512, (qt + 1) * 512)
            oT, oT_b = c.pb[3 + qt % 2]
            la, la_b = Lacc[qt % 2]
            sT, sT_b, p_ap, p_b = slots.pop(i)
            c.act("activation", [sT_b], [p_b], out=p_ap, in_=sT, func=AF.Exp, scale=float(M_SCALE))
            c.pe("matmul", [vh_b, p_b], [oT_b], oT, lhsT=vh[:, kt, :], rhs=p_ap, start=(kt == 0), stop=(kt == nt - 1))
            if kt == 0:
                c.dve("tensor_copy", [p_b], [la_b], out=la, in_=p_ap)
            else:
                c.dve("tensor_tensor", [p_b, la_b], [la_b], out=la, in0=la, in1=p_ap, op=ALU.add)
            if kt == nt - 1:
                RB, RB_b = c.pb[5]
                c.pe("matmul", [ones_fb, la_b], [RB_b], RB, lhsT=ones_f, rhs=la, start=True, stop=True)
                R_ap, R_b = RLs[qt % 2]
                c.dve("reciprocal", [RB_b], [R_b], out=R_ap, in_=RB)
                c.dve("tensor_tensor", [oT_b, R_b], [OT_b[h]], out=OT[:, h, qs], in0=oT, in1=R_ap, op=ALU.mult)
    ar.release()
    Tracker.phase = "mla_out"
    wo = ar.alloc(M_HEADS * D, BF16).rearrange("p (k n) -> p k n", k=M_HEADS)
    wo_b = Buf("mwo")
    c.pool("dma_start", [], [wo_b], out=wo, in_=w_o.rearrange("(k p) n -> p k n", p=128))
    xv = x_io.rearrange("(n p) d -> n p d", p=128)
    for t in range(nt):
        for nb in range(2):
            o_ap, o_buf = c.psum_o[c.po_i % len(c.psum_o)]
            c.po_i += 1
            for h in range(M_HEADS):
                c.pe("matmul", [OT_b[h], wo_b], [o_buf], o_ap, lhsT=OT[:, h, t * 128:(t + 1) * 128],
                     rhs=wo[:, h, nb * 512:(nb + 1) * 512], start=(h == 0), stop=(h == M_HEADS - 1))
            csl = slice(nb * 512, (nb + 1) * 512)
            emit_resid(c, o_ap, o_buf, xv[t][:, csl], xio_b[t][nb], xv[t][:, csl], xio_b[t][nb], c.G[0][:, csl])
    ar.release()


def phase_final(c, xin, out, out_b, fg, S):
    Tracker.phase = "final"
    x_in, xin_b = xin
    c.sp("dma_start", [], [c.A[1]], out=c.A[0], in_=fg.partition_broadcast(128))
    xv = x_in.rearrange("(n p) d -> n p d", p=128)
    ov = out.rearrange("(n p) d -> n p d", p=128)
    for t in range(S // 128):
        slot = c.xslot
        c.xslot = (c.xslot + 1) % len(c.xt)
        xt, xb = c.xt[slot]
        hb_ap, hb_buf = c.hb[slot % len(c.hb)]
        ss_ap, ss_buf = c.ss[slot % len(c.ss)]
        c.sp("dma_start", list(xin_b[t]), [xb], out=xt, in_=xv[t])
        c.dve("scalar_tensor_tensor", [xb], [hb_buf, ss_buf], out=hb_ap, in0=xt, scalar=1.0, in1=xt,
              op0=ALU.mult, op1=ALU.mult, accum_out=ss_ap[:, 0:1])
        c.act("activation", [ss_buf, c.eps_rms[1]], [ss_buf], out=ss_ap[:, 1:2], in_=ss_ap[:, 0:1], func=AF.Sqrt,
              scale=1.0 / D, bias=c.eps_rms[0])
        c.dve("reciprocal", [ss_buf], [ss_buf], out=ss_ap[:, 2:3], in_=ss_ap[:, 1:2])
        c.dve("scalar_tensor_tensor", [xb, ss_buf, c.A[1]], [xb], out=xt, in0=xt, scalar=ss_ap[:, 2:3],
              in1=c.A[0], op0=ALU.mult, op1=ALU.mult)
        c.sp("dma_start", [xb], list(out_b[t]), out=ov[t], in_=xt)


def xbufs(S, name):
    return [[Buf(f"{name}{t}_{h}") for h in range(2)] for t in range(S // 128)]


SCRATCH_KIND = "Internal"


def alloc_ret_scratch(c, nc, S):
    nt = S // 128
    c.QT = nc.dram_tensor("QT", [R_QK, S], BF16, kind=SCRATCH_KIND).ap()
    c.KT = nc.dram_tensor("KT", [R_QK, S], BF16, kind=SCRATCH_KIND).ap()
    c.KF = nc.dram_tensor("KF", [S, R_QK], BF16, kind=SCRATCH_KIND).ap()
    c.KB = nc.dram_tensor("KB", [S, R_QK], BF16, kind=SCRATCH_KIND).ap()
    c.V = nc.dram_tensor("Vr", [S, R_VTOT], BF16, kind=SCRATCH_KIND).ap()
    c.Gs = nc.dram_tensor("Gs", [S, R_VTOT], F32, kind=SCRATCH_KIND).ap()
    c.SB = nc.dram_tensor("SBs", [nt, R_HEADS, 128, 1024], BF16, kind=SCRATCH_KIND).ap()
    c.QT_b = [[Buf("QT") for h in range(R_HEADS)] for _ in range(S // 512)]
    c.KT_b = [Buf("KT") for _ in range(S // 512)]
    c.KF_b = [Buf("KF") for _ in range(nt)]
    c.KB_b = [Buf("KB") for _ in range(nt)]
    c.V_b = [[Buf("V") for _ in range(4)] for _ in range(nt)]
    c.Gs_b = [[Buf("Gs") for _ in range(4)] for _ in range(nt)]
    c.SB_b = [[Buf("SB") for _ in range(R_HEADS)] for _ in range(nt)]


def alloc_tables(c, nc, S):
    c.CR = nc.dram_tensor("CR", [128, S], F32, kind=SCRATCH_KIND).ap()
    c.SR = nc.dram_tensor("SR", [128, S], F32, kind=SCRATCH_KIND).ap()
    c.CM = nc.dram_tensor("CM", [S, 32], F32, kind=SCRATCH_KIND).ap()
    c.SM = nc.dram_tensor("SM", [S, 32], F32, kind=SCRATCH_KIND).ap()
    c.CR_b, c.SR_b, c.CM_b, c.SM_b = Buf("CR"), Buf("SR"), Buf("CM"), Buf("SM")


def load_ctab(c, ctab_dram):
    c.ctab_dram = ctab_dram


def fetch_ctab(c):
    ar = c.arena
    ap, b = ar.alloc(CTW, F32), Buf("ctab")
    c.sp("dma_start", [], [b], out=ap, in_=c.ctab_dram)
    c.ctab = (ap, b)
    return c.ctab


def emit_copy_x(c, src, dst, dst_b, S):
    for t in range(S // 128):
        for hf in range(2):
            r_ap, r_buf = c.xr[c.xr_i % 2]
            c.xr_i += 1
            sl = (slice(t * 128, (t + 1) * 128), slice(hf * 512, (hf + 1) * 512))
            c.sp("dma_start", [], [r_buf], out=r_ap, in_=src[sl])
            c.sp("dma_start", [r_buf], [dst_b[t][hf]], out=dst[sl], in_=r_ap)


def build_ret_test(S):
    nc = bass.Bass("TRN2", target_bir_lowering=False)
    x = nc.dram_tensor("x", [S, D], F32, kind="ExternalInput").ap()
    pos = nc.dram_tensor("pos", [S], I32, kind="ExternalInput").ap()
    w_in = nc.dram_tensor("w_in", [D, R_IN], F32, kind="ExternalInput").ap()
    w_out = nc.dram_tensor("w_out", [R_VTOT, D], F32, kind="ExternalInput").ap()
    gn_g = nc.dram_tensor("gn_g", [R_VTOT], F32, kind="ExternalInput").ap()
    gn_b = nc.dram_tensor("gn_b", [R_VTOT], F32, kind="ExternalInput").ap()
    dec_f = nc.dram_tensor("dec_f", [4], F32, kind="ExternalInput").ap()
    dec_b = nc.dram_tensor("dec_b", [4], F32, kind="ExternalInput").ap()
    rows3 = nc.dram_tensor("rows3", [3, D], F32, kind="ExternalInput").ap()
    ident = nc.dram_tensor("ident", [128, 128], BF16, kind="ExternalInput").ap()
    ctab = nc.dram_tensor("ctab", [128, CTW], F32, kind="ExternalInput").ap()
    out = nc.dram_tensor("out", [S, D], F32, kind="ExternalOutput").ap()
    c = Ctx()
    c.tr = Tracker()
    c.ident_dram = ident
    alloc_tables(c, nc, S)
    alloc_ret_scratch(c, nc, S)
    with nc.sbuf_tensor("arena", [128, ARENA_BYTES // 4], F32) as ah, \
            nc.psum_tensor("psum", [128, 4096], F32) as ps:
        setup_common(c, nc, ah, ARENA_BYTES, ps)
        load_ctab(c, ctab)
        phase_setup_tables(c, pos, S)
        xb = xbufs(S, "x")
        ob = xbufs(S, "o")
        emit_copy_x(c, x, out, ob, S)
        c.arena.mark()
        dt = ret_tables(c, dec_f, dec_b)
        phase_ret_in(c, (x, xb), w_in, rows3, S, dt)
        phase_ret_bwd(c, S, dt)
        phase_ret_fwd(c, (out, ob), w_out, gn_g, gn_b, S, dt)
        c.arena.release()
        n = c.tr.emit(nc)
    print("ops", n)
    return nc


def build_ffn_test(S):
    nc = bass.Bass("TRN2", target_bir_lowering=False)
    x = nc.dram_tensor("x", [S, D], F32, kind="ExternalInput").ap()
    w_in = nc.dram_tensor("w_in", [D, 2 * DFF], F32, kind="ExternalInput").ap()
    w_out = nc.dram_tensor("w_out", [DFF, D], F32, kind="ExternalInput").ap()
    rows3 = nc.dram_tensor("rows3", [3, D], F32, kind="ExternalInput").ap()
    ident = nc.dram_tensor("ident", [128, 128], BF16, kind="ExternalInput").ap()
    out = nc.dram_tensor("out", [S, D], F32, kind="ExternalOutput").ap()
    c = Ctx()
    c.tr = Tracker()
    c.ident_dram = ident
    with nc.sbuf_tensor("arena", [128, ARENA_BYTES // 4], F32) as ah, \
            nc.psum_tensor("psum", [128, 4096], F32) as ps:
        setup_common(c, nc, ah, ARENA_BYTES, ps)
        phase_ffn(c, (x, xbufs(S, "x")), (out, xbufs(S, "o")), w_in, w_out, rows3, S)
        n = c.tr.emit(nc)
    print("ops", n)
    return nc


def build_mla_test(S):
    nc = bass.Bass("TRN2", target_bir_lowering=False)
    x = nc.dram_tensor("x", [S, D], F32, kind="ExternalInput").ap()
    pos = nc.dram_tensor("pos", [S], I32, kind="ExternalInput").ap()
    w_down = nc.dram_tensor("w_down", [D, M_DOWN], F32, kind="ExternalInput").ap()
    qng = nc.dram_tensor("qng", [Q_LORA], F32, kind="ExternalInput").ap()
    kvng = nc.dram_tensor("kvng", [KV_LORA], F32, kind="ExternalInput").ap()
    w_uq = nc.dram_tensor("w_uq", [Q_LORA, 1536], F32, kind="ExternalInput").ap()
    w_ukv = nc.dram_tensor("w_ukv", [KV_LORA, 2048], F32, kind="ExternalInput").ap()
    w_o = nc.dram_tensor("w_o", [1024, D], F32, kind="ExternalInput").ap()
    rows3 = nc.dram_tensor("rows3", [3, D], F32, kind="ExternalInput").ap()
    ident = nc.dram_tensor("ident", [128, 128], BF16, kind="ExternalInput").ap()
    ctab = nc.dram_tensor("ctab", [128, CTW], F32, kind="ExternalInput").ap()
    out = nc.dram_tensor("out", [S, D], F32, kind="ExternalOutput").ap()
    c = Ctx()
    c.tr = Tracker()
    c.ident_dram = ident
    alloc_tables(c, nc, S)
    alloc_mla_scratch(c, nc, S)
    with nc.sbuf_tensor("arena", [128, ARENA_BYTES // 4], F32) as ah, \
            nc.psum_tensor("psum", [128, 4096], F32) as ps:
        setup_common(c, nc, ah, ARENA_BYTES, ps)
        load_ctab(c, ctab)
        phase_setup_tables(c, pos, S)
        xb = xbufs(S, "x")
        ob = xbufs(S, "o")
        emit_copy_x(c, x, out, ob, S)
        phase_mla_in(c, (x, xb), w_down, qng, kvng, w_uq, w_ukv, rows3, S)
        phase_mla_attn(c, (out, ob), w_o, S)
        n = c.tr.emit(nc)
    print("ops", n)
    return nc


DEPTH = 4
_NC_CACHE = {}
W_SPECS = [
    ("norm_g", [DEPTH, 3, D]), ("final_norm_g", [D]), ("mod_w", [DEPTH, D, 9 * D]), ("mod_b", [DEPTH, 9 * D]),
    ("ffn_w_in", [DEPTH, 2, D, 2 * DFF]), ("ffn_w_out", [DEPTH, 2, DFF, D]),
    ("ret_w_in", [2, D, R_IN]), ("ret_w_out", [2, R_VTOT, D]), ("ret_gn_g", [2, R_VTOT]), ("ret_gn_b", [2, R_VTOT]),
    ("ret_decay_fwd", [2, 4]), ("ret_decay_bwd", [2, 4]),
    ("mla_w_down", [2, D, M_DOWN]), ("mla_q_norm_g", [2, Q_LORA]), ("mla_kv_norm_g", [2, KV_LORA]),
    ("mla_w_uq", [2, Q_LORA, 1536]), ("mla_w_ukv", [2, KV_LORA, 2048]), ("mla_w_o", [2, 1024, D]),
]


def build_full(S, depth=DEPTH, layers=None):
    nc = bass.Bass("TRN2", target_bir_lowering=False)
    x = nc.dram_tensor("x", [S, D], F32, kind="ExternalInput").ap()
    cvec = nc.dram_tensor("c", [D], F32, kind="ExternalInput").ap()
    pos = nc.dram_tensor("positions", [S], I32, kind="ExternalInput").ap()
    W = {n: nc.dram_tensor(n, shp, F32, kind="ExternalInput").ap() for n, shp in W_SPECS}
    ident = nc.dram_tensor("ident", [128, 128], BF16, kind="ExternalInput").ap()
    ctab = nc.dram_tensor("ctab", [128, CTW], F32, kind="ExternalInput").ap()
    out = nc.dram_tensor("out", [S, D], F32, kind="ExternalOutput").ap()
    xres = nc.dram_tensor("xres", [S, D], F32, kind=SCRATCH_KIND).ap()
    c = Ctx()
    c.tr = Tracker()
    c.ident_dram = ident
    c.modrows = nc.dram_tensor("modrows", [DEPTH, 3, 3, D], F32, kind=SCRATCH_KIND).ap()
    c.modrows_b = [Buf(f"modrows{i}") for i in range(DEPTH)]
    alloc_tables(c, nc, S)
    alloc_ret_scratch(c, nc, S)
    alloc_mla_scratch(c, nc, S)
    with nc.sbuf_tensor("arena", [128, ARENA_BYTES // 4], F32) as ah, \
            nc.psum_tensor("psum", [128, 4096], F32) as ps:
        setup_common(c, nc, ah, ARENA_BYTES, ps)
        load_ctab(c, ctab)
        phase_setup_tables(c, pos, S)
        phase_mod(c, cvec, W["mod_w"], W["mod_b"], W["norm_g"], depth)
        xin_b = xbufs(S, "xin")
        xb = xbufs(S, "xres")
        ob = xbufs(S, "out")
        for i in (layers if layers is not None else range(depth)):
            rows = lambda sl: (c.modrows[i, sl], [c.modrows_b[i]])
            src = (x, xin_b) if i == (layers[0] if layers is not None else 0) else (xres, xb)
            phase_ffn(c, src, (xres, xb), W["ffn_w_in"][i, 0], W["ffn_w_out"][i, 0], rows(0), S)
            j = i // 2
            if i % 2 == 0:
                c.arena.mark()
                dt = ret_tables(c, W["ret_decay_fwd"][j], W["ret_decay_bwd"][j])
                phase_ret_in(c, (xres, xb), W["ret_w_in"][j], rows(1), S, dt)
                phase_ret_bwd(c, S, dt)
                phase_ret_fwd(c, (xres, xb), W["ret_w_out"][j], W["ret_gn_g"][j], W["ret_gn_b"][j], S, dt)
                c.arena.release()
            else:
                phase_mla_in(c, (xres, xb), W["mla_w_down"][j], W["mla_q_norm_g"][j], W["mla_kv_norm_g"][j],
                             W["mla_w_uq"][j], W["mla_w_ukv"][j], rows(1), S)
                phase_mla_attn(c, (xres, xb), W["mla_w_o"][j], S)
            phase_ffn(c, (xres, xb), (xres, xb), W["ffn_w_in"][i, 1], W["ffn_w_out"][i, 1], rows(2), S)
        phase_final(c, (xres, xb), out, ob, W["final_norm_g"], S)
        n = c.tr.emit(nc)
    _NC_CACHE["last_tr"] = c.tr
    return nc, n


SEQ = 4096
BATCH = 8


def kernel(**inputs):
    if "nc" not in _NC_CACHE:
        _NC_CACHE["nc"] = build_full(SEQ)[0]
    nc = _NC_CACHE["nc"]
    cst = host_consts()
    f32 = lambda a: np.ascontiguousarray(np.asarray(a), dtype=np.float32)
    shared = {n: f32(inputs[n]) for n, _ in W_SPECS}
    shared["ident"] = cst["ident"]
    shared["ctab"] = cst["ctab"]
    x = f32(inputs["x"])
    cc = f32(inputs["c"])
    pos = np.ascontiguousarray(np.asarray(inputs["positions"]), dtype=np.int32)
    in_maps = []
    for b in range(BATCH):
        m = dict(shared)
        m["x"] = x[b]
        m["c"] = cc[b]
        m["positions"] = pos[b]
        in_maps.append(m)
    res = run_bass_kernel_spmd(nc, in_maps, core_ids=list(range(BATCH)))
    return np.stack([np.asarray(res.results[b]["out"]) for b in range(BATCH)]).astype(np.float32)
```
